# Optimizing a Trainium2 kernel written in Bass

```python
import math
import jax, jax.numpy as jnp
from jax import lax
import numpy as np

D_MODEL = 1024
BATCH = 4
SEQ = 4096
DEPTH = 1

MEM_LEN = 256
HEAD_DIM = 64
SSD_HEADS = 16
SSD_WIDTH = SSD_HEADS * HEAD_DIM
SSD_GROUPS = 2
SSD_STATE = 128
SSD_CONV = 4
CHUNK = 128
XBC_WIDTH = SSD_WIDTH + 2 * SSD_GROUPS * SSD_STATE
CF_GROUPS = 16
CF_WIDTH = CF_GROUPS * HEAD_DIM
CF_CONV = 31
MIX_WIDTH = SSD_WIDTH + CF_WIDTH
Z_END = SSD_WIDTH
XBC_END = Z_END + XBC_WIDTH
DT_END = XBC_END + SSD_HEADS
IN_WIDTH = DT_END + 2 * CF_WIDTH
X_HEADS = 4
X_HEAD_DIM = D_MODEL // X_HEADS
D_FF = int(math.ceil(8 * D_MODEL / 3 / 256) * 256)
EPS = 1e-6

kernel_name = "hybrid_ssd_conformer_xattn_block"


def rmsnorm(x, g):
    xf = x.astype(jnp.float32)
    y = xf * lax.rsqrt(jnp.mean(xf * xf, axis=-1, keepdims=True) + EPS)
    return (y * g.astype(jnp.float32)).astype(x.dtype)


def layernorm(x, g, b):
    xf = x.astype(jnp.float32)
    mu = jnp.mean(xf, axis=-1, keepdims=True)
    var = jnp.mean(jnp.square(xf - mu), axis=-1, keepdims=True)
    y = (xf - mu) * lax.rsqrt(var + EPS)
    return (y * g.astype(jnp.float32) + b.astype(jnp.float32)).astype(x.dtype)


def causal_depthwise_conv(x, w, b):
    K, C = w.shape
    y = lax.conv_general_dilated(
        x, w[:, None, :].astype(x.dtype), window_strides=(1,), padding=[(K - 1, 0)],
        dimension_numbers=("NWC", "WIO", "NWC"), feature_group_count=C)
    return y + b.astype(x.dtype)


def ssd_chunked(xh, dt, A, Bg, Cg):
    Bsz, S, H, P = xh.shape
    G, N = Bg.shape[-2:]
    R = H // G
    nc = S // CHUNK
    f32 = jnp.float32
    x = xh.astype(f32).reshape(Bsz, nc, CHUNK, G, R, P)
    dtc = dt.astype(f32).reshape(Bsz, nc, CHUNK, G, R)
    Bc = Bg.astype(f32).reshape(Bsz, nc, CHUNK, G, N)
    Cc = Cg.astype(f32).reshape(Bsz, nc, CHUNK, G, N)
    dA = dtc * A.astype(f32).reshape(G, R)
    Acs = jnp.cumsum(dA, axis=2)
    seg = Acs[:, :, :, None] - Acs[:, :, None]
    causal = jnp.tril(jnp.ones((CHUNK, CHUNK), dtype=bool))[:, :, None, None]
    decay = jnp.exp(jnp.where(causal, seg, -jnp.inf))
    CB = jnp.einsum("bclgn,bcsgn->bclsg", Cc, Bc)
    scores = CB[..., None] * decay * dtc[:, :, None]
    y_diag = jnp.einsum("bclsgr,bcsgrp->bclgrp", scores, x)
    decay_to_end = jnp.exp(Acs[:, :, -1:] - Acs)
    states = jnp.einsum("bclgn,bclgr,bclgrp->bcgrpn", Bc, decay_to_end * dtc, x)
    chunk_decay = jnp.exp(Acs[:, :, -1])

    def step(carry, inp):
        st, dec = inp
        new = carry * dec[..., None, None] + st
        return new, carry

    init = jnp.zeros((Bsz, G, R, P, N), f32)
    _, prev = lax.scan(step, init, (jnp.moveaxis(states, 1, 0), jnp.moveaxis(chunk_decay, 1, 0)))
    prev = jnp.moveaxis(prev, 0, 1)
    y_off = jnp.einsum("bclgn,bcgrpn,bclgr->bclgrp", Cc, prev, jnp.exp(Acs))
    return (y_diag + y_off).reshape(Bsz, S, H, P)


def setup_inputs(seed: int = 0) -> dict:
    key = jax.random.key(seed)
    ks = jax.random.split(key, 26)
    f32 = jnp.float32

    def nrm(k, shape, scale):
        return jax.random.normal(k, shape, f32) * scale

    def gain(k, shape):
        return 1.0 + 0.05 * jax.random.normal(k, shape, f32)

    dt0 = jnp.exp(jax.random.uniform(ks[5], (DEPTH, SSD_HEADS), f32, math.log(1e-3), math.log(1e-1)))
    return {
        "x": nrm(ks[0], (BATCH, SEQ, D_MODEL), 1.0),
        "mem": nrm(ks[1], (BATCH, MEM_LEN, D_MODEL), 1.0),
        "norm_mix_g": gain(ks[2], (DEPTH, D_MODEL)),
        "w_in": nrm(ks[3], (DEPTH, D_MODEL, IN_WIDTH), D_MODEL ** -0.5),
        "ssd_conv_w": nrm(ks[4], (DEPTH, SSD_CONV, XBC_WIDTH), SSD_CONV ** -0.5),
        "ssd_conv_b": nrm(ks[6], (DEPTH, XBC_WIDTH), 0.02),
        "ssd_dt_bias": dt0 + jnp.log(-jnp.expm1(-dt0)),
        "ssd_A_log": jnp.log(jax.random.uniform(ks[7], (DEPTH, SSD_HEADS), f32, 1.0, 16.0)),
        "ssd_D": gain(ks[8], (DEPTH, SSD_HEADS)),
        "ssd_norm_g": gain(ks[9], (DEPTH, SSD_WIDTH)),
        "cf_conv_w": nrm(ks[10], (DEPTH, CF_CONV, CF_WIDTH), CF_CONV ** -0.5),
        "cf_conv_b": nrm(ks[11], (DEPTH, CF_WIDTH), 0.02),
        "cf_ln_g": gain(ks[12], (DEPTH, CF_WIDTH)),
        "cf_ln_b": nrm(ks[13], (DEPTH, CF_WIDTH), 0.02),
        "w_out": nrm(ks[14], (DEPTH, MIX_WIDTH, D_MODEL), MIX_WIDTH ** -0.5),
        "norm_xattn_g": gain(ks[15], (DEPTH, D_MODEL)),
        "norm_mem_g": gain(ks[16], (DEPTH, D_MODEL)),
        "w_q": nrm(ks[17], (DEPTH, D_MODEL, D_MODEL), D_MODEL ** -0.5),
        "w_kv": nrm(ks[18], (DEPTH, D_MODEL, 2 * D_MODEL), D_MODEL ** -0.5),
        "w_o": nrm(ks[19], (DEPTH, D_MODEL, D_MODEL), D_MODEL ** -0.5),
        "norm_ffn_g": gain(ks[20], (DEPTH, D_MODEL)),
        "w_gate": nrm(ks[21], (DEPTH, D_MODEL, D_FF), D_MODEL ** -0.5),
        "w_up": nrm(ks[22], (DEPTH, D_MODEL, D_FF), D_MODEL ** -0.5),
        "w_down": nrm(ks[23], (DEPTH, D_FF, D_MODEL), D_FF ** -0.5),
        "norm_final_g": gain(ks[24], (D_MODEL,)),
    }


def reference(x, mem, norm_mix_g, w_in, ssd_conv_w, ssd_conv_b, ssd_dt_bias, ssd_A_log, ssd_D,
              ssd_norm_g, cf_conv_w, cf_conv_b, cf_ln_g, cf_ln_b, w_out, norm_xattn_g, norm_mem_g,
              w_q, w_kv, w_o, norm_ffn_g, w_gate, w_up, w_down, norm_final_g):
    Bsz, S, _ = x.shape
    M = mem.shape[1]
    for i in range(DEPTH):
        h = rmsnorm(x, norm_mix_g[i])
        proj = h @ w_in[i]
        z, xbc, dt_raw, glu = jnp.split(proj, [Z_END, XBC_END, DT_END], axis=-1)

        xbc = jax.nn.silu(causal_depthwise_conv(xbc, ssd_conv_w[i], ssd_conv_b[i]))
        xs, Bm, Cm = jnp.split(xbc, [SSD_WIDTH, SSD_WIDTH + SSD_GROUPS * SSD_STATE], axis=-1)
        dt = jax.nn.softplus(dt_raw.astype(jnp.float32) + ssd_dt_bias[i].astype(jnp.float32))
        A = -jnp.exp(ssd_A_log[i].astype(jnp.float32))
        xh = xs.reshape(Bsz, S, SSD_HEADS, HEAD_DIM)
        y = ssd_chunked(xh, dt, A,
                        Bm.reshape(Bsz, S, SSD_GROUPS, SSD_STATE),
                        Cm.reshape(Bsz, S, SSD_GROUPS, SSD_STATE))
        y = y + ssd_D[i].astype(jnp.float32)[:, None] * xh.astype(jnp.float32)
        y = y.reshape(Bsz, S, SSD_WIDTH) * jax.nn.silu(z.astype(jnp.float32))
        y = rmsnorm(y.reshape(Bsz, S, SSD_GROUPS, SSD_WIDTH // SSD_GROUPS),
                    ssd_norm_g[i].reshape(SSD_GROUPS, SSD_WIDTH // SSD_GROUPS))
        y = y.reshape(Bsz, S, SSD_WIDTH).astype(x.dtype)

        a, g = jnp.split(glu, 2, axis=-1)
        u = a * jax.nn.sigmoid(g)
        u = causal_depthwise_conv(u, cf_conv_w[i], cf_conv_b[i])
        u = jax.nn.silu(layernorm(u, cf_ln_g[i], cf_ln_b[i]))

        x = x + jnp.concatenate([y, u], axis=-1) @ w_out[i]

        q = (rmsnorm(x, norm_xattn_g[i]) @ w_q[i]).reshape(Bsz, S, X_HEADS, X_HEAD_DIM)
        kv = rmsnorm(mem, norm_mem_g[i]) @ w_kv[i]
        k, v = jnp.split(kv, 2, axis=-1)
        k = k.reshape(Bsz, M, X_HEADS, X_HEAD_DIM)
        v = v.reshape(Bsz, M, X_HEADS, X_HEAD_DIM)
        s = jnp.einsum("bshd,bmhd->bhsm", q.astype(jnp.float32), k.astype(jnp.float32))
        p = jax.nn.softmax(s * (X_HEAD_DIM ** -0.5), axis=-1).astype(v.dtype)
        o = jnp.einsum("bhsm,bmhd->bshd", p, v).reshape(Bsz, S, D_MODEL)
        x = x + o @ w_o[i]

        hf = rmsnorm(x, norm_ffn_g[i])
        x = x + (jax.nn.silu(hf @ w_gate[i]) * (hf @ w_up[i])) @ w_down[i]
    return rmsnorm(x, norm_final_g)
```

```python
import contextlib
import os
import numpy as np
import ml_dtypes
import concourse.bass as bass
import concourse.mybir as mybir
from concourse.bass_utils import run_bass_kernel_spmd

F32 = mybir.dt.float32
BF16 = mybir.dt.bfloat16
ALU = mybir.AluOpType
AF = mybir.ActivationFunctionType

ENGS = ["pe", "dve", "act", "pool", "sp"]
SEG_CFG = {0: dict(lat=0.4, tsw=1.2), 1: dict(lat=0.4, tsw=1.2), 2: dict(lat=0.7, tsw=1.0, seed=396330, jit=0.05),
           3: dict(lat=0.7, tsw=1.0, seed=748492, jit=0.05)}

D = 1024
SEQ = 4096
NB = 4
HALF = 2048
NT = 16
MT = 512
MEM = 256
INW = 4624
DFF = 2816
NFF = 22
EPS = 1e-6
XBC0 = 1024
DT0 = 2560
A0 = 2576
G0 = 3600


class Sched:
    LAT = 0.5
    DELTA_DEFAULT = 0.5

    def __init__(self, nc, n_dma_sems=4):
        self.nc = nc
        self.ops = []
        self.seg = 0
        self.n_dma = n_dma_sems
        self._stream = {}
        self._seg_counter = 0

    def barrier(self):
        self.seg += 1

    def add(self, eng, fn, reads=(), writes=(), dma=False, cost=1.0, tset=None):
        self.ops.append(dict(eng=eng, fn=fn, reads=tuple(reads), writes=tuple(writes), dma=int(dma), cost=float(cost),
                             seg=self.seg, idx=len(self.ops), tset=tset))

    def _schedule_segment(self, ops):
        n = len(ops)
        cfg = dict(lat=self.LAT, tsw=1.4, seed=0, jit=0.0)
        cfg.update(SEG_CFG.get(self._seg_counter, {}))
        env = os.environ.get("SCHED_CFG")
        if env:
            import json as _json
            cfg.update(_json.loads(env))
        self._seg_counter += 1
        LATV = cfg["lat"]
        rng = np.random.default_rng(cfg["seed"]) if cfg["jit"] > 0 else None
        DMAF = float(os.environ.get("DMAF", "0.75"))
        PEW = float(os.environ.get("PEW", "1.0"))
        preds = [set() for _ in range(n)]
        raw = [set() for _ in range(n)]
        last_writer, readers = {}, {}
        for i, op in enumerate(ops):
            for r in op["reads"]:
                lw = last_writer.get(r)
                if lw is not None:
                    preds[i].add(lw)
                    raw[i].add(lw)
            for w in op["writes"]:
                lw = last_writer.get(w)
                if lw is not None:
                    preds[i].add(lw)
                for rd in readers.get(w, ()):
                    if rd != i:
                        preds[i].add(rd)
            for r in op["reads"]:
                readers.setdefault(r, []).append(i)
            for w in op["writes"]:
                last_writer[w] = i
                readers[w] = []
        succs = [[] for _ in range(n)]
        for i in range(n):
            for p in preds[i]:
                succs[p].append(i)

        def dur(op):
            return max(0.06, op["cost"] * DMAF) if op["dma"] else op["cost"]

        def done_lat(op):
            return (2.0 + op["cost"]) if op["dma"] else op["cost"]

        prio = [0.0] * n
        for i in range(n - 1, -1, -1):
            m = 0.0
            for sidx in succs[i]:
                if prio[sidx] > m:
                    m = prio[sidx]
            prio[i] = m + done_lat(ops[i]) * (PEW if ops[i]["eng"] == "pe" else 1.0)
        if rng is not None:
            jit = rng.random(n)
            prio = [p * (1.0 + cfg["jit"] * (j - 0.5)) for p, j in zip(prio, jit)]
        npred = [len(preds[i]) for i in range(n)]
        ready_t = [0.0] * n
        finish = [0.0] * n
        free = {e: 0.0 for e in ENGS}
        ready = {e: [] for e in ENGS}
        for i in range(n):
            if npred[i] == 0:
                ready[ops[i]["eng"]].append(i)
        order = {e: [] for e in ENGS}
        left = n
        cur_set = [None]
        TSW = cfg["tsw"]
        DELTA = float(os.environ.get('SCHED_DELTA', self.DELTA_DEFAULT))
        while left:
            best = None
            for e in ENGS:
                if not ready[e]:
                    continue
                fe = free[e]
                cand = None
                stts = {}
                mn = None
                for i in ready[e]:
                    stt = ready_t[i] if ready_t[i] > fe else fe
                    if e == "act" and ops[i]["tset"] is not None and cur_set[0] is not None and ops[i]["tset"] != cur_set[0]:
                        stt = stt + TSW
                    stts[i] = stt
                    if mn is None or stt < mn:
                        mn = stt
                for i in ready[e]:
                    stt = stts[i]
                    if stt > mn + DELTA:
                        continue
                    key = (-prio[i], stt, i)
                    if cand is None or key < cand[0]:
                        cand = (key, i, stt)
                if best is None or cand[2] < best[2] - 1e-9 or (abs(cand[2] - best[2]) <= 1e-9 and cand[0] < best[0]):
                    best = cand
            _, i, stt = best
            op = ops[i]
            e = op["eng"]
            ready[e].remove(i)
            if e == "act" and op["tset"] is not None:
                cur_set[0] = op["tset"]
            free[e] = stt + dur(op)
            finish[i] = stt + done_lat(op)
            order[e].append(i)
            left -= 1
            for sidx in succs[i]:
                rt = finish[i] + (LATV if ops[sidx]["eng"] != e or op["dma"] else 0.0)
                if rt > ready_t[sidx]:
                    ready_t[sidx] = rt
                npred[sidx] -= 1
                if npred[sidx] == 0:
                    ready[ops[sidx]["eng"]].append(sidx)
        return order, preds, raw

    def emit(self, final_eng="sp"):
        nc = self.nc
        nseg = self.seg + 1
        segs = [[] for _ in range(nseg)]
        for op in self.ops:
            segs[op["seg"]].append(op)
        gorder = {e: [] for e in ENGS}
        dma_val = {}
        dma_rr = {e: 0 for e in ENGS}
        known_pos = {e: {e2: -1 for e2 in ENGS} for e in ENGS}
        known_dma = {e: {} for e in ENGS}
        pending_pos = {e: {} for e in ENGS}
        pending_dma = {e: {} for e in ENGS}
        for sops in segs:
            if not sops:
                continue
            order, preds, raw = self._schedule_segment(sops)
            rec = {}
            for e in ENGS:
                for i in order[e]:
                    op = sops[i]
                    r = dict(op=op, eng=e, pos=None, signal=False, waits_pos={}, waits_dma={}, dma_tok=None)
                    if op["dma"]:
                        j = dma_rr[e]
                        dma_rr[e] = (j + 1) % (self.n_dma if e == "sp" else 3)
                        key = ("dma", e, j)
                        prev = dma_val.get(key, 0)
                        val = prev + 16 * op["dma"]
                        dma_val[key] = val
                        r["dma_tok"] = (key, val)
                        r["dma_prev"] = (key, prev) if prev > 0 else None
                    else:
                        r["pos"] = len(gorder[e])
                        gorder[e].append(r)
                    rec[i] = r
            for e in ENGS:
                for i in order[e]:
                    r = rec[i]
                    op = sops[i]
                    wp, wd = {}, {}
                    for e2, p in pending_pos[e].items():
                        if known_pos[e][e2] < p:
                            wp[e2] = p
                    for kk, vv in pending_dma[e].items():
                        if known_dma[e].get(kk, 0) < vv:
                            wd[kk] = vv
                    pending_pos[e] = {}
                    pending_dma[e] = {}
                    for p in preds[i]:
                        pr = rec[p]
                        if pr["dma_tok"] is not None:
                            kk, vv = pr["dma_tok"]
                            if known_dma[e].get(kk, 0) < vv and wd.get(kk, 0) < vv:
                                wd[kk] = vv
                        elif pr["eng"] != e or e != "pe":
                            e2 = pr["eng"]
                            if known_pos[e][e2] < pr["pos"] and wp.get(e2, -1) < pr["pos"]:
                                wp[e2] = pr["pos"]
                    if op["dma"] and r["dma_prev"] is not None:
                        kk, vv = r["dma_prev"]
                        if known_dma[e].get(kk, 0) < vv and wd.get(kk, 0) < vv:
                            wd[kk] = vv
                    for e2, p in wp.items():
                        known_pos[e][e2] = p
                        gorder[e2][p]["signal"] = True
                    for kk, vv in wd.items():
                        known_dma[e][kk] = vv
                    r["waits_pos"] = wp
                    r["waits_dma"] = wd
                    r["stream_eng"] = e
            for e in ENGS:
                for e2 in ENGS:
                    if e2 != e and gorder[e2]:
                        pending_pos[e][e2] = len(gorder[e2]) - 1
                        gorder[e2][-1]["signal"] = True
                for kk, vv in dma_val.items():
                    pending_dma[e][kk] = vv
            self._last_rec = rec
            for e in ENGS:
                for i in order[e]:
                    self._stream.setdefault(e, []).append(rec[i])
        for e in ENGS:
            if gorder[e]:
                gorder[e][-1]["signal"] = True
        count = {e: 0 for e in ENGS}
        for e in ENGS:
            for r in gorder[e]:
                if r["signal"]:
                    count[e] += 1
                    r["val"] = count[e]
        keys = [("eng", e) for e in ENGS] + sorted(dma_val.keys())
        with contextlib.ExitStack() as st:
            sems = {}
            for kk in keys:
                sems[kk] = st.enter_context(nc.semaphore("s_" + "_".join(str(x) for x in kk)))
            fin_waits = []
            for e in ENGS:
                if gorder[e] and e != final_eng:
                    fin_waits.append((("eng", e), count[e]))
            for kk, vv in dma_val.items():
                fin_waits.append((kk, vv))
            block = st.enter_context(nc.Block())
            streams = self._stream

            def run(ename):
                def body(eng):
                    for r in streams.get(ename, []):
                        for e2, p in r["waits_pos"].items():
                            eng.wait_ge(sems[("eng", e2)], gorder[e2][p]["val"])
                        for kk, vv in r["waits_dma"].items():
                            eng.wait_ge(sems[kk], vv)
                        res = r["op"]["fn"](eng)
                        if r["op"]["dma"]:
                            if not isinstance(res, (list, tuple)):
                                res = [res]
                            assert len(res) == int(r["op"]["dma"])
                            for ins in res:
                                ins.then_inc(sems[r["dma_tok"][0]], 16)
                        elif r["signal"]:
                            if isinstance(res, (list, tuple)):
                                res = res[-1]
                            res.then_inc(sems[("eng", ename)], 1)
                    if ename == final_eng:
                        for kk, vv in fin_waits:
                            eng.wait_ge(sems[kk], vv)
                return body

            block.tensor(run("pe"))
            block.vector(run("dve"))
            block.scalar(run("act"))
            block.gpsimd(run("pool"))
            block.sync(run("sp"))


class Rot:
    def __init__(self, K, name, n, shape, dt):
        self.bufs = [K.sb(f"{name}{i}", shape, dt) for i in range(n)]
        self.keys = [f"{name}{i}" for i in range(n)]
        self.i = 0

    def next(self):
        j = self.i
        self.i = (j + 1) % len(self.bufs)
        return self.bufs[j], self.keys[j]


class K:
    def __init__(self, dbg=None):
        self.dbg = dbg
        self.nc = bass.Bass("TRN2", target_bir_lowering=False)
        self.st = contextlib.ExitStack()
        self.stacks = [self.st]
        self.S = Sched(self.nc)
        self.dram = {}

    def din(self, name, shape, dt=F32):
        ap = self.nc.dram_tensor(name, list(shape), dt, kind="ExternalInput").ap()
        self.dram[name] = ap
        return ap

    def sb(self, name, shape, dt):
        return self.stacks[-1].enter_context(self.nc.sbuf_tensor("sb_" + name, list(shape), dt))

    def push(self):
        self.stacks.append(contextlib.ExitStack())

    def pop(self):
        self.S.barrier()
        self.stacks.pop().close()

    @staticmethod
    def _fs(ap):
        try:
            return float(ap.free_size())
        except Exception:
            return 512.0

    TSETS = {AF.Silu: "silu", AF.Exp: "lnexp", AF.Ln: "lnexp", AF.Sigmoid: "sigm"}

    def act(self, r, w, **kw):
        c = 0.25 + self._fs(kw["out"]) / 1200.0
        self.S.add("act", lambda e: e.activation(**kw), r, w, cost=c, tset=self.TSETS.get(kw["func"]))

    def dve(self, opname, r, w, **kw):
        o = kw.get("out", kw.get("ap"))
        c = 0.08 + self._fs(o) / 960.0
        self.S.add("dve", lambda e: getattr(e, opname)(**kw), r, w, cost=c)

    def pool(self, opname, r, w, **kw):
        o = kw.get("out", kw.get("ap"))
        c = 0.15 + self._fs(o) / 500.0
        self.S.add("pool", lambda e: getattr(e, opname)(**kw), r, w, cost=c)

    def mm(self, lst, r, w):
        def f(e):
            ins = None
            for (o, l, rh, st, sp) in lst:
                ins = e.matmul(out=o, lhsT=l, rhs=rh, start=st, stop=sp)
            return ins
        c = 0.1
        for (o, l, rh, st, sp) in lst:
            n = self._fs(o)
            c += max(0.065, n / 2400.0) * (4.0 if l.dtype == F32 else 1.0)
        self.S.add("pe", f, r, w, cost=c)

    def tr(self, lst, ident, r, w):
        def f(e):
            ins = None
            for (o, i) in lst:
                ins = e.transpose(out=o, in_=i, identity=ident)
            return ins
        self.S.add("pe", f, r, w, cost=0.1 + 0.11 * len(lst))

    def dma(self, eng, out, in_, r, w):
        try:
            nbytes = float(out.nbytes())
        except Exception:
            nbytes = 5e5
        self.S.add(eng, lambda e: e.dma_start(out=out, in_=in_), r, w, dma=1, cost=nbytes / 1.5e5)


def bc(ap, shape):
    return ap.to_broadcast(list(shape))


import os
LVL = int(os.environ.get("SSD_LVL", "9"))


def build(dbg=None):
    k = K(dbg)
    nc = k.nc
    S = k.S
    x_own = k.din("x_own", [HALF, D])
    x_prev = k.din("x_prev", [HALF, D])
    flag_d = k.din("flag", [128, 1])
    mem_d = k.din("mem", [MEM, D])
    w_in = k.din("w_in", [D, INW])
    w_out = k.din("w_out", [2 * D, D])
    w_q = k.din("w_q", [D, D])
    w_kv = k.din("w_kv", [D, 2 * D])
    w_o = k.din("w_o", [D, D])
    w_gate = k.din("w_gate", [D, DFF])
    w_up = k.din("w_up", [D, DFF])
    w_down = k.din("w_down", [DFF, D])
    gbc_d = k.din("gbc", [128, 6, D])
    dbc_d = k.din("dbc", [128, D])
    sm_d = k.din("small", [128, 64])
    cw_d = k.din("cw", [128, 12, 4])
    cb_d = k.din("cb", [128, 12])
    fw_d = k.din("fw", [128, 8, 31])
    fv_d = k.din("fv", [128, 8, 3])
    ident_d = k.din("ident", [128, 128], BF16)
    cst_d = k.din("cst", [128, 4, 128])
    out_d = nc.dram_tensor("out", [HALF, D], F32, kind="ExternalOutput").ap()
    xr_d = nc.dram_tensor("xr", [HALF, D], F32, kind="Internal").ap()

    xo_t = x_own.rearrange("(t p) d -> t p d", p=128)
    xp_t = x_prev.rearrange("(t p) d -> t p d", p=128)
    xr_t = xr_d.rearrange("(t p) d -> t p d", p=128)
    out_t = out_d.rearrange("(t p) d -> t p d", p=128)

    gains = {}

    def load_gains(idxs):
        k.gcount = getattr(k, "gcount", 0) + 1
        gb = k.sb(f"gains{k.gcount}", [128, len(idxs), D], F32)
        for i, gi in enumerate(idxs):
            key = f"gain{gi}_{k.gcount}"
            k.dma("sp", gb[:, i, :], gbc_d[:, gi, :], [], [key])
            gains[gi] = (gb[:, i, :], key)
    sm = k.sb("sm", [128, 64], F32)
    cw = k.sb("cw", [128, 12, 4], F32)
    cb = k.sb("cb", [128, 12], F32)
    fw = k.sb("fw", [128, 8, 31], F32)
    fv = k.sb("fv", [128, 8, 3], F32)
    ident = k.sb("ident", [128, 128], BF16)
    cst = k.sb("cst", [128, 4, 128], F32)
    flag = k.sb("flag", [128, 1], F32)
    epsT = k.sb("epsT", [128, 1], F32)
    ones_bf = k.sb("ones_bf", [128, 128], BF16)
    for (dst, src, nm) in [ (sm, sm_d, "sm"), (cw, cw_d, "cw"),
                           (cb, cb_d, "cb"), (fw, fw_d, "fw"), (fv, fv_d, "fv"), (ident, ident_d, "ident"),
                           (cst, cst_d, "cst"), (flag, flag_d, "flag")]:
        k.dma("sp", dst[:], src, [], [nm])
    k.dve("memset", [], ["epsT"], ap=epsT[:], constant=EPS)
    k.dve("memset", [], ["ones_bf"], ap=ones_bf[:], constant=1.0)

    PS = k.st.enter_context(nc.psum_tensor("PS", [128, 8, 512], F32))

    def pb(i):
        return PS[:, i, :]

    def pbk(i):
        return f"ps{i}"

    ss_rot = Rot(k, "ss", int(os.environ.get("SSROT", "4")), [128, 4], F32)
    sq_rot = Rot(k, "sq", 1, [128, D], BF16)
    h_rot = Rot(k, "hb", int(os.environ.get("HROT", "2")), [128, D], BF16)

    def rstd_from_ss(ss, sskey, n, inv_n):
        k.act([sskey, "epsT"], [sskey], out=ss[:, 0:n], in_=ss[:, 0:n], func=AF.Ln, scale=inv_n, bias=epsT[:])
        k.act([sskey], [sskey], out=ss[:, 0:n], in_=ss[:, 0:n], func=AF.Exp, scale=-0.5)

    def norm_tile(x_ap, xkey, gidx, h_out, hkey):
        ss, sskey = ss_rot.next()
        sq, sqkey = sq_rot.next()
        k.act([xkey], [sqkey, sskey], out=sq[:], in_=x_ap, func=AF.Square, accum_out=ss[:, 0:1])
        rstd_from_ss(ss, sskey, 1, 1.0 / D)
        k.dve("scalar_tensor_tensor", [xkey, sskey, gains[gidx][1]], [hkey], out=h_out, in0=x_ap, scalar=ss[:, 0:1],
              in1=gains[gidx][0], op0=ALU.mult, op1=ALU.mult)

    def transpose_to(h_ap, hkey, dstT, dkey, col0, bank):
        pT = pb(bank).bitcast(BF16)
        k.tr([(pT[:, kc * 128:(kc + 1) * 128], h_ap[:, kc * 128:(kc + 1) * 128]) for kc in range(8)], ident[:],
             [hkey, "ident"], [pbk(bank)])
        k.act([pbk(bank)], [dkey], out=dstT[:, :, col0:col0 + 128], in_=pT.rearrange("p (a b) -> p a b", a=8),
              func=AF.Copy)

    x_res = None

    src_t = xo_t if dbg == "nomixer" else xr_t
    tri = cst[:, 0, :]
    su = cst[:, 1, :]
    onesf = cst[:, 2, :]
    mask01 = cst[:, 3, :]
    do_ssd = dbg in (None, "ssd")
    do_conf = dbg in (None, "conf")

    if do_ssd:
        k.push()
        load_gains([0, 5])
        dbc = k.sb("dbc", [128, D], F32)
        k.dma("sp", dbc[:], dbc_d, [], ["dbc"])
        xt_rot = Rot(k, "xta", int(os.environ.get("XTA", "3")), [128, D], F32)
        xb3_rot = Rot(k, "xtc", int(os.environ.get("XTC", "1")), [128, D], F32)
        Wa = k.sb("Wa", [128, 8, 2576], BF16)
        Wot = k.sb("Wot", [128, 8, D], BF16)
        w_in_v = w_in.rearrange("(kc p) n -> p kc n", p=128)
        for q in range(4):
            k.dma("pool", Wa[:, :, 1024 + 384 * q:1024 + 384 * (q + 1)], w_in_v[:, :, 1024 + 384 * q:1024 + 384 * (q + 1)], [], [f"Wa_x{q}"])
        k.dma("pool", Wa[:, :, 2560:2576], w_in_v[:, :, 2560:2576], [], ["Wa_dt"])
        k.dma("pool", Wa[:, :, 0:1024], w_in_v[:, :, 0:1024], [], ["Wa_z"])
        k.dma("pool", Wot[:], w_out[0:D, :].rearrange("(kc p) n -> p kc n", p=128), [], ["Wot"])
        diag4 = k.sb("diag4", [128, 12, 4, 128], BF16)
        for c in range(12):
            k.dve("tensor_tensor", ["ident", "cw"], [f"diag4_{c}"], out=diag4[:, c, :, :], in0=bc(ident[:].unsqueeze(1), [128, 4, 128]),
                  in1=bc(cw[:, c, :].unsqueeze(2), [128, 4, 128]), op=ALU.mult)
        hT_r = Rot(k, "hT1_", 2, [128, 8, MT], BF16)
        xbcT_r = Rot(k, "xbcT_", 2, [128, 12, MT], BF16)
        dt4_r = Rot(k, "dt4_", 2, [128, 4, 16], F32)
        dA4_r = Rot(k, "dA4_", 2, [128, 4, 16], F32)
        sz_r = Rot(k, "sz_", 2, [128, 4, D], BF16)
        pre_rot = Rot(k, "pre", int(os.environ.get("PRE", "3")), [128, 515], BF16)
        hist = k.sb("hist", [128, 12, 3], BF16)
        dtp = k.sb("dtp", [128, 4, 16], F32)
        nA = k.sb("nA", [128, 16], F32)
        ex3_r = Rot(k, "ex3_", 2, [128, 48], F32)
        xdt_r = Rot(k, "xdt_", 2, [128, D], BF16)
        xD_r = Rot(k, "xD_", 1, [128, D], BF16)
        Btok_r = Rot(k, "Btok_", 2, [128, 256], BF16)
        E_r = Rot(k, "E_", 2, [128, 16, 128], BF16)
        CBm_r = Rot(k, "CBm_", 2, [128, 2, 128], BF16)
        SUhi = k.sb("SUhi", [128, 8, 128], BF16)
        SUlo = k.sb("SUlo", [128, 8, 128], BF16)
        su_bf = k.sb("su_bf", [128, 128], BF16)
        tri_bf = k.sb("tri_bf", [128, 128], BF16)
        dAh = k.sb("dAh", [128, 16], BF16)
        dAh32 = k.sb("dAh32", [128, 16], F32)
        dAl = k.sb("dAl", [128, 16], F32)
        k.dve("tensor_copy", ["cst"], ["su_bf"], out=su_bf[:], in_=cst[:, 1, :])
        k.dve("tensor_copy", ["cst"], ["tri_bf"], out=tri_bf[:], in_=cst[:, 0, :])
        xw_r = Rot(k, "xw_", 2, [128, D], BF16)
        y1 = k.sb("y1", [128, D], F32)
        yn = k.sb("yn", [128, D], BF16)
        ynT = k.sb("ynT", [128, 8, 128], BF16)
        Sst = k.sb("Sst", [128, D], F32)
        Sbf = k.sb("Sbf", [128, D], BF16)
        k.dve("memset", [], ["hist"], ap=hist[:], constant=0.0)
        k.dve("memset", [], ["Sst"], ap=Sst[:], constant=0.0)
        k.dve("memset", [], ["Sbf"], ap=Sbf[:], constant=0.0)
        k.act(["sm"], ["nA"], out=nA[:], in_=sm[:, 16:32], func=AF.Exp)
        k.dve("tensor_scalar", ["nA"], ["nA"], out=nA[:], in0=nA[:], scalar1=-1.0, scalar2=None, op0=ALU.mult)
        ATB = int(os.environ.get("ATB", "2"))
        YTB = int(os.environ.get("YTB", "0"))
        pT0 = pb(ATB).bitcast(BF16)
        pTy = pb(YTB).bitcast(BF16)
        G1 = gains[5]

        mt_state = {}

        def src_tile(g_mt, st):
            return (xp_t if g_mt < 4 else xo_t)[(g_mt % 4) * 4 + st]

        mt_h = {}

        def F1(mt, pair):
            if pair == 0:
                mt_h[mt] = hT_r.next()
            hT, hTk = mt_h[mt]
            items = []
            for st in (2 * pair, 2 * pair + 1):
                xt, xtk = xt_rot.next()
                k.dma("sp", xt[:], src_tile(mt, st), [], [xtk])
                ss, sskey = ss_rot.next()
                sq, sqkey = sq_rot.next()
                k.act([xtk], [sqkey, sskey], out=sq[:], in_=xt[:], func=AF.Square, accum_out=ss[:, 0:1])
                rstd_from_ss(ss, sskey, 1, 1.0 / D)
                items.append((st, xt, xtk, ss, sskey))
            hbs = []
            for (st, xt, xtk, ss, sskey) in items:
                hb, hk = h_rot.next()
                k.dve("scalar_tensor_tensor", [xtk, sskey, gains[0][1]], [hk], out=hb[:], in0=xt[:], scalar=ss[:, 0:1],
                      in1=gains[0][0], op0=ALU.mult, op1=ALU.mult)
                hbs.append((st, hb, hk))
            F1B = int(os.environ.get("F1B", "0"))
            for i, (st, hb, hk) in enumerate(hbs):
                pT = pb(F1B + i).bitcast(BF16)
                k.tr([(pT[:, kc * 128:(kc + 1) * 128], hb[:, kc * 128:(kc + 1) * 128]) for kc in range(8)], ident[:],
                     [hk, "ident"], [pbk(F1B + i)])
            for i, (st, hb, hk) in enumerate(hbs):
                pT = pb(F1B + i).bitcast(BF16)
                k.act([pbk(F1B + i)], [hTk], out=hT[:, :, st * 128:(st + 1) * 128], in_=pT.rearrange("p (a b) -> p a b", a=8), func=AF.Copy)

        def F2(mt, q):
            hT, hTk = mt_h[mt]
            if q == 0:
                xbcT, xbk = xbcT_r.next()
                dt4, dt4k = dt4_r.next()
                dA4, dA4k = dA4_r.next()
                szm, szk = sz_r.next()
                mt_state[mt] = (hT, hTk, xbcT, xbk, dt4, dt4k, dA4, dA4k, szm, szk)
            hT, hTk, xbcT, xbk, dt4, dt4k, dA4, dA4k, szm, szk = mt_state[mt]
            pres = {}

            def inproj(c):
                b1 = c % 2
                k.mm([(pb(b1), Wa[:, kc, 1024 + c * 128:1024 + (c + 1) * 128], hT[:, kc, :], kc == 0, kc == 7)
                      for kc in range(8)], [f"Wa_x{c // 3}", hTk], [pbk(b1)])
                pre, prek = pre_rot.next()
                pres[c] = (pre, prek)
                k.act([pbk(b1)], [prek], out=pre[:, 3:515], in_=pb(b1), func=AF.Copy)
                k.dve("tensor_copy", ["hist", prek], [prek], out=pre[:, 0:3], in_=hist[:, c, :])
                k.dve("tensor_copy", [prek], ["hist"], out=hist[:, c, :], in_=pre[:, 512:515])

            def conv(c):
                pre, prek = pres[c]
                b2 = 2 + c % 2
                k.mm([(pb(b2), diag4[:, c, kk, :], pre[:, kk:kk + 512], kk == 0, kk == 3) for kk in range(4)],
                     [f"diag4_{c}", prek], [pbk(b2)])
                k.act([pbk(b2), "cb"], [f"{xbk}_{c}"], out=xbcT[:, c, :], in_=pb(b2), func=AF.Silu, bias=cb[:, c:c + 1], scale=1.0)

            c0 = 3 * q
            skipC = (mt < 3 and q == 3)
            inproj(c0)
            if not skipC:
                inproj(c0 + 1)
            conv(c0)
            if not skipC:
                inproj(c0 + 2)
                conv(c0 + 1)
            if mt >= 4:
                st = q
                k.mm([(pb(hf2), hT[:, kc, st * 128:(st + 1) * 128], Wa[:, kc, hf2 * 512:(hf2 + 1) * 512], kc == 0, kc == 7)
                      for hf2 in range(2) for kc in range(8)], ["Wa_z", hTk], [pbk(0), pbk(1)])
            if not skipC:
                conv(c0 + 2)
            if mt >= 4:
                k.act([pbk(0), pbk(1)], [szk], out=szm[:, q, :].rearrange("p (a b) -> p a b", a=2), in_=PS[:, 0:2, :], func=AF.Silu)
            if q == 3:
                for st in range(4):
                    k.mm([(pb(1)[:, st * 16:(st + 1) * 16], hT[:, kc, st * 128:(st + 1) * 128], Wa[:, kc, 2560:2576], kc == 0, kc == 7)
                          for kc in range(8)], ["Wa_dt", hTk], [pbk(1)])
                k.dve("tensor_tensor", [pbk(1), "sm"], ["dtp"], out=dtp[:], in0=pb(1)[:, 0:64].rearrange("p (a b) -> p a b", a=4),
                      in1=bc(sm[:, 0:16].unsqueeze(1), [128, 4, 16]), op=ALU.add)
                k.act(["dtp"], ["dtp"], out=dtp[:], in_=dtp[:], func=AF.Exp)
                k.act(["dtp", "cst"], [dt4k], out=dt4[:], in_=dtp[:], func=AF.Ln, bias=cst[:, 2, 0:1], scale=1.0)
                k.dve("tensor_tensor", [dt4k, "nA"], [dA4k], out=dA4[:], in0=dt4[:], in1=bc(nA[:].unsqueeze(1), [128, 4, 16]), op=ALU.mult)

        ch_state = {}

        def A(g):
            mt, st = g // 4, g % 4
            full = mt >= 4
            hT, hTk, xbcT, xbk, dt4, dt4k, dA4, dA4k, szm, szk = mt_state[mt]
            cs = slice(st * 128, (st + 1) * 128)
            xdt, xdtk = xdt_r.next()
            xD, xDk = xD_r.next()
            Btok, Btk = Btok_r.next()
            ex3, ex3k = ex3_r.next()
            E, Ek = E_r.next()
            CBm, CBk = CBm_r.next()
            xw, xwk = xw_r.next()
            ch_state[g] = (xdt, xdtk, xD, xDk, Btok, Btk, ex3, ex3k, E, Ek, xw, xwk)
            k.tr([(pT0[:, c * 128:(c + 1) * 128], xbcT[:, c, cs]) for c in range(8)], ident[:],
                 [f"{xbk}_{c}" for c in range(8)] + ["ident"], [pbk(ATB)])
            k.dve("tensor_tensor", [pbk(ATB), dt4k], [xdtk], out=xdt[:].rearrange("p (h q) -> p h q", h=16),
                  in0=pT0.rearrange("p (h q) -> p h q", h=16), in1=bc(dt4[:, st, :].unsqueeze(2), [128, 16, 64]), op=ALU.mult)
            if full:
                k.dve("tensor_tensor", [pbk(ATB), "dbc"], [xDk], out=xD[:], in0=pT0, in1=dbc[:], op=ALU.mult)
            k.tr([(pT0[:, gg * 128:(gg + 1) * 128], xbcT[:, 8 + gg, cs]) for gg in range(2)], ident[:],
                 [f"{xbk}_8", f"{xbk}_9", "ident"], [pbk(ATB)])
            k.act([pbk(ATB)], [Btk], out=Btok[:], in_=pT0[:, 0:256], func=AF.Copy)
            dA = dA4[:, st, :]
            k.mm([(pb(1)[:, 0:16], tri, dA, True, True), (pb(1)[:, 16:32], su, dA, True, True),
                  (pb(1)[:, 32:48], onesf, dA, True, True)], ["cst", dA4k], [pbk(1)])
            k.act([pbk(1)], [ex3k], out=ex3[:], in_=pb(1)[:, 0:48], func=AF.Exp)
            k.pool("tensor_tensor", [xdtk, ex3k], [xwk], out=xw[:].rearrange("p (h q) -> p h q", h=16),
                   in0=xdt[:].rearrange("p (h q) -> p h q", h=16), in1=bc(ex3[:, 16:32].unsqueeze(2), [128, 16, 64]), op=ALU.mult)
            if full:
                k.dve("tensor_copy", [dA4k], ["dAh"], out=dAh[:], in_=dA)
                k.dve("tensor_copy", ["dAh"], ["dAh32"], out=dAh32[:], in_=dAh[:])
                k.dve("tensor_tensor", [dA4k, "dAh32"], ["dAl"], out=dAl[:], in0=dA, in1=dAh32[:], op=ALU.subtract)
                for r in range(2):
                    k.dve("tensor_tensor", ["su_bf", "dAh"], ["SUhi"], out=SUhi[:], in0=bc(su_bf[:].unsqueeze(1), [128, 8, 128]),
                          in1=bc(dAh[:, r * 8:(r + 1) * 8].unsqueeze(2), [128, 8, 128]), op=ALU.mult)
                    k.dve("tensor_tensor", ["su_bf", "dAl"], ["SUlo"], out=SUlo[:], in0=bc(su_bf[:].unsqueeze(1), [128, 8, 128]),
                          in1=bc(dAl[:, r * 8:(r + 1) * 8].unsqueeze(2), [128, 8, 128]), op=ALU.mult)
                    for hq in range(2):
                        bank = 2 + hq
                        lst = []
                        for hh in range(4):
                            h = hq * 4 + hh
                            lst.append((pb(bank)[:, hh * 128:(hh + 1) * 128], SUhi[:, h, :], tri_bf[:], True, False))
                            lst.append((pb(bank)[:, hh * 128:(hh + 1) * 128], SUlo[:, h, :], tri_bf[:], False, True))
                        k.mm(lst, ["SUhi", "SUlo", "tri_bf"], [pbk(bank)])
                        k.act([pbk(bank)], [Ek], out=E[:, r * 8 + hq * 4:r * 8 + hq * 4 + 4, :],
                              in_=pb(bank).rearrange("p (a b) -> p a b", a=4), func=AF.Exp)
                k.mm([(pb(1)[:, 64 + gg * 128:64 + (gg + 1) * 128], xbcT[:, 8 + gg, cs], xbcT[:, 10 + gg, cs], True, True) for gg in range(2)],
                     [f"{xbk}_8", f"{xbk}_9", f"{xbk}_10", f"{xbk}_11"], [pbk(1)])
                k.dve("tensor_tensor", [pbk(1), "cst"], [CBk], out=CBm[:], in0=pb(1)[:, 64:320].rearrange("p (a b) -> p a b", a=2),
                      in1=bc(mask01.unsqueeze(1), [128, 2, 128]), op=ALU.mult)
                k.dve("tensor_tensor", [Ek, CBk], [Ek], out=E[:].rearrange("p (g r) l -> p g r l", g=2),
                      in0=E[:].rearrange("p (g r) l -> p g r l", g=2), in1=bc(CBm[:].unsqueeze(2), [128, 2, 8, 128]), op=ALU.mult)

        bst = {}

        def B1(g):
            mt, st = g // 4, g % 4
            hT, hTk, xbcT, xbk, dt4, dt4k, dA4, dA4k, szm, szk = mt_state[mt]
            xdt, xdtk, xD, xDk, Btok, Btk, ex3, ex3k, E, Ek, xw, xwk = ch_state[g]
            cs = slice(st * 128, (st + 1) * 128)
            t = (mt % 4) * 4 + st
            lst = []
            for hf2 in range(2):
                lst.append((pb(4 + hf2), ident[:], xD[:, hf2 * 512:(hf2 + 1) * 512], True, False))
            for h in range(16):
                lst.append((pb(4 + h // 8)[:, (h % 8) * 64:(h % 8 + 1) * 64], E[:, h, :], xdt[:, h * 64:(h + 1) * 64], False, h % 8 == 7))
            k.mm(lst, ["ident", xDk, Ek, xdtk], [pbk(4), pbk(5)])
            k.mm([(pb(6 + gg), xbcT[:, 10 + gg, cs], Sbf[:, gg * 512:(gg + 1) * 512], True, True) for gg in range(2)],
                 [f"{xbk}_10", f"{xbk}_11", "Sbf"], [pbk(6), pbk(7)])
            k.dve("tensor_tensor", [pbk(6), pbk(7), ex3k], ["y1"], out=y1[:].rearrange("p (h q) -> p h q", h=16),
                  in0=PS[:, 6:8, :].rearrange("p a (h q) -> p (a h) q", q=64), in1=bc(ex3[:, 0:16].unsqueeze(2), [128, 16, 64]), op=ALU.mult)
            k.dve("tensor_tensor", [pbk(4), pbk(5), "y1"], ["y1"], out=y1[:].rearrange("p (a b) -> p a b", a=2),
                  in0=PS[:, 4:6, :], in1=y1[:].rearrange("p (a b) -> p a b", a=2), op=ALU.add)
            k.dve("tensor_tensor", ["y1", szk], ["y1"], out=y1[:], in0=y1[:], in1=szm[:, st, :], op=ALU.mult)
            k.dve("tensor_tensor", ["y1", G1[1]], ["yn"], out=yn[:], in0=y1[:], in1=G1[0], op=ALU.mult)
            ss, sskey = ss_rot.next()
            sq, sqkey = sq_rot.next()
            for gg in range(2):
                k.act(["y1"], [sqkey, sskey], out=sq[:, 0:512], in_=y1[:, gg * 512:(gg + 1) * 512], func=AF.Square, accum_out=ss[:, gg:gg + 1])
            rstd_from_ss(ss, sskey, 2, 1.0 / 512)
            bst[g] = (ss, sskey, t)

        def B2(g):
            k.tr([(pTy[:, kc * 128:(kc + 1) * 128], yn[:, kc * 128:(kc + 1) * 128]) for kc in range(8)], ident[:], ["yn", "ident"], [pbk(YTB)])
            k.act([pbk(YTB)], ["ynT"], out=ynT[:], in_=pTy.rearrange("p (a b) -> p a b", a=8), func=AF.Copy)

        def B3(g):
            ss, sskey, t = bst[g]
            k.mm([(pb(4 + 2 * gg + hf2), ynT[:, 4 * gg + kc, :], Wot[:, 4 * gg + kc, hf2 * 512:(hf2 + 1) * 512], kc == 0, kc == 3)
                  for gg in range(2) for hf2 in range(2) for kc in range(4)], ["Wot", "ynT"], [pbk(4), pbk(5), pbk(6), pbk(7)])
            xt, xtk = xb3_rot.next()
            k.dma("sp", xt[:], xo_t[t], [], [xtk])
            for gg in range(2):
                k.dve("scalar_tensor_tensor", [pbk(4 + 2 * gg), pbk(5 + 2 * gg), sskey, xtk], [xtk], out=xt[:].rearrange("p (a b) -> p a b", a=2),
                      in0=PS[:, 4 + 2 * gg:6 + 2 * gg, :], scalar=ss[:, gg:gg + 1], in1=xt[:].rearrange("p (a b) -> p a b", a=2),
                      op0=ALU.mult, op1=ALU.add)
            k.dma("sp", xr_t[t], xt[:], [xtk], [f"xr{t}"])

        def C(g):
            xdt, xdtk, xD, xDk, Btok, Btk, ex3, ex3k, E, Ek, xw, xwk = ch_state[g]
            k.mm([(pb(2 + gg), Btok[:, gg * 128:(gg + 1) * 128], xw[:, gg * 512:(gg + 1) * 512], True, True) for gg in range(2)],
                 [Btk, xwk], [pbk(2), pbk(3)])
            k.dve("tensor_tensor", ["Sst", ex3k], ["Sst"], out=Sst[:].rearrange("p (h q) -> p h q", h=16),
                  in0=Sst[:].rearrange("p (h q) -> p h q", h=16), in1=bc(ex3[:, 32:48].unsqueeze(2), [128, 16, 64]), op=ALU.mult)
            k.dve("tensor_tensor", ["Sst", pbk(2), pbk(3)], ["Sst"], out=Sst[:].rearrange("p (a b) -> p a b", a=2),
                  in0=Sst[:].rearrange("p (a b) -> p a b", a=2), in1=PS[:, 2:4, :], op=ALU.add)
            k.act(["Sst"], ["Sbf"], out=Sbf[:], in_=Sst[:], func=AF.Copy)

        NG = 32
        F1(0, 0)
        F1(0, 1)
        for q in range(4):
            F2(0, q)
        F1(1, 0)
        F1(1, 1)
        A(0)
        for g in range(NG):
            mt, st = g // 4, g % 4
            own = mt >= 4
            if own:
                B1(g)
            if g < NG - 1:
                C(g)
            if g == 15:
                k.dve("tensor_scalar", ["Sst", "flag"], ["Sst"], out=Sst[:], in0=Sst[:], scalar1=flag[:, 0:1], scalar2=None, op0=ALU.mult)
                k.act(["Sst"], ["Sbf"], out=Sbf[:], in_=Sst[:], func=AF.Copy)
            if mt + 1 < 8:
                F2(mt + 1, st)
            if own:
                B2(g)
            if g + 1 < NG:
                A(g + 1)
            if own:
                B3(g)
            if st in (0, 2) and mt + 2 < 8:
                F1(mt + 2, st // 2)
        k.pop()

    if do_conf:
        k.push()
        base_t = xr_t if do_ssd else xo_t
        load_gains([0])
        xt_rot = Rot(k, "xtb", int(os.environ.get("XTB", "3")), [128, D], F32)
        xb_rot = Rot(k, "xbb", int(os.environ.get("XBB", "3")), [128, D], F32)
        Wag = k.sb("Wag", [128, 8, 2048], BF16)
        Wob = k.sb("Wob", [128, 8, D], BF16)
        w_in_v2 = w_in.rearrange("(kc p) n -> p kc n", p=128)
        for cp in range(4):
            k.dma("pool", Wag[:, :, cp * 256:(cp + 1) * 256], w_in_v2[:, :, A0 + cp * 256:A0 + (cp + 1) * 256], [], [f"Wag_a{cp}"])
            k.dma("pool", Wag[:, :, D + cp * 256:D + (cp + 1) * 256], w_in_v2[:, :, G0 + cp * 256:G0 + (cp + 1) * 256], [], [f"Wag_g{cp}"])
        k.dma("pool", Wob[:], w_out[D:2 * D, :].rearrange("(kc p) n -> p kc n", p=128), [], ["Wob"])
        NDV = int(os.environ.get("NDV", "11"))
        NPE = 31 - NDV
        diag = k.sb("diag", [128, 8, NPE, 128], BF16)
        DGP = os.environ.get("DGP", "dve")
        for c in range(8):
            on_pool = (DGP == "pool") or (DGP == "alt" and c % 2 == 1)
            (k.pool if on_pool else k.dve)("tensor_tensor", ["ident", "fw"], [f"diag{c}"], out=diag[:, c, :, :], in0=bc(ident[:].unsqueeze(1), [128, NPE, 128]),
                  in1=bc(fw[:, c, NDV:31].unsqueeze(2), [128, NPE, 128]), op=ALU.mult)
        hT2_r = Rot(k, "hT2_", 2, [128, 8, MT], BF16)
        unT = k.sb("unT", [128, 8, MT], BF16)
        uT_r = Rot(k, "uT_", 2, [128, 8, 30 + MT], BF16)
        sgm_rot = Rot(k, "sgm", 2, [128, MT], F32)
        cv_r = Rot(k, "cv_", 2, [128, 8, MT], BF16)
        cvsq_rot = Rot(k, "cvsq", 2, [128, MT], BF16)
        mean_r = Rot(k, "mean_", 1, [128, MT], F32)
        rstdv_r = Rot(k, "rstdv_", 1, [128, MT], F32)
        t1_rot = Rot(k, "t1", 2, [128, MT], F32)
        accv_rot = Rot(k, "accv", 2, [128, MT], F32)
        gl_i = [0]

        def glu_chunks(hT2, hT2k, uT, uTk, ncols, col0):
            for c in range(8):
                ba = gl_i[0] % 2
                bg = 2 + gl_i[0] % 2
                gl_i[0] += 1
                k.mm([(pb(ba)[:, 0:ncols], Wag[:, kc, c * 128:(c + 1) * 128], hT2[:, kc, col0:col0 + ncols], kc == 0, kc == 7)
                      for kc in range(8)], [f"Wag_a{c // 2}", hT2k], [pbk(ba)])
                k.mm([(pb(bg)[:, 0:ncols], Wag[:, kc, D + c * 128:D + (c + 1) * 128], hT2[:, kc, col0:col0 + ncols], kc == 0, kc == 7)
                      for kc in range(8)], [f"Wag_g{c // 2}", hT2k], [pbk(bg)])
                sgm, sgk = sgm_rot.next()
                k.act([pbk(bg)], [sgk], out=sgm[:, 0:ncols], in_=pb(bg)[:, 0:ncols], func=AF.Sigmoid)
                k.dve("tensor_tensor", [pbk(ba), sgk], [f"{uTk}_{c}"], out=uT[:, c, 30 + col0:30 + col0 + ncols], in0=pb(ba)[:, 0:ncols],
                      in1=sgm[:, 0:ncols], op=ALU.mult)

        uTp, uTpk = uT_r.next()
        k.dve("memset", [], [f"{uTpk}_{c}" for c in range(8)] + [f"{uTpk}_h"], ap=uTp[:], constant=0.0)
        hTp, hTpk = hT2_r.next()
        xt, xtk = xt_rot.next()
        k.dma("sp", xt[:], xp_t[NT - 1], [], [xtk])
        hb, hk = h_rot.next()
        norm_tile(xt[:], xtk, 0, hb[:], hk)
        transpose_to(hb, hk, hTp, hTpk, 384, 7)
        glu_chunks(hTp, hTpk, uTp, uTpk, 128, 384)
        for m in range(NT // 4):
            hT2, hT2k = hT2_r.next()
            uT, uTk = uT_r.next()
            cv, cvk = cv_r.next()
            mean, meank = mean_r.next()
            rstdv, rstdk = rstdv_r.next()
            k.dve("tensor_copy", [f"{uTpk}_{c}" for c in range(8)], [f"{uTk}_h"], out=uT[:, :, 0:30], in_=uTp[:, :, MT:MT + 30])
            for st in range(4):
                t = m * 4 + st
                xt, xtk = xt_rot.next()
                k.dma("sp", xt[:], xo_t[t], [], [xtk])
                hb, hk = h_rot.next()
                norm_tile(xt[:], xtk, 0, hb[:], hk)
                transpose_to(hb, hk, hT2, hT2k, st * 128, 7)
            glu_chunks(hT2, hT2k, uT, uTk, MT, 0)
            for c in range(8):
                bcv = 4 + c % 2
                k.mm([(pb(bcv), diag[:, c, kk - NDV, :], uT[:, c, kk:kk + MT], kk == NDV, kk == 30) for kk in range(NDV, 31)],
                     [f"diag{c}", f"{uTk}_{c}", f"{uTk}_h"], [pbk(bcv)])
                accv, acck = accv_rot.next()
                if os.environ.get("TAP0_ACT", "1") == "1":
                    k.act([f"{uTk}_{c}", f"{uTk}_h", "fw", "fv"], [acck], out=accv[:], in_=uT[:, c, 0:MT], func=AF.Identity,
                          scale=fw[:, c, 0:1], bias=fv[:, c, 0:1])
                else:
                    k.dve("tensor_scalar", [f"{uTk}_{c}", f"{uTk}_h", "fw", "fv"], [acck], out=accv[:], in0=uT[:, c, 0:MT], scalar1=fw[:, c, 0:1],
                          scalar2=fv[:, c, 0:1], op0=ALU.mult, op1=ALU.add)
                for kk in range(1, NDV):
                    k.dve("scalar_tensor_tensor", [f"{uTk}_{c}", f"{uTk}_h", "fw", acck], [acck], out=accv[:], in0=uT[:, c, kk:kk + MT],
                          scalar=fw[:, c, kk:kk + 1], in1=accv[:], op0=ALU.mult, op1=ALU.add)
                k.dve("tensor_tensor", [pbk(bcv), acck], [f"{cvk}_{c}"], out=cv[:, c, :], in0=pb(bcv), in1=accv[:], op=ALU.add)
                cvsq, cqk = cvsq_rot.next()
                k.act([f"{cvk}_{c}"], [cqk], out=cvsq[:], in_=cv[:, c, :], func=AF.Square)
                k.mm([(pb(6), ones_bf[:], cv[:, c, :], c == 0, c == 7)], ["ones_bf", f"{cvk}_{c}"], [pbk(6)])
                k.mm([(pb(7), ones_bf[:], cvsq[:], c == 0, c == 7)], ["ones_bf", cqk], [pbk(7)])
            k.dve("tensor_scalar", [pbk(6)], [meank], out=mean[:], in0=pb(6), scalar1=1.0 / D, scalar2=None, op0=ALU.mult)
            k.dve("tensor_tensor", [meank], [rstdk], out=rstdv[:], in0=mean[:], in1=mean[:], op=ALU.mult)
            k.dve("scalar_tensor_tensor", [pbk(7), rstdk], [rstdk], out=rstdv[:], in0=pb(7), scalar=1.0 / D, in1=rstdv[:],
                  op0=ALU.mult, op1=ALU.subtract)
            k.act([rstdk, "epsT"], [rstdk], out=rstdv[:], in_=rstdv[:], func=AF.Ln, bias=epsT[:], scale=1.0)
            k.act([rstdk], [rstdk], out=rstdv[:], in_=rstdv[:], func=AF.Exp, scale=-0.5)
            for c in range(8):
                t1, t1k = t1_rot.next()
                lnp = os.environ.get("LNP", "0")
                e1 = k.pool if (lnp == "all" or (lnp == "half" and c % 2 == 1)) else k.dve
                e1("tensor_tensor", [f"{cvk}_{c}", meank], [t1k], out=t1[:], in0=cv[:, c, :], in1=mean[:], op=ALU.subtract)
                e1("tensor_tensor", [t1k, rstdk], [t1k], out=t1[:], in0=t1[:], in1=rstdv[:], op=ALU.mult)
                k.act([t1k, "fv"], [f"unT{c}"], out=unT[:, c, :], in_=t1[:], func=AF.Silu, scale=fv[:, c, 1:2], bias=fv[:, c, 2:3])
            for st in range(4):
                t = m * 4 + st
                ob = 2 * (st % 2)
                k.mm([(pb(ob + hf2), unT[:, kc, st * 128:(st + 1) * 128], Wob[:, kc, hf2 * 512:(hf2 + 1) * 512], kc == 0, kc == 7)
                      for hf2 in range(2) for kc in range(8)], ["Wob"] + [f"unT{c}" for c in range(8)], [pbk(ob), pbk(ob + 1)])
                xt, xtk = xb_rot.next()
                k.dma("sp", xt[:], base_t[t], [f"xr{t}"], [xtk])
                k.dve("tensor_tensor", [pbk(ob), pbk(ob + 1), xtk], [xtk], out=xt[:].rearrange("p (a b) -> p a b", a=2),
                      in0=xt[:].rearrange("p (a b) -> p a b", a=2), in1=PS[:, ob:ob + 2, :], op=ALU.add)
                k.dma("sp", xr_t[t], xt[:], [xtk], [f"xr{t}"])
            uTp, uTpk = uT, uTk
        k.pop()
    elif dbg == "ssd":
        pass

    k.push()
    x_res = k.sb("x_res", [128, NT, D], F32)
    KT = k.sb("KT", [128, 8, MEM], BF16)
    V = k.sb("V", [128, 2, D], BF16)
    with_kv = True
    if with_kv:
        k.push()
        load_gains([1])
        load_gains([2])
        WA3 = k.sb("WA3", [128, 16 * D], BF16)
        wq = WA3[:, 0:8 * D].rearrange("p (a b) -> p a b", a=8)
        wo = WA3[:, 8 * D:16 * D].rearrange("p (a b) -> p a b", a=8)
        for t in range(4):
            k.dma("sp", x_res[:, t, :], src_t[t], [f"xr{t}"], [f"xres{t}"])
        for qq in range(2):
            k.dma("pool", wq[:, :, qq * 512:(qq + 1) * 512], w_q.rearrange("(kc p) n -> p kc n", p=128)[:, :, qq * 512:(qq + 1) * 512], [], [f"WAq{qq}"])
        WA = k.sb("WA", [128, 8 * 2048], BF16)
        wkv = WA[:, 0:8 * 2048].rearrange("p (a b) -> p a b", a=8)
        w_kv_v = w_kv.rearrange("(kc p) n -> p kc n", p=128)
        for qq in range(4):
            k.dma("pool", wkv[:, :, qq * 512:(qq + 1) * 512], w_kv_v[:, :, qq * 512:(qq + 1) * 512], ["WAq1"], [f"WAkv{qq}"])
        for t in range(4, NT):
            k.dma("sp", x_res[:, t, :], src_t[t], [f"xr{t}", "WAkv1" if t < 8 else "WAkv3"], [f"xres{t}"])
        memT = k.sb("memT", [128, 8, MEM], BF16)
        mrot = Rot(k, "memx", 1, [128, D], F32)
        for mc in range(2):
            mx, mxk = mrot.next()
            k.dma("sp", mx[:], mem_d[mc * 128:(mc + 1) * 128, :], [], [mxk])
            hb, hk = h_rot.next()
            norm_tile(mx[:], mxk, 2, hb[:], hk)
            transpose_to(hb, hk, memT, "memT", mc * 128, 0)
        for c in range(8):
            bank = 1 + (c % 2)
            k.mm([(pb(bank)[:, 0:MEM], wkv[:, kc, c * 128:(c + 1) * 128], memT[:, kc, :], kc == 0, kc == 7)
                  for kc in range(8)], [f"WAkv{c // 4}", "memT"], [pbk(bank)])
            k.act([pbk(bank)], ["KT"], out=KT[:, c, :], in_=pb(bank)[:, 0:MEM], func=AF.Copy)
        for mc in range(2):
            for hf2 in range(2):
                bank = 3 + hf2
                k.mm([(pb(bank), memT[:, kc, mc * 128:(mc + 1) * 128],
                       wkv[:, kc, D + hf2 * 512:D + (hf2 + 1) * 512], kc == 0, kc == 7) for kc in range(8)],
                     [f"WAkv{2 + hf2}", "memT"], [pbk(bank)])
                k.dve("tensor_copy", [pbk(bank)], ["V"], out=V[:, mc, hf2 * 512:(hf2 + 1) * 512], in_=pb(bank))

    for qq in range(2):
        k.dma("pool", wo[:, :, qq * 512:(qq + 1) * 512], w_o.rearrange("(kc p) n -> p kc n", p=128)[:, :, qq * 512:(qq + 1) * 512], [f"WAkv{3}"], [f"WAo{qq}"])
    hxT_rot = Rot(k, "hxT", 2, [128, 8, MT], BF16)
    qT = k.sb("qT", [128, 8, MT], BF16)
    ET_rot = Rot(k, "ET", 2, [128, 2, MT], BF16)
    rden_rot = Rot(k, "rden", 2, [128, MT], F32)
    oT = k.sb("oT", [128, 8, MT], BF16)
    for m in range(NT // 4):
        hxT, hxk = hxT_rot.next()
        for st in range(4):
            t = m * 4 + st
            hb, hk = h_rot.next()
            norm_tile(x_res[:, t, :], f"xres{t}", 1, hb[:], hk)
            transpose_to(hb, hk, hxT, hxk, st * 128, 0)
        for c in range(8):
            bank = 1 + (c % 2)
            k.mm([(pb(bank), wq[:, kc, c * 128:(c + 1) * 128], hxT[:, kc, :], kc == 0, kc == 7) for kc in range(8)],
                 [f"WAq{c // 4}", hxk], [pbk(bank)])
            k.act([pbk(bank)], [f"qT{c}"], out=qT[:, c, :], in_=pb(bank), func=AF.Copy)
        for hd in range(4):
            ET, etk = ET_rot.next()
            for mc in range(2):
                bank = 3 + mc
                k.mm([(pb(bank), KT[:, 2 * hd + dc, mc * 128:(mc + 1) * 128], qT[:, 2 * hd + dc, :], dc == 0, dc == 1)
                      for dc in range(2)], ["KT", f"qT{2 * hd}", f"qT{2 * hd + 1}"], [pbk(bank)])
                k.act([pbk(bank)], [etk], out=ET[:, mc, :], in_=pb(bank), func=AF.Exp, scale=1.0 / 16.0)
            k.mm([(pb(5), ones_bf[:], ET[:, mc, :], mc == 0, mc == 1) for mc in range(2)], ["ones_bf", etk], [pbk(5)])
            rden, rdk = rden_rot.next()
            k.act([pbk(5)], [rdk], out=rden[:], in_=pb(5), func=AF.Ln)
            k.act([rdk], [rdk], out=rden[:], in_=rden[:], func=AF.Exp, scale=-1.0)
            for dc in range(2):
                bank = 6 + dc
                k.mm([(pb(bank), V[:, mc, hd * 256 + dc * 128:hd * 256 + (dc + 1) * 128], ET[:, mc, :], mc == 0, mc == 1)
                      for mc in range(2)], ["V", etk], [pbk(bank)])
                k.dve("tensor_tensor", [pbk(bank), rdk], [f"oT{2 * hd + dc}"], out=oT[:, 2 * hd + dc, :], in0=pb(bank),
                      in1=rden[:], op=ALU.mult)
        for st in range(4):
            t = m * 4 + st
            for hf2 in range(2):
                bank = 1 + hf2
                k.mm([(pb(bank), oT[:, kc, st * 128:(st + 1) * 128], wo[:, kc, hf2 * 512:(hf2 + 1) * 512], kc == 0, kc == 7)
                      for kc in range(8)], [f"WAo{hf2}"] + [f"oT{c}" for c in range(8)], [pbk(bank)])
                k.dve("tensor_tensor", [pbk(bank), f"xres{t}"], [f"xres{t}"], out=x_res[:, t, hf2 * 512:(hf2 + 1) * 512],
                      in0=x_res[:, t, hf2 * 512:(hf2 + 1) * 512], in1=pb(bank), op=ALU.add)

    k.pop()
    k.push()
    load_gains([3])
    hfT = k.sb("hfT", [128, 8, HALF], BF16)
    groups = [(0, 4), (4, 4), (8, 4), (12, 4), (16, 3), (19, 3)]
    wg_rot = Rot(k, "wg", 2, [128, 8, 4 * 128], BF16)
    wu_rot = Rot(k, "wu", 2, [128, 8, 4 * 128], BF16)
    wd_rot = Rot(k, "wd", 2, [128, 4, D], BF16)
    sg_rot = Rot(k, "sg", 2, [128, MT], BF16)
    aT_rot = Rot(k, "aT", 2, [128, 4, MT], BF16)
    for gi, (c0, nch) in enumerate(groups):
        wg, wgk = wg_rot.next()
        wu, wuk = wu_rot.next()
        wd, wdk = wd_rot.next()
        k.dma("pool", wg[:, :, 0:nch * 128], w_gate.rearrange("(kc p) n -> p kc n", p=128)[:, :, c0 * 128:(c0 + nch) * 128], [], [wgk])
        k.dma("pool", wu[:, :, 0:nch * 128], w_up.rearrange("(kc p) n -> p kc n", p=128)[:, :, c0 * 128:(c0 + nch) * 128], [], [wuk])
        k.dma("pool", wd[:, 0:nch, :], w_down[c0 * 128:(c0 + nch) * 128, :].rearrange("(c p) n -> p c n", p=128), [], [wdk])
        for m in range(NT // 4):
            if gi == 0:
                for st in range(4):
                    t = m * 4 + st
                    hb, hk = h_rot.next()
                    norm_tile(x_res[:, t, :], f"xres{t}", 3, hb[:], hk)
                    transpose_to(hb, hk, hfT, f"hfT{m}", t * 128, 0)
            aT, aTk = aT_rot.next()
            for ci in range(nch):
                bg = 1 + (ci % 2)
                bu = 3 + (ci % 2)
                k.mm([(pb(bg), wg[:, kc, ci * 128:(ci + 1) * 128], hfT[:, kc, m * MT:(m + 1) * MT], kc == 0, kc == 7)
                      for kc in range(8)], [wgk, f"hfT{m}"], [pbk(bg)])
                k.mm([(pb(bu), wu[:, kc, ci * 128:(ci + 1) * 128], hfT[:, kc, m * MT:(m + 1) * MT], kc == 0, kc == 7)
                      for kc in range(8)], [wuk, f"hfT{m}"], [pbk(bu)])
                sg, sgk = sg_rot.next()
                k.act([pbk(bg)], [sgk], out=sg[:], in_=pb(bg), func=AF.Silu)
                k.dve("tensor_tensor", [pbk(bu), sgk], [aTk], out=aT[:, ci, :], in0=pb(bu), in1=sg[:], op=ALU.mult)
            for st in range(4):
                t = m * 4 + st
                for hf2 in range(2):
                    bank = 5 + hf2
                    k.mm([(pb(bank), aT[:, ci, st * 128:(st + 1) * 128], wd[:, ci, hf2 * 512:(hf2 + 1) * 512], ci == 0, ci == nch - 1)
                          for ci in range(nch)], [wdk, aTk], [pbk(bank)])
                    k.dve("tensor_tensor", [pbk(bank), f"xres{t}"], [f"xres{t}"],
                          out=x_res[:, t, hf2 * 512:(hf2 + 1) * 512], in0=x_res[:, t, hf2 * 512:(hf2 + 1) * 512],
                          in1=pb(bank), op=ALU.add)

    load_gains([4])
    o_rot = Rot(k, "ob", 2, [128, D], F32)
    for t in range(NT):
        ob, obk = o_rot.next()
        ss, sskey = ss_rot.next()
        sq, sqkey = sq_rot.next()
        k.act([f"xres{t}"], [sqkey, sskey], out=sq[:], in_=x_res[:, t, :], func=AF.Square, accum_out=ss[:, 0:1])
        rstd_from_ss(ss, sskey, 1, 1.0 / D)
        k.dve("scalar_tensor_tensor", [f"xres{t}", sskey, gains[4][1]], [obk], out=ob[:], in0=x_res[:, t, :], scalar=ss[:, 0:1],
              in1=gains[4][0], op0=ALU.mult, op1=ALU.mult)
        k.dma("sp", out_t[t], ob[:], [obk], [])
    S.emit()
    k.pop()
    k.pop()
    k.st.close()
    return nc


def make_inputs(inp, dbg=None):
    f = np.float32
    x = np.asarray(inp["x"], f)
    mem = np.asarray(inp["mem"], f)

    def row_bc(v):
        return np.broadcast_to(np.asarray(v, f).reshape(1, -1), (128, np.asarray(v).size))

    gbc = np.stack([row_bc(inp["norm_mix_g"][0]), row_bc(inp["norm_xattn_g"][0]), row_bc(inp["norm_mem_g"][0]),
                    row_bc(inp["norm_ffn_g"][0]), row_bc(inp["norm_final_g"]), row_bc(inp["ssd_norm_g"][0])], axis=1)
    dbc = row_bc(np.repeat(np.asarray(inp["ssd_D"][0], f), 64))
    small = np.zeros((128, 64), f)
    small[:, 0:16] = row_bc(inp["ssd_dt_bias"][0])
    small[:, 16:32] = row_bc(inp["ssd_A_log"][0])
    cw = np.asarray(inp["ssd_conv_w"][0], f).reshape(4, 12, 128).transpose(2, 1, 0)
    cb = np.asarray(inp["ssd_conv_b"][0], f).reshape(12, 128).T
    fw = np.asarray(inp["cf_conv_w"][0], f).reshape(31, 8, 128).transpose(2, 1, 0)
    fv = np.stack([np.asarray(inp[n][0], f).reshape(8, 128).T for n in ("cf_conv_b", "cf_ln_g", "cf_ln_b")], axis=2)
    ident = np.eye(128, dtype=f).astype(ml_dtypes.bfloat16)
    j = np.arange(128)
    tri = (j[:, None] <= j[None, :]).astype(f)
    su = (j[:, None] > j[None, :]).astype(f)
    cst = np.stack([tri, su, np.ones((128, 128), f), tri], axis=1)
    common = {
        "w_in": np.ascontiguousarray(inp["w_in"][0], f), "w_out": np.ascontiguousarray(inp["w_out"][0], f),
        "w_q": np.ascontiguousarray(inp["w_q"][0], f), "w_kv": np.ascontiguousarray(inp["w_kv"][0], f),
        "w_o": np.ascontiguousarray(inp["w_o"][0], f), "w_gate": np.ascontiguousarray(inp["w_gate"][0], f),
        "w_up": np.ascontiguousarray(inp["w_up"][0], f), "w_down": np.ascontiguousarray(inp["w_down"][0], f),
        "gbc": np.ascontiguousarray(gbc), "dbc": np.ascontiguousarray(dbc), "small": small,
        "cw": np.ascontiguousarray(cw), "cb": np.ascontiguousarray(cb), "fw": np.ascontiguousarray(fw),
        "fv": np.ascontiguousarray(fv), "ident": ident, "cst": np.ascontiguousarray(cst),
    }
    maps = []
    for c in range(8):
        b, hf = c // 2, c % 2
        d = dict(common)
        d["x_own"] = np.ascontiguousarray(x[b, hf * HALF:(hf + 1) * HALF])
        d["x_prev"] = np.ascontiguousarray(x[b, 0:HALF]) if hf == 1 else np.zeros((HALF, D), f)
        d["flag"] = np.full((128, 1), float(hf), f)
        d["mem"] = np.ascontiguousarray(mem[b])
        maps.append(d)
    return maps


_NC_CACHE = {}


def kernel(_dbg=None, **inputs):
    if _dbg not in _NC_CACHE:
        _NC_CACHE[_dbg] = build(_dbg)
    nc = _NC_CACHE[_dbg]
    maps = make_inputs(inputs, _dbg)
    res = run_bass_kernel_spmd(nc, maps, core_ids=list(range(8)))
    out = np.zeros((NB, SEQ, D), np.float32)
    for c in range(8):
        b, hf = c // 2, c % 2
        out[b, hf * HALF:(hf + 1) * HALF] = res.results[c]["out"]
    return out
```

```python
import contextlib
import os
import numpy as np
import ml_dtypes
import concourse.bass as bass
import concourse.mybir as mybir
from concourse.bass_utils import run_bass_kernel_spmd

F32 = mybir.dt.float32
BF16 = mybir.dt.bfloat16
ALU = mybir.AluOpType
AF = mybir.ActivationFunctionType

ENGS = ["pe", "dve", "act", "pool", "sp"]
SEG_CFG = {0: dict(lat=0.4, tsw=1.2), 1: dict(lat=0.4, tsw=1.2), 2: dict(lat=0.7, tsw=1.0, seed=396330, jit=0.05),
           3: dict(lat=0.7, tsw=1.0, seed=748492, jit=0.05)}

D = 1024
SEQ = 4096
NB = 4
HALF = 2048
NT = 16
MT = 512
MEM = 256
INW = 4624
DFF = 2816
NFF = 22
EPS = 1e-6
XBC0 = 1024
DT0 = 2560
A0 = 2576
G0 = 3600


class Sched:
    LAT = 0.5
    DELTA_DEFAULT = 0.5

    def __init__(self, nc, n_dma_sems=4):
        self.nc = nc
        self.ops = []
        self.seg = 0
        self.n_dma = n_dma_sems
        self._stream = {}
        self._seg_counter = 0

    def barrier(self):
        self.seg += 1

    def add(self, eng, fn, reads=(), writes=(), dma=False, cost=1.0, tset=None):
        self.ops.append(dict(eng=eng, fn=fn, reads=tuple(reads), writes=tuple(writes), dma=int(dma), cost=float(cost),
                             seg=self.seg, idx=len(self.ops), tset=tset))

    def _schedule_segment(self, ops):
        n = len(ops)
        cfg = dict(lat=self.LAT, tsw=1.4, seed=0, jit=0.0)
        cfg.update(SEG_CFG.get(self._seg_counter, {}))
        env = os.environ.get("SCHED_CFG")
        if env:
            import json as _json
            cfg.update(_json.loads(env))
        self._seg_counter += 1
        LATV = cfg["lat"]
        rng = np.random.default_rng(cfg["seed"]) if cfg["jit"] > 0 else None
        DMAF = float(os.environ.get("DMAF", "0.75"))
        PEW = float(os.environ.get("PEW", "1.0"))
        preds = [set() for _ in range(n)]
        raw = [set() for _ in range(n)]
        last_writer, readers = {}, {}
        for i, op in enumerate(ops):
            for r in op["reads"]:
                lw = last_writer.get(r)
                if lw is not None:
                    preds[i].add(lw)
                    raw[i].add(lw)
            for w in op["writes"]:
                lw = last_writer.get(w)
                if lw is not None:
                    preds[i].add(lw)
                for rd in readers.get(w, ()):
                    if rd != i:
                        preds[i].add(rd)
            for r in op["reads"]:
                readers.setdefault(r, []).append(i)
            for w in op["writes"]:
                last_writer[w] = i
                readers[w] = []
        succs = [[] for _ in range(n)]
        for i in range(n):
            for p in preds[i]:
                succs[p].append(i)

        def dur(op):
            return max(0.06, op["cost"] * DMAF) if op["dma"] else op["cost"]

        def done_lat(op):
            return (2.0 + op["cost"]) if op["dma"] else op["cost"]

        prio = [0.0] * n
        for i in range(n - 1, -1, -1):
            m = 0.0
            for sidx in succs[i]:
                if prio[sidx] > m:
                    m = prio[sidx]
            prio[i] = m + done_lat(ops[i]) * (PEW if ops[i]["eng"] == "pe" else 1.0)
        if rng is not None:
            jit = rng.random(n)
            prio = [p * (1.0 + cfg["jit"] * (j - 0.5)) for p, j in zip(prio, jit)]
        npred = [len(preds[i]) for i in range(n)]
        ready_t = [0.0] * n
        finish = [0.0] * n
        free = {e: 0.0 for e in ENGS}
        ready = {e: [] for e in ENGS}
        for i in range(n):
            if npred[i] == 0:
                ready[ops[i]["eng"]].append(i)
        order = {e: [] for e in ENGS}
        left = n
        cur_set = [None]
        TSW = cfg["tsw"]
        DELTA = float(os.environ.get('SCHED_DELTA', self.DELTA_DEFAULT))
        while left:
            best = None
            for e in ENGS:
                if not ready[e]:
                    continue
                fe = free[e]
                cand = None
                stts = {}
                mn = None
                for i in ready[e]:
                    stt = ready_t[i] if ready_t[i] > fe else fe
                    if e == "act" and ops[i]["tset"] is not None and cur_set[0] is not None and ops[i]["tset"] != cur_set[0]:
                        stt = stt + TSW
                    stts[i] = stt
                    if mn is None or stt < mn:
                        mn = stt
                for i in ready[e]:
                    stt = stts[i]
                    if stt > mn + DELTA:
                        continue
                    key = (-prio[i], stt, i)
                    if cand is None or key < cand[0]:
                        cand = (key, i, stt)
                if best is None or cand[2] < best[2] - 1e-9 or (abs(cand[2] - best[2]) <= 1e-9 and cand[0] < best[0]):
                    best = cand
            _, i, stt = best
            op = ops[i]
            e = op["eng"]
            ready[e].remove(i)
            if e == "act" and op["tset"] is not None:
                cur_set[0] = op["tset"]
            free[e] = stt + dur(op)
            finish[i] = stt + done_lat(op)
            order[e].append(i)
            left -= 1
            for sidx in succs[i]:
                rt = finish[i] + (LATV if ops[sidx]["eng"] != e or op["dma"] else 0.0)
                if rt > ready_t[sidx]:
                    ready_t[sidx] = rt
                npred[sidx] -= 1
                if npred[sidx] == 0:
                    ready[ops[sidx]["eng"]].append(sidx)
        return order, preds, raw

    def emit(self, final_eng="sp"):
        nc = self.nc
        nseg = self.seg + 1
        segs = [[] for _ in range(nseg)]
        for op in self.ops:
            segs[op["seg"]].append(op)
        gorder = {e: [] for e in ENGS}
        dma_val = {}
        dma_rr = {e: 0 for e in ENGS}
        known_pos = {e: {e2: -1 for e2 in ENGS} for e in ENGS}
        known_dma = {e: {} for e in ENGS}
        pending_pos = {e: {} for e in ENGS}
        pending_dma = {e: {} for e in ENGS}
        for sops in segs:
            if not sops:
                continue
            order, preds, raw = self._schedule_segment(sops)
            rec = {}
            for e in ENGS:
                for i in order[e]:
                    op = sops[i]
                    r = dict(op=op, eng=e, pos=None, signal=False, waits_pos={}, waits_dma={}, dma_tok=None)
                    if op["dma"]:
                        j = dma_rr[e]
                        dma_rr[e] = (j + 1) % (self.n_dma if e == "sp" else 3)
                        key = ("dma", e, j)
                        prev = dma_val.get(key, 0)
                        val = prev + 16 * op["dma"]
                        dma_val[key] = val
                        r["dma_tok"] = (key, val)
                        r["dma_prev"] = (key, prev) if prev > 0 else None
                    else:
                        r["pos"] = len(gorder[e])
                        gorder[e].append(r)
                    rec[i] = r
            for e in ENGS:
                for i in order[e]:
                    r = rec[i]
                    op = sops[i]
                    wp, wd = {}, {}
                    for e2, p in pending_pos[e].items():
                        if known_pos[e][e2] < p:
                            wp[e2] = p
                    for kk, vv in pending_dma[e].items():
                        if known_dma[e].get(kk, 0) < vv:
                            wd[kk] = vv
                    pending_pos[e] = {}
                    pending_dma[e] = {}
                    for p in preds[i]:
                        pr = rec[p]
                        if pr["dma_tok"] is not None:
                            kk, vv = pr["dma_tok"]
                            if known_dma[e].get(kk, 0) < vv and wd.get(kk, 0) < vv:
                                wd[kk] = vv
                        elif pr["eng"] != e or e != "pe":
                            e2 = pr["eng"]
                            if known_pos[e][e2] < pr["pos"] and wp.get(e2, -1) < pr["pos"]:
                                wp[e2] = pr["pos"]
                    if op["dma"] and r["dma_prev"] is not None:
                        kk, vv = r["dma_prev"]
                        if known_dma[e].get(kk, 0) < vv and wd.get(kk, 0) < vv:
                            wd[kk] = vv
                    for e2, p in wp.items():
                        known_pos[e][e2] = p
                        gorder[e2][p]["signal"] = True
                    for kk, vv in wd.items():
                        known_dma[e][kk] = vv
                    r["waits_pos"] = wp
                    r["waits_dma"] = wd
                    r["stream_eng"] = e
            for e in ENGS:
                for e2 in ENGS:
                    if e2 != e and gorder[e2]:
                        pending_pos[e][e2] = len(gorder[e2]) - 1
                        gorder[e2][-1]["signal"] = True
                for kk, vv in dma_val.items():
                    pending_dma[e][kk] = vv
            self._last_rec = rec
            for e in ENGS:
                for i in order[e]:
                    self._stream.setdefault(e, []).append(rec[i])
        for e in ENGS:
            if gorder[e]:
                gorder[e][-1]["signal"] = True
        count = {e: 0 for e in ENGS}
        for e in ENGS:
            for r in gorder[e]:
                if r["signal"]:
                    count[e] += 1
                    r["val"] = count[e]
        keys = [("eng", e) for e in ENGS] + sorted(dma_val.keys())
        with contextlib.ExitStack() as st:
            sems = {}
            for kk in keys:
                sems[kk] = st.enter_context(nc.semaphore("s_" + "_".join(str(x) for x in kk)))
            fin_waits = []
            for e in ENGS:
                if gorder[e] and e != final_eng:
                    fin_waits.append((("eng", e), count[e]))
            for kk, vv in dma_val.items():
                fin_waits.append((kk, vv))
            block = st.enter_context(nc.Block())
            streams = self._stream

            def run(ename):
                def body(eng):
                    for r in streams.get(ename, []):
                        for e2, p in r["waits_pos"].items():
                            eng.wait_ge(sems[("eng", e2)], gorder[e2][p]["val"])
                        for kk, vv in r["waits_dma"].items():
                            eng.wait_ge(sems[kk], vv)
                        res = r["op"]["fn"](eng)
                        if r["op"]["dma"]:
                            if not isinstance(res, (list, tuple)):
                                res = [res]
                            assert len(res) == int(r["op"]["dma"])
                            for ins in res:
                                ins.then_inc(sems[r["dma_tok"][0]], 16)
                        elif r["signal"]:
                            if isinstance(res, (list, tuple)):
                                res = res[-1]
                            res.then_inc(sems[("eng", ename)], 1)
                    if ename == final_eng:
                        for kk, vv in fin_waits:
                            eng.wait_ge(sems[kk], vv)
                return body

            block.tensor(run("pe"))
            block.vector(run("dve"))
            block.scalar(run("act"))
            block.gpsimd(run("pool"))
            block.sync(run("sp"))


class Rot:
    def __init__(self, K, name, n, shape, dt):
        self.bufs = [K.sb(f"{name}{i}", shape, dt) for i in range(n)]
        self.keys = [f"{name}{i}" for i in range(n)]
        self.i = 0

    def next(self):
        j = self.i
        self.i = (j + 1) % len(self.bufs)
        return self.bufs[j], self.keys[j]


class K:
    def __init__(self, dbg=None):
        self.dbg = dbg
        self.nc = bass.Bass("TRN2", target_bir_lowering=False)
        self.st = contextlib.ExitStack()
        self.stacks = [self.st]
        self.S = Sched(self.nc)
        self.dram = {}

    def din(self, name, shape, dt=F32):
        ap = self.nc.dram_tensor(name, list(shape), dt, kind="ExternalInput").ap()
        self.dram[name] = ap
        return ap

    def sb(self, name, shape, dt):
        return self.stacks[-1].enter_context(self.nc.sbuf_tensor("sb_" + name, list(shape), dt))

    def push(self):
        self.stacks.append(contextlib.ExitStack())

    def pop(self):
        self.S.barrier()
        self.stacks.pop().close()

    @staticmethod
    def _fs(ap):
        try:
            return float(ap.free_size())
        except Exception:
            return 512.0

    TSETS = {AF.Silu: "silu", AF.Exp: "lnexp", AF.Ln: "lnexp", AF.Sigmoid: "sigm"}

    def act(self, r, w, **kw):
        c = 0.25 + self._fs(kw["out"]) / 1200.0
        self.S.add("act", lambda e: e.activation(**kw), r, w, cost=c, tset=self.TSETS.get(kw["func"]))

    def dve(self, opname, r, w, **kw):
        o = kw.get("out", kw.get("ap"))
        c = 0.08 + self._fs(o) / 960.0
        self.S.add("dve", lambda e: getattr(e, opname)(**kw), r, w, cost=c)

    def pool(self, opname, r, w, **kw):
        o = kw.get("out", kw.get("ap"))
        c = 0.15 + self._fs(o) / 500.0
        self.S.add("pool", lambda e: getattr(e, opname)(**kw), r, w, cost=c)

    def mm(self, lst, r, w):
        def f(e):
            ins = None
            for (o, l, rh, st, sp) in lst:
                ins = e.matmul(out=o, lhsT=l, rhs=rh, start=st, stop=sp)
            return ins
        c = 0.1
        for (o, l, rh, st, sp) in lst:
            n = self._fs(o)
            c += max(0.065, n / 2400.0) * (4.0 if l.dtype == F32 else 1.0)
        self.S.add("pe", f, r, w, cost=c)

    def tr(self, lst, ident, r, w):
        def f(e):
            ins = None
            for (o, i) in lst:
                ins = e.transpose(out=o, in_=i, identity=ident)
            return ins
        self.S.add("pe", f, r, w, cost=0.1 + 0.11 * len(lst))

    def dma(self, eng, out, in_, r, w):
        try:
            nbytes = float(out.nbytes())
        except Exception:
            nbytes = 5e5
        self.S.add(eng, lambda e: e.dma_start(out=out, in_=in_), r, w, dma=1, cost=nbytes / 1.5e5)


def bc(ap, shape):
    return ap.to_broadcast(list(shape))


import os
LVL = int(os.environ.get("SSD_LVL", "9"))


def build(dbg=None):
    k = K(dbg)
    nc = k.nc
    S = k.S
    x_own = k.din("x_own", [HALF, D])
    x_prev = k.din("x_prev", [HALF, D])
    flag_d = k.din("flag", [128, 1])
    mem_d = k.din("mem", [MEM, D])
    w_in = k.din("w_in", [D, INW])
    w_out = k.din("w_out", [2 * D, D])
    w_q = k.din("w_q", [D, D])
    w_kv = k.din("w_kv", [D, 2 * D])
    w_o = k.din("w_o", [D, D])
    w_gate = k.din("w_gate", [D, DFF])
    w_up = k.din("w_up", [D, DFF])
    w_down = k.din("w_down", [DFF, D])
    gbc_d = k.din("gbc", [128, 6, D])
    dbc_d = k.din("dbc", [128, D])
    sm_d = k.din("small", [128, 64])
    cw_d = k.din("cw", [128, 12, 4])
    cb_d = k.din("cb", [128, 12])
    fw_d = k.din("fw", [128, 8, 31])
    fv_d = k.din("fv", [128, 8, 3])
    ident_d = k.din("ident", [128, 128], BF16)
    cst_d = k.din("cst", [128, 4, 128])
    out_d = nc.dram_tensor("out", [HALF, D], F32, kind="ExternalOutput").ap()
    xr_d = nc.dram_tensor("xr", [HALF, D], F32, kind="Internal").ap()

    xo_t = x_own.rearrange("(t p) d -> t p d", p=128)
    xp_t = x_prev.rearrange("(t p) d -> t p d", p=128)
    xr_t = xr_d.rearrange("(t p) d -> t p d", p=128)
    out_t = out_d.rearrange("(t p) d -> t p d", p=128)

    gains = {}

    def load_gains(idxs):
        k.gcount = getattr(k, "gcount", 0) + 1
        gb = k.sb(f"gains{k.gcount}", [128, len(idxs), D], F32)
        for i, gi in enumerate(idxs):
            key = f"gain{gi}_{k.gcount}"
            k.dma("sp", gb[:, i, :], gbc_d[:, gi, :], [], [key])
            gains[gi] = (gb[:, i, :], key)
    sm = k.sb("sm", [128, 64], F32)
    cw = k.sb("cw", [128, 12, 4], F32)
    cb = k.sb("cb", [128, 12], F32)
    fw = k.sb("fw", [128, 8, 31], F32)
    fv = k.sb("fv", [128, 8, 3], F32)
    ident = k.sb("ident", [128, 128], BF16)
    cst = k.sb("cst", [128, 4, 128], F32)
    flag = k.sb("flag", [128, 1], F32)
    epsT = k.sb("epsT", [128, 1], F32)
    ones_bf = k.sb("ones_bf", [128, 128], BF16)
    for (dst, src, nm) in [ (sm, sm_d, "sm"), (cw, cw_d, "cw"),
                           (cb, cb_d, "cb"), (fw, fw_d, "fw"), (fv, fv_d, "fv"), (ident, ident_d, "ident"),
                           (cst, cst_d, "cst"), (flag, flag_d, "flag")]:
        k.dma("sp", dst[:], src, [], [nm])
    k.dve("memset", [], ["epsT"], ap=epsT[:], constant=EPS)
    k.dve("memset", [], ["ones_bf"], ap=ones_bf[:], constant=1.0)

    PS = k.st.enter_context(nc.psum_tensor("PS", [128, 8, 512], F32))

    def pb(i):
        return PS[:, i, :]

    def pbk(i):
        return f"ps{i}"

    ss_rot = Rot(k, "ss", int(os.environ.get("SSROT", "4")), [128, 4], F32)
    sq_rot = Rot(k, "sq", 1, [128, D], BF16)
    h_rot = Rot(k, "hb", int(os.environ.get("HROT", "2")), [128, D], BF16)

    def rstd_from_ss(ss, sskey, n, inv_n):
        k.act([sskey, "epsT"], [sskey], out=ss[:, 0:n], in_=ss[:, 0:n], func=AF.Ln, scale=inv_n, bias=epsT[:])
        k.act([sskey], [sskey], out=ss[:, 0:n], in_=ss[:, 0:n], func=AF.Exp, scale=-0.5)

    def norm_tile(x_ap, xkey, gidx, h_out, hkey):
        ss, sskey = ss_rot.next()
        sq, sqkey = sq_rot.next()
        k.act([xkey], [sqkey, sskey], out=sq[:], in_=x_ap, func=AF.Square, accum_out=ss[:, 0:1])
        rstd_from_ss(ss, sskey, 1, 1.0 / D)
        k.dve("scalar_tensor_tensor", [xkey, sskey, gains[gidx][1]], [hkey], out=h_out, in0=x_ap, scalar=ss[:, 0:1],
              in1=gains[gidx][0], op0=ALU.mult, op1=ALU.mult)

    def transpose_to(h_ap, hkey, dstT, dkey, col0, bank):
        pT = pb(bank).bitcast(BF16)
        k.tr([(pT[:, kc * 128:(kc + 1) * 128], h_ap[:, kc * 128:(kc + 1) * 128]) for kc in range(8)], ident[:],
             [hkey, "ident"], [pbk(bank)])
        k.act([pbk(bank)], [dkey], out=dstT[:, :, col0:col0 + 128], in_=pT.rearrange("p (a b) -> p a b", a=8),
              func=AF.Copy)

    x_res = None

    src_t = xo_t if dbg == "nomixer" else xr_t
    tri = cst[:, 0, :]
    su = cst[:, 1, :]
    onesf = cst[:, 2, :]
    mask01 = cst[:, 3, :]
    do_ssd = dbg in (None, "ssd")
    do_conf = dbg in (None, "conf")

    if do_ssd:
        k.push()
        load_gains([0, 5])
        dbc = k.sb("dbc", [128, D], F32)
        k.dma("sp", dbc[:], dbc_d, [], ["dbc"])
        xt_rot = Rot(k, "xta", int(os.environ.get("XTA", "3")), [128, D], F32)
        xb3_rot = Rot(k, "xtc", int(os.environ.get("XTC", "1")), [128, D], F32)
        Wa = k.sb("Wa", [128, 8, 2576], BF16)
        Wot = k.sb("Wot", [128, 8, D], BF16)
        w_in_v = w_in.rearrange("(kc p) n -> p kc n", p=128)
        for q in range(4):
            k.dma("pool", Wa[:, :, 1024 + 384 * q:1024 + 384 * (q + 1)], w_in_v[:, :, 1024 + 384 * q:1024 + 384 * (q + 1)], [], [f"Wa_x{q}"])
        k.dma("pool", Wa[:, :, 2560:2576], w_in_v[:, :, 2560:2576], [], ["Wa_dt"])
        k.dma("pool", Wa[:, :, 0:1024], w_in_v[:, :, 0:1024], [], ["Wa_z"])
        k.dma("pool", Wot[:], w_out[0:D, :].rearrange("(kc p) n -> p kc n", p=128), [], ["Wot"])
        diag4 = k.sb("diag4", [128, 12, 4, 128], BF16)
        for c in range(12):
            k.dve("tensor_tensor", ["ident", "cw"], [f"diag4_{c}"], out=diag4[:, c, :, :], in0=bc(ident[:].unsqueeze(1), [128, 4, 128]),
                  in1=bc(cw[:, c, :].unsqueeze(2), [128, 4, 128]), op=ALU.mult)
        hT_r = Rot(k, "hT1_", 2, [128, 8, MT], BF16)
        xbcT_r = Rot(k, "xbcT_", 2, [128, 12, MT], BF16)
        dt4_r = Rot(k, "dt4_", 2, [128, 4, 16], F32)
        dA4_r = Rot(k, "dA4_", 2, [128, 4, 16], F32)
        sz_r = Rot(k, "sz_", 2, [128, 4, D], BF16)
        pre_rot = Rot(k, "pre", int(os.environ.get("PRE", "3")), [128, 515], BF16)
        hist = k.sb("hist", [128, 12, 3], BF16)
        dtp = k.sb("dtp", [128, 4, 16], F32)
        nA = k.sb("nA", [128, 16], F32)
        ex3_r = Rot(k, "ex3_", 2, [128, 48], F32)
        xdt_r = Rot(k, "xdt_", 2, [128, D], BF16)
        xD_r = Rot(k, "xD_", 1, [128, D], BF16)
        Btok_r = Rot(k, "Btok_", 2, [128, 256], BF16)
        E_r = Rot(k, "E_", 2, [128, 16, 128], BF16)
        CBm_r = Rot(k, "CBm_", 2, [128, 2, 128], BF16)
        SUhi = k.sb("SUhi", [128, 8, 128], BF16)
        SUlo = k.sb("SUlo", [128, 8, 128], BF16)
        su_bf = k.sb("su_bf", [128, 128], BF16)
        tri_bf = k.sb("tri_bf", [128, 128], BF16)
        dAh = k.sb("dAh", [128, 16], BF16)
        dAh32 = k.sb("dAh32", [128, 16], F32)
        dAl = k.sb("dAl", [128, 16], F32)
        k.dve("tensor_copy", ["cst"], ["su_bf"], out=su_bf[:], in_=cst[:, 1, :])
        k.dve("tensor_copy", ["cst"], ["tri_bf"], out=tri_bf[:], in_=cst[:, 0, :])
        xw_r = Rot(k, "xw_", 2, [128, D], BF16)
        y1 = k.sb("y1", [128, D], F32)
        yn = k.sb("yn", [128, D], BF16)
        ynT = k.sb("ynT", [128, 8, 128], BF16)
        Sst = k.sb("Sst", [128, D], F32)
        Sbf = k.sb("Sbf", [128, D], BF16)
        k.dve("memset", [], ["hist"], ap=hist[:], constant=0.0)
        k.dve("memset", [], ["Sst"], ap=Sst[:], constant=0.0)
        k.dve("memset", [], ["Sbf"], ap=Sbf[:], constant=0.0)
        k.act(["sm"], ["nA"], out=nA[:], in_=sm[:, 16:32], func=AF.Exp)
        k.dve("tensor_scalar", ["nA"], ["nA"], out=nA[:], in0=nA[:], scalar1=-1.0, scalar2=None, op0=ALU.mult)
        ATB = int(os.environ.get("ATB", "2"))
        YTB = int(os.environ.get("YTB", "0"))
        pT0 = pb(ATB).bitcast(BF16)
        pTy = pb(YTB).bitcast(BF16)
        G1 = gains[5]

        mt_state = {}

        def src_tile(g_mt, st):
            return (xp_t if g_mt < 4 else xo_t)[(g_mt % 4) * 4 + st]

        mt_h = {}

        def F1(mt, pair):
            if pair == 0:
                mt_h[mt] = hT_r.next()
            hT, hTk = mt_h[mt]
            items = []
            for st in (2 * pair, 2 * pair + 1):
                xt, xtk = xt_rot.next()
                k.dma("sp", xt[:], src_tile(mt, st), [], [xtk])
                ss, sskey = ss_rot.next()
                sq, sqkey = sq_rot.next()
                k.act([xtk], [sqkey, sskey], out=sq[:], in_=xt[:], func=AF.Square, accum_out=ss[:, 0:1])
                rstd_from_ss(ss, sskey, 1, 1.0 / D)
                items.append((st, xt, xtk, ss, sskey))
            hbs = []
            for (st, xt, xtk, ss, sskey) in items:
                hb, hk = h_rot.next()
                k.dve("scalar_tensor_tensor", [xtk, sskey, gains[0][1]], [hk], out=hb[:], in0=xt[:], scalar=ss[:, 0:1],
                      in1=gains[0][0], op0=ALU.mult, op1=ALU.mult)
                hbs.append((st, hb, hk))
            F1B = int(os.environ.get("F1B", "0"))
            for i, (st, hb, hk) in enumerate(hbs):
                pT = pb(F1B + i).bitcast(BF16)
                k.tr([(pT[:, kc * 128:(kc + 1) * 128], hb[:, kc * 128:(kc + 1) * 128]) for kc in range(8)], ident[:],
                     [hk, "ident"], [pbk(F1B + i)])
            for i, (st, hb, hk) in enumerate(hbs):
                pT = pb(F1B + i).bitcast(BF16)
                k.act([pbk(F1B + i)], [hTk], out=hT[:, :, st * 128:(st + 1) * 128], in_=pT.rearrange("p (a b) -> p a b", a=8), func=AF.Copy)

        def F2(mt, q):
            hT, hTk = mt_h[mt]
            if q == 0:
                xbcT, xbk = xbcT_r.next()
                dt4, dt4k = dt4_r.next()
                dA4, dA4k = dA4_r.next()
                szm, szk = sz_r.next()
                mt_state[mt] = (hT, hTk, xbcT, xbk, dt4, dt4k, dA4, dA4k, szm, szk)
            hT, hTk, xbcT, xbk, dt4, dt4k, dA4, dA4k, szm, szk = mt_state[mt]
            pres = {}

            def inproj(c):
                b1 = c % 2
                k.mm([(pb(b1), Wa[:, kc, 1024 + c * 128:1024 + (c + 1) * 128], hT[:, kc, :], kc == 0, kc == 7)
                      for kc in range(8)], [f"Wa_x{c // 3}", hTk], [pbk(b1)])
                pre, prek = pre_rot.next()
                pres[c] = (pre, prek)
                k.act([pbk(b1)], [prek], out=pre[:, 3:515], in_=pb(b1), func=AF.Copy)
                k.dve("tensor_copy", ["hist", prek], [prek], out=pre[:, 0:3], in_=hist[:, c, :])
                k.dve("tensor_copy", [prek], ["hist"], out=hist[:, c, :], in_=pre[:, 512:515])

            def conv(c):
                pre, prek = pres[c]
                b2 = 2 + c % 2
                k.mm([(pb(b2), diag4[:, c, kk, :], pre[:, kk:kk + 512], kk == 0, kk == 3) for kk in range(4)],
                     [f"diag4_{c}", prek], [pbk(b2)])
                k.act([pbk(b2), "cb"], [f"{xbk}_{c}"], out=xbcT[:, c, :], in_=pb(b2), func=AF.Silu, bias=cb[:, c:c + 1], scale=1.0)

            c0 = 3 * q
            skipC = (mt < 3 and q == 3)
            inproj(c0)
            if not skipC:
                inproj(c0 + 1)
            conv(c0)
            if not skipC:
                inproj(c0 + 2)
                conv(c0 + 1)
            if mt >= 4:
                st = q
                k.mm([(pb(hf2), hT[:, kc, st * 128:(st + 1) * 128], Wa[:, kc, hf2 * 512:(hf2 + 1) * 512], kc == 0, kc == 7)
                      for hf2 in range(2) for kc in range(8)], ["Wa_z", hTk], [pbk(0), pbk(1)])
            if not skipC:
                conv(c0 + 2)
            if mt >= 4:
                k.act([pbk(0), pbk(1)], [szk], out=szm[:, q, :].rearrange("p (a b) -> p a b", a=2), in_=PS[:, 0:2, :], func=AF.Silu)
            if q == 3:
                for st in range(4):
                    k.mm([(pb(1)[:, st * 16:(st + 1) * 16], hT[:, kc, st * 128:(st + 1) * 128], Wa[:, kc, 2560:2576], kc == 0, kc == 7)
                          for kc in range(8)], ["Wa_dt", hTk], [pbk(1)])
                k.dve("tensor_tensor", [pbk(1), "sm"], ["dtp"], out=dtp[:], in0=pb(1)[:, 0:64].rearrange("p (a b) -> p a b", a=4),
                      in1=bc(sm[:, 0:16].unsqueeze(1), [128, 4, 16]), op=ALU.add)
                k.act(["dtp"], ["dtp"], out=dtp[:], in_=dtp[:], func=AF.Exp)
                k.act(["dtp", "cst"], [dt4k], out=dt4[:], in_=dtp[:], func=AF.Ln, bias=cst[:, 2, 0:1], scale=1.0)
                k.dve("tensor_tensor", [dt4k, "nA"], [dA4k], out=dA4[:], in0=dt4[:], in1=bc(nA[:].unsqueeze(1), [128, 4, 16]), op=ALU.mult)

        ch_state = {}

        def A(g):
            mt, st = g // 4, g % 4
            full = mt >= 4
            hT, hTk, xbcT, xbk, dt4, dt4k, dA4, dA4k, szm, szk = mt_state[mt]
            cs = slice(st * 128, (st + 1) * 128)
            xdt, xdtk = xdt_r.next()
            xD, xDk = xD_r.next()
            Btok, Btk = Btok_r.next()
            ex3, ex3k = ex3_r.next()
            E, Ek = E_r.next()
            CBm, CBk = CBm_r.next()
            xw, xwk = xw_r.next()
            ch_state[g] = (xdt, xdtk, xD, xDk, Btok, Btk, ex3, ex3k, E, Ek, xw, xwk)
            k.tr([(pT0[:, c * 128:(c + 1) * 128], xbcT[:, c, cs]) for c in range(8)], ident[:],
                 [f"{xbk}_{c}" for c in range(8)] + ["ident"], [pbk(ATB)])
            k.dve("tensor_tensor", [pbk(ATB), dt4k], [xdtk], out=xdt[:].rearrange("p (h q) -> p h q", h=16),
                  in0=pT0.rearrange("p (h q) -> p h q", h=16), in1=bc(dt4[:, st, :].unsqueeze(2), [128, 16, 64]), op=ALU.mult)
            if full:
                k.dve("tensor_tensor", [pbk(ATB), "dbc"], [xDk], out=xD[:], in0=pT0, in1=dbc[:], op=ALU.mult)
            k.tr([(pT0[:, gg * 128:(gg + 1) * 128], xbcT[:, 8 + gg, cs]) for gg in range(2)], ident[:],
                 [f"{xbk}_8", f"{xbk}_9", "ident"], [pbk(ATB)])
            k.act([pbk(ATB)], [Btk], out=Btok[:], in_=pT0[:, 0:256], func=AF.Copy)
            dA = dA4[:, st, :]
            k.mm([(pb(1)[:, 0:16], tri, dA, True, True), (pb(1)[:, 16:32], su, dA, True, True),
                  (pb(1)[:, 32:48], onesf, dA, True, True)], ["cst", dA4k], [pbk(1)])
            k.act([pbk(1)], [ex3k], out=ex3[:], in_=pb(1)[:, 0:48], func=AF.Exp)
            k.pool("tensor_tensor", [xdtk, ex3k], [xwk], out=xw[:].rearrange("p (h q) -> p h q", h=16),
                   in0=xdt[:].rearrange("p (h q) -> p h q", h=16), in1=bc(ex3[:, 16:32].unsqueeze(2), [128, 16, 64]), op=ALU.mult)
            if full:
                k.dve("tensor_copy", [dA4k], ["dAh"], out=dAh[:], in_=dA)
                k.dve("tensor_copy", ["dAh"], ["dAh32"], out=dAh32[:], in_=dAh[:])
                k.dve("tensor_tensor", [dA4k, "dAh32"], ["dAl"], out=dAl[:], in0=dA, in1=dAh32[:], op=ALU.subtract)
                for r in range(2):
                    k.dve("tensor_tensor", ["su_bf", "dAh"], ["SUhi"], out=SUhi[:], in0=bc(su_bf[:].unsqueeze(1), [128, 8, 128]),
                          in1=bc(dAh[:, r * 8:(r + 1) * 8].unsqueeze(2), [128, 8, 128]), op=ALU.mult)
                    k.dve("tensor_tensor", ["su_bf", "dAl"], ["SUlo"], out=SUlo[:], in0=bc(su_bf[:].unsqueeze(1), [128, 8, 128]),
                          in1=bc(dAl[:, r * 8:(r + 1) * 8].unsqueeze(2), [128, 8, 128]), op=ALU.mult)
                    for hq in range(2):
                        bank = 2 + hq
                        lst = []
                        for hh in range(4):
                            h = hq * 4 + hh
                            lst.append((pb(bank)[:, hh * 128:(hh + 1) * 128], SUhi[:, h, :], tri_bf[:], True, False))
                            lst.append((pb(bank)[:, hh * 128:(hh + 1) * 128], SUlo[:, h, :], tri_bf[:], False, True))
                        k.mm(lst, ["SUhi", "SUlo", "tri_bf"], [pbk(bank)])
                        k.act([pbk(bank)], [Ek], out=E[:, r * 8 + hq * 4:r * 8 + hq * 4 + 4, :],
                              in_=pb(bank).rearrange("p (a b) -> p a b", a=4), func=AF.Exp)
                k.mm([(pb(1)[:, 64 + gg * 128:64 + (gg + 1) * 128], xbcT[:, 8 + gg, cs], xbcT[:, 10 + gg, cs], True, True) for gg in range(2)],
                     [f"{xbk}_8", f"{xbk}_9", f"{xbk}_10", f"{xbk}_11"], [pbk(1)])
                k.dve("tensor_tensor", [pbk(1), "cst"], [CBk], out=CBm[:], in0=pb(1)[:, 64:320].rearrange("p (a b) -> p a b", a=2),
                      in1=bc(mask01.unsqueeze(1), [128, 2, 128]), op=ALU.mult)
                k.dve("tensor_tensor", [Ek, CBk], [Ek], out=E[:].rearrange("p (g r) l -> p g r l", g=2),
                      in0=E[:].rearrange("p (g r) l -> p g r l", g=2), in1=bc(CBm[:].unsqueeze(2), [128, 2, 8, 128]), op=ALU.mult)

        bst = {}

        def B1(g):
            mt, st = g // 4, g % 4
            hT, hTk, xbcT, xbk, dt4, dt4k, dA4, dA4k, szm, szk = mt_state[mt]
            xdt, xdtk, xD, xDk, Btok, Btk, ex3, ex3k, E, Ek, xw, xwk = ch_state[g]
            cs = slice(st * 128, (st + 1) * 128)
            t = (mt % 4) * 4 + st
            lst = []
            for hf2 in range(2):
                lst.append((pb(4 + hf2), ident[:], xD[:, hf2 * 512:(hf2 + 1) * 512], True, False))
            for h in range(16):
                lst.append((pb(4 + h // 8)[:, (h % 8) * 64:(h % 8 + 1) * 64], E[:, h, :], xdt[:, h * 64:(h + 1) * 64], False, h % 8 == 7))
            k.mm(lst, ["ident", xDk, Ek, xdtk], [pbk(4), pbk(5)])
            k.mm([(pb(6 + gg), xbcT[:, 10 + gg, cs], Sbf[:, gg * 512:(gg + 1) * 512], True, True) for gg in range(2)],
                 [f"{xbk}_10", f"{xbk}_11", "Sbf"], [pbk(6), pbk(7)])
            k.dve("tensor_tensor", [pbk(6), pbk(7), ex3k], ["y1"], out=y1[:].rearrange("p (h q) -> p h q", h=16),
                  in0=PS[:, 6:8, :].rearrange("p a (h q) -> p (a h) q", q=64), in1=bc(ex3[:, 0:16].unsqueeze(2), [128, 16, 64]), op=ALU.mult)
            k.dve("tensor_tensor", [pbk(4), pbk(5), "y1"], ["y1"], out=y1[:].rearrange("p (a b) -> p a b", a=2),
                  in0=PS[:, 4:6, :], in1=y1[:].rearrange("p (a b) -> p a b", a=2), op=ALU.add)
            k.dve("tensor_tensor", ["y1", szk], ["y1"], out=y1[:], in0=y1[:], in1=szm[:, st, :], op=ALU.mult)
            k.dve("tensor_tensor", ["y1", G1[1]], ["yn"], out=yn[:], in0=y1[:], in1=G1[0], op=ALU.mult)
            ss, sskey = ss_rot.next()
            sq, sqkey = sq_rot.next()
            for gg in range(2):
                k.act(["y1"], [sqkey, sskey], out=sq[:, 0:512], in_=y1[:, gg * 512:(gg + 1) * 512], func=AF.Square, accum_out=ss[:, gg:gg + 1])
            rstd_from_ss(ss, sskey, 2, 1.0 / 512)
            bst[g] = (ss, sskey, t)

        def B2(g):
            k.tr([(pTy[:, kc * 128:(kc + 1) * 128], yn[:, kc * 128:(kc + 1) * 128]) for kc in range(8)], ident[:], ["yn", "ident"], [pbk(YTB)])
            k.act([pbk(YTB)], ["ynT"], out=ynT[:], in_=pTy.rearrange("p (a b) -> p a b", a=8), func=AF.Copy)

        def B3(g):
            ss, sskey, t = bst[g]
            k.mm([(pb(4 + 2 * gg + hf2), ynT[:, 4 * gg + kc, :], Wot[:, 4 * gg + kc, hf2 * 512:(hf2 + 1) * 512], kc == 0, kc == 3)
                  for gg in range(2) for hf2 in range(2) for kc in range(4)], ["Wot", "ynT"], [pbk(4), pbk(5), pbk(6), pbk(7)])
            xt, xtk = xb3_rot.next()
            k.dma("sp", xt[:], xo_t[t], [], [xtk])
            for gg in range(2):
                k.dve("scalar_tensor_tensor", [pbk(4 + 2 * gg), pbk(5 + 2 * gg), sskey, xtk], [xtk], out=xt[:].rearrange("p (a b) -> p a b", a=2),
                      in0=PS[:, 4 + 2 * gg:6 + 2 * gg, :], scalar=ss[:, gg:gg + 1], in1=xt[:].rearrange("p (a b) -> p a b", a=2),
                      op0=ALU.mult, op1=ALU.add)
            k.dma("sp", xr_t[t], xt[:], [xtk], [f"xr{t}"])

        def C(g):
            xdt, xdtk, xD, xDk, Btok, Btk, ex3, ex3k, E, Ek, xw, xwk = ch_state[g]
            k.mm([(pb(2 + gg), Btok[:, gg * 128:(gg + 1) * 128], xw[:, gg * 512:(gg + 1) * 512], True, True) for gg in range(2)],
                 [Btk, xwk], [pbk(2), pbk(3)])
            k.dve("tensor_tensor", ["Sst", ex3k], ["Sst"], out=Sst[:].rearrange("p (h q) -> p h q", h=16),
                  in0=Sst[:].rearrange("p (h q) -> p h q", h=16), in1=bc(ex3[:, 32:48].unsqueeze(2), [128, 16, 64]), op=ALU.mult)
            k.dve("tensor_tensor", ["Sst", pbk(2), pbk(3)], ["Sst"], out=Sst[:].rearrange("p (a b) -> p a b", a=2),
                  in0=Sst[:].rearrange("p (a b) -> p a b", a=2), in1=PS[:, 2:4, :], op=ALU.add)
            k.act(["Sst"], ["Sbf"], out=Sbf[:], in_=Sst[:], func=AF.Copy)

        NG = 32
        F1(0, 0)
        F1(0, 1)
        for q in range(4):
            F2(0, q)
        F1(1, 0)
        F1(1, 1)
        A(0)
        for g in range(NG):
            mt, st = g // 4, g % 4
            own = mt >= 4
            if own:
                B1(g)
            if g < NG - 1:
                C(g)
            if g == 15:
                k.dve("tensor_scalar", ["Sst", "flag"], ["Sst"], out=Sst[:], in0=Sst[:], scalar1=flag[:, 0:1], scalar2=None, op0=ALU.mult)
                k.act(["Sst"], ["Sbf"], out=Sbf[:], in_=Sst[:], func=AF.Copy)
            if mt + 1 < 8:
                F2(mt + 1, st)
            if own:
                B2(g)
            if g + 1 < NG:
                A(g + 1)
            if own:
                B3(g)
            if st in (0, 2) and mt + 2 < 8:
                F1(mt + 2, st // 2)
        k.pop()

    if do_conf:
        k.push()
        base_t = xr_t if do_ssd else xo_t
        load_gains([0])
        xt_rot = Rot(k, "xtb", int(os.environ.get("XTB", "3")), [128, D], F32)
        xb_rot = Rot(k, "xbb", int(os.environ.get("XBB", "3")), [128, D], F32)
        Wag = k.sb("Wag", [128, 8, 2048], BF16)
        Wob = k.sb("Wob", [128, 8, D], BF16)
        w_in_v2 = w_in.rearrange("(kc p) n -> p kc n", p=128)
        for cp in range(4):
            k.dma("pool", Wag[:, :, cp * 256:(cp + 1) * 256], w_in_v2[:, :, A0 + cp * 256:A0 + (cp + 1) * 256], [], [f"Wag_a{cp}"])
            k.dma("pool", Wag[:, :, D + cp * 256:D + (cp + 1) * 256], w_in_v2[:, :, G0 + cp * 256:G0 + (cp + 1) * 256], [], [f"Wag_g{cp}"])
        k.dma("pool", Wob[:], w_out[D:2 * D, :].rearrange("(kc p) n -> p kc n", p=128), [], ["Wob"])
        NDV = int(os.environ.get("NDV", "10"))
        NPE = 31 - NDV
        diag = k.sb("diag", [128, 8, NPE, 128], BF16)
        DGP = os.environ.get("DGP", "dve")
        for c in range(8):
            on_pool = (DGP == "pool") or (DGP == "alt" and c % 2 == 1)
            (k.pool if on_pool else k.dve)("tensor_tensor", ["ident", "fw"], [f"diag{c}"], out=diag[:, c, :, :], in0=bc(ident[:].unsqueeze(1), [128, NPE, 128]),
                  in1=bc(fw[:, c, NDV:31].unsqueeze(2), [128, NPE, 128]), op=ALU.mult)
        hT2_r = Rot(k, "hT2_", 2, [128, 8, MT], BF16)
        unT = k.sb("unT", [128, 8, MT], BF16)
        uT_r = Rot(k, "uT_", 2, [128, 8, 30 + MT], BF16)
        sgm_rot = Rot(k, "sgm", 2, [128, MT], F32)
        cv_r = Rot(k, "cv_", 2, [128, 8, MT], BF16)
        cvsq_rot = Rot(k, "cvsq", 2, [128, MT], BF16)
        mean_r = Rot(k, "mean_", 1, [128, MT], F32)
        rstdv_r = Rot(k, "rstdv_", 1, [128, MT], F32)
        t1_rot = Rot(k, "t1", 2, [128, MT], F32)
        accv_rot = Rot(k, "accv", 2, [128, MT], F32)
        gl_i = [0]

        def glu_chunks(hT2, hT2k, uT, uTk, ncols, col0):
            for c in range(8):
                ba = gl_i[0] % 2
                bg = 2 + gl_i[0] % 2
                gl_i[0] += 1
                k.mm([(pb(ba)[:, 0:ncols], Wag[:, kc, c * 128:(c + 1) * 128], hT2[:, kc, col0:col0 + ncols], kc == 0, kc == 7)
                      for kc in range(8)], [f"Wag_a{c // 2}", hT2k], [pbk(ba)])
                k.mm([(pb(bg)[:, 0:ncols], Wag[:, kc, D + c * 128:D + (c + 1) * 128], hT2[:, kc, col0:col0 + ncols], kc == 0, kc == 7)
                      for kc in range(8)], [f"Wag_g{c // 2}", hT2k], [pbk(bg)])
                sgm, sgk = sgm_rot.next()
                k.act([pbk(bg)], [sgk], out=sgm[:, 0:ncols], in_=pb(bg)[:, 0:ncols], func=AF.Sigmoid)
                k.dve("tensor_tensor", [pbk(ba), sgk], [f"{uTk}_{c}"], out=uT[:, c, 30 + col0:30 + col0 + ncols], in0=pb(ba)[:, 0:ncols],
                      in1=sgm[:, 0:ncols], op=ALU.mult)

        uTp, uTpk = uT_r.next()
        k.dve("memset", [], [f"{uTpk}_{c}" for c in range(8)] + [f"{uTpk}_h"], ap=uTp[:], constant=0.0)
        hTp, hTpk = hT2_r.next()
        xt, xtk = xt_rot.next()
        k.dma("sp", xt[:], xp_t[NT - 1], [], [xtk])
        hb, hk = h_rot.next()
        norm_tile(xt[:], xtk, 0, hb[:], hk)
        transpose_to(hb, hk, hTp, hTpk, 384, 7)
        glu_chunks(hTp, hTpk, uTp, uTpk, 128, 384)
        for m in range(NT // 4):
            hT2, hT2k = hT2_r.next()
            uT, uTk = uT_r.next()
            cv, cvk = cv_r.next()
            mean, meank = mean_r.next()
            rstdv, rstdk = rstdv_r.next()
            k.dve("tensor_copy", [f"{uTpk}_{c}" for c in range(8)], [f"{uTk}_h"], out=uT[:, :, 0:30], in_=uTp[:, :, MT:MT + 30])
            for st in range(4):
                t = m * 4 + st
                xt, xtk = xt_rot.next()
                k.dma("sp", xt[:], xo_t[t], [], [xtk])
                hb, hk = h_rot.next()
                norm_tile(xt[:], xtk, 0, hb[:], hk)
                transpose_to(hb, hk, hT2, hT2k, st * 128, 7)
            glu_chunks(hT2, hT2k, uT, uTk, MT, 0)
            for c in range(8):
                bcv = 4 + c % 2
                k.mm([(pb(bcv), diag[:, c, kk - NDV, :], uT[:, c, kk:kk + MT], kk == NDV, kk == 30) for kk in range(NDV, 31)],
                     [f"diag{c}", f"{uTk}_{c}", f"{uTk}_h"], [pbk(bcv)])
                accv, acck = accv_rot.next()
                k.dve("tensor_scalar", [f"{uTk}_{c}", f"{uTk}_h", "fw", "fv"], [acck], out=accv[:], in0=uT[:, c, 0:MT], scalar1=fw[:, c, 0:1],
                      scalar2=fv[:, c, 0:1], op0=ALU.mult, op1=ALU.add)
                for kk in range(1, NDV):
                    k.dve("scalar_tensor_tensor", [f"{uTk}_{c}", f"{uTk}_h", "fw", acck], [acck], out=accv[:], in0=uT[:, c, kk:kk + MT],
                          scalar=fw[:, c, kk:kk + 1], in1=accv[:], op0=ALU.mult, op1=ALU.add)
                k.dve("tensor_tensor", [pbk(bcv), acck], [f"{cvk}_{c}"], out=cv[:, c, :], in0=pb(bcv), in1=accv[:], op=ALU.add)
                cvsq, cqk = cvsq_rot.next()
                k.act([f"{cvk}_{c}"], [cqk], out=cvsq[:], in_=cv[:, c, :], func=AF.Square)
                k.mm([(pb(6), ones_bf[:], cv[:, c, :], c == 0, c == 7)], ["ones_bf", f"{cvk}_{c}"], [pbk(6)])
                k.mm([(pb(7), ones_bf[:], cvsq[:], c == 0, c == 7)], ["ones_bf", cqk], [pbk(7)])
            k.dve("tensor_scalar", [pbk(6)], [meank], out=mean[:], in0=pb(6), scalar1=1.0 / D, scalar2=None, op0=ALU.mult)
            k.dve("tensor_tensor", [meank], [rstdk], out=rstdv[:], in0=mean[:], in1=mean[:], op=ALU.mult)
            k.dve("scalar_tensor_tensor", [pbk(7), rstdk], [rstdk], out=rstdv[:], in0=pb(7), scalar=1.0 / D, in1=rstdv[:],
                  op0=ALU.mult, op1=ALU.subtract)
            k.act([rstdk, "epsT"], [rstdk], out=rstdv[:], in_=rstdv[:], func=AF.Ln, bias=epsT[:], scale=1.0)
            k.act([rstdk], [rstdk], out=rstdv[:], in_=rstdv[:], func=AF.Exp, scale=-0.5)
            for c in range(8):
                t1, t1k = t1_rot.next()
                lnp = os.environ.get("LNP", "0")
                e1 = k.pool if (lnp == "all" or (lnp == "half" and c % 2 == 1)) else k.dve
                e1("tensor_tensor", [f"{cvk}_{c}", meank], [t1k], out=t1[:], in0=cv[:, c, :], in1=mean[:], op=ALU.subtract)
                e1("tensor_tensor", [t1k, rstdk], [t1k], out=t1[:], in0=t1[:], in1=rstdv[:], op=ALU.mult)
                k.act([t1k, "fv"], [f"unT{c}"], out=unT[:, c, :], in_=t1[:], func=AF.Silu, scale=fv[:, c, 1:2], bias=fv[:, c, 2:3])
            for st in range(4):
                t = m * 4 + st
                ob = 2 * (st % 2)
                OPS = int(os.environ.get("OPS", "4"))
                per = 8 // OPS
                for part in range(OPS):
                    kcs = range(part * per, (part + 1) * per)
                    k.mm([(pb(ob + hf2), unT[:, kc, st * 128:(st + 1) * 128], Wob[:, kc, hf2 * 512:(hf2 + 1) * 512], kc == 0, kc == 7)
                          for hf2 in range(2) for kc in kcs], ["Wob"] + [f"unT{c}" for c in kcs], [pbk(ob), pbk(ob + 1)])
                xt, xtk = xb_rot.next()
                k.dma("sp", xt[:], base_t[t], [f"xr{t}"], [xtk])
                k.dve("tensor_tensor", [pbk(ob), pbk(ob + 1), xtk], [xtk], out=xt[:].rearrange("p (a b) -> p a b", a=2),
                      in0=xt[:].rearrange("p (a b) -> p a b", a=2), in1=PS[:, ob:ob + 2, :], op=ALU.add)
                k.dma("sp", xr_t[t], xt[:], [xtk], [f"xr{t}"])
            uTp, uTpk = uT, uTk
        k.pop()
    elif dbg == "ssd":
        pass

    k.push()
    x_res = k.sb("x_res", [128, NT, D], F32)
    KT = k.sb("KT", [128, 8, MEM], BF16)
    V = k.sb("V", [128, 2, D], BF16)
    with_kv = True
    if with_kv:
        k.push()
        load_gains([1])
        load_gains([2])
        WA3 = k.sb("WA3", [128, 16 * D], BF16)
        wq = WA3[:, 0:8 * D].rearrange("p (a b) -> p a b", a=8)
        wo = WA3[:, 8 * D:16 * D].rearrange("p (a b) -> p a b", a=8)
        for t in range(4):
            k.dma("sp", x_res[:, t, :], src_t[t], [f"xr{t}"], [f"xres{t}"])
        for qq in range(2):
            k.dma("pool", wq[:, :, qq * 512:(qq + 1) * 512], w_q.rearrange("(kc p) n -> p kc n", p=128)[:, :, qq * 512:(qq + 1) * 512], [], [f"WAq{qq}"])
        WA = k.sb("WA", [128, 8 * 2048], BF16)
        wkv = WA[:, 0:8 * 2048].rearrange("p (a b) -> p a b", a=8)
        w_kv_v = w_kv.rearrange("(kc p) n -> p kc n", p=128)
        for qq in range(4):
            k.dma("pool", wkv[:, :, qq * 512:(qq + 1) * 512], w_kv_v[:, :, qq * 512:(qq + 1) * 512], ["WAq1"], [f"WAkv{qq}"])
        for t in range(4, NT):
            k.dma("sp", x_res[:, t, :], src_t[t], [f"xr{t}", "WAkv1" if t < 8 else "WAkv3"], [f"xres{t}"])
        memT = k.sb("memT", [128, 8, MEM], BF16)
        mrot = Rot(k, "memx", 1, [128, D], F32)
        for mc in range(2):
            mx, mxk = mrot.next()
            k.dma("sp", mx[:], mem_d[mc * 128:(mc + 1) * 128, :], [], [mxk])
            hb, hk = h_rot.next()
            norm_tile(mx[:], mxk, 2, hb[:], hk)
            transpose_to(hb, hk, memT, "memT", mc * 128, 0)
        for c in range(8):
            bank = 1 + (c % 2)
            k.mm([(pb(bank)[:, 0:MEM], wkv[:, kc, c * 128:(c + 1) * 128], memT[:, kc, :], kc == 0, kc == 7)
                  for kc in range(8)], [f"WAkv{c // 4}", "memT"], [pbk(bank)])
            k.act([pbk(bank)], ["KT"], out=KT[:, c, :], in_=pb(bank)[:, 0:MEM], func=AF.Copy)
        for mc in range(2):
            for hf2 in range(2):
                bank = 3 + hf2
                k.mm([(pb(bank), memT[:, kc, mc * 128:(mc + 1) * 128],
                       wkv[:, kc, D + hf2 * 512:D + (hf2 + 1) * 512], kc == 0, kc == 7) for kc in range(8)],
                     [f"WAkv{2 + hf2}", "memT"], [pbk(bank)])
                k.dve("tensor_copy", [pbk(bank)], ["V"], out=V[:, mc, hf2 * 512:(hf2 + 1) * 512], in_=pb(bank))

    for qq in range(2):
        k.dma("pool", wo[:, :, qq * 512:(qq + 1) * 512], w_o.rearrange("(kc p) n -> p kc n", p=128)[:, :, qq * 512:(qq + 1) * 512], [f"WAkv{3}"], [f"WAo{qq}"])
    hxT_rot = Rot(k, "hxT", 2, [128, 8, MT], BF16)
    qT = k.sb("qT", [128, 8, MT], BF16)
    ET_rot = Rot(k, "ET", 2, [128, 2, MT], BF16)
    rden_rot = Rot(k, "rden", 2, [128, MT], F32)
    oT = k.sb("oT", [128, 8, MT], BF16)
    for m in range(NT // 4):
        hxT, hxk = hxT_rot.next()
        for st in range(4):
            t = m * 4 + st
            hb, hk = h_rot.next()
            norm_tile(x_res[:, t, :], f"xres{t}", 1, hb[:], hk)
            transpose_to(hb, hk, hxT, hxk, st * 128, 0)
        for c in range(8):
            bank = 1 + (c % 2)
            k.mm([(pb(bank), wq[:, kc, c * 128:(c + 1) * 128], hxT[:, kc, :], kc == 0, kc == 7) for kc in range(8)],
                 [f"WAq{c // 4}", hxk], [pbk(bank)])
            k.act([pbk(bank)], [f"qT{c}"], out=qT[:, c, :], in_=pb(bank), func=AF.Copy)
        for hd in range(4):
            ET, etk = ET_rot.next()
            for mc in range(2):
                bank = 3 + mc
                k.mm([(pb(bank), KT[:, 2 * hd + dc, mc * 128:(mc + 1) * 128], qT[:, 2 * hd + dc, :], dc == 0, dc == 1)
                      for dc in range(2)], ["KT", f"qT{2 * hd}", f"qT{2 * hd + 1}"], [pbk(bank)])
                k.act([pbk(bank)], [etk], out=ET[:, mc, :], in_=pb(bank), func=AF.Exp, scale=1.0 / 16.0)
            k.mm([(pb(5), ones_bf[:], ET[:, mc, :], mc == 0, mc == 1) for mc in range(2)], ["ones_bf", etk], [pbk(5)])
            rden, rdk = rden_rot.next()
            k.act([pbk(5)], [rdk], out=rden[:], in_=pb(5), func=AF.Ln)
            k.act([rdk], [rdk], out=rden[:], in_=rden[:], func=AF.Exp, scale=-1.0)
            for dc in range(2):
                bank = 6 + dc
                k.mm([(pb(bank), V[:, mc, hd * 256 + dc * 128:hd * 256 + (dc + 1) * 128], ET[:, mc, :], mc == 0, mc == 1)
                      for mc in range(2)], ["V", etk], [pbk(bank)])
                k.dve("tensor_tensor", [pbk(bank), rdk], [f"oT{2 * hd + dc}"], out=oT[:, 2 * hd + dc, :], in0=pb(bank),
                      in1=rden[:], op=ALU.mult)
        for st in range(4):
            t = m * 4 + st
            for hf2 in range(2):
                bank = 1 + hf2
                OP3 = int(os.environ.get("OP3", "1"))
                per3 = 8 // OP3
                for part in range(OP3):
                    kcs = range(part * per3, (part + 1) * per3)
                    k.mm([(pb(bank), oT[:, kc, st * 128:(st + 1) * 128], wo[:, kc, hf2 * 512:(hf2 + 1) * 512], kc == 0, kc == 7)
                          for kc in kcs], [f"WAo{hf2}"] + [f"oT{c}" for c in kcs], [pbk(bank)])
                k.dve("tensor_tensor", [pbk(bank), f"xres{t}"], [f"xres{t}"], out=x_res[:, t, hf2 * 512:(hf2 + 1) * 512],
                      in0=x_res[:, t, hf2 * 512:(hf2 + 1) * 512], in1=pb(bank), op=ALU.add)

    k.pop()
    k.push()
    load_gains([3])
    hfT = k.sb("hfT", [128, 8, HALF], BF16)
    groups = [(0, 4), (4, 4), (8, 4), (12, 4), (16, 3), (19, 3)]
    wg_rot = Rot(k, "wg", 2, [128, 8, 4 * 128], BF16)
    wu_rot = Rot(k, "wu", 2, [128, 8, 4 * 128], BF16)
    wd_rot = Rot(k, "wd", 2, [128, 4, D], BF16)
    sg_rot = Rot(k, "sg", 2, [128, MT], BF16)
    aT_rot = Rot(k, "aT", 2, [128, 4, MT], BF16)
    for gi, (c0, nch) in enumerate(groups):
        wg, wgk = wg_rot.next()
        wu, wuk = wu_rot.next()
        wd, wdk = wd_rot.next()
        k.dma("pool", wg[:, :, 0:nch * 128], w_gate.rearrange("(kc p) n -> p kc n", p=128)[:, :, c0 * 128:(c0 + nch) * 128], [], [wgk])
        k.dma("pool", wu[:, :, 0:nch * 128], w_up.rearrange("(kc p) n -> p kc n", p=128)[:, :, c0 * 128:(c0 + nch) * 128], [], [wuk])
        k.dma("pool", wd[:, 0:nch, :], w_down[c0 * 128:(c0 + nch) * 128, :].rearrange("(c p) n -> p c n", p=128), [], [wdk])
        for m in range(NT // 4):
            if gi == 0:
                for st in range(4):
                    t = m * 4 + st
                    hb, hk = h_rot.next()
                    norm_tile(x_res[:, t, :], f"xres{t}", 3, hb[:], hk)
                    transpose_to(hb, hk, hfT, f"hfT{m}", t * 128, 0)
            aT, aTk = aT_rot.next()
            for ci in range(nch):
                bg = 1 + (ci % 2)
                bu = 3 + (ci % 2)
                k.mm([(pb(bg), wg[:, kc, ci * 128:(ci + 1) * 128], hfT[:, kc, m * MT:(m + 1) * MT], kc == 0, kc == 7)
                      for kc in range(8)], [wgk, f"hfT{m}"], [pbk(bg)])
                k.mm([(pb(bu), wu[:, kc, ci * 128:(ci + 1) * 128], hfT[:, kc, m * MT:(m + 1) * MT], kc == 0, kc == 7)
                      for kc in range(8)], [wuk, f"hfT{m}"], [pbk(bu)])
                sg, sgk = sg_rot.next()
                k.act([pbk(bg)], [sgk], out=sg[:], in_=pb(bg), func=AF.Silu)
                k.dve("tensor_tensor", [pbk(bu), sgk], [aTk], out=aT[:, ci, :], in0=pb(bu), in1=sg[:], op=ALU.mult)
            for st in range(4):
                t = m * 4 + st
                for hf2 in range(2):
                    bank = 5 + hf2
                    k.mm([(pb(bank), aT[:, ci, st * 128:(st + 1) * 128], wd[:, ci, hf2 * 512:(hf2 + 1) * 512], ci == 0, ci == nch - 1)
                          for ci in range(nch)], [wdk, aTk], [pbk(bank)])
                    k.dve("tensor_tensor", [pbk(bank), f"xres{t}"], [f"xres{t}"],
                          out=x_res[:, t, hf2 * 512:(hf2 + 1) * 512], in0=x_res[:, t, hf2 * 512:(hf2 + 1) * 512],
                          in1=pb(bank), op=ALU.add)

    load_gains([4])
    o_rot = Rot(k, "ob", 2, [128, D], F32)
    for t in range(NT):
        ob, obk = o_rot.next()
        ss, sskey = ss_rot.next()
        sq, sqkey = sq_rot.next()
        k.act([f"xres{t}"], [sqkey, sskey], out=sq[:], in_=x_res[:, t, :], func=AF.Square, accum_out=ss[:, 0:1])
        rstd_from_ss(ss, sskey, 1, 1.0 / D)
        k.dve("scalar_tensor_tensor", [f"xres{t}", sskey, gains[4][1]], [obk], out=ob[:], in0=x_res[:, t, :], scalar=ss[:, 0:1],
              in1=gains[4][0], op0=ALU.mult, op1=ALU.mult)
        k.dma("sp", out_t[t], ob[:], [obk], [])
    S.emit()
    k.pop()
    k.pop()
    k.st.close()
    return nc


def make_inputs(inp, dbg=None):
    f = np.float32
    x = np.asarray(inp["x"], f)
    mem = np.asarray(inp["mem"], f)

    def row_bc(v):
        return np.broadcast_to(np.asarray(v, f).reshape(1, -1), (128, np.asarray(v).size))

    gbc = np.stack([row_bc(inp["norm_mix_g"][0]), row_bc(inp["norm_xattn_g"][0]), row_bc(inp["norm_mem_g"][0]),
                    row_bc(inp["norm_ffn_g"][0]), row_bc(inp["norm_final_g"]), row_bc(inp["ssd_norm_g"][0])], axis=1)
    dbc = row_bc(np.repeat(np.asarray(inp["ssd_D"][0], f), 64))
    small = np.zeros((128, 64), f)
    small[:, 0:16] = row_bc(inp["ssd_dt_bias"][0])
    small[:, 16:32] = row_bc(inp["ssd_A_log"][0])
    cw = np.asarray(inp["ssd_conv_w"][0], f).reshape(4, 12, 128).transpose(2, 1, 0)
    cb = np.asarray(inp["ssd_conv_b"][0], f).reshape(12, 128).T
    fw = np.asarray(inp["cf_conv_w"][0], f).reshape(31, 8, 128).transpose(2, 1, 0)
    fv = np.stack([np.asarray(inp[n][0], f).reshape(8, 128).T for n in ("cf_conv_b", "cf_ln_g", "cf_ln_b")], axis=2)
    ident = np.eye(128, dtype=f).astype(ml_dtypes.bfloat16)
    j = np.arange(128)
    tri = (j[:, None] <= j[None, :]).astype(f)
    su = (j[:, None] > j[None, :]).astype(f)
    cst = np.stack([tri, su, np.ones((128, 128), f), tri], axis=1)
    common = {
        "w_in": np.ascontiguousarray(inp["w_in"][0], f), "w_out": np.ascontiguousarray(inp["w_out"][0], f),
        "w_q": np.ascontiguousarray(inp["w_q"][0], f), "w_kv": np.ascontiguousarray(inp["w_kv"][0], f),
        "w_o": np.ascontiguousarray(inp["w_o"][0], f), "w_gate": np.ascontiguousarray(inp["w_gate"][0], f),
        "w_up": np.ascontiguousarray(inp["w_up"][0], f), "w_down": np.ascontiguousarray(inp["w_down"][0], f),
        "gbc": np.ascontiguousarray(gbc), "dbc": np.ascontiguousarray(dbc), "small": small,
        "cw": np.ascontiguousarray(cw), "cb": np.ascontiguousarray(cb), "fw": np.ascontiguousarray(fw),
        "fv": np.ascontiguousarray(fv), "ident": ident, "cst": np.ascontiguousarray(cst),
    }
    maps = []
    for c in range(8):
        b, hf = c // 2, c % 2
        d = dict(common)
        d["x_own"] = np.ascontiguousarray(x[b, hf * HALF:(hf + 1) * HALF])
        d["x_prev"] = np.ascontiguousarray(x[b, 0:HALF]) if hf == 1 else np.zeros((HALF, D), f)
        d["flag"] = np.full((128, 1), float(hf), f)
        d["mem"] = np.ascontiguousarray(mem[b])
        maps.append(d)
    return maps


_NC_CACHE = {}


def kernel(_dbg=None, **inputs):
    if _dbg not in _NC_CACHE:
        _NC_CACHE[_dbg] = build(_dbg)
    nc = _NC_CACHE[_dbg]
    maps = make_inputs(inputs, _dbg)
    res = run_bass_kernel_spmd(nc, maps, core_ids=list(range(8)))
    out = np.zeros((NB, SEQ, D), np.float32)
    for c in range(8):
        b, hf = c // 2, c % 2
        out[b, hf * HALF:(hf + 1) * HALF] = res.results[c]["out"]
    return out
```

```python
import contextlib
import os
import numpy as np
import ml_dtypes
import concourse.bass as bass
import concourse.mybir as mybir
from concourse.bass_utils import run_bass_kernel_spmd

F32 = mybir.dt.float32
BF16 = mybir.dt.bfloat16
ALU = mybir.AluOpType
AF = mybir.ActivationFunctionType

ENGS = ["pe", "dve", "act", "pool", "sp"]
SEG_CFG = {0: dict(lat=0.4, tsw=1.4, seed=199369, jit=0.02),
           1: dict(lat=0.2, tsw=1.0, seed=979871, jit=0.0),
           2: dict(lat=0.7, tsw=1.0, seed=396330, jit=0.05),
           3: dict(lat=0.7, tsw=0.8, seed=362957, jit=0.0)}

D = 1024
SEQ = 4096
NB = 4
HALF = 2048
NT = 16
MT = 512
MEM = 256
INW = 4624
DFF = 2816
NFF = 22
EPS = 1e-6
XBC0 = 1024
DT0 = 2560
A0 = 2576
G0 = 3600


class Sched:
    LAT = 0.5
    DELTA_DEFAULT = 0.5

    def __init__(self, nc, n_dma_sems=4):
        self.nc = nc
        self.ops = []
        self.seg = 0
        self.n_dma = n_dma_sems
        self._stream = {}
        self._seg_counter = 0

    def barrier(self):
        self.seg += 1

    def add(self, eng, fn, reads=(), writes=(), dma=False, cost=1.0, tset=None):
        self.ops.append(dict(eng=eng, fn=fn, reads=tuple(reads), writes=tuple(writes), dma=int(dma), cost=float(cost),
                             seg=self.seg, idx=len(self.ops), tset=tset))

    def _schedule_segment(self, ops):
        n = len(ops)
        cfg = dict(lat=self.LAT, tsw=1.4, seed=0, jit=0.0)
        cfg.update(SEG_CFG.get(self._seg_counter, {}))
        env = os.environ.get("SCHED_CFG")
        if env:
            import json as _json
            cfg.update(_json.loads(env))
        self._seg_counter += 1
        LATV = cfg["lat"]
        rng = np.random.default_rng(cfg["seed"]) if cfg["jit"] > 0 else None
        DMAF = float(os.environ.get("DMAF", "0.75"))
        PEW = float(os.environ.get("PEW", "1.0"))
        preds = [set() for _ in range(n)]
        raw = [set() for _ in range(n)]
        last_writer, readers = {}, {}
        for i, op in enumerate(ops):
            for r in op["reads"]:
                lw = last_writer.get(r)
                if lw is not None:
                    preds[i].add(lw)
                    raw[i].add(lw)
            for w in op["writes"]:
                lw = last_writer.get(w)
                if lw is not None:
                    preds[i].add(lw)
                for rd in readers.get(w, ()):
                    if rd != i:
                        preds[i].add(rd)
            for r in op["reads"]:
                readers.setdefault(r, []).append(i)
            for w in op["writes"]:
                last_writer[w] = i
                readers[w] = []
        succs = [[] for _ in range(n)]
        for i in range(n):
            for p in preds[i]:
                succs[p].append(i)

        def dur(op):
            return max(0.06, op["cost"] * DMAF) if op["dma"] else op["cost"]

        def done_lat(op):
            return (2.0 + op["cost"]) if op["dma"] else op["cost"]

        prio = [0.0] * n
        for i in range(n - 1, -1, -1):
            m = 0.0
            for sidx in succs[i]:
                if prio[sidx] > m:
                    m = prio[sidx]
            prio[i] = m + done_lat(ops[i]) * (PEW if ops[i]["eng"] == "pe" else 1.0)
        if rng is not None:
            jit = rng.random(n)
            prio = [p * (1.0 + cfg["jit"] * (j - 0.5)) for p, j in zip(prio, jit)]
        npred = [len(preds[i]) for i in range(n)]
        ready_t = [0.0] * n
        finish = [0.0] * n
        free = {e: 0.0 for e in ENGS}
        ready = {e: [] for e in ENGS}
        for i in range(n):
            if npred[i] == 0:
                ready[ops[i]["eng"]].append(i)
        order = {e: [] for e in ENGS}
        left = n
        cur_set = [None]
        TSW = cfg["tsw"]
        DELTA = float(os.environ.get('SCHED_DELTA', self.DELTA_DEFAULT))
        while left:
            best = None
            for e in ENGS:
                if not ready[e]:
                    continue
                fe = free[e]
                cand = None
                stts = {}
                mn = None
                for i in ready[e]:
                    stt = ready_t[i] if ready_t[i] > fe else fe
                    if e == "act" and ops[i]["tset"] is not None and cur_set[0] is not None and ops[i]["tset"] != cur_set[0]:
                        stt = stt + TSW
                    stts[i] = stt
                    if mn is None or stt < mn:
                        mn = stt
                for i in ready[e]:
                    stt = stts[i]
                    if stt > mn + DELTA:
                        continue
                    key = (-prio[i], stt, i)
                    if cand is None or key < cand[0]:
                        cand = (key, i, stt)
                if best is None or cand[2] < best[2] - 1e-9 or (abs(cand[2] - best[2]) <= 1e-9 and cand[0] < best[0]):
                    best = cand
            _, i, stt = best
            op = ops[i]
            e = op["eng"]
            ready[e].remove(i)
            if e == "act" and op["tset"] is not None:
                cur_set[0] = op["tset"]
            free[e] = stt + dur(op)
            finish[i] = stt + done_lat(op)
            order[e].append(i)
            left -= 1
            for sidx in succs[i]:
                rt = finish[i] + (LATV if ops[sidx]["eng"] != e or op["dma"] else 0.0)
                if rt > ready_t[sidx]:
                    ready_t[sidx] = rt
                npred[sidx] -= 1
                if npred[sidx] == 0:
                    ready[ops[sidx]["eng"]].append(sidx)
        return order, preds, raw

    def emit(self, final_eng="sp"):
        nc = self.nc
        nseg = self.seg + 1
        segs = [[] for _ in range(nseg)]
        for op in self.ops:
            segs[op["seg"]].append(op)
        gorder = {e: [] for e in ENGS}
        dma_val = {}
        dma_rr = {e: 0 for e in ENGS}
        known_pos = {e: {e2: -1 for e2 in ENGS} for e in ENGS}
        known_dma = {e: {} for e in ENGS}
        pending_pos = {e: {} for e in ENGS}
        pending_dma = {e: {} for e in ENGS}
        for sops in segs:
            if not sops:
                continue
            order, preds, raw = self._schedule_segment(sops)
            rec = {}
            for e in ENGS:
                for i in order[e]:
                    op = sops[i]
                    r = dict(op=op, eng=e, pos=None, signal=False, waits_pos={}, waits_dma={}, dma_tok=None)
                    if op["dma"]:
                        j = dma_rr[e]
                        dma_rr[e] = (j + 1) % (self.n_dma if e == "sp" else 3)
                        key = ("dma", e, j)
                        prev = dma_val.get(key, 0)
                        val = prev + 16 * op["dma"]
                        dma_val[key] = val
                        r["dma_tok"] = (key, val)
                        r["dma_prev"] = (key, prev) if prev > 0 else None
                    else:
                        r["pos"] = len(gorder[e])
                        gorder[e].append(r)
                    rec[i] = r
            for e in ENGS:
                for i in order[e]:
                    r = rec[i]
                    op = sops[i]
                    wp, wd = {}, {}
                    for e2, p in pending_pos[e].items():
                        if known_pos[e][e2] < p:
                            wp[e2] = p
                    for kk, vv in pending_dma[e].items():
                        if known_dma[e].get(kk, 0) < vv:
                            wd[kk] = vv
                    pending_pos[e] = {}
                    pending_dma[e] = {}
                    for p in preds[i]:
                        pr = rec[p]
                        if pr["dma_tok"] is not None:
                            kk, vv = pr["dma_tok"]
                            if known_dma[e].get(kk, 0) < vv and wd.get(kk, 0) < vv:
                                wd[kk] = vv
                        elif pr["eng"] != e or e != "pe":
                            e2 = pr["eng"]
                            if known_pos[e][e2] < pr["pos"] and wp.get(e2, -1) < pr["pos"]:
                                wp[e2] = pr["pos"]
                    if op["dma"] and r["dma_prev"] is not None:
                        kk, vv = r["dma_prev"]
                        if known_dma[e].get(kk, 0) < vv and wd.get(kk, 0) < vv:
                            wd[kk] = vv
                    for e2, p in wp.items():
                        known_pos[e][e2] = p
                        gorder[e2][p]["signal"] = True
                    for kk, vv in wd.items():
                        known_dma[e][kk] = vv
                    r["waits_pos"] = wp
                    r["waits_dma"] = wd
                    r["stream_eng"] = e
            for e in ENGS:
                for e2 in ENGS:
                    if e2 != e and gorder[e2]:
                        pending_pos[e][e2] = len(gorder[e2]) - 1
                        gorder[e2][-1]["signal"] = True
                for kk, vv in dma_val.items():
                    pending_dma[e][kk] = vv
            self._last_rec = rec
            for e in ENGS:
                for i in order[e]:
                    self._stream.setdefault(e, []).append(rec[i])
        for e in ENGS:
            if gorder[e]:
                gorder[e][-1]["signal"] = True
        count = {e: 0 for e in ENGS}
        for e in ENGS:
            for r in gorder[e]:
                if r["signal"]:
                    count[e] += 1
                    r["val"] = count[e]
        keys = [("eng", e) for e in ENGS] + sorted(dma_val.keys())
        with contextlib.ExitStack() as st:
            sems = {}
            for kk in keys:
                sems[kk] = st.enter_context(nc.semaphore("s_" + "_".join(str(x) for x in kk)))
            fin_waits = []
            for e in ENGS:
                if gorder[e] and e != final_eng:
                    fin_waits.append((("eng", e), count[e]))
            for kk, vv in dma_val.items():
                fin_waits.append((kk, vv))
            block = st.enter_context(nc.Block())
            streams = self._stream

            def run(ename):
                def body(eng):
                    for r in streams.get(ename, []):
                        for e2, p in r["waits_pos"].items():
                            eng.wait_ge(sems[("eng", e2)], gorder[e2][p]["val"])
                        for kk, vv in r["waits_dma"].items():
                            eng.wait_ge(sems[kk], vv)
                        res = r["op"]["fn"](eng)
                        if r["op"]["dma"]:
                            if not isinstance(res, (list, tuple)):
                                res = [res]
                            assert len(res) == int(r["op"]["dma"])
                            for ins in res:
                                ins.then_inc(sems[r["dma_tok"][0]], 16)
                        elif r["signal"]:
                            if isinstance(res, (list, tuple)):
                                res = res[-1]
                            res.then_inc(sems[("eng", ename)], 1)
                    if ename == final_eng:
                        for kk, vv in fin_waits:
                            eng.wait_ge(sems[kk], vv)
                return body

            block.tensor(run("pe"))
            block.vector(run("dve"))
            block.scalar(run("act"))
            block.gpsimd(run("pool"))
            block.sync(run("sp"))


class Rot:
    def __init__(self, K, name, n, shape, dt):
        self.bufs = [K.sb(f"{name}{i}", shape, dt) for i in range(n)]
        self.keys = [f"{name}{i}" for i in range(n)]
        self.i = 0

    def next(self):
        j = self.i
        self.i = (j + 1) % len(self.bufs)
        return self.bufs[j], self.keys[j]


class K:
    def __init__(self, dbg=None):
        self.dbg = dbg
        self.nc = bass.Bass("TRN2", target_bir_lowering=False)
        self.st = contextlib.ExitStack()
        self.stacks = [self.st]
        self.S = Sched(self.nc)
        self.dram = {}

    def din(self, name, shape, dt=F32):
        ap = self.nc.dram_tensor(name, list(shape), dt, kind="ExternalInput").ap()
        self.dram[name] = ap
        return ap

    def sb(self, name, shape, dt):
        return self.stacks[-1].enter_context(self.nc.sbuf_tensor("sb_" + name, list(shape), dt))

    def push(self):
        self.stacks.append(contextlib.ExitStack())

    def pop(self):
        self.S.barrier()
        self.stacks.pop().close()

    @staticmethod
    def _fs(ap):
        try:
            return float(ap.free_size())
        except Exception:
            return 512.0

    TSETS = {AF.Silu: "silu", AF.Exp: "lnexp", AF.Ln: "lnexp", AF.Sigmoid: "sigm"}

    def act(self, r, w, **kw):
        c = 0.25 + self._fs(kw["out"]) / 1200.0
        self.S.add("act", lambda e: e.activation(**kw), r, w, cost=c, tset=self.TSETS.get(kw["func"]))

    def dve(self, opname, r, w, **kw):
        o = kw.get("out", kw.get("ap"))
        c = 0.08 + self._fs(o) / 960.0
        self.S.add("dve", lambda e: getattr(e, opname)(**kw), r, w, cost=c)

    def pool(self, opname, r, w, **kw):
        o = kw.get("out", kw.get("ap"))
        c = 0.15 + self._fs(o) / 500.0
        self.S.add("pool", lambda e: getattr(e, opname)(**kw), r, w, cost=c)

    def mm(self, lst, r, w):
        def f(e):
            ins = None
            for (o, l, rh, st, sp) in lst:
                ins = e.matmul(out=o, lhsT=l, rhs=rh, start=st, stop=sp)
            return ins
        c = 0.1
        for (o, l, rh, st, sp) in lst:
            n = self._fs(o)
            c += max(0.065, n / 2400.0) * (4.0 if l.dtype == F32 else 1.0)
        self.S.add("pe", f, r, w, cost=c)

    def tr(self, lst, ident, r, w):
        def f(e):
            ins = None
            for (o, i) in lst:
                ins = e.transpose(out=o, in_=i, identity=ident)
            return ins
        self.S.add("pe", f, r, w, cost=0.1 + 0.11 * len(lst))

    def dma(self, eng, out, in_, r, w):
        try:
            nbytes = float(out.nbytes())
        except Exception:
            nbytes = 5e5
        self.S.add(eng, lambda e: e.dma_start(out=out, in_=in_), r, w, dma=1, cost=nbytes / 1.5e5)


def bc(ap, shape):
    return ap.to_broadcast(list(shape))


import os
LVL = int(os.environ.get("SSD_LVL", "9"))


def build(dbg=None):
    k = K(dbg)
    nc = k.nc
    S = k.S
    x_own = k.din("x_own", [HALF, D])
    x_prev = k.din("x_prev", [HALF, D])
    flag_d = k.din("flag", [128, 1])
    mem_d = k.din("mem", [MEM, D])
    w_in = k.din("w_in", [D, INW])
    w_out = k.din("w_out", [2 * D, D])
    w_q = k.din("w_q", [D, D])
    w_kv = k.din("w_kv", [D, 2 * D])
    w_o = k.din("w_o", [D, D])
    w_gate = k.din("w_gate", [D, DFF])
    w_up = k.din("w_up", [D, DFF])
    w_down = k.din("w_down", [DFF, D])
    gbc_d = k.din("gbc", [128, 6, D])
    dbc_d = k.din("dbc", [128, D])
    sm_d = k.din("small", [128, 64])
    cw_d = k.din("cw", [128, 12, 4])
    cb_d = k.din("cb", [128, 12])
    fw_d = k.din("fw", [128, 8, 31])
    fv_d = k.din("fv", [128, 8, 3])
    ident_d = k.din("ident", [128, 128], BF16)
    cst_d = k.din("cst", [128, 4, 128])
    out_d = nc.dram_tensor("out", [HALF, D], F32, kind="ExternalOutput").ap()
    xr_d = nc.dram_tensor("xr", [HALF, D], F32, kind="Internal").ap()

    xo_t = x_own.rearrange("(t p) d -> t p d", p=128)
    xp_t = x_prev.rearrange("(t p) d -> t p d", p=128)
    xr_t = xr_d.rearrange("(t p) d -> t p d", p=128)
    out_t = out_d.rearrange("(t p) d -> t p d", p=128)

    gains = {}

    def load_gains(idxs):
        k.gcount = getattr(k, "gcount", 0) + 1
        gb = k.sb(f"gains{k.gcount}", [128, len(idxs), D], F32)
        for i, gi in enumerate(idxs):
            key = f"gain{gi}_{k.gcount}"
            k.dma("sp", gb[:, i, :], gbc_d[:, gi, :], [], [key])
            gains[gi] = (gb[:, i, :], key)
    sm = k.sb("sm", [128, 64], F32)
    cw = k.sb("cw", [128, 12, 4], F32)
    cb = k.sb("cb", [128, 12], F32)
    fw = k.sb("fw", [128, 8, 31], F32)
    fv = k.sb("fv", [128, 8, 3], F32)
    ident = k.sb("ident", [128, 128], BF16)
    cst = k.sb("cst", [128, 4, 128], F32)
    flag = k.sb("flag", [128, 1], F32)
    epsT = k.sb("epsT", [128, 1], F32)
    ones_bf = k.sb("ones_bf", [128, 128], BF16)
    for (dst, src, nm) in [ (sm, sm_d, "sm"), (cw, cw_d, "cw"),
                           (cb, cb_d, "cb"), (fw, fw_d, "fw"), (fv, fv_d, "fv"), (ident, ident_d, "ident"),
                           (cst, cst_d, "cst"), (flag, flag_d, "flag")]:
        k.dma("sp", dst[:], src, [], [nm])
    k.dve("memset", [], ["epsT"], ap=epsT[:], constant=EPS)
    k.dve("memset", [], ["ones_bf"], ap=ones_bf[:], constant=1.0)

    PS = k.st.enter_context(nc.psum_tensor("PS", [128, 8, 512], F32))

    def pb(i):
        return PS[:, i, :]

    def pbk(i):
        return f"ps{i}"

    ss_rot = Rot(k, "ss", int(os.environ.get("SSROT", "4")), [128, 4], F32)
    sq_rot = Rot(k, "sq", 1, [128, D], BF16)
    h_rot = Rot(k, "hb", int(os.environ.get("HROT", "2")), [128, D], BF16)

    def rstd_from_ss(ss, sskey, n, inv_n):
        k.act([sskey, "epsT"], [sskey], out=ss[:, 0:n], in_=ss[:, 0:n], func=AF.Ln, scale=inv_n, bias=epsT[:])
        k.act([sskey], [sskey], out=ss[:, 0:n], in_=ss[:, 0:n], func=AF.Exp, scale=-0.5)

    def norm_tile(x_ap, xkey, gidx, h_out, hkey):
        ss, sskey = ss_rot.next()
        sq, sqkey = sq_rot.next()
        k.act([xkey], [sqkey, sskey], out=sq[:], in_=x_ap, func=AF.Square, accum_out=ss[:, 0:1])
        rstd_from_ss(ss, sskey, 1, 1.0 / D)
        k.dve("scalar_tensor_tensor", [xkey, sskey, gains[gidx][1]], [hkey], out=h_out, in0=x_ap, scalar=ss[:, 0:1],
              in1=gains[gidx][0], op0=ALU.mult, op1=ALU.mult)

    def transpose_to(h_ap, hkey, dstT, dkey, col0, bank):
        pT = pb(bank).bitcast(BF16)
        k.tr([(pT[:, kc * 128:(kc + 1) * 128], h_ap[:, kc * 128:(kc + 1) * 128]) for kc in range(8)], ident[:],
             [hkey, "ident"], [pbk(bank)])
        k.act([pbk(bank)], [dkey], out=dstT[:, :, col0:col0 + 128], in_=pT.rearrange("p (a b) -> p a b", a=8),
              func=AF.Copy)

    x_res = None

    src_t = xo_t if dbg == "nomixer" else xr_t
    tri = cst[:, 0, :]
    su = cst[:, 1, :]
    onesf = cst[:, 2, :]
    mask01 = cst[:, 3, :]
    do_ssd = dbg in (None, "ssd")
    do_conf = dbg in (None, "conf")

    if do_ssd:
        k.push()
        load_gains([0, 5])
        dbc = k.sb("dbc", [128, D], F32)
        k.dma("sp", dbc[:], dbc_d, [], ["dbc"])
        xt_rot = Rot(k, "xta", int(os.environ.get("XTA", "3")), [128, D], F32)
        xb3_rot = Rot(k, "xtc", int(os.environ.get("XTC", "1")), [128, D], F32)
        Wa = k.sb("Wa", [128, 8, 2576], BF16)
        Wot = k.sb("Wot", [128, 8, D], BF16)
        w_in_v = w_in.rearrange("(kc p) n -> p kc n", p=128)
        for q in range(4):
            k.dma("pool", Wa[:, :, 1024 + 384 * q:1024 + 384 * (q + 1)], w_in_v[:, :, 1024 + 384 * q:1024 + 384 * (q + 1)], [], [f"Wa_x{q}"])
        k.dma("pool", Wa[:, :, 2560:2576], w_in_v[:, :, 2560:2576], [], ["Wa_dt"])
        k.dma("pool", Wa[:, :, 0:1024], w_in_v[:, :, 0:1024], [], ["Wa_z"])
        k.dma("pool", Wot[:], w_out[0:D, :].rearrange("(kc p) n -> p kc n", p=128), [], ["Wot"])
        diag4 = k.sb("diag4", [128, 12, 4, 128], BF16)
        for c in range(12):
            k.dve("tensor_tensor", ["ident", "cw"], [f"diag4_{c}"], out=diag4[:, c, :, :], in0=bc(ident[:].unsqueeze(1), [128, 4, 128]),
                  in1=bc(cw[:, c, :].unsqueeze(2), [128, 4, 128]), op=ALU.mult)
        hT_r = Rot(k, "hT1_", 2, [128, 8, MT], BF16)
        xbcT_r = Rot(k, "xbcT_", 2, [128, 12, MT], BF16)
        dt4_r = Rot(k, "dt4_", 2, [128, 4, 16], F32)
        dA4_r = Rot(k, "dA4_", 2, [128, 4, 16], F32)
        sz_r = Rot(k, "sz_", 2, [128, 4, D], BF16)
        pre_rot = Rot(k, "pre", int(os.environ.get("PRE", "3")), [128, 515], BF16)
        hist = k.sb("hist", [128, 12, 3], BF16)
        dtp = k.sb("dtp", [128, 4, 16], F32)
        nA = k.sb("nA", [128, 16], F32)
        ex3_r = Rot(k, "ex3_", 2, [128, 48], F32)
        xdt_r = Rot(k, "xdt_", 2, [128, D], BF16)
        xD_r = Rot(k, "xD_", 1, [128, D], BF16)
        Btok_r = Rot(k, "Btok_", 2, [128, 256], BF16)
        E_r = Rot(k, "E_", 2, [128, 16, 128], BF16)
        CBm_r = Rot(k, "CBm_", 2, [128, 2, 128], BF16)
        SUhi = k.sb("SUhi", [128, 8, 128], BF16)
        SUlo = k.sb("SUlo", [128, 8, 128], BF16)
        su_bf = k.sb("su_bf", [128, 128], BF16)
        tri_bf = k.sb("tri_bf", [128, 128], BF16)
        dAh = k.sb("dAh", [128, 16], BF16)
        dAh32 = k.sb("dAh32", [128, 16], F32)
        dAl = k.sb("dAl", [128, 16], F32)
        k.dve("tensor_copy", ["cst"], ["su_bf"], out=su_bf[:], in_=cst[:, 1, :])
        k.dve("tensor_copy", ["cst"], ["tri_bf"], out=tri_bf[:], in_=cst[:, 0, :])
        xw_r = Rot(k, "xw_", 2, [128, D], BF16)
        y1 = k.sb("y1", [128, D], F32)
        yn = k.sb("yn", [128, D], BF16)
        ynT = k.sb("ynT", [128, 8, 128], BF16)
        Sst = k.sb("Sst", [128, D], F32)
        Sbf = k.sb("Sbf", [128, D], BF16)
        k.dve("memset", [], ["hist"], ap=hist[:], constant=0.0)
        k.dve("memset", [], ["Sst"], ap=Sst[:], constant=0.0)
        k.dve("memset", [], ["Sbf"], ap=Sbf[:], constant=0.0)
        k.act(["sm"], ["nA"], out=nA[:], in_=sm[:, 16:32], func=AF.Exp)
        k.dve("tensor_scalar", ["nA"], ["nA"], out=nA[:], in0=nA[:], scalar1=-1.0, scalar2=None, op0=ALU.mult)
        ATB = int(os.environ.get("ATB", "2"))
        YTB = int(os.environ.get("YTB", "0"))
        pT0 = pb(ATB).bitcast(BF16)
        pTy = pb(YTB).bitcast(BF16)
        G1 = gains[5]

        mt_state = {}

        def src_tile(g_mt, st):
            return (xp_t if g_mt < 4 else xo_t)[(g_mt % 4) * 4 + st]

        mt_h = {}

        def F1(mt, pair):
            if pair == 0:
                mt_h[mt] = hT_r.next()
            hT, hTk = mt_h[mt]
            items = []
            for st in (2 * pair, 2 * pair + 1):
                xt, xtk = xt_rot.next()
                k.dma("sp", xt[:], src_tile(mt, st), [], [xtk])
                ss, sskey = ss_rot.next()
                sq, sqkey = sq_rot.next()
                k.act([xtk], [sqkey, sskey], out=sq[:], in_=xt[:], func=AF.Square, accum_out=ss[:, 0:1])
                rstd_from_ss(ss, sskey, 1, 1.0 / D)
                items.append((st, xt, xtk, ss, sskey))
            hbs = []
            for (st, xt, xtk, ss, sskey) in items:
                hb, hk = h_rot.next()
                k.dve("scalar_tensor_tensor", [xtk, sskey, gains[0][1]], [hk], out=hb[:], in0=xt[:], scalar=ss[:, 0:1],
                      in1=gains[0][0], op0=ALU.mult, op1=ALU.mult)
                hbs.append((st, hb, hk))
            F1B = int(os.environ.get("F1B", "0"))
            for i, (st, hb, hk) in enumerate(hbs):
                pT = pb(F1B + i).bitcast(BF16)
                k.tr([(pT[:, kc * 128:(kc + 1) * 128], hb[:, kc * 128:(kc + 1) * 128]) for kc in range(8)], ident[:],
                     [hk, "ident"], [pbk(F1B + i)])
            for i, (st, hb, hk) in enumerate(hbs):
                pT = pb(F1B + i).bitcast(BF16)
                k.act([pbk(F1B + i)], [hTk], out=hT[:, :, st * 128:(st + 1) * 128], in_=pT.rearrange("p (a b) -> p a b", a=8), func=AF.Copy)

        def F2(mt, q):
            hT, hTk = mt_h[mt]
            if q == 0:
                xbcT, xbk = xbcT_r.next()
                dt4, dt4k = dt4_r.next()
                dA4, dA4k = dA4_r.next()
                szm, szk = sz_r.next()
                mt_state[mt] = (hT, hTk, xbcT, xbk, dt4, dt4k, dA4, dA4k, szm, szk)
            hT, hTk, xbcT, xbk, dt4, dt4k, dA4, dA4k, szm, szk = mt_state[mt]
            pres = {}

            def inproj(c):
                b1 = c % 2
                k.mm([(pb(b1), Wa[:, kc, 1024 + c * 128:1024 + (c + 1) * 128], hT[:, kc, :], kc == 0, kc == 7)
                      for kc in range(8)], [f"Wa_x{c // 3}", hTk], [pbk(b1)])
                pre, prek = pre_rot.next()
                pres[c] = (pre, prek)
                k.act([pbk(b1)], [prek], out=pre[:, 3:515], in_=pb(b1), func=AF.Copy)
                k.dve("tensor_copy", ["hist", prek], [prek], out=pre[:, 0:3], in_=hist[:, c, :])
                k.dve("tensor_copy", [prek], ["hist"], out=hist[:, c, :], in_=pre[:, 512:515])

            def conv(c):
                pre, prek = pres[c]
                b2 = 2 + c % 2
                k.mm([(pb(b2), diag4[:, c, kk, :], pre[:, kk:kk + 512], kk == 0, kk == 3) for kk in range(4)],
                     [f"diag4_{c}", prek], [pbk(b2)])
                k.act([pbk(b2), "cb"], [f"{xbk}_{c}"], out=xbcT[:, c, :], in_=pb(b2), func=AF.Silu, bias=cb[:, c:c + 1], scale=1.0)

            c0 = 3 * q
            skipC = (mt < 3 and q == 3)
            inproj(c0)
            if not skipC:
                inproj(c0 + 1)
            conv(c0)
            if not skipC:
                inproj(c0 + 2)
                conv(c0 + 1)
            if mt >= 4:
                st = q
                k.mm([(pb(hf2), hT[:, kc, st * 128:(st + 1) * 128], Wa[:, kc, hf2 * 512:(hf2 + 1) * 512], kc == 0, kc == 7)
                      for hf2 in range(2) for kc in range(8)], ["Wa_z", hTk], [pbk(0), pbk(1)])
            if not skipC:
                conv(c0 + 2)
            if mt >= 4:
                k.act([pbk(0), pbk(1)], [szk], out=szm[:, q, :].rearrange("p (a b) -> p a b", a=2), in_=PS[:, 0:2, :], func=AF.Silu)
            if q == 3:
                for st in range(4):
                    k.mm([(pb(1)[:, st * 16:(st + 1) * 16], hT[:, kc, st * 128:(st + 1) * 128], Wa[:, kc, 2560:2576], kc == 0, kc == 7)
                          for kc in range(8)], ["Wa_dt", hTk], [pbk(1)])
                k.dve("tensor_tensor", [pbk(1), "sm"], ["dtp"], out=dtp[:], in0=pb(1)[:, 0:64].rearrange("p (a b) -> p a b", a=4),
                      in1=bc(sm[:, 0:16].unsqueeze(1), [128, 4, 16]), op=ALU.add)
                k.act(["dtp"], ["dtp"], out=dtp[:], in_=dtp[:], func=AF.Exp)
                k.act(["dtp", "cst"], [dt4k], out=dt4[:], in_=dtp[:], func=AF.Ln, bias=cst[:, 2, 0:1], scale=1.0)
                k.dve("tensor_tensor", [dt4k, "nA"], [dA4k], out=dA4[:], in0=dt4[:], in1=bc(nA[:].unsqueeze(1), [128, 4, 16]), op=ALU.mult)

        ch_state = {}

        def A(g):
            mt, st = g // 4, g % 4
            full = mt >= 4
            hT, hTk, xbcT, xbk, dt4, dt4k, dA4, dA4k, szm, szk = mt_state[mt]
            cs = slice(st * 128, (st + 1) * 128)
            xdt, xdtk = xdt_r.next()
            xD, xDk = xD_r.next()
            Btok, Btk = Btok_r.next()
            ex3, ex3k = ex3_r.next()
            E, Ek = E_r.next()
            CBm, CBk = CBm_r.next()
            xw, xwk = xw_r.next()
            ch_state[g] = (xdt, xdtk, xD, xDk, Btok, Btk, ex3, ex3k, E, Ek, xw, xwk)
            k.tr([(pT0[:, c * 128:(c + 1) * 128], xbcT[:, c, cs]) for c in range(8)], ident[:],
                 [f"{xbk}_{c}" for c in range(8)] + ["ident"], [pbk(ATB)])
            k.dve("tensor_tensor", [pbk(ATB), dt4k], [xdtk], out=xdt[:].rearrange("p (h q) -> p h q", h=16),
                  in0=pT0.rearrange("p (h q) -> p h q", h=16), in1=bc(dt4[:, st, :].unsqueeze(2), [128, 16, 64]), op=ALU.mult)
            if full:
                k.dve("tensor_tensor", [pbk(ATB), "dbc"], [xDk], out=xD[:], in0=pT0, in1=dbc[:], op=ALU.mult)
            k.tr([(pT0[:, gg * 128:(gg + 1) * 128], xbcT[:, 8 + gg, cs]) for gg in range(2)], ident[:],
                 [f"{xbk}_8", f"{xbk}_9", "ident"], [pbk(ATB)])
            k.act([pbk(ATB)], [Btk], out=Btok[:], in_=pT0[:, 0:256], func=AF.Copy)
            dA = dA4[:, st, :]
            k.mm([(pb(1)[:, 0:16], tri, dA, True, True), (pb(1)[:, 16:32], su, dA, True, True),
                  (pb(1)[:, 32:48], onesf, dA, True, True)], ["cst", dA4k], [pbk(1)])
            k.act([pbk(1)], [ex3k], out=ex3[:], in_=pb(1)[:, 0:48], func=AF.Exp)
            k.pool("tensor_tensor", [xdtk, ex3k], [xwk], out=xw[:].rearrange("p (h q) -> p h q", h=16),
                   in0=xdt[:].rearrange("p (h q) -> p h q", h=16), in1=bc(ex3[:, 16:32].unsqueeze(2), [128, 16, 64]), op=ALU.mult)
            if full:
                k.dve("tensor_copy", [dA4k], ["dAh"], out=dAh[:], in_=dA)
                k.dve("tensor_copy", ["dAh"], ["dAh32"], out=dAh32[:], in_=dAh[:])
                k.dve("tensor_tensor", [dA4k, "dAh32"], ["dAl"], out=dAl[:], in0=dA, in1=dAh32[:], op=ALU.subtract)
                for r in range(2):
                    k.dve("tensor_tensor", ["su_bf", "dAh"], ["SUhi"], out=SUhi[:], in0=bc(su_bf[:].unsqueeze(1), [128, 8, 128]),
                          in1=bc(dAh[:, r * 8:(r + 1) * 8].unsqueeze(2), [128, 8, 128]), op=ALU.mult)
                    k.dve("tensor_tensor", ["su_bf", "dAl"], ["SUlo"], out=SUlo[:], in0=bc(su_bf[:].unsqueeze(1), [128, 8, 128]),
                          in1=bc(dAl[:, r * 8:(r + 1) * 8].unsqueeze(2), [128, 8, 128]), op=ALU.mult)
                    for hq in range(2):
                        bank = 2 + hq
                        lst = []
                        for hh in range(4):
                            h = hq * 4 + hh
                            lst.append((pb(bank)[:, hh * 128:(hh + 1) * 128], SUhi[:, h, :], tri_bf[:], True, False))
                            lst.append((pb(bank)[:, hh * 128:(hh + 1) * 128], SUlo[:, h, :], tri_bf[:], False, True))
                        k.mm(lst, ["SUhi", "SUlo", "tri_bf"], [pbk(bank)])
                        k.act([pbk(bank)], [Ek], out=E[:, r * 8 + hq * 4:r * 8 + hq * 4 + 4, :],
                              in_=pb(bank).rearrange("p (a b) -> p a b", a=4), func=AF.Exp)
                k.mm([(pb(1)[:, 64 + gg * 128:64 + (gg + 1) * 128], xbcT[:, 8 + gg, cs], xbcT[:, 10 + gg, cs], True, True) for gg in range(2)],
                     [f"{xbk}_8", f"{xbk}_9", f"{xbk}_10", f"{xbk}_11"], [pbk(1)])
                k.dve("tensor_tensor", [pbk(1), "cst"], [CBk], out=CBm[:], in0=pb(1)[:, 64:320].rearrange("p (a b) -> p a b", a=2),
                      in1=bc(mask01.unsqueeze(1), [128, 2, 128]), op=ALU.mult)
                k.dve("tensor_tensor", [Ek, CBk], [Ek], out=E[:].rearrange("p (g r) l -> p g r l", g=2),
                      in0=E[:].rearrange("p (g r) l -> p g r l", g=2), in1=bc(CBm[:].unsqueeze(2), [128, 2, 8, 128]), op=ALU.mult)

        bst = {}

        def B1(g):
            mt, st = g // 4, g % 4
            hT, hTk, xbcT, xbk, dt4, dt4k, dA4, dA4k, szm, szk = mt_state[mt]
            xdt, xdtk, xD, xDk, Btok, Btk, ex3, ex3k, E, Ek, xw, xwk = ch_state[g]
            cs = slice(st * 128, (st + 1) * 128)
            t = (mt % 4) * 4 + st
            lst = []
            for hf2 in range(2):
                lst.append((pb(4 + hf2), ident[:], xD[:, hf2 * 512:(hf2 + 1) * 512], True, False))
            for h in range(16):
                lst.append((pb(4 + h // 8)[:, (h % 8) * 64:(h % 8 + 1) * 64], E[:, h, :], xdt[:, h * 64:(h + 1) * 64], False, h % 8 == 7))
            k.mm(lst, ["ident", xDk, Ek, xdtk], [pbk(4), pbk(5)])
            k.mm([(pb(6 + gg), xbcT[:, 10 + gg, cs], Sbf[:, gg * 512:(gg + 1) * 512], True, True) for gg in range(2)],
                 [f"{xbk}_10", f"{xbk}_11", "Sbf"], [pbk(6), pbk(7)])
            k.dve("tensor_tensor", [pbk(6), pbk(7), ex3k], ["y1"], out=y1[:].rearrange("p (h q) -> p h q", h=16),
                  in0=PS[:, 6:8, :].rearrange("p a (h q) -> p (a h) q", q=64), in1=bc(ex3[:, 0:16].unsqueeze(2), [128, 16, 64]), op=ALU.mult)
            k.dve("tensor_tensor", [pbk(4), pbk(5), "y1"], ["y1"], out=y1[:].rearrange("p (a b) -> p a b", a=2),
                  in0=PS[:, 4:6, :], in1=y1[:].rearrange("p (a b) -> p a b", a=2), op=ALU.add)
            k.dve("tensor_tensor", ["y1", szk], ["y1"], out=y1[:], in0=y1[:], in1=szm[:, st, :], op=ALU.mult)
            k.dve("tensor_tensor", ["y1", G1[1]], ["yn"], out=yn[:], in0=y1[:], in1=G1[0], op=ALU.mult)
            ss, sskey = ss_rot.next()
            sq, sqkey = sq_rot.next()
            for gg in range(2):
                k.act(["y1"], [sqkey, sskey], out=sq[:, 0:512], in_=y1[:, gg * 512:(gg + 1) * 512], func=AF.Square, accum_out=ss[:, gg:gg + 1])
            rstd_from_ss(ss, sskey, 2, 1.0 / 512)
            bst[g] = (ss, sskey, t)

        def B2(g):
            k.tr([(pTy[:, kc * 128:(kc + 1) * 128], yn[:, kc * 128:(kc + 1) * 128]) for kc in range(8)], ident[:], ["yn", "ident"], [pbk(YTB)])
            k.act([pbk(YTB)], ["ynT"], out=ynT[:], in_=pTy.rearrange("p (a b) -> p a b", a=8), func=AF.Copy)

        def B3(g):
            ss, sskey, t = bst[g]
            k.mm([(pb(4 + 2 * gg + hf2), ynT[:, 4 * gg + kc, :], Wot[:, 4 * gg + kc, hf2 * 512:(hf2 + 1) * 512], kc == 0, kc == 3)
                  for gg in range(2) for hf2 in range(2) for kc in range(4)], ["Wot", "ynT"], [pbk(4), pbk(5), pbk(6), pbk(7)])
            xt, xtk = xb3_rot.next()
            k.dma("sp", xt[:], xo_t[t], [], [xtk])
            for gg in range(2):
                k.dve("scalar_tensor_tensor", [pbk(4 + 2 * gg), pbk(5 + 2 * gg), sskey, xtk], [xtk], out=xt[:].rearrange("p (a b) -> p a b", a=2),
                      in0=PS[:, 4 + 2 * gg:6 + 2 * gg, :], scalar=ss[:, gg:gg + 1], in1=xt[:].rearrange("p (a b) -> p a b", a=2),
                      op0=ALU.mult, op1=ALU.add)
            k.dma("sp", xr_t[t], xt[:], [xtk], [f"xr{t}"])

        def C(g):
            xdt, xdtk, xD, xDk, Btok, Btk, ex3, ex3k, E, Ek, xw, xwk = ch_state[g]
            k.mm([(pb(2 + gg), Btok[:, gg * 128:(gg + 1) * 128], xw[:, gg * 512:(gg + 1) * 512], True, True) for gg in range(2)],
                 [Btk, xwk], [pbk(2), pbk(3)])
            k.dve("tensor_tensor", ["Sst", ex3k], ["Sst"], out=Sst[:].rearrange("p (h q) -> p h q", h=16),
                  in0=Sst[:].rearrange("p (h q) -> p h q", h=16), in1=bc(ex3[:, 32:48].unsqueeze(2), [128, 16, 64]), op=ALU.mult)
            k.dve("tensor_tensor", ["Sst", pbk(2), pbk(3)], ["Sst"], out=Sst[:].rearrange("p (a b) -> p a b", a=2),
                  in0=Sst[:].rearrange("p (a b) -> p a b", a=2), in1=PS[:, 2:4, :], op=ALU.add)
            k.act(["Sst"], ["Sbf"], out=Sbf[:], in_=Sst[:], func=AF.Copy)

        NG = 32
        F1(0, 0)
        F1(0, 1)
        for q in range(4):
            F2(0, q)
        F1(1, 0)
        F1(1, 1)
        A(0)
        for g in range(NG):
            mt, st = g // 4, g % 4
            own = mt >= 4
            if own:
                B1(g)
            if g < NG - 1:
                C(g)
            if g == 15:
                k.dve("tensor_scalar", ["Sst", "flag"], ["Sst"], out=Sst[:], in0=Sst[:], scalar1=flag[:, 0:1], scalar2=None, op0=ALU.mult)
                k.act(["Sst"], ["Sbf"], out=Sbf[:], in_=Sst[:], func=AF.Copy)
            if mt + 1 < 8:
                F2(mt + 1, st)
            if own:
                B2(g)
            if g + 1 < NG:
                A(g + 1)
            if own:
                B3(g)
            if st in (0, 2) and mt + 2 < 8:
                F1(mt + 2, st // 2)
        k.pop()

    if do_conf:
        k.push()
        base_t = xr_t if do_ssd else xo_t
        load_gains([0])
        xt_rot = Rot(k, "xtb", int(os.environ.get("XTB", "3")), [128, D], F32)
        xb_rot = Rot(k, "xbb", int(os.environ.get("XBB", "3")), [128, D], F32)
        Wag = k.sb("Wag", [128, 8, 2048], BF16)
        Wob = k.sb("Wob", [128, 8, D], BF16)
        w_in_v2 = w_in.rearrange("(kc p) n -> p kc n", p=128)
        for cp in range(4):
            k.dma("pool", Wag[:, :, cp * 256:(cp + 1) * 256], w_in_v2[:, :, A0 + cp * 256:A0 + (cp + 1) * 256], [], [f"Wag_a{cp}"])
            k.dma("pool", Wag[:, :, D + cp * 256:D + (cp + 1) * 256], w_in_v2[:, :, G0 + cp * 256:G0 + (cp + 1) * 256], [], [f"Wag_g{cp}"])
        k.dma("pool", Wob[:], w_out[D:2 * D, :].rearrange("(kc p) n -> p kc n", p=128), [], ["Wob"])
        NDV = int(os.environ.get("NDV", "10"))
        NPE = 31 - NDV
        diag = k.sb("diag", [128, 8, NPE, 128], BF16)
        DGP = os.environ.get("DGP", "dve")
        for c in range(8):
            on_pool = (DGP == "pool") or (DGP == "alt" and c % 2 == 1)
            (k.pool if on_pool else k.dve)("tensor_tensor", ["ident", "fw"], [f"diag{c}"], out=diag[:, c, :, :], in0=bc(ident[:].unsqueeze(1), [128, NPE, 128]),
                  in1=bc(fw[:, c, NDV:31].unsqueeze(2), [128, NPE, 128]), op=ALU.mult)
        hT2_r = Rot(k, "hT2_", 2, [128, 8, MT], BF16)
        unT = k.sb("unT", [128, 8, MT], BF16)
        uT_r = Rot(k, "uT_", 2, [128, 8, 30 + MT], BF16)
        sgm_rot = Rot(k, "sgm", 2, [128, MT], F32)
        cv_r = Rot(k, "cv_", 2, [128, 8, MT], BF16)
        cvsq_rot = Rot(k, "cvsq", 2, [128, MT], BF16)
        mean_r = Rot(k, "mean_", 1, [128, MT], F32)
        rstdv_r = Rot(k, "rstdv_", 1, [128, MT], F32)
        t1_rot = Rot(k, "t1", 2, [128, MT], F32)
        accv_rot = Rot(k, "accv", 2, [128, MT], F32)
        gl_i = [0]

        def glu_chunks(hT2, hT2k, uT, uTk, ncols, col0):
            for c in range(8):
                ba = gl_i[0] % 2
                bg = 2 + gl_i[0] % 2
                gl_i[0] += 1
                k.mm([(pb(ba)[:, 0:ncols], Wag[:, kc, c * 128:(c + 1) * 128], hT2[:, kc, col0:col0 + ncols], kc == 0, kc == 7)
                      for kc in range(8)], [f"Wag_a{c // 2}", hT2k], [pbk(ba)])
                k.mm([(pb(bg)[:, 0:ncols], Wag[:, kc, D + c * 128:D + (c + 1) * 128], hT2[:, kc, col0:col0 + ncols], kc == 0, kc == 7)
                      for kc in range(8)], [f"Wag_g{c // 2}", hT2k], [pbk(bg)])
                sgm, sgk = sgm_rot.next()
                k.act([pbk(bg)], [sgk], out=sgm[:, 0:ncols], in_=pb(bg)[:, 0:ncols], func=AF.Sigmoid)
                k.dve("tensor_tensor", [pbk(ba), sgk], [f"{uTk}_{c}"], out=uT[:, c, 30 + col0:30 + col0 + ncols], in0=pb(ba)[:, 0:ncols],
                      in1=sgm[:, 0:ncols], op=ALU.mult)

        uTp, uTpk = uT_r.next()
        k.dve("memset", [], [f"{uTpk}_{c}" for c in range(8)] + [f"{uTpk}_h"], ap=uTp[:], constant=0.0)
        hTp, hTpk = hT2_r.next()
        xt, xtk = xt_rot.next()
        k.dma("sp", xt[:], xp_t[NT - 1], [], [xtk])
        hb, hk = h_rot.next()
        norm_tile(xt[:], xtk, 0, hb[:], hk)
        transpose_to(hb, hk, hTp, hTpk, 384, 7)
        glu_chunks(hTp, hTpk, uTp, uTpk, 128, 384)
        for m in range(NT // 4):
            hT2, hT2k = hT2_r.next()
            uT, uTk = uT_r.next()
            cv, cvk = cv_r.next()
            mean, meank = mean_r.next()
            rstdv, rstdk = rstdv_r.next()
            k.dve("tensor_copy", [f"{uTpk}_{c}" for c in range(8)], [f"{uTk}_h"], out=uT[:, :, 0:30], in_=uTp[:, :, MT:MT + 30])
            for st in range(4):
                t = m * 4 + st
                xt, xtk = xt_rot.next()
                k.dma("sp", xt[:], xo_t[t], [], [xtk])
                hb, hk = h_rot.next()
                norm_tile(xt[:], xtk, 0, hb[:], hk)
                transpose_to(hb, hk, hT2, hT2k, st * 128, 7)
            glu_chunks(hT2, hT2k, uT, uTk, MT, 0)
            for c in range(8):
                bcv = 4 + c % 2
                k.mm([(pb(bcv), diag[:, c, kk - NDV, :], uT[:, c, kk:kk + MT], kk == NDV, kk == 30) for kk in range(NDV, 31)],
                     [f"diag{c}", f"{uTk}_{c}", f"{uTk}_h"], [pbk(bcv)])
                accv, acck = accv_rot.next()
                k.dve("tensor_scalar", [f"{uTk}_{c}", f"{uTk}_h", "fw", "fv"], [acck], out=accv[:], in0=uT[:, c, 0:MT], scalar1=fw[:, c, 0:1],
                      scalar2=fv[:, c, 0:1], op0=ALU.mult, op1=ALU.add)
                for kk in range(1, NDV):
                    k.dve("scalar_tensor_tensor", [f"{uTk}_{c}", f"{uTk}_h", "fw", acck], [acck], out=accv[:], in0=uT[:, c, kk:kk + MT],
                          scalar=fw[:, c, kk:kk + 1], in1=accv[:], op0=ALU.mult, op1=ALU.add)
                k.dve("tensor_tensor", [pbk(bcv), acck], [f"{cvk}_{c}"], out=cv[:, c, :], in0=pb(bcv), in1=accv[:], op=ALU.add)
                cvsq, cqk = cvsq_rot.next()
                k.act([f"{cvk}_{c}"], [cqk], out=cvsq[:], in_=cv[:, c, :], func=AF.Square)
                k.mm([(pb(6), ones_bf[:], cv[:, c, :], c == 0, c == 7)], ["ones_bf", f"{cvk}_{c}"], [pbk(6)])
                k.mm([(pb(7), ones_bf[:], cvsq[:], c == 0, c == 7)], ["ones_bf", cqk], [pbk(7)])
            k.dve("tensor_scalar", [pbk(6)], [meank], out=mean[:], in0=pb(6), scalar1=1.0 / D, scalar2=None, op0=ALU.mult)
            k.dve("tensor_tensor", [meank], [rstdk], out=rstdv[:], in0=mean[:], in1=mean[:], op=ALU.mult)
            k.dve("scalar_tensor_tensor", [pbk(7), rstdk], [rstdk], out=rstdv[:], in0=pb(7), scalar=1.0 / D, in1=rstdv[:],
                  op0=ALU.mult, op1=ALU.subtract)
            k.act([rstdk, "epsT"], [rstdk], out=rstdv[:], in_=rstdv[:], func=AF.Ln, bias=epsT[:], scale=1.0)
            k.act([rstdk], [rstdk], out=rstdv[:], in_=rstdv[:], func=AF.Exp, scale=-0.5)
            for c in range(8):
                t1, t1k = t1_rot.next()
                lnp = os.environ.get("LNP", "0")
                e1 = k.pool if (lnp == "all" or (lnp == "half" and c % 2 == 1)) else k.dve
                e1("tensor_tensor", [f"{cvk}_{c}", meank], [t1k], out=t1[:], in0=cv[:, c, :], in1=mean[:], op=ALU.subtract)
                e1("tensor_tensor", [t1k, rstdk], [t1k], out=t1[:], in0=t1[:], in1=rstdv[:], op=ALU.mult)
                k.act([t1k, "fv"], [f"unT{c}"], out=unT[:, c, :], in_=t1[:], func=AF.Silu, scale=fv[:, c, 1:2], bias=fv[:, c, 2:3])
            for st in range(4):
                t = m * 4 + st
                ob = 2 * (st % 2)
                OPS = int(os.environ.get("OPS", "4"))
                per = 8 // OPS
                for part in range(OPS):
                    kcs = range(part * per, (part + 1) * per)
                    k.mm([(pb(ob + hf2), unT[:, kc, st * 128:(st + 1) * 128], Wob[:, kc, hf2 * 512:(hf2 + 1) * 512], kc == 0, kc == 7)
                          for hf2 in range(2) for kc in kcs], ["Wob"] + [f"unT{c}" for c in kcs], [pbk(ob), pbk(ob + 1)])
                xt, xtk = xb_rot.next()
                k.dma("sp", xt[:], base_t[t], [f"xr{t}"], [xtk])
                k.dve("tensor_tensor", [pbk(ob), pbk(ob + 1), xtk], [xtk], out=xt[:].rearrange("p (a b) -> p a b", a=2),
                      in0=xt[:].rearrange("p (a b) -> p a b", a=2), in1=PS[:, ob:ob + 2, :], op=ALU.add)
                k.dma("sp", xr_t[t], xt[:], [xtk], [f"xr{t}"])
            uTp, uTpk = uT, uTk
        k.pop()
    elif dbg == "ssd":
        pass

    k.push()
    x_res = k.sb("x_res", [128, NT, D], F32)
    KT = k.sb("KT", [128, 8, MEM], BF16)
    V = k.sb("V", [128, 2, D], BF16)
    with_kv = True
    if with_kv:
        k.push()
        load_gains([1])
        load_gains([2])
        WA3 = k.sb("WA3", [128, 16 * D], BF16)
        wq = WA3[:, 0:8 * D].rearrange("p (a b) -> p a b", a=8)
        wo = WA3[:, 8 * D:16 * D].rearrange("p (a b) -> p a b", a=8)
        for t in range(4):
            k.dma("sp", x_res[:, t, :], src_t[t], [f"xr{t}"], [f"xres{t}"])
        for qq in range(2):
            k.dma("pool", wq[:, :, qq * 512:(qq + 1) * 512], w_q.rearrange("(kc p) n -> p kc n", p=128)[:, :, qq * 512:(qq + 1) * 512], [], [f"WAq{qq}"])
        WA = k.sb("WA", [128, 8 * 2048], BF16)
        wkv = WA[:, 0:8 * 2048].rearrange("p (a b) -> p a b", a=8)
        w_kv_v = w_kv.rearrange("(kc p) n -> p kc n", p=128)
        for qq in range(4):
            k.dma("pool", wkv[:, :, qq * 512:(qq + 1) * 512], w_kv_v[:, :, qq * 512:(qq + 1) * 512], ["WAq1"], [f"WAkv{qq}"])
        for t in range(4, NT):
            k.dma("sp", x_res[:, t, :], src_t[t], [f"xr{t}", "WAkv1" if t < 8 else "WAkv3"], [f"xres{t}"])
        memT = k.sb("memT", [128, 8, MEM], BF16)
        mrot = Rot(k, "memx", 1, [128, D], F32)
        for mc in range(2):
            mx, mxk = mrot.next()
            k.dma("sp", mx[:], mem_d[mc * 128:(mc + 1) * 128, :], [], [mxk])
            hb, hk = h_rot.next()
            norm_tile(mx[:], mxk, 2, hb[:], hk)
            transpose_to(hb, hk, memT, "memT", mc * 128, 0)
        for c in range(8):
            bank = 1 + (c % 2)
            k.mm([(pb(bank)[:, 0:MEM], wkv[:, kc, c * 128:(c + 1) * 128], memT[:, kc, :], kc == 0, kc == 7)
                  for kc in range(8)], [f"WAkv{c // 4}", "memT"], [pbk(bank)])
            k.act([pbk(bank)], ["KT"], out=KT[:, c, :], in_=pb(bank)[:, 0:MEM], func=AF.Copy)
        for mc in range(2):
            for hf2 in range(2):
                bank = 3 + hf2
                k.mm([(pb(bank), memT[:, kc, mc * 128:(mc + 1) * 128],
                       wkv[:, kc, D + hf2 * 512:D + (hf2 + 1) * 512], kc == 0, kc == 7) for kc in range(8)],
                     [f"WAkv{2 + hf2}", "memT"], [pbk(bank)])
                k.dve("tensor_copy", [pbk(bank)], ["V"], out=V[:, mc, hf2 * 512:(hf2 + 1) * 512], in_=pb(bank))

    for qq in range(2):
        k.dma("pool", wo[:, :, qq * 512:(qq + 1) * 512], w_o.rearrange("(kc p) n -> p kc n", p=128)[:, :, qq * 512:(qq + 1) * 512], [f"WAkv{3}"], [f"WAo{qq}"])
    hxT_rot = Rot(k, "hxT", 2, [128, 8, MT], BF16)
    qT = k.sb("qT", [128, 8, MT], BF16)
    ET_rot = Rot(k, "ET", 2, [128, 2, MT], BF16)
    rden_rot = Rot(k, "rden", 2, [128, MT], F32)
    oT = k.sb("oT", [128, 8, MT], BF16)
    for m in range(NT // 4):
        hxT, hxk = hxT_rot.next()
        for st in range(4):
            t = m * 4 + st
            hb, hk = h_rot.next()
            norm_tile(x_res[:, t, :], f"xres{t}", 1, hb[:], hk)
            transpose_to(hb, hk, hxT, hxk, st * 128, 0)
        for c in range(8):
            bank = 1 + (c % 2)
            k.mm([(pb(bank), wq[:, kc, c * 128:(c + 1) * 128], hxT[:, kc, :], kc == 0, kc == 7) for kc in range(8)],
                 [f"WAq{c // 4}", hxk], [pbk(bank)])
            k.act([pbk(bank)], [f"qT{c}"], out=qT[:, c, :], in_=pb(bank), func=AF.Copy)
        for hd in range(4):
            ET, etk = ET_rot.next()
            for mc in range(2):
                bank = 3 + mc
                k.mm([(pb(bank), KT[:, 2 * hd + dc, mc * 128:(mc + 1) * 128], qT[:, 2 * hd + dc, :], dc == 0, dc == 1)
                      for dc in range(2)], ["KT", f"qT{2 * hd}", f"qT{2 * hd + 1}"], [pbk(bank)])
                k.act([pbk(bank)], [etk], out=ET[:, mc, :], in_=pb(bank), func=AF.Exp, scale=1.0 / 16.0)
            k.mm([(pb(5), ones_bf[:], ET[:, mc, :], mc == 0, mc == 1) for mc in range(2)], ["ones_bf", etk], [pbk(5)])
            rden, rdk = rden_rot.next()
            k.act([pbk(5)], [rdk], out=rden[:], in_=pb(5), func=AF.Ln)
            k.act([rdk], [rdk], out=rden[:], in_=rden[:], func=AF.Exp, scale=-1.0)
            for dc in range(2):
                bank = 6 + dc
                k.mm([(pb(bank), V[:, mc, hd * 256 + dc * 128:hd * 256 + (dc + 1) * 128], ET[:, mc, :], mc == 0, mc == 1)
                      for mc in range(2)], ["V", etk], [pbk(bank)])
                k.dve("tensor_tensor", [pbk(bank), rdk], [f"oT{2 * hd + dc}"], out=oT[:, 2 * hd + dc, :], in0=pb(bank),
                      in1=rden[:], op=ALU.mult)
        for st in range(4):
            t = m * 4 + st
            for hf2 in range(2):
                bank = 1 + hf2
                OP3 = int(os.environ.get("OP3", "1"))
                per3 = 8 // OP3
                for part in range(OP3):
                    kcs = range(part * per3, (part + 1) * per3)
                    k.mm([(pb(bank), oT[:, kc, st * 128:(st + 1) * 128], wo[:, kc, hf2 * 512:(hf2 + 1) * 512], kc == 0, kc == 7)
                          for kc in kcs], [f"WAo{hf2}"] + [f"oT{c}" for c in kcs], [pbk(bank)])
                k.dve("tensor_tensor", [pbk(bank), f"xres{t}"], [f"xres{t}"], out=x_res[:, t, hf2 * 512:(hf2 + 1) * 512],
                      in0=x_res[:, t, hf2 * 512:(hf2 + 1) * 512], in1=pb(bank), op=ALU.add)

    k.pop()
    k.push()
    load_gains([3])
    hfT = k.sb("hfT", [128, 8, HALF], BF16)
    groups = [(0, 4), (4, 4), (8, 4), (12, 4), (16, 3), (19, 3)]
    wg_rot = Rot(k, "wg", 2, [128, 8, 4 * 128], BF16)
    wu_rot = Rot(k, "wu", 2, [128, 8, 4 * 128], BF16)
    wd_rot = Rot(k, "wd", 2, [128, 4, D], BF16)
    sg_rot = Rot(k, "sg", 2, [128, MT], BF16)
    aT_rot = Rot(k, "aT", 2, [128, 4, MT], BF16)
    for gi, (c0, nch) in enumerate(groups):
        wg, wgk = wg_rot.next()
        wu, wuk = wu_rot.next()
        wd, wdk = wd_rot.next()
        k.dma("pool", wg[:, :, 0:nch * 128], w_gate.rearrange("(kc p) n -> p kc n", p=128)[:, :, c0 * 128:(c0 + nch) * 128], [], [wgk])
        k.dma("pool", wu[:, :, 0:nch * 128], w_up.rearrange("(kc p) n -> p kc n", p=128)[:, :, c0 * 128:(c0 + nch) * 128], [], [wuk])
        k.dma("pool", wd[:, 0:nch, :], w_down[c0 * 128:(c0 + nch) * 128, :].rearrange("(c p) n -> p c n", p=128), [], [wdk])
        for m in range(NT // 4):
            if gi == 0:
                for st in range(4):
                    t = m * 4 + st
                    hb, hk = h_rot.next()
                    norm_tile(x_res[:, t, :], f"xres{t}", 3, hb[:], hk)
                    transpose_to(hb, hk, hfT, f"hfT{m}", t * 128, 0)
            aT, aTk = aT_rot.next()
            for ci in range(nch):
                bg = 1 + (ci % 2)
                bu = 3 + (ci % 2)
                k.mm([(pb(bg), wg[:, kc, ci * 128:(ci + 1) * 128], hfT[:, kc, m * MT:(m + 1) * MT], kc == 0, kc == 7)
                      for kc in range(8)], [wgk, f"hfT{m}"], [pbk(bg)])
                k.mm([(pb(bu), wu[:, kc, ci * 128:(ci + 1) * 128], hfT[:, kc, m * MT:(m + 1) * MT], kc == 0, kc == 7)
                      for kc in range(8)], [wuk, f"hfT{m}"], [pbk(bu)])
                sg, sgk = sg_rot.next()
                k.act([pbk(bg)], [sgk], out=sg[:], in_=pb(bg), func=AF.Silu)
                k.dve("tensor_tensor", [pbk(bu), sgk], [aTk], out=aT[:, ci, :], in0=pb(bu), in1=sg[:], op=ALU.mult)
            for st in range(4):
                t = m * 4 + st
                for hf2 in range(2):
                    bank = 5 + hf2
                    k.mm([(pb(bank), aT[:, ci, st * 128:(st + 1) * 128], wd[:, ci, hf2 * 512:(hf2 + 1) * 512], ci == 0, ci == nch - 1)
                          for ci in range(nch)], [wdk, aTk], [pbk(bank)])
                    k.dve("tensor_tensor", [pbk(bank), f"xres{t}"], [f"xres{t}"],
                          out=x_res[:, t, hf2 * 512:(hf2 + 1) * 512], in0=x_res[:, t, hf2 * 512:(hf2 + 1) * 512],
                          in1=pb(bank), op=ALU.add)

    load_gains([4])
    o_rot = Rot(k, "ob", 2, [128, D], F32)
    for t in range(NT):
        ob, obk = o_rot.next()
        ss, sskey = ss_rot.next()
        sq, sqkey = sq_rot.next()
        k.act([f"xres{t}"], [sqkey, sskey], out=sq[:], in_=x_res[:, t, :], func=AF.Square, accum_out=ss[:, 0:1])
        rstd_from_ss(ss, sskey, 1, 1.0 / D)
        k.dve("scalar_tensor_tensor", [f"xres{t}", sskey, gains[4][1]], [obk], out=ob[:], in0=x_res[:, t, :], scalar=ss[:, 0:1],
              in1=gains[4][0], op0=ALU.mult, op1=ALU.mult)
        k.dma("sp", out_t[t], ob[:], [obk], [])
    S.emit()
    k.pop()
    k.pop()
    k.st.close()
    return nc


def make_inputs(inp, dbg=None):
    f = np.float32
    x = np.asarray(inp["x"], f)
    mem = np.asarray(inp["mem"], f)

    def row_bc(v):
        return np.broadcast_to(np.asarray(v, f).reshape(1, -1), (128, np.asarray(v).size))

    gbc = np.stack([row_bc(inp["norm_mix_g"][0]), row_bc(inp["norm_xattn_g"][0]), row_bc(inp["norm_mem_g"][0]),
                    row_bc(inp["norm_ffn_g"][0]), row_bc(inp["norm_final_g"]), row_bc(inp["ssd_norm_g"][0])], axis=1)
    dbc = row_bc(np.repeat(np.asarray(inp["ssd_D"][0], f), 64))
    small = np.zeros((128, 64), f)
    small[:, 0:16] = row_bc(inp["ssd_dt_bias"][0])
    small[:, 16:32] = row_bc(inp["ssd_A_log"][0])
    cw = np.asarray(inp["ssd_conv_w"][0], f).reshape(4, 12, 128).transpose(2, 1, 0)
    cb = np.asarray(inp["ssd_conv_b"][0], f).reshape(12, 128).T
    fw = np.asarray(inp["cf_conv_w"][0], f).reshape(31, 8, 128).transpose(2, 1, 0)
    fv = np.stack([np.asarray(inp[n][0], f).reshape(8, 128).T for n in ("cf_conv_b", "cf_ln_g", "cf_ln_b")], axis=2)
    ident = np.eye(128, dtype=f).astype(ml_dtypes.bfloat16)
    j = np.arange(128)
    tri = (j[:, None] <= j[None, :]).astype(f)
    su = (j[:, None] > j[None, :]).astype(f)
    cst = np.stack([tri, su, np.ones((128, 128), f), tri], axis=1)
    common = {
        "w_in": np.ascontiguousarray(inp["w_in"][0], f), "w_out": np.ascontiguousarray(inp["w_out"][0], f),
        "w_q": np.ascontiguousarray(inp["w_q"][0], f), "w_kv": np.ascontiguousarray(inp["w_kv"][0], f),
        "w_o": np.ascontiguousarray(inp["w_o"][0], f), "w_gate": np.ascontiguousarray(inp["w_gate"][0], f),
        "w_up": np.ascontiguousarray(inp["w_up"][0], f), "w_down": np.ascontiguousarray(inp["w_down"][0], f),
        "gbc": np.ascontiguousarray(gbc), "dbc": np.ascontiguousarray(dbc), "small": small,
        "cw": np.ascontiguousarray(cw), "cb": np.ascontiguousarray(cb), "fw": np.ascontiguousarray(fw),
        "fv": np.ascontiguousarray(fv), "ident": ident, "cst": np.ascontiguousarray(cst),
    }
    maps = []
    for c in range(8):
        b, hf = c // 2, c % 2
        d = dict(common)
        d["x_own"] = np.ascontiguousarray(x[b, hf * HALF:(hf + 1) * HALF])
        d["x_prev"] = np.ascontiguousarray(x[b, 0:HALF]) if hf == 1 else np.zeros((HALF, D), f)
        d["flag"] = np.full((128, 1), float(hf), f)
        d["mem"] = np.ascontiguousarray(mem[b])
        maps.append(d)
    return maps


_NC_CACHE = {}


def kernel(_dbg=None, **inputs):
    if _dbg not in _NC_CACHE:
        _NC_CACHE[_dbg] = build(_dbg)
    nc = _NC_CACHE[_dbg]
    maps = make_inputs(inputs, _dbg)
    res = run_bass_kernel_spmd(nc, maps, core_ids=list(range(8)))
    out = np.zeros((NB, SEQ, D), np.float32)
    for c in range(8):
        b, hf = c // 2, c % 2
        out[b, hf * HALF:(hf + 1) * HALF] = res.results[c]["out"]
    return out
```

```python
import contextlib
import os
import numpy as np
import ml_dtypes
import concourse.bass as bass
import concourse.mybir as mybir
from concourse.bass_utils import run_bass_kernel_spmd

F32 = mybir.dt.float32
BF16 = mybir.dt.bfloat16
ALU = mybir.AluOpType
AF = mybir.ActivationFunctionType

ENGS = ["pe", "dve", "act", "pool", "sp"]
SEG_CFG = {0: dict(lat=0.4, tsw=1.2), 1: dict(lat=0.4, tsw=1.2), 2: dict(lat=0.7, tsw=1.0, seed=396330, jit=0.05),
           3: dict(lat=0.7, tsw=1.0, seed=748492, jit=0.05)}

D = 1024
SEQ = 4096
NB = 4
HALF = 2048
NT = 16
MT = 512
MEM = 256
INW = 4624
DFF = 2816
NFF = 22
EPS = 1e-6
XBC0 = 1024
DT0 = 2560
A0 = 2576
G0 = 3600


class Sched:
    LAT = 0.5
    DELTA_DEFAULT = 0.5

    def __init__(self, nc, n_dma_sems=4):
        self.nc = nc
        self.ops = []
        self.seg = 0
        self.n_dma = n_dma_sems
        self._stream = {}
        self._seg_counter = 0

    def barrier(self):
        self.seg += 1

    def add(self, eng, fn, reads=(), writes=(), dma=False, cost=1.0, tset=None):
        self.ops.append(dict(eng=eng, fn=fn, reads=tuple(reads), writes=tuple(writes), dma=int(dma), cost=float(cost),
                             seg=self.seg, idx=len(self.ops), tset=tset))

    def _schedule_segment(self, ops):
        n = len(ops)
        cfg = dict(lat=self.LAT, tsw=1.4, seed=0, jit=0.0)
        cfg.update(SEG_CFG.get(self._seg_counter, {}))
        env = os.environ.get("SCHED_CFG")
        if env:
            import json as _json
            cfg.update(_json.loads(env))
        self._seg_counter += 1
        LATV = cfg["lat"]
        rng = np.random.default_rng(cfg["seed"]) if cfg["jit"] > 0 else None
        DMAF = float(os.environ.get("DMAF", "0.75"))
        PEW = float(os.environ.get("PEW", "1.0"))
        preds = [set() for _ in range(n)]
        raw = [set() for _ in range(n)]
        last_writer, readers = {}, {}
        for i, op in enumerate(ops):
            for r in op["reads"]:
                lw = last_writer.get(r)
                if lw is not None:
                    preds[i].add(lw)
                    raw[i].add(lw)
            for w in op["writes"]:
                lw = last_writer.get(w)
                if lw is not None:
                    preds[i].add(lw)
                for rd in readers.get(w, ()):
                    if rd != i:
                        preds[i].add(rd)
            for r in op["reads"]:
                readers.setdefault(r, []).append(i)
            for w in op["writes"]:
                last_writer[w] = i
                readers[w] = []
        succs = [[] for _ in range(n)]
        for i in range(n):
            for p in preds[i]:
                succs[p].append(i)

        def dur(op):
            return max(0.06, op["cost"] * DMAF) if op["dma"] else op["cost"]

        def done_lat(op):
            return (2.0 + op["cost"]) if op["dma"] else op["cost"]

        prio = [0.0] * n
        for i in range(n - 1, -1, -1):
            m = 0.0
            for sidx in succs[i]:
                if prio[sidx] > m:
                    m = prio[sidx]
            prio[i] = m + done_lat(ops[i]) * (PEW if ops[i]["eng"] == "pe" else 1.0)
        if rng is not None:
            jit = rng.random(n)
            prio = [p * (1.0 + cfg["jit"] * (j - 0.5)) for p, j in zip(prio, jit)]
        npred = [len(preds[i]) for i in range(n)]
        ready_t = [0.0] * n
        finish = [0.0] * n
        free = {e: 0.0 for e in ENGS}
        ready = {e: [] for e in ENGS}
        for i in range(n):
            if npred[i] == 0:
                ready[ops[i]["eng"]].append(i)
        order = {e: [] for e in ENGS}
        left = n
        cur_set = [None]
        TSW = cfg["tsw"]
        DELTA = float(os.environ.get('SCHED_DELTA', self.DELTA_DEFAULT))
        while left:
            best = None
            for e in ENGS:
                if not ready[e]:
                    continue
                fe = free[e]
                cand = None
                stts = {}
                mn = None
                for i in ready[e]:
                    stt = ready_t[i] if ready_t[i] > fe else fe
                    if e == "act" and ops[i]["tset"] is not None and cur_set[0] is not None and ops[i]["tset"] != cur_set[0]:
                        stt = stt + TSW
                    stts[i] = stt
                    if mn is None or stt < mn:
                        mn = stt
                for i in ready[e]:
                    stt = stts[i]
                    if stt > mn + DELTA:
                        continue
                    key = (-prio[i], stt, i)
                    if cand is None or key < cand[0]:
                        cand = (key, i, stt)
                if best is None or cand[2] < best[2] - 1e-9 or (abs(cand[2] - best[2]) <= 1e-9 and cand[0] < best[0]):
                    best = cand
            _, i, stt = best
            op = ops[i]
            e = op["eng"]
            ready[e].remove(i)
            if e == "act" and op["tset"] is not None:
                cur_set[0] = op["tset"]
            free[e] = stt + dur(op)
            finish[i] = stt + done_lat(op)
            order[e].append(i)
            left -= 1
            for sidx in succs[i]:
                rt = finish[i] + (LATV if ops[sidx]["eng"] != e or op["dma"] else 0.0)
                if rt > ready_t[sidx]:
                    ready_t[sidx] = rt
                npred[sidx] -= 1
                if npred[sidx] == 0:
                    ready[ops[sidx]["eng"]].append(sidx)
        return order, preds, raw

    def emit(self, final_eng="sp"):
        nc = self.nc
        nseg = self.seg + 1
        segs = [[] for _ in range(nseg)]
        for op in self.ops:
            segs[op["seg"]].append(op)
        gorder = {e: [] for e in ENGS}
        dma_val = {}
        dma_rr = {e: 0 for e in ENGS}
        known_pos = {e: {e2: -1 for e2 in ENGS} for e in ENGS}
        known_dma = {e: {} for e in ENGS}
        pending_pos = {e: {} for e in ENGS}
        pending_dma = {e: {} for e in ENGS}
        for sops in segs:
            if not sops:
                continue
            order, preds, raw = self._schedule_segment(sops)
            rec = {}
            for e in ENGS:
                for i in order[e]:
                    op = sops[i]
                    r = dict(op=op, eng=e, pos=None, signal=False, waits_pos={}, waits_dma={}, dma_tok=None)
                    if op["dma"]:
                        j = dma_rr[e]
                        dma_rr[e] = (j + 1) % (self.n_dma if e == "sp" else 3)
                        key = ("dma", e, j)
                        prev = dma_val.get(key, 0)
                        val = prev + 16 * op["dma"]
                        dma_val[key] = val
                        r["dma_tok"] = (key, val)
                        r["dma_prev"] = (key, prev) if prev > 0 else None
                    else:
                        r["pos"] = len(gorder[e])
                        gorder[e].append(r)
                    rec[i] = r
            for e in ENGS:
                for i in order[e]:
                    r = rec[i]
                    op = sops[i]
                    wp, wd = {}, {}
                    for e2, p in pending_pos[e].items():
                        if known_pos[e][e2] < p:
                            wp[e2] = p
                    for kk, vv in pending_dma[e].items():
                        if known_dma[e].get(kk, 0) < vv:
                            wd[kk] = vv
                    pending_pos[e] = {}
                    pending_dma[e] = {}
                    for p in preds[i]:
                        pr = rec[p]
                        if pr["dma_tok"] is not None:
                            kk, vv = pr["dma_tok"]
                            if known_dma[e].get(kk, 0) < vv and wd.get(kk, 0) < vv:
                                wd[kk] = vv
                        elif pr["eng"] != e or e != "pe":
                            e2 = pr["eng"]
                            if known_pos[e][e2] < pr["pos"] and wp.get(e2, -1) < pr["pos"]:
                                wp[e2] = pr["pos"]
                    if op["dma"] and r["dma_prev"] is not None:
                        kk, vv = r["dma_prev"]
                        if known_dma[e].get(kk, 0) < vv and wd.get(kk, 0) < vv:
                            wd[kk] = vv
                    for e2, p in wp.items():
                        known_pos[e][e2] = p
                        gorder[e2][p]["signal"] = True
                    for kk, vv in wd.items():
                        known_dma[e][kk] = vv
                    r["waits_pos"] = wp
                    r["waits_dma"] = wd
                    r["stream_eng"] = e
            for e in ENGS:
                for e2 in ENGS:
                    if e2 != e and gorder[e2]:
                        pending_pos[e][e2] = len(gorder[e2]) - 1
                        gorder[e2][-1]["signal"] = True
                for kk, vv in dma_val.items():
                    pending_dma[e][kk] = vv
            self._last_rec = rec
            for e in ENGS:
                for i in order[e]:
                    self._stream.setdefault(e, []).append(rec[i])
        for e in ENGS:
            if gorder[e]:
                gorder[e][-1]["signal"] = True
        count = {e: 0 for e in ENGS}
        for e in ENGS:
            for r in gorder[e]:
                if r["signal"]:
                    count[e] += 1
                    r["val"] = count[e]
        keys = [("eng", e) for e in ENGS] + sorted(dma_val.keys())
        with contextlib.ExitStack() as st:
            sems = {}
            for kk in keys:
                sems[kk] = st.enter_context(nc.semaphore("s_" + "_".join(str(x) for x in kk)))
            fin_waits = []
            for e in ENGS:
                if gorder[e] and e != final_eng:
                    fin_waits.append((("eng", e), count[e]))
            for kk, vv in dma_val.items():
                fin_waits.append((kk, vv))
            block = st.enter_context(nc.Block())
            streams = self._stream

            def run(ename):
                def body(eng):
                    for r in streams.get(ename, []):
                        for e2, p in r["waits_pos"].items():
                            eng.wait_ge(sems[("eng", e2)], gorder[e2][p]["val"])
                        for kk, vv in r["waits_dma"].items():
                            eng.wait_ge(sems[kk], vv)
                        res = r["op"]["fn"](eng)
                        if r["op"]["dma"]:
                            if not isinstance(res, (list, tuple)):
                                res = [res]
                            assert len(res) == int(r["op"]["dma"])
                            for ins in res:
                                ins.then_inc(sems[r["dma_tok"][0]], 16)
                        elif r["signal"]:
                            if isinstance(res, (list, tuple)):
                                res = res[-1]
                            res.then_inc(sems[("eng", ename)], 1)
                    if ename == final_eng:
                        for kk, vv in fin_waits:
                            eng.wait_ge(sems[kk], vv)
                return body

            block.tensor(run("pe"))
            block.vector(run("dve"))
            block.scalar(run("act"))
            block.gpsimd(run("pool"))
            block.sync(run("sp"))


class Rot:
    def __init__(self, K, name, n, shape, dt):
        self.bufs = [K.sb(f"{name}{i}", shape, dt) for i in range(n)]
        self.keys = [f"{name}{i}" for i in range(n)]
        self.i = 0

    def next(self):
        j = self.i
        self.i = (j + 1) % len(self.bufs)
        return self.bufs[j], self.keys[j]


class K:
    def __init__(self, dbg=None):
        self.dbg = dbg
        self.nc = bass.Bass("TRN2", target_bir_lowering=False)
        self.st = contextlib.ExitStack()
        self.stacks = [self.st]
        self.S = Sched(self.nc)
        self.dram = {}

    def din(self, name, shape, dt=F32):
        ap = self.nc.dram_tensor(name, list(shape), dt, kind="ExternalInput").ap()
        self.dram[name] = ap
        return ap

    def sb(self, name, shape, dt):
        return self.stacks[-1].enter_context(self.nc.sbuf_tensor("sb_" + name, list(shape), dt))

    def push(self):
        self.stacks.append(contextlib.ExitStack())

    def pop(self):
        self.S.barrier()
        self.stacks.pop().close()

    @staticmethod
    def _fs(ap):
        try:
            return float(ap.free_size())
        except Exception:
            return 512.0

    TSETS = {AF.Silu: "silu", AF.Exp: "lnexp", AF.Ln: "lnexp", AF.Sigmoid: "sigm"}

    def act(self, r, w, **kw):
        c = 0.25 + self._fs(kw["out"]) / 1200.0
        self.S.add("act", lambda e: e.activation(**kw), r, w, cost=c, tset=self.TSETS.get(kw["func"]))

    def dve(self, opname, r, w, **kw):
        o = kw.get("out", kw.get("ap"))
        c = 0.08 + self._fs(o) / 960.0
        self.S.add("dve", lambda e: getattr(e, opname)(**kw), r, w, cost=c)

    def pool(self, opname, r, w, **kw):
        o = kw.get("out", kw.get("ap"))
        c = 0.15 + self._fs(o) / 500.0
        self.S.add("pool", lambda e: getattr(e, opname)(**kw), r, w, cost=c)

    def mm(self, lst, r, w):
        def f(e):
            ins = None
            for (o, l, rh, st, sp) in lst:
                ins = e.matmul(out=o, lhsT=l, rhs=rh, start=st, stop=sp)
            return ins
        c = 0.1
        for (o, l, rh, st, sp) in lst:
            n = self._fs(o)
            c += max(0.065, n / 2400.0) * (4.0 if l.dtype == F32 else 1.0)
        self.S.add("pe", f, r, w, cost=c)

    def tr(self, lst, ident, r, w):
        def f(e):
            ins = None
            for (o, i) in lst:
                ins = e.transpose(out=o, in_=i, identity=ident)
            return ins
        self.S.add("pe", f, r, w, cost=0.1 + 0.11 * len(lst))

    def dma(self, eng, out, in_, r, w):
        try:
            nbytes = float(out.nbytes())
        except Exception:
            nbytes = 5e5
        self.S.add(eng, lambda e: e.dma_start(out=out, in_=in_), r, w, dma=1, cost=nbytes / 1.5e5)


def bc(ap, shape):
    return ap.to_broadcast(list(shape))


import os
LVL = int(os.environ.get("SSD_LVL", "9"))


def build(dbg=None):
    k = K(dbg)
    nc = k.nc
    S = k.S
    x_own = k.din("x_own", [HALF, D])
    x_prev = k.din("x_prev", [HALF, D])
    flag_d = k.din("flag", [128, 1])
    mem_d = k.din("mem", [MEM, D])
    w_in = k.din("w_in", [D, INW])
    w_out = k.din("w_out", [2 * D, D])
    w_q = k.din("w_q", [D, D])
    w_kv = k.din("w_kv", [D, 2 * D])
    w_o = k.din("w_o", [D, D])
    w_gate = k.din("w_gate", [D, DFF])
    w_up = k.din("w_up", [D, DFF])
    w_down = k.din("w_down", [DFF, D])
    gbc_d = k.din("gbc", [128, 6, D])
    dbc_d = k.din("dbc", [128, D])
    sm_d = k.din("small", [128, 64])
    cw_d = k.din("cw", [128, 12, 4])
    cb_d = k.din("cb", [128, 12])
    fw_d = k.din("fw", [128, 8, 31])
    fv_d = k.din("fv", [128, 8, 3])
    ident_d = k.din("ident", [128, 128], BF16)
    cst_d = k.din("cst", [128, 4, 128])
    out_d = nc.dram_tensor("out", [HALF, D], F32, kind="ExternalOutput").ap()
    xr_d = nc.dram_tensor("xr", [HALF, D], F32, kind="Internal").ap()

    xo_t = x_own.rearrange("(t p) d -> t p d", p=128)
    xp_t = x_prev.rearrange("(t p) d -> t p d", p=128)
    xr_t = xr_d.rearrange("(t p) d -> t p d", p=128)
    out_t = out_d.rearrange("(t p) d -> t p d", p=128)

    gains = {}

    def load_gains(idxs):
        k.gcount = getattr(k, "gcount", 0) + 1
        gb = k.sb(f"gains{k.gcount}", [128, len(idxs), D], F32)
        for i, gi in enumerate(idxs):
            key = f"gain{gi}_{k.gcount}"
            k.dma("sp", gb[:, i, :], gbc_d[:, gi, :], [], [key])
            gains[gi] = (gb[:, i, :], key)
    sm = k.sb("sm", [128, 64], F32)
    cw = k.sb("cw", [128, 12, 4], F32)
    cb = k.sb("cb", [128, 12], F32)
    fw = k.sb("fw", [128, 8, 31], F32)
    fv = k.sb("fv", [128, 8, 3], F32)
    ident = k.sb("ident", [128, 128], BF16)
    cst = k.sb("cst", [128, 4, 128], F32)
    flag = k.sb("flag", [128, 1], F32)
    epsT = k.sb("epsT", [128, 1], F32)
    ones_bf = k.sb("ones_bf", [128, 128], BF16)
    for (dst, src, nm) in [ (sm, sm_d, "sm"), (cw, cw_d, "cw"),
                           (cb, cb_d, "cb"), (fw, fw_d, "fw"), (fv, fv_d, "fv"), (ident, ident_d, "ident"),
                           (cst, cst_d, "cst"), (flag, flag_d, "flag")]:
        k.dma("sp", dst[:], src, [], [nm])
    k.dve("memset", [], ["epsT"], ap=epsT[:], constant=EPS)
    k.dve("memset", [], ["ones_bf"], ap=ones_bf[:], constant=1.0)

    PS = k.st.enter_context(nc.psum_tensor("PS", [128, 8, 512], F32))

    def pb(i):
        return PS[:, i, :]

    def pbk(i):
        return f"ps{i}"

    ss_rot = Rot(k, "ss", int(os.environ.get("SSROT", "4")), [128, 4], F32)
    sq_rot = Rot(k, "sq", 1, [128, D], BF16)
    h_rot = Rot(k, "hb", int(os.environ.get("HROT", "2")), [128, D], BF16)

    def rstd_from_ss(ss, sskey, n, inv_n):
        k.act([sskey, "epsT"], [sskey], out=ss[:, 0:n], in_=ss[:, 0:n], func=AF.Ln, scale=inv_n, bias=epsT[:])
        k.act([sskey], [sskey], out=ss[:, 0:n], in_=ss[:, 0:n], func=AF.Exp, scale=-0.5)

    def norm_tile(x_ap, xkey, gidx, h_out, hkey):
        ss, sskey = ss_rot.next()
        sq, sqkey = sq_rot.next()
        k.act([xkey], [sqkey, sskey], out=sq[:], in_=x_ap, func=AF.Square, accum_out=ss[:, 0:1])
        rstd_from_ss(ss, sskey, 1, 1.0 / D)
        k.dve("scalar_tensor_tensor", [xkey, sskey, gains[gidx][1]], [hkey], out=h_out, in0=x_ap, scalar=ss[:, 0:1],
              in1=gains[gidx][0], op0=ALU.mult, op1=ALU.mult)

    def transpose_to(h_ap, hkey, dstT, dkey, col0, bank):
        pT = pb(bank).bitcast(BF16)
        k.tr([(pT[:, kc * 128:(kc + 1) * 128], h_ap[:, kc * 128:(kc + 1) * 128]) for kc in range(8)], ident[:],
             [hkey, "ident"], [pbk(bank)])
        k.act([pbk(bank)], [dkey], out=dstT[:, :, col0:col0 + 128], in_=pT.rearrange("p (a b) -> p a b", a=8),
              func=AF.Copy)

    x_res = None

    src_t = xo_t if dbg == "nomixer" else xr_t
    tri = cst[:, 0, :]
    su = cst[:, 1, :]
    onesf = cst[:, 2, :]
    mask01 = cst[:, 3, :]
    do_ssd = dbg in (None, "ssd")
    do_conf = dbg in (None, "conf")

    if do_ssd:
        k.push()
        load_gains([0, 5])
        dbc = k.sb("dbc", [128, D], F32)
        k.dma("sp", dbc[:], dbc_d, [], ["dbc"])
        xt_rot = Rot(k, "xta", int(os.environ.get("XTA", "3")), [128, D], F32)
        xb3_rot = Rot(k, "xtc", int(os.environ.get("XTC", "1")), [128, D], F32)
        Wa = k.sb("Wa", [128, 8, 2576], BF16)
        Wot = k.sb("Wot", [128, 8, D], BF16)
        w_in_v = w_in.rearrange("(kc p) n -> p kc n", p=128)
        for q in range(4):
            k.dma("pool", Wa[:, :, 1024 + 384 * q:1024 + 384 * (q + 1)], w_in_v[:, :, 1024 + 384 * q:1024 + 384 * (q + 1)], [], [f"Wa_x{q}"])
        k.dma("pool", Wa[:, :, 2560:2576], w_in_v[:, :, 2560:2576], [], ["Wa_dt"])
        k.dma("pool", Wa[:, :, 0:1024], w_in_v[:, :, 0:1024], [], ["Wa_z"])
        k.dma("pool", Wot[:], w_out[0:D, :].rearrange("(kc p) n -> p kc n", p=128), [], ["Wot"])
        diag4 = k.sb("diag4", [128, 12, 4, 128], BF16)
        for c in range(12):
            k.dve("tensor_tensor", ["ident", "cw"], [f"diag4_{c}"], out=diag4[:, c, :, :], in0=bc(ident[:].unsqueeze(1), [128, 4, 128]),
                  in1=bc(cw[:, c, :].unsqueeze(2), [128, 4, 128]), op=ALU.mult)
        hT_r = Rot(k, "hT1_", 2, [128, 8, MT], BF16)
        xbcT_r = Rot(k, "xbcT_", 2, [128, 12, MT], BF16)
        dt4_r = Rot(k, "dt4_", 2, [128, 4, 16], F32)
        dA4_r = Rot(k, "dA4_", 2, [128, 4, 16], F32)
        sz_r = Rot(k, "sz_", 2, [128, 4, D], BF16)
        pre_rot = Rot(k, "pre", int(os.environ.get("PRE", "3")), [128, 515], BF16)
        hist = k.sb("hist", [128, 12, 3], BF16)
        dtp = k.sb("dtp", [128, 4, 16], F32)
        nA = k.sb("nA", [128, 16], F32)
        ex3_r = Rot(k, "ex3_", 2, [128, 48], F32)
        xdt_r = Rot(k, "xdt_", 2, [128, D], BF16)
        xD_r = Rot(k, "xD_", 1, [128, D], BF16)
        Btok_r = Rot(k, "Btok_", 2, [128, 256], BF16)
        E_r = Rot(k, "E_", 2, [128, 16, 128], BF16)
        CBm_r = Rot(k, "CBm_", 2, [128, 2, 128], BF16)
        SUhi = k.sb("SUhi", [128, 8, 128], BF16)
        SUlo = k.sb("SUlo", [128, 8, 128], BF16)
        su_bf = k.sb("su_bf", [128, 128], BF16)
        tri_bf = k.sb("tri_bf", [128, 128], BF16)
        dAh = k.sb("dAh", [128, 16], BF16)
        dAh32 = k.sb("dAh32", [128, 16], F32)
        dAl = k.sb("dAl", [128, 16], F32)
        k.dve("tensor_copy", ["cst"], ["su_bf"], out=su_bf[:], in_=cst[:, 1, :])
        k.dve("tensor_copy", ["cst"], ["tri_bf"], out=tri_bf[:], in_=cst[:, 0, :])
        xw_r = Rot(k, "xw_", 2, [128, D], BF16)
        y1 = k.sb("y1", [128, D], F32)
        yn = k.sb("yn", [128, D], BF16)
        ynT = k.sb("ynT", [128, 8, 128], BF16)
        Sst = k.sb("Sst", [128, D], F32)
        Sbf = k.sb("Sbf", [128, D], BF16)
        k.dve("memset", [], ["hist"], ap=hist[:], constant=0.0)
        k.dve("memset", [], ["Sst"], ap=Sst[:], constant=0.0)
        k.dve("memset", [], ["Sbf"], ap=Sbf[:], constant=0.0)
        k.act(["sm"], ["nA"], out=nA[:], in_=sm[:, 16:32], func=AF.Exp)
        k.dve("tensor_scalar", ["nA"], ["nA"], out=nA[:], in0=nA[:], scalar1=-1.0, scalar2=None, op0=ALU.mult)
        ATB = int(os.environ.get("ATB", "2"))
        YTB = int(os.environ.get("YTB", "0"))
        pT0 = pb(ATB).bitcast(BF16)
        pTy = pb(YTB).bitcast(BF16)
        G1 = gains[5]

        mt_state = {}

        def src_tile(g_mt, st):
            return (xp_t if g_mt < 4 else xo_t)[(g_mt % 4) * 4 + st]

        mt_h = {}

        def F1(mt, pair):
            if pair == 0:
                mt_h[mt] = hT_r.next()
            hT, hTk = mt_h[mt]
            items = []
            for st in (2 * pair, 2 * pair + 1):
                xt, xtk = xt_rot.next()
                k.dma("sp", xt[:], src_tile(mt, st), [], [xtk])
                ss, sskey = ss_rot.next()
                sq, sqkey = sq_rot.next()
                k.act([xtk], [sqkey, sskey], out=sq[:], in_=xt[:], func=AF.Square, accum_out=ss[:, 0:1])
                rstd_from_ss(ss, sskey, 1, 1.0 / D)
                items.append((st, xt, xtk, ss, sskey))
            hbs = []
            for (st, xt, xtk, ss, sskey) in items:
                hb, hk = h_rot.next()
                k.dve("scalar_tensor_tensor", [xtk, sskey, gains[0][1]], [hk], out=hb[:], in0=xt[:], scalar=ss[:, 0:1],
                      in1=gains[0][0], op0=ALU.mult, op1=ALU.mult)
                hbs.append((st, hb, hk))
            F1B = int(os.environ.get("F1B", "0"))
            for i, (st, hb, hk) in enumerate(hbs):
                pT = pb(F1B + i).bitcast(BF16)
                k.tr([(pT[:, kc * 128:(kc + 1) * 128], hb[:, kc * 128:(kc + 1) * 128]) for kc in range(8)], ident[:],
                     [hk, "ident"], [pbk(F1B + i)])
            for i, (st, hb, hk) in enumerate(hbs):
                pT = pb(F1B + i).bitcast(BF16)
                k.act([pbk(F1B + i)], [hTk], out=hT[:, :, st * 128:(st + 1) * 128], in_=pT.rearrange("p (a b) -> p a b", a=8), func=AF.Copy)

        def F2(mt, q):
            hT, hTk = mt_h[mt]
            if q == 0:
                xbcT, xbk = xbcT_r.next()
                dt4, dt4k = dt4_r.next()
                dA4, dA4k = dA4_r.next()
                szm, szk = sz_r.next()
                mt_state[mt] = (hT, hTk, xbcT, xbk, dt4, dt4k, dA4, dA4k, szm, szk)
            hT, hTk, xbcT, xbk, dt4, dt4k, dA4, dA4k, szm, szk = mt_state[mt]
            pres = {}

            def inproj(c):
                b1 = c % 2
                k.mm([(pb(b1), Wa[:, kc, 1024 + c * 128:1024 + (c + 1) * 128], hT[:, kc, :], kc == 0, kc == 7)
                      for kc in range(8)], [f"Wa_x{c // 3}", hTk], [pbk(b1)])
                pre, prek = pre_rot.next()
                pres[c] = (pre, prek)
                k.act([pbk(b1)], [prek], out=pre[:, 3:515], in_=pb(b1), func=AF.Copy)
                k.dve("tensor_copy", ["hist", prek], [prek], out=pre[:, 0:3], in_=hist[:, c, :])
                k.dve("tensor_copy", [prek], ["hist"], out=hist[:, c, :], in_=pre[:, 512:515])

            def conv(c):
                pre, prek = pres[c]
                b2 = 2 + c % 2
                k.mm([(pb(b2), diag4[:, c, kk, :], pre[:, kk:kk + 512], kk == 0, kk == 3) for kk in range(4)],
                     [f"diag4_{c}", prek], [pbk(b2)])
                k.act([pbk(b2), "cb"], [f"{xbk}_{c}"], out=xbcT[:, c, :], in_=pb(b2), func=AF.Silu, bias=cb[:, c:c + 1], scale=1.0)

            c0 = 3 * q
            skipC = (mt < 3 and q == 3)
            inproj(c0)
            if not skipC:
                inproj(c0 + 1)
            conv(c0)
            if not skipC:
                inproj(c0 + 2)
                conv(c0 + 1)
            if mt >= 4:
                st = q
                k.mm([(pb(hf2), hT[:, kc, st * 128:(st + 1) * 128], Wa[:, kc, hf2 * 512:(hf2 + 1) * 512], kc == 0, kc == 7)
                      for hf2 in range(2) for kc in range(8)], ["Wa_z", hTk], [pbk(0), pbk(1)])
            if not skipC:
                conv(c0 + 2)
            if mt >= 4:
                k.act([pbk(0), pbk(1)], [szk], out=szm[:, q, :].rearrange("p (a b) -> p a b", a=2), in_=PS[:, 0:2, :], func=AF.Silu)
            if q == 3:
                for st in range(4):
                    k.mm([(pb(1)[:, st * 16:(st + 1) * 16], hT[:, kc, st * 128:(st + 1) * 128], Wa[:, kc, 2560:2576], kc == 0, kc == 7)
                          for kc in range(8)], ["Wa_dt", hTk], [pbk(1)])
                k.dve("tensor_tensor", [pbk(1), "sm"], ["dtp"], out=dtp[:], in0=pb(1)[:, 0:64].rearrange("p (a b) -> p a b", a=4),
                      in1=bc(sm[:, 0:16].unsqueeze(1), [128, 4, 16]), op=ALU.add)
                k.act(["dtp"], ["dtp"], out=dtp[:], in_=dtp[:], func=AF.Exp)
                k.act(["dtp", "cst"], [dt4k], out=dt4[:], in_=dtp[:], func=AF.Ln, bias=cst[:, 2, 0:1], scale=1.0)
                k.dve("tensor_tensor", [dt4k, "nA"], [dA4k], out=dA4[:], in0=dt4[:], in1=bc(nA[:].unsqueeze(1), [128, 4, 16]), op=ALU.mult)

        ch_state = {}

        def A(g):
            mt, st = g // 4, g % 4
            full = mt >= 4
            hT, hTk, xbcT, xbk, dt4, dt4k, dA4, dA4k, szm, szk = mt_state[mt]
            cs = slice(st * 128, (st + 1) * 128)
            xdt, xdtk = xdt_r.next()
            xD, xDk = xD_r.next()
            Btok, Btk = Btok_r.next()
            ex3, ex3k = ex3_r.next()
            E, Ek = E_r.next()
            CBm, CBk = CBm_r.next()
            xw, xwk = xw_r.next()
            ch_state[g] = (xdt, xdtk, xD, xDk, Btok, Btk, ex3, ex3k, E, Ek, xw, xwk)
            k.tr([(pT0[:, c * 128:(c + 1) * 128], xbcT[:, c, cs]) for c in range(8)], ident[:],
                 [f"{xbk}_{c}" for c in range(8)] + ["ident"], [pbk(ATB)])
            k.dve("tensor_tensor", [pbk(ATB), dt4k], [xdtk], out=xdt[:].rearrange("p (h q) -> p h q", h=16),
                  in0=pT0.rearrange("p (h q) -> p h q", h=16), in1=bc(dt4[:, st, :].unsqueeze(2), [128, 16, 64]), op=ALU.mult)
            if full:
                k.dve("tensor_tensor", [pbk(ATB), "dbc"], [xDk], out=xD[:], in0=pT0, in1=dbc[:], op=ALU.mult)
            k.tr([(pT0[:, gg * 128:(gg + 1) * 128], xbcT[:, 8 + gg, cs]) for gg in range(2)], ident[:],
                 [f"{xbk}_8", f"{xbk}_9", "ident"], [pbk(ATB)])
            k.act([pbk(ATB)], [Btk], out=Btok[:], in_=pT0[:, 0:256], func=AF.Copy)
            dA = dA4[:, st, :]
            k.mm([(pb(1)[:, 0:16], tri, dA, True, True), (pb(1)[:, 16:32], su, dA, True, True),
                  (pb(1)[:, 32:48], onesf, dA, True, True)], ["cst", dA4k], [pbk(1)])
            k.act([pbk(1)], [ex3k], out=ex3[:], in_=pb(1)[:, 0:48], func=AF.Exp)
            k.pool("tensor_tensor", [xdtk, ex3k], [xwk], out=xw[:].rearrange("p (h q) -> p h q", h=16),
                   in0=xdt[:].rearrange("p (h q) -> p h q", h=16), in1=bc(ex3[:, 16:32].unsqueeze(2), [128, 16, 64]), op=ALU.mult)
            if full:
                k.dve("tensor_copy", [dA4k], ["dAh"], out=dAh[:], in_=dA)
                k.dve("tensor_copy", ["dAh"], ["dAh32"], out=dAh32[:], in_=dAh[:])
                k.dve("tensor_tensor", [dA4k, "dAh32"], ["dAl"], out=dAl[:], in0=dA, in1=dAh32[:], op=ALU.subtract)
                for r in range(2):
                    k.dve("tensor_tensor", ["su_bf", "dAh"], ["SUhi"], out=SUhi[:], in0=bc(su_bf[:].unsqueeze(1), [128, 8, 128]),
                          in1=bc(dAh[:, r * 8:(r + 1) * 8].unsqueeze(2), [128, 8, 128]), op=ALU.mult)
                    k.dve("tensor_tensor", ["su_bf", "dAl"], ["SUlo"], out=SUlo[:], in0=bc(su_bf[:].unsqueeze(1), [128, 8, 128]),
                          in1=bc(dAl[:, r * 8:(r + 1) * 8].unsqueeze(2), [128, 8, 128]), op=ALU.mult)
                    for hq in range(2):
                        bank = 2 + hq
                        lst = []
                        for hh in range(4):
                            h = hq * 4 + hh
                            lst.append((pb(bank)[:, hh * 128:(hh + 1) * 128], SUhi[:, h, :], tri_bf[:], True, False))
                            lst.append((pb(bank)[:, hh * 128:(hh + 1) * 128], SUlo[:, h, :], tri_bf[:], False, True))
                        k.mm(lst, ["SUhi", "SUlo", "tri_bf"], [pbk(bank)])
                        k.act([pbk(bank)], [Ek], out=E[:, r * 8 + hq * 4:r * 8 + hq * 4 + 4, :],
                              in_=pb(bank).rearrange("p (a b) -> p a b", a=4), func=AF.Exp)
                k.mm([(pb(1)[:, 64 + gg * 128:64 + (gg + 1) * 128], xbcT[:, 8 + gg, cs], xbcT[:, 10 + gg, cs], True, True) for gg in range(2)],
                     [f"{xbk}_8", f"{xbk}_9", f"{xbk}_10", f"{xbk}_11"], [pbk(1)])
                k.dve("tensor_tensor", [pbk(1), "cst"], [CBk], out=CBm[:], in0=pb(1)[:, 64:320].rearrange("p (a b) -> p a b", a=2),
                      in1=bc(mask01.unsqueeze(1), [128, 2, 128]), op=ALU.mult)
                k.dve("tensor_tensor", [Ek, CBk], [Ek], out=E[:].rearrange("p (g r) l -> p g r l", g=2),
                      in0=E[:].rearrange("p (g r) l -> p g r l", g=2), in1=bc(CBm[:].unsqueeze(2), [128, 2, 8, 128]), op=ALU.mult)

        bst = {}

        def B1(g):
            mt, st = g // 4, g % 4
            hT, hTk, xbcT, xbk, dt4, dt4k, dA4, dA4k, szm, szk = mt_state[mt]
            xdt, xdtk, xD, xDk, Btok, Btk, ex3, ex3k, E, Ek, xw, xwk = ch_state[g]
            cs = slice(st * 128, (st + 1) * 128)
            t = (mt % 4) * 4 + st
            lst = []
            for hf2 in range(2):
                lst.append((pb(4 + hf2), ident[:], xD[:, hf2 * 512:(hf2 + 1) * 512], True, False))
            for h in range(16):
                lst.append((pb(4 + h // 8)[:, (h % 8) * 64:(h % 8 + 1) * 64], E[:, h, :], xdt[:, h * 64:(h + 1) * 64], False, h % 8 == 7))
            k.mm(lst, ["ident", xDk, Ek, xdtk], [pbk(4), pbk(5)])
            k.mm([(pb(6 + gg), xbcT[:, 10 + gg, cs], Sbf[:, gg * 512:(gg + 1) * 512], True, True) for gg in range(2)],
                 [f"{xbk}_10", f"{xbk}_11", "Sbf"], [pbk(6), pbk(7)])
            k.dve("tensor_tensor", [pbk(6), pbk(7), ex3k], ["y1"], out=y1[:].rearrange("p (h q) -> p h q", h=16),
                  in0=PS[:, 6:8, :].rearrange("p a (h q) -> p (a h) q", q=64), in1=bc(ex3[:, 0:16].unsqueeze(2), [128, 16, 64]), op=ALU.mult)
            k.dve("tensor_tensor", [pbk(4), pbk(5), "y1"], ["y1"], out=y1[:].rearrange("p (a b) -> p a b", a=2),
                  in0=PS[:, 4:6, :], in1=y1[:].rearrange("p (a b) -> p a b", a=2), op=ALU.add)
            k.dve("tensor_tensor", ["y1", szk], ["y1"], out=y1[:], in0=y1[:], in1=szm[:, st, :], op=ALU.mult)
            k.dve("tensor_tensor", ["y1", G1[1]], ["yn"], out=yn[:], in0=y1[:], in1=G1[0], op=ALU.mult)
            ss, sskey = ss_rot.next()
            sq, sqkey = sq_rot.next()
            for gg in range(2):
                k.act(["y1"], [sqkey, sskey], out=sq[:, 0:512], in_=y1[:, gg * 512:(gg + 1) * 512], func=AF.Square, accum_out=ss[:, gg:gg + 1])
            rstd_from_ss(ss, sskey, 2, 1.0 / 512)
            bst[g] = (ss, sskey, t)

        def B2(g):
            k.tr([(pTy[:, kc * 128:(kc + 1) * 128], yn[:, kc * 128:(kc + 1) * 128]) for kc in range(8)], ident[:], ["yn", "ident"], [pbk(YTB)])
            pv = pTy.rearrange("p (a b) -> p a b", a=8)
            for gg in range(2):
                k.act([pbk(YTB)], [f"ynT{gg}"], out=ynT[:, 4 * gg:4 * gg + 4, :], in_=pv[:, 4 * gg:4 * gg + 4, :], func=AF.Copy)

        def B3(g):
            ss, sskey, t = bst[g]
            for gg in range(2):
                k.mm([(pb(4 + 2 * gg + hf2), ynT[:, 4 * gg + kc, :], Wot[:, 4 * gg + kc, hf2 * 512:(hf2 + 1) * 512], kc == 0, kc == 3)
                      for hf2 in range(2) for kc in range(4)], ["Wot", f"ynT{gg}"], [pbk(4 + 2 * gg), pbk(5 + 2 * gg)])
            xt, xtk = xb3_rot.next()
            k.dma("sp", xt[:], xo_t[t], [], [xtk])
            for gg in range(2):
                k.dve("scalar_tensor_tensor", [pbk(4 + 2 * gg), pbk(5 + 2 * gg), sskey, xtk], [xtk], out=xt[:].rearrange("p (a b) -> p a b", a=2),
                      in0=PS[:, 4 + 2 * gg:6 + 2 * gg, :], scalar=ss[:, gg:gg + 1], in1=xt[:].rearrange("p (a b) -> p a b", a=2),
                      op0=ALU.mult, op1=ALU.add)
            k.dma("sp", xr_t[t], xt[:], [xtk], [f"xr{t}"])

        def C(g):
            xdt, xdtk, xD, xDk, Btok, Btk, ex3, ex3k, E, Ek, xw, xwk = ch_state[g]
            k.mm([(pb(2 + gg), Btok[:, gg * 128:(gg + 1) * 128], xw[:, gg * 512:(gg + 1) * 512], True, True) for gg in range(2)],
                 [Btk, xwk], [pbk(2), pbk(3)])
            k.dve("tensor_tensor", ["Sst", ex3k], ["Sst"], out=Sst[:].rearrange("p (h q) -> p h q", h=16),
                  in0=Sst[:].rearrange("p (h q) -> p h q", h=16), in1=bc(ex3[:, 32:48].unsqueeze(2), [128, 16, 64]), op=ALU.mult)
            k.dve("tensor_tensor", ["Sst", pbk(2), pbk(3)], ["Sst"], out=Sst[:].rearrange("p (a b) -> p a b", a=2),
                  in0=Sst[:].rearrange("p (a b) -> p a b", a=2), in1=PS[:, 2:4, :], op=ALU.add)
            k.act(["Sst"], ["Sbf"], out=Sbf[:], in_=Sst[:], func=AF.Copy)

        NG = 32
        F1(0, 0)
        F1(0, 1)
        for q in range(4):
            F2(0, q)
        F1(1, 0)
        F1(1, 1)
        A(0)
        for g in range(NG):
            mt, st = g // 4, g % 4
            own = mt >= 4
            if own:
                B1(g)
            if g < NG - 1:
                C(g)
            if g == 15:
                k.dve("tensor_scalar", ["Sst", "flag"], ["Sst"], out=Sst[:], in0=Sst[:], scalar1=flag[:, 0:1], scalar2=None, op0=ALU.mult)
                k.act(["Sst"], ["Sbf"], out=Sbf[:], in_=Sst[:], func=AF.Copy)
            if mt + 1 < 8:
                F2(mt + 1, st)
            if own:
                B2(g)
            if g + 1 < NG:
                A(g + 1)
            if own:
                B3(g)
            if st in (0, 2) and mt + 2 < 8:
                F1(mt + 2, st // 2)
        k.pop()

    if do_conf:
        k.push()
        base_t = xr_t if do_ssd else xo_t
        load_gains([0])
        xt_rot = Rot(k, "xtb", int(os.environ.get("XTB", "3")), [128, D], F32)
        xb_rot = Rot(k, "xbb", int(os.environ.get("XBB", "3")), [128, D], F32)
        Wag = k.sb("Wag", [128, 8, 2048], BF16)
        Wob = k.sb("Wob", [128, 8, D], BF16)
        w_in_v2 = w_in.rearrange("(kc p) n -> p kc n", p=128)
        for cp in range(4):
            k.dma("pool", Wag[:, :, cp * 256:(cp + 1) * 256], w_in_v2[:, :, A0 + cp * 256:A0 + (cp + 1) * 256], [], [f"Wag_a{cp}"])
            k.dma("pool", Wag[:, :, D + cp * 256:D + (cp + 1) * 256], w_in_v2[:, :, G0 + cp * 256:G0 + (cp + 1) * 256], [], [f"Wag_g{cp}"])
        k.dma("pool", Wob[:], w_out[D:2 * D, :].rearrange("(kc p) n -> p kc n", p=128), [], ["Wob"])
        NDV = int(os.environ.get("NDV", "10"))
        NPE = 31 - NDV
        diag = k.sb("diag", [128, 8, NPE, 128], BF16)
        DGP = os.environ.get("DGP", "dve")
        for c in range(8):
            on_pool = (DGP == "pool") or (DGP == "alt" and c % 2 == 1)
            (k.pool if on_pool else k.dve)("tensor_tensor", ["ident", "fw"], [f"diag{c}"], out=diag[:, c, :, :], in0=bc(ident[:].unsqueeze(1), [128, NPE, 128]),
                  in1=bc(fw[:, c, NDV:31].unsqueeze(2), [128, NPE, 128]), op=ALU.mult)
        hT2_r = Rot(k, "hT2_", 2, [128, 8, MT], BF16)
        unT = k.sb("unT", [128, 8, MT], BF16)
        uT_r = Rot(k, "uT_", 2, [128, 8, 30 + MT], BF16)
        sgm_rot = Rot(k, "sgm", 2, [128, MT], F32)
        cv_r = Rot(k, "cv_", 2, [128, 8, MT], BF16)
        cvsq_rot = Rot(k, "cvsq", 2, [128, MT], BF16)
        mean_r = Rot(k, "mean_", 1, [128, MT], F32)
        rstdv_r = Rot(k, "rstdv_", 1, [128, MT], F32)
        t1_rot = Rot(k, "t1", 2, [128, MT], F32)
        accv_rot = Rot(k, "accv", 2, [128, MT], F32)
        gl_i = [0]

        def glu_chunks(hT2, hT2k, uT, uTk, ncols, col0):
            for c in range(8):
                ba = gl_i[0] % 2
                bg = 2 + gl_i[0] % 2
                gl_i[0] += 1
                k.mm([(pb(ba)[:, 0:ncols], Wag[:, kc, c * 128:(c + 1) * 128], hT2[:, kc, col0:col0 + ncols], kc == 0, kc == 7)
                      for kc in range(8)], [f"Wag_a{c // 2}", hT2k], [pbk(ba)])
                k.mm([(pb(bg)[:, 0:ncols], Wag[:, kc, D + c * 128:D + (c + 1) * 128], hT2[:, kc, col0:col0 + ncols], kc == 0, kc == 7)
                      for kc in range(8)], [f"Wag_g{c // 2}", hT2k], [pbk(bg)])
                sgm, sgk = sgm_rot.next()
                k.act([pbk(bg)], [sgk], out=sgm[:, 0:ncols], in_=pb(bg)[:, 0:ncols], func=AF.Sigmoid)
                k.dve("tensor_tensor", [pbk(ba), sgk], [f"{uTk}_{c}"], out=uT[:, c, 30 + col0:30 + col0 + ncols], in0=pb(ba)[:, 0:ncols],
                      in1=sgm[:, 0:ncols], op=ALU.mult)

        uTp, uTpk = uT_r.next()
        k.dve("memset", [], [f"{uTpk}_{c}" for c in range(8)] + [f"{uTpk}_h"], ap=uTp[:], constant=0.0)
        hTp, hTpk = hT2_r.next()
        xt, xtk = xt_rot.next()
        k.dma("sp", xt[:], xp_t[NT - 1], [], [xtk])
        hb, hk = h_rot.next()
        norm_tile(xt[:], xtk, 0, hb[:], hk)
        transpose_to(hb, hk, hTp, hTpk, 384, 7)
        glu_chunks(hTp, hTpk, uTp, uTpk, 128, 384)
        for m in range(NT // 4):
            hT2, hT2k = hT2_r.next()
            uT, uTk = uT_r.next()
            cv, cvk = cv_r.next()
            mean, meank = mean_r.next()
            rstdv, rstdk = rstdv_r.next()
            k.dve("tensor_copy", [f"{uTpk}_{c}" for c in range(8)], [f"{uTk}_h"], out=uT[:, :, 0:30], in_=uTp[:, :, MT:MT + 30])
            for st in range(4):
                t = m * 4 + st
                xt, xtk = xt_rot.next()
                k.dma("sp", xt[:], xo_t[t], [], [xtk])
                hb, hk = h_rot.next()
                norm_tile(xt[:], xtk, 0, hb[:], hk)
                transpose_to(hb, hk, hT2, hT2k, st * 128, 7)
            glu_chunks(hT2, hT2k, uT, uTk, MT, 0)
            for c in range(8):
                bcv = 4 + c % 2
                k.mm([(pb(bcv), diag[:, c, kk - NDV, :], uT[:, c, kk:kk + MT], kk == NDV, kk == 30) for kk in range(NDV, 31)],
                     [f"diag{c}", f"{uTk}_{c}", f"{uTk}_h"], [pbk(bcv)])
                accv, acck = accv_rot.next()
                k.dve("tensor_scalar", [f"{uTk}_{c}", f"{uTk}_h", "fw", "fv"], [acck], out=accv[:], in0=uT[:, c, 0:MT], scalar1=fw[:, c, 0:1],
                      scalar2=fv[:, c, 0:1], op0=ALU.mult, op1=ALU.add)
                for kk in range(1, NDV):
                    k.dve("scalar_tensor_tensor", [f"{uTk}_{c}", f"{uTk}_h", "fw", acck], [acck], out=accv[:], in0=uT[:, c, kk:kk + MT],
                          scalar=fw[:, c, kk:kk + 1], in1=accv[:], op0=ALU.mult, op1=ALU.add)
                k.dve("tensor_tensor", [pbk(bcv), acck], [f"{cvk}_{c}"], out=cv[:, c, :], in0=pb(bcv), in1=accv[:], op=ALU.add)
                cvsq, cqk = cvsq_rot.next()
                k.act([f"{cvk}_{c}"], [cqk], out=cvsq[:], in_=cv[:, c, :], func=AF.Square)
                k.mm([(pb(6), ones_bf[:], cv[:, c, :], c == 0, c == 7)], ["ones_bf", f"{cvk}_{c}"], [pbk(6)])
                k.mm([(pb(7), ones_bf[:], cvsq[:], c == 0, c == 7)], ["ones_bf", cqk], [pbk(7)])
            k.dve("tensor_scalar", [pbk(6)], [meank], out=mean[:], in0=pb(6), scalar1=1.0 / D, scalar2=None, op0=ALU.mult)
            k.dve("tensor_tensor", [meank], [rstdk], out=rstdv[:], in0=mean[:], in1=mean[:], op=ALU.mult)
            k.dve("scalar_tensor_tensor", [pbk(7), rstdk], [rstdk], out=rstdv[:], in0=pb(7), scalar=1.0 / D, in1=rstdv[:],
                  op0=ALU.mult, op1=ALU.subtract)
            k.act([rstdk, "epsT"], [rstdk], out=rstdv[:], in_=rstdv[:], func=AF.Ln, bias=epsT[:], scale=1.0)
            k.act([rstdk], [rstdk], out=rstdv[:], in_=rstdv[:], func=AF.Exp, scale=-0.5)
            for c in range(8):
                t1, t1k = t1_rot.next()
                lnp = os.environ.get("LNP", "0")
                e1 = k.pool if (lnp == "all" or (lnp == "half" and c % 2 == 1)) else k.dve
                e1("tensor_tensor", [f"{cvk}_{c}", meank], [t1k], out=t1[:], in0=cv[:, c, :], in1=mean[:], op=ALU.subtract)
                e1("tensor_tensor", [t1k, rstdk], [t1k], out=t1[:], in0=t1[:], in1=rstdv[:], op=ALU.mult)
                k.act([t1k, "fv"], [f"unT{c}"], out=unT[:, c, :], in_=t1[:], func=AF.Silu, scale=fv[:, c, 1:2], bias=fv[:, c, 2:3])
            for st in range(4):
                t = m * 4 + st
                ob = 2 * (st % 2)
                OPS = int(os.environ.get("OPS", "4"))
                per = 8 // OPS
                for part in range(OPS):
                    kcs = range(part * per, (part + 1) * per)
                    k.mm([(pb(ob + hf2), unT[:, kc, st * 128:(st + 1) * 128], Wob[:, kc, hf2 * 512:(hf2 + 1) * 512], kc == 0, kc == 7)
                          for hf2 in range(2) for kc in kcs], ["Wob"] + [f"unT{c}" for c in kcs], [pbk(ob), pbk(ob + 1)])
                xt, xtk = xb_rot.next()
                k.dma("sp", xt[:], base_t[t], [f"xr{t}"], [xtk])
                k.dve("tensor_tensor", [pbk(ob), pbk(ob + 1), xtk], [xtk], out=xt[:].rearrange("p (a b) -> p a b", a=2),
                      in0=xt[:].rearrange("p (a b) -> p a b", a=2), in1=PS[:, ob:ob + 2, :], op=ALU.add)
                k.dma("sp", xr_t[t], xt[:], [xtk], [f"xr{t}"])
            uTp, uTpk = uT, uTk
        k.pop()
    elif dbg == "ssd":
        pass

    k.push()
    x_res = k.sb("x_res", [128, NT, D], F32)
    KT = k.sb("KT", [128, 8, MEM], BF16)
    V = k.sb("V", [128, 2, D], BF16)
    with_kv = True
    if with_kv:
        k.push()
        load_gains([1])
        load_gains([2])
        WA3 = k.sb("WA3", [128, 16 * D], BF16)
        wq = WA3[:, 0:8 * D].rearrange("p (a b) -> p a b", a=8)
        wo = WA3[:, 8 * D:16 * D].rearrange("p (a b) -> p a b", a=8)
        for t in range(4):
            k.dma("sp", x_res[:, t, :], src_t[t], [f"xr{t}"], [f"xres{t}"])
        for qq in range(2):
            k.dma("pool", wq[:, :, qq * 512:(qq + 1) * 512], w_q.rearrange("(kc p) n -> p kc n", p=128)[:, :, qq * 512:(qq + 1) * 512], [], [f"WAq{qq}"])
        WA = k.sb("WA", [128, 8 * 2048], BF16)
        wkv = WA[:, 0:8 * 2048].rearrange("p (a b) -> p a b", a=8)
        w_kv_v = w_kv.rearrange("(kc p) n -> p kc n", p=128)
        for qq in range(4):
            k.dma("pool", wkv[:, :, qq * 512:(qq + 1) * 512], w_kv_v[:, :, qq * 512:(qq + 1) * 512], ["WAq1"], [f"WAkv{qq}"])
        for t in range(4, NT):
            k.dma("sp", x_res[:, t, :], src_t[t], [f"xr{t}", "WAkv1" if t < 8 else "WAkv3"], [f"xres{t}"])
        memT = k.sb("memT", [128, 8, MEM], BF16)
        mrot = Rot(k, "memx", 1, [128, D], F32)
        for mc in range(2):
            mx, mxk = mrot.next()
            k.dma("sp", mx[:], mem_d[mc * 128:(mc + 1) * 128, :], [], [mxk])
            hb, hk = h_rot.next()
            norm_tile(mx[:], mxk, 2, hb[:], hk)
            transpose_to(hb, hk, memT, "memT", mc * 128, 0)
        for c in range(8):
            bank = 1 + (c % 2)
            k.mm([(pb(bank)[:, 0:MEM], wkv[:, kc, c * 128:(c + 1) * 128], memT[:, kc, :], kc == 0, kc == 7)
                  for kc in range(8)], [f"WAkv{c // 4}", "memT"], [pbk(bank)])
            k.act([pbk(bank)], ["KT"], out=KT[:, c, :], in_=pb(bank)[:, 0:MEM], func=AF.Copy)
        for mc in range(2):
            for hf2 in range(2):
                bank = 3 + hf2
                k.mm([(pb(bank), memT[:, kc, mc * 128:(mc + 1) * 128],
                       wkv[:, kc, D + hf2 * 512:D + (hf2 + 1) * 512], kc == 0, kc == 7) for kc in range(8)],
                     [f"WAkv{2 + hf2}", "memT"], [pbk(bank)])
                k.dve("tensor_copy", [pbk(bank)], ["V"], out=V[:, mc, hf2 * 512:(hf2 + 1) * 512], in_=pb(bank))

    for qq in range(2):
        k.dma("pool", wo[:, :, qq * 512:(qq + 1) * 512], w_o.rearrange("(kc p) n -> p kc n", p=128)[:, :, qq * 512:(qq + 1) * 512], [f"WAkv{3}"], [f"WAo{qq}"])
    hxT_rot = Rot(k, "hxT", 2, [128, 8, MT], BF16)
    qT = k.sb("qT", [128, 8, MT], BF16)
    ET_rot = Rot(k, "ET", 2, [128, 2, MT], BF16)
    rden_rot = Rot(k, "rden", 2, [128, MT], F32)
    oT = k.sb("oT", [128, 8, MT], BF16)
    for m in range(NT // 4):
        hxT, hxk = hxT_rot.next()
        for st in range(4):
            t = m * 4 + st
            hb, hk = h_rot.next()
            norm_tile(x_res[:, t, :], f"xres{t}", 1, hb[:], hk)
            transpose_to(hb, hk, hxT, hxk, st * 128, 0)
        for c in range(8):
            bank = 1 + (c % 2)
            k.mm([(pb(bank), wq[:, kc, c * 128:(c + 1) * 128], hxT[:, kc, :], kc == 0, kc == 7) for kc in range(8)],
                 [f"WAq{c // 4}", hxk], [pbk(bank)])
            k.act([pbk(bank)], [f"qT{c}"], out=qT[:, c, :], in_=pb(bank), func=AF.Copy)
        for hd in range(4):
            ET, etk = ET_rot.next()
            for mc in range(2):
                bank = 3 + mc
                k.mm([(pb(bank), KT[:, 2 * hd + dc, mc * 128:(mc + 1) * 128], qT[:, 2 * hd + dc, :], dc == 0, dc == 1)
                      for dc in range(2)], ["KT", f"qT{2 * hd}", f"qT{2 * hd + 1}"], [pbk(bank)])
                k.act([pbk(bank)], [etk], out=ET[:, mc, :], in_=pb(bank), func=AF.Exp, scale=1.0 / 16.0)
            k.mm([(pb(5), ones_bf[:], ET[:, mc, :], mc == 0, mc == 1) for mc in range(2)], ["ones_bf", etk], [pbk(5)])
            rden, rdk = rden_rot.next()
            k.act([pbk(5)], [rdk], out=rden[:], in_=pb(5), func=AF.Ln)
            k.act([rdk], [rdk], out=rden[:], in_=rden[:], func=AF.Exp, scale=-1.0)
            for dc in range(2):
                bank = 6 + dc
                k.mm([(pb(bank), V[:, mc, hd * 256 + dc * 128:hd * 256 + (dc + 1) * 128], ET[:, mc, :], mc == 0, mc == 1)
                      for mc in range(2)], ["V", etk], [pbk(bank)])
                k.dve("tensor_tensor", [pbk(bank), rdk], [f"oT{2 * hd + dc}"], out=oT[:, 2 * hd + dc, :], in0=pb(bank),
                      in1=rden[:], op=ALU.mult)
        for st in range(4):
            t = m * 4 + st
            for hf2 in range(2):
                bank = 1 + hf2
                OP3 = int(os.environ.get("OP3", "1"))
                per3 = 8 // OP3
                for part in range(OP3):
                    kcs = range(part * per3, (part + 1) * per3)
                    k.mm([(pb(bank), oT[:, kc, st * 128:(st + 1) * 128], wo[:, kc, hf2 * 512:(hf2 + 1) * 512], kc == 0, kc == 7)
                          for kc in kcs], [f"WAo{hf2}"] + [f"oT{c}" for c in kcs], [pbk(bank)])
                k.dve("tensor_tensor", [pbk(bank), f"xres{t}"], [f"xres{t}"], out=x_res[:, t, hf2 * 512:(hf2 + 1) * 512],
                      in0=x_res[:, t, hf2 * 512:(hf2 + 1) * 512], in1=pb(bank), op=ALU.add)

    k.pop()
    k.push()
    load_gains([3])
    hfT = k.sb("hfT", [128, 8, HALF], BF16)
    groups = [(0, 4), (4, 4), (8, 4), (12, 4), (16, 3), (19, 3)]
    wg_rot = Rot(k, "wg", 2, [128, 8, 4 * 128], BF16)
    wu_rot = Rot(k, "wu", 2, [128, 8, 4 * 128], BF16)
    wd_rot = Rot(k, "wd", 2, [128, 4, D], BF16)
    sg_rot = Rot(k, "sg", 2, [128, MT], BF16)
    aT_rot = Rot(k, "aT", 2, [128, 4, MT], BF16)
    for gi, (c0, nch) in enumerate(groups):
        wg, wgk = wg_rot.next()
        wu, wuk = wu_rot.next()
        wd, wdk = wd_rot.next()
        k.dma("pool", wg[:, :, 0:nch * 128], w_gate.rearrange("(kc p) n -> p kc n", p=128)[:, :, c0 * 128:(c0 + nch) * 128], [], [wgk])
        k.dma("pool", wu[:, :, 0:nch * 128], w_up.rearrange("(kc p) n -> p kc n", p=128)[:, :, c0 * 128:(c0 + nch) * 128], [], [wuk])
        k.dma("pool", wd[:, 0:nch, :], w_down[c0 * 128:(c0 + nch) * 128, :].rearrange("(c p) n -> p c n", p=128), [], [wdk])
        for m in range(NT // 4):
            if gi == 0:
                for st in range(4):
                    t = m * 4 + st
                    hb, hk = h_rot.next()
                    norm_tile(x_res[:, t, :], f"xres{t}", 3, hb[:], hk)
                    transpose_to(hb, hk, hfT, f"hfT{m}", t * 128, 0)
            aT, aTk = aT_rot.next()
            for ci in range(nch):
                bg = 1 + (ci % 2)
                bu = 3 + (ci % 2)
                k.mm([(pb(bg), wg[:, kc, ci * 128:(ci + 1) * 128], hfT[:, kc, m * MT:(m + 1) * MT], kc == 0, kc == 7)
                      for kc in range(8)], [wgk, f"hfT{m}"], [pbk(bg)])
                k.mm([(pb(bu), wu[:, kc, ci * 128:(ci + 1) * 128], hfT[:, kc, m * MT:(m + 1) * MT], kc == 0, kc == 7)
                      for kc in range(8)], [wuk, f"hfT{m}"], [pbk(bu)])
                sg, sgk = sg_rot.next()
                k.act([pbk(bg)], [sgk], out=sg[:], in_=pb(bg), func=AF.Silu)
                k.dve("tensor_tensor", [pbk(bu), sgk], [aTk], out=aT[:, ci, :], in0=pb(bu), in1=sg[:], op=ALU.mult)
            for st in range(4):
                t = m * 4 + st
                for hf2 in range(2):
                    bank = 5 + hf2
                    k.mm([(pb(bank), aT[:, ci, st * 128:(st + 1) * 128], wd[:, ci, hf2 * 512:(hf2 + 1) * 512], ci == 0, ci == nch - 1)
                          for ci in range(nch)], [wdk, aTk], [pbk(bank)])
                    k.dve("tensor_tensor", [pbk(bank), f"xres{t}"], [f"xres{t}"],
                          out=x_res[:, t, hf2 * 512:(hf2 + 1) * 512], in0=x_res[:, t, hf2 * 512:(hf2 + 1) * 512],
                          in1=pb(bank), op=ALU.add)

    load_gains([4])
    o_rot = Rot(k, "ob", 2, [128, D], F32)
    for t in range(NT):
        ob, obk = o_rot.next()
        ss, sskey = ss_rot.next()
        sq, sqkey = sq_rot.next()
        k.act([f"xres{t}"], [sqkey, sskey], out=sq[:], in_=x_res[:, t, :], func=AF.Square, accum_out=ss[:, 0:1])
        rstd_from_ss(ss, sskey, 1, 1.0 / D)
        k.dve("scalar_tensor_tensor", [f"xres{t}", sskey, gains[4][1]], [obk], out=ob[:], in0=x_res[:, t, :], scalar=ss[:, 0:1],
              in1=gains[4][0], op0=ALU.mult, op1=ALU.mult)
        k.dma("sp", out_t[t], ob[:], [obk], [])
    S.emit()
    k.pop()
    k.pop()
    k.st.close()
    return nc


def make_inputs(inp, dbg=None):
    f = np.float32
    x = np.asarray(inp["x"], f)
    mem = np.asarray(inp["mem"], f)

    def row_bc(v):
        return np.broadcast_to(np.asarray(v, f).reshape(1, -1), (128, np.asarray(v).size))

    gbc = np.stack([row_bc(inp["norm_mix_g"][0]), row_bc(inp["norm_xattn_g"][0]), row_bc(inp["norm_mem_g"][0]),
                    row_bc(inp["norm_ffn_g"][0]), row_bc(inp["norm_final_g"]), row_bc(inp["ssd_norm_g"][0])], axis=1)
    dbc = row_bc(np.repeat(np.asarray(inp["ssd_D"][0], f), 64))
    small = np.zeros((128, 64), f)
    small[:, 0:16] = row_bc(inp["ssd_dt_bias"][0])
    small[:, 16:32] = row_bc(inp["ssd_A_log"][0])
    cw = np.asarray(inp["ssd_conv_w"][0], f).reshape(4, 12, 128).transpose(2, 1, 0)
    cb = np.asarray(inp["ssd_conv_b"][0], f).reshape(12, 128).T
    fw = np.asarray(inp["cf_conv_w"][0], f).reshape(31, 8, 128).transpose(2, 1, 0)
    fv = np.stack([np.asarray(inp[n][0], f).reshape(8, 128).T for n in ("cf_conv_b", "cf_ln_g", "cf_ln_b")], axis=2)
    ident = np.eye(128, dtype=f).astype(ml_dtypes.bfloat16)
    j = np.arange(128)
    tri = (j[:, None] <= j[None, :]).astype(f)
    su = (j[:, None] > j[None, :]).astype(f)
    cst = np.stack([tri, su, np.ones((128, 128), f), tri], axis=1)
    common = {
        "w_in": np.ascontiguousarray(inp["w_in"][0], f), "w_out": np.ascontiguousarray(inp["w_out"][0], f),
        "w_q": np.ascontiguousarray(inp["w_q"][0], f), "w_kv": np.ascontiguousarray(inp["w_kv"][0], f),
        "w_o": np.ascontiguousarray(inp["w_o"][0], f), "w_gate": np.ascontiguousarray(inp["w_gate"][0], f),
        "w_up": np.ascontiguousarray(inp["w_up"][0], f), "w_down": np.ascontiguousarray(inp["w_down"][0], f),
        "gbc": np.ascontiguousarray(gbc), "dbc": np.ascontiguousarray(dbc), "small": small,
        "cw": np.ascontiguousarray(cw), "cb": np.ascontiguousarray(cb), "fw": np.ascontiguousarray(fw),
        "fv": np.ascontiguousarray(fv), "ident": ident, "cst": np.ascontiguousarray(cst),
    }
    maps = []
    for c in range(8):
        b, hf = c // 2, c % 2
        d = dict(common)
        d["x_own"] = np.ascontiguousarray(x[b, hf * HALF:(hf + 1) * HALF])
        d["x_prev"] = np.ascontiguousarray(x[b, 0:HALF]) if hf == 1 else np.zeros((HALF, D), f)
        d["flag"] = np.full((128, 1), float(hf), f)
        d["mem"] = np.ascontiguousarray(mem[b])
        maps.append(d)
    return maps


_NC_CACHE = {}


def kernel(_dbg=None, **inputs):
    if _dbg not in _NC_CACHE:
        _NC_CACHE[_dbg] = build(_dbg)
    nc = _NC_CACHE[_dbg]
    maps = make_inputs(inputs, _dbg)
    res = run_bass_kernel_spmd(nc, maps, core_ids=list(range(8)))
    out = np.zeros((NB, SEQ, D), np.float32)
    for c in range(8):
        b, hf = c // 2, c % 2
        out[b, hf * HALF:(hf + 1) * HALF] = res.results[c]["out"]
    return out
```

```python
import contextlib
import os
import numpy as np
import ml_dtypes
import concourse.bass as bass
import concourse.mybir as mybir
from concourse.bass_utils import run_bass_kernel_spmd

F32 = mybir.dt.float32
BF16 = mybir.dt.bfloat16
ALU = mybir.AluOpType
AF = mybir.ActivationFunctionType

ENGS = ["pe", "dve", "act", "pool", "sp"]
SEG_CFG = {0: dict(lat=0.4, tsw=1.2), 1: dict(lat=0.2, tsw=1.0, seed=979871, jit=0.0),
           2: dict(lat=0.7, tsw=1.0, seed=396330, jit=0.05), 3: dict(lat=0.7, tsw=0.8, seed=362957, jit=0.0)}

D = 1024
SEQ = 4096
NB = 4
HALF = 2048
NT = 16
MT = 512
MEM = 256
INW = 4624
DFF = 2816
NFF = 22
EPS = 1e-6
XBC0 = 1024
DT0 = 2560
A0 = 2576
G0 = 3600


class Sched:
    LAT = 0.5
    DELTA_DEFAULT = 0.5

    def __init__(self, nc, n_dma_sems=4):
        self.nc = nc
        self.ops = []
        self.seg = 0
        self.n_dma = n_dma_sems
        self._stream = {}
        self._seg_counter = 0

    def barrier(self):
        self.seg += 1

    def add(self, eng, fn, reads=(), writes=(), dma=False, cost=1.0, tset=None):
        self.ops.append(dict(eng=eng, fn=fn, reads=tuple(reads), writes=tuple(writes), dma=int(dma), cost=float(cost),
                             seg=self.seg, idx=len(self.ops), tset=tset))

    def _schedule_segment(self, ops):
        n = len(ops)
        cfg = dict(lat=self.LAT, tsw=1.4, seed=0, jit=0.0)
        cfg.update(SEG_CFG.get(self._seg_counter, {}))
        env = os.environ.get("SCHED_CFG")
        if env:
            import json as _json
            cfg.update(_json.loads(env))
        self._seg_counter += 1
        LATV = cfg["lat"]
        rng = np.random.default_rng(cfg["seed"]) if cfg["jit"] > 0 else None
        DMAF = float(os.environ.get("DMAF", "0.75"))
        PEW = float(os.environ.get("PEW", "1.0"))
        preds = [set() for _ in range(n)]
        raw = [set() for _ in range(n)]
        last_writer, readers = {}, {}
        for i, op in enumerate(ops):
            for r in op["reads"]:
                lw = last_writer.get(r)
                if lw is not None:
                    preds[i].add(lw)
                    raw[i].add(lw)
            for w in op["writes"]:
                lw = last_writer.get(w)
                if lw is not None:
                    preds[i].add(lw)
                for rd in readers.get(w, ()):
                    if rd != i:
                        preds[i].add(rd)
            for r in op["reads"]:
                readers.setdefault(r, []).append(i)
            for w in op["writes"]:
                last_writer[w] = i
                readers[w] = []
        succs = [[] for _ in range(n)]
        for i in range(n):
            for p in preds[i]:
                succs[p].append(i)

        def dur(op):
            return max(0.06, op["cost"] * DMAF) if op["dma"] else op["cost"]

        def done_lat(op):
            return (2.0 + op["cost"]) if op["dma"] else op["cost"]

        prio = [0.0] * n
        for i in range(n - 1, -1, -1):
            m = 0.0
            for sidx in succs[i]:
                if prio[sidx] > m:
                    m = prio[sidx]
            prio[i] = m + done_lat(ops[i]) * (PEW if ops[i]["eng"] == "pe" else 1.0)
        if rng is not None:
            jit = rng.random(n)
            prio = [p * (1.0 + cfg["jit"] * (j - 0.5)) for p, j in zip(prio, jit)]
        npred = [len(preds[i]) for i in range(n)]
        ready_t = [0.0] * n
        finish = [0.0] * n
        free = {e: 0.0 for e in ENGS}
        ready = {e: [] for e in ENGS}
        for i in range(n):
            if npred[i] == 0:
                ready[ops[i]["eng"]].append(i)
        order = {e: [] for e in ENGS}
        left = n
        cur_set = [None]
        TSW = cfg["tsw"]
        DELTA = float(os.environ.get('SCHED_DELTA', self.DELTA_DEFAULT))
        while left:
            best = None
            for e in ENGS:
                if not ready[e]:
                    continue
                fe = free[e]
                cand = None
                stts = {}
                mn = None
                for i in ready[e]:
                    stt = ready_t[i] if ready_t[i] > fe else fe
                    if e == "act" and ops[i]["tset"] is not None and cur_set[0] is not None and ops[i]["tset"] != cur_set[0]:
                        stt = stt + TSW
                    stts[i] = stt
                    if mn is None or stt < mn:
                        mn = stt
                for i in ready[e]:
                    stt = stts[i]
                    if stt > mn + DELTA:
                        continue
                    key = (-prio[i], stt, i)
                    if cand is None or key < cand[0]:
                        cand = (key, i, stt)
                if best is None or cand[2] < best[2] - 1e-9 or (abs(cand[2] - best[2]) <= 1e-9 and cand[0] < best[0]):
                    best = cand
            _, i, stt = best
            op = ops[i]
            e = op["eng"]
            ready[e].remove(i)
            if e == "act" and op["tset"] is not None:
                cur_set[0] = op["tset"]
            free[e] = stt + dur(op)
            finish[i] = stt + done_lat(op)
            order[e].append(i)
            left -= 1
            for sidx in succs[i]:
                rt = finish[i] + (LATV if ops[sidx]["eng"] != e or op["dma"] else 0.0)
                if rt > ready_t[sidx]:
                    ready_t[sidx] = rt
                npred[sidx] -= 1
                if npred[sidx] == 0:
                    ready[ops[sidx]["eng"]].append(sidx)
        return order, preds, raw

    def emit(self, final_eng="sp"):
        nc = self.nc
        nseg = self.seg + 1
        segs = [[] for _ in range(nseg)]
        for op in self.ops:
            segs[op["seg"]].append(op)
        gorder = {e: [] for e in ENGS}
        dma_val = {}
        dma_rr = {e: 0 for e in ENGS}
        known_pos = {e: {e2: -1 for e2 in ENGS} for e in ENGS}
        known_dma = {e: {} for e in ENGS}
        pending_pos = {e: {} for e in ENGS}
        pending_dma = {e: {} for e in ENGS}
        for sops in segs:
            if not sops:
                continue
            order, preds, raw = self._schedule_segment(sops)
            rec = {}
            for e in ENGS:
                for i in order[e]:
                    op = sops[i]
                    r = dict(op=op, eng=e, pos=None, signal=False, waits_pos={}, waits_dma={}, dma_tok=None)
                    if op["dma"]:
                        j = dma_rr[e]
                        dma_rr[e] = (j + 1) % (self.n_dma if e == "sp" else 3)
                        key = ("dma", e, j)
                        prev = dma_val.get(key, 0)
                        val = prev + 16 * op["dma"]
                        dma_val[key] = val
                        r["dma_tok"] = (key, val)
                        r["dma_prev"] = (key, prev) if prev > 0 else None
                    else:
                        r["pos"] = len(gorder[e])
                        gorder[e].append(r)
                    rec[i] = r
            for e in ENGS:
                for i in order[e]:
                    r = rec[i]
                    op = sops[i]
                    wp, wd = {}, {}
                    for e2, p in pending_pos[e].items():
                        if known_pos[e][e2] < p:
                            wp[e2] = p
                    for kk, vv in pending_dma[e].items():
                        if known_dma[e].get(kk, 0) < vv:
                            wd[kk] = vv
                    pending_pos[e] = {}
                    pending_dma[e] = {}
                    for p in preds[i]:
                        pr = rec[p]
                        if pr["dma_tok"] is not None:
                            kk, vv = pr["dma_tok"]
                            if known_dma[e].get(kk, 0) < vv and wd.get(kk, 0) < vv:
                                wd[kk] = vv
                        elif pr["eng"] != e or e != "pe":
                            e2 = pr["eng"]
                            if known_pos[e][e2] < pr["pos"] and wp.get(e2, -1) < pr["pos"]:
                                wp[e2] = pr["pos"]
                    if op["dma"] and r["dma_prev"] is not None:
                        kk, vv = r["dma_prev"]
                        if known_dma[e].get(kk, 0) < vv and wd.get(kk, 0) < vv:
                            wd[kk] = vv
                    for e2, p in wp.items():
                        known_pos[e][e2] = p
                        gorder[e2][p]["signal"] = True
                    for kk, vv in wd.items():
                        known_dma[e][kk] = vv
                    r["waits_pos"] = wp
                    r["waits_dma"] = wd
                    r["stream_eng"] = e
            for e in ENGS:
                for e2 in ENGS:
                    if e2 != e and gorder[e2]:
                        pending_pos[e][e2] = len(gorder[e2]) - 1
                        gorder[e2][-1]["signal"] = True
                for kk, vv in dma_val.items():
                    pending_dma[e][kk] = vv
            self._last_rec = rec
            for e in ENGS:
                for i in order[e]:
                    self._stream.setdefault(e, []).append(rec[i])
        for e in ENGS:
            if gorder[e]:
                gorder[e][-1]["signal"] = True
        count = {e: 0 for e in ENGS}
        for e in ENGS:
            for r in gorder[e]:
                if r["signal"]:
                    count[e] += 1
                    r["val"] = count[e]
        keys = [("eng", e) for e in ENGS] + sorted(dma_val.keys())
        with contextlib.ExitStack() as st:
            sems = {}
            for kk in keys:
                sems[kk] = st.enter_context(nc.semaphore("s_" + "_".join(str(x) for x in kk)))
            fin_waits = []
            for e in ENGS:
                if gorder[e] and e != final_eng:
                    fin_waits.append((("eng", e), count[e]))
            for kk, vv in dma_val.items():
                fin_waits.append((kk, vv))
            block = st.enter_context(nc.Block())
            streams = self._stream

            def run(ename):
                def body(eng):
                    for r in streams.get(ename, []):
                        for e2, p in r["waits_pos"].items():
                            eng.wait_ge(sems[("eng", e2)], gorder[e2][p]["val"])
                        for kk, vv in r["waits_dma"].items():
                            eng.wait_ge(sems[kk], vv)
                        res = r["op"]["fn"](eng)
                        if r["op"]["dma"]:
                            if not isinstance(res, (list, tuple)):
                                res = [res]
                            assert len(res) == int(r["op"]["dma"])
                            for ins in res:
                                ins.then_inc(sems[r["dma_tok"][0]], 16)
                        elif r["signal"]:
                            if isinstance(res, (list, tuple)):
                                res = res[-1]
                            res.then_inc(sems[("eng", ename)], 1)
                    if ename == final_eng:
                        for kk, vv in fin_waits:
                            eng.wait_ge(sems[kk], vv)
                return body

            block.tensor(run("pe"))
            block.vector(run("dve"))
            block.scalar(run("act"))
            block.gpsimd(run("pool"))
            block.sync(run("sp"))


class Rot:
    def __init__(self, K, name, n, shape, dt):
        self.bufs = [K.sb(f"{name}{i}", shape, dt) for i in range(n)]
        self.keys = [f"{name}{i}" for i in range(n)]
        self.i = 0

    def next(self):
        j = self.i
        self.i = (j + 1) % len(self.bufs)
        return self.bufs[j], self.keys[j]


class K:
    def __init__(self, dbg=None):
        self.dbg = dbg
        self.nc = bass.Bass("TRN2", target_bir_lowering=False)
        self.st = contextlib.ExitStack()
        self.stacks = [self.st]
        self.S = Sched(self.nc)
        self.dram = {}

    def din(self, name, shape, dt=F32):
        ap = self.nc.dram_tensor(name, list(shape), dt, kind="ExternalInput").ap()
        self.dram[name] = ap
        return ap

    def sb(self, name, shape, dt):
        return self.stacks[-1].enter_context(self.nc.sbuf_tensor("sb_" + name, list(shape), dt))

    def push(self):
        self.stacks.append(contextlib.ExitStack())

    def pop(self):
        self.S.barrier()
        self.stacks.pop().close()

    @staticmethod
    def _fs(ap):
        try:
            return float(ap.free_size())
        except Exception:
            return 512.0

    TSETS = {AF.Silu: "silu", AF.Exp: "lnexp", AF.Ln: "lnexp", AF.Sigmoid: "sigm"}

    def act(self, r, w, **kw):
        c = 0.25 + self._fs(kw["out"]) / 1200.0
        self.S.add("act", lambda e: e.activation(**kw), r, w, cost=c, tset=self.TSETS.get(kw["func"]))

    def dve(self, opname, r, w, **kw):
        o = kw.get("out", kw.get("ap"))
        c = 0.08 + self._fs(o) / 960.0
        self.S.add("dve", lambda e: getattr(e, opname)(**kw), r, w, cost=c)

    def pool(self, opname, r, w, **kw):
        o = kw.get("out", kw.get("ap"))
        c = 0.15 + self._fs(o) / 500.0
        self.S.add("pool", lambda e: getattr(e, opname)(**kw), r, w, cost=c)

    def mm(self, lst, r, w):
        def f(e):
            ins = None
            for (o, l, rh, st, sp) in lst:
                ins = e.matmul(out=o, lhsT=l, rhs=rh, start=st, stop=sp)
            return ins
        c = 0.1
        for (o, l, rh, st, sp) in lst:
            n = self._fs(o)
            c += max(0.065, n / 2400.0) * (4.0 if l.dtype == F32 else 1.0)
        self.S.add("pe", f, r, w, cost=c)

    def tr(self, lst, ident, r, w):
        def f(e):
            ins = None
            for (o, i) in lst:
                ins = e.transpose(out=o, in_=i, identity=ident)
            return ins
        self.S.add("pe", f, r, w, cost=0.1 + 0.11 * len(lst))

    def dma(self, eng, out, in_, r, w):
        try:
            nbytes = float(out.nbytes())
        except Exception:
            nbytes = 5e5
        self.S.add(eng, lambda e: e.dma_start(out=out, in_=in_), r, w, dma=1, cost=nbytes / 1.5e5)


def bc(ap, shape):
    return ap.to_broadcast(list(shape))


import os
LVL = int(os.environ.get("SSD_LVL", "9"))


def build(dbg=None):
    k = K(dbg)
    nc = k.nc
    S = k.S
    x_own = k.din("x_own", [HALF, D])
    x_prev = k.din("x_prev", [HALF, D])
    flag_d = k.din("flag", [128, 1])
    mem_d = k.din("mem", [MEM, D])
    w_in = k.din("w_in", [D, INW])
    w_out = k.din("w_out", [2 * D, D])
    w_q = k.din("w_q", [D, D])
    w_kv = k.din("w_kv", [D, 2 * D])
    w_o = k.din("w_o", [D, D])
    w_gate = k.din("w_gate", [D, DFF])
    w_up = k.din("w_up", [D, DFF])
    w_down = k.din("w_down", [DFF, D])
    gbc_d = k.din("gbc", [128, 6, D])
    dbc_d = k.din("dbc", [128, D])
    sm_d = k.din("small", [128, 64])
    cw_d = k.din("cw", [128, 12, 4])
    cb_d = k.din("cb", [128, 12])
    fw_d = k.din("fw", [128, 8, 31])
    fv_d = k.din("fv", [128, 8, 3])
    ident_d = k.din("ident", [128, 128], BF16)
    cst_d = k.din("cst", [128, 4, 128])
    out_d = nc.dram_tensor("out", [HALF, D], F32, kind="ExternalOutput").ap()
    xr_d = nc.dram_tensor("xr", [HALF, D], F32, kind="Internal").ap()

    xo_t = x_own.rearrange("(t p) d -> t p d", p=128)
    xp_t = x_prev.rearrange("(t p) d -> t p d", p=128)
    xr_t = xr_d.rearrange("(t p) d -> t p d", p=128)
    out_t = out_d.rearrange("(t p) d -> t p d", p=128)

    gains = {}

    def load_gains(idxs):
        k.gcount = getattr(k, "gcount", 0) + 1
        gb = k.sb(f"gains{k.gcount}", [128, len(idxs), D], F32)
        for i, gi in enumerate(idxs):
            key = f"gain{gi}_{k.gcount}"
            k.dma("sp", gb[:, i, :], gbc_d[:, gi, :], [], [key])
            gains[gi] = (gb[:, i, :], key)
    sm = k.sb("sm", [128, 64], F32)
    cw = k.sb("cw", [128, 12, 4], F32)
    cb = k.sb("cb", [128, 12], F32)
    fw = k.sb("fw", [128, 8, 31], F32)
    fv = k.sb("fv", [128, 8, 3], F32)
    ident = k.sb("ident", [128, 128], BF16)
    cst = k.sb("cst", [128, 4, 128], F32)
    flag = k.sb("flag", [128, 1], F32)
    epsT = k.sb("epsT", [128, 1], F32)
    ones_bf = k.sb("ones_bf", [128, 128], BF16)
    for (dst, src, nm) in [ (sm, sm_d, "sm"), (cw, cw_d, "cw"),
                           (cb, cb_d, "cb"), (fw, fw_d, "fw"), (fv, fv_d, "fv"), (ident, ident_d, "ident"),
                           (cst, cst_d, "cst"), (flag, flag_d, "flag")]:
        k.dma("sp", dst[:], src, [], [nm])
    k.dve("memset", [], ["epsT"], ap=epsT[:], constant=EPS)
    k.dve("memset", [], ["ones_bf"], ap=ones_bf[:], constant=1.0)

    PS = k.st.enter_context(nc.psum_tensor("PS", [128, 8, 512], F32))

    def pb(i):
        return PS[:, i, :]

    def pbk(i):
        return f"ps{i}"

    ss_rot = Rot(k, "ss", int(os.environ.get("SSROT", "4")), [128, 4], F32)
    sq_rot = Rot(k, "sq", 1, [128, D], BF16)
    h_rot = Rot(k, "hb", int(os.environ.get("HROT", "2")), [128, D], BF16)

    def rstd_from_ss(ss, sskey, n, inv_n):
        k.act([sskey, "epsT"], [sskey], out=ss[:, 0:n], in_=ss[:, 0:n], func=AF.Ln, scale=inv_n, bias=epsT[:])
        k.act([sskey], [sskey], out=ss[:, 0:n], in_=ss[:, 0:n], func=AF.Exp, scale=-0.5)

    def norm_tile(x_ap, xkey, gidx, h_out, hkey):
        ss, sskey = ss_rot.next()
        sq, sqkey = sq_rot.next()
        k.act([xkey], [sqkey, sskey], out=sq[:], in_=x_ap, func=AF.Square, accum_out=ss[:, 0:1])
        rstd_from_ss(ss, sskey, 1, 1.0 / D)
        k.dve("scalar_tensor_tensor", [xkey, sskey, gains[gidx][1]], [hkey], out=h_out, in0=x_ap, scalar=ss[:, 0:1],
              in1=gains[gidx][0], op0=ALU.mult, op1=ALU.mult)

    def transpose_to(h_ap, hkey, dstT, dkey, col0, bank):
        pT = pb(bank).bitcast(BF16)
        k.tr([(pT[:, kc * 128:(kc + 1) * 128], h_ap[:, kc * 128:(kc + 1) * 128]) for kc in range(8)], ident[:],
             [hkey, "ident"], [pbk(bank)])
        k.act([pbk(bank)], [dkey], out=dstT[:, :, col0:col0 + 128], in_=pT.rearrange("p (a b) -> p a b", a=8),
              func=AF.Copy)

    x_res = None

    src_t = xo_t if dbg == "nomixer" else xr_t
    tri = cst[:, 0, :]
    su = cst[:, 1, :]
    onesf = cst[:, 2, :]
    mask01 = cst[:, 3, :]
    do_ssd = dbg in (None, "ssd")
    do_conf = dbg in (None, "conf")

    if do_ssd:
        k.push()
        load_gains([0, 5])
        dbc = k.sb("dbc", [128, D], F32)
        k.dma("sp", dbc[:], dbc_d, [], ["dbc"])
        xt_rot = Rot(k, "xta", int(os.environ.get("XTA", "3")), [128, D], F32)
        xb3_rot = Rot(k, "xtc", int(os.environ.get("XTC", "1")), [128, D], F32)
        Wa = k.sb("Wa", [128, 8, 2576], BF16)
        Wot = k.sb("Wot", [128, 8, D], BF16)
        w_in_v = w_in.rearrange("(kc p) n -> p kc n", p=128)
        for q in range(4):
            k.dma("pool", Wa[:, :, 1024 + 384 * q:1024 + 384 * (q + 1)], w_in_v[:, :, 1024 + 384 * q:1024 + 384 * (q + 1)], [], [f"Wa_x{q}"])
        k.dma("pool", Wa[:, :, 2560:2576], w_in_v[:, :, 2560:2576], [], ["Wa_dt"])
        k.dma("pool", Wa[:, :, 0:1024], w_in_v[:, :, 0:1024], [], ["Wa_z"])
        k.dma("pool", Wot[:], w_out[0:D, :].rearrange("(kc p) n -> p kc n", p=128), [], ["Wot"])
        diag4 = k.sb("diag4", [128, 12, 4, 128], BF16)
        for c in range(12):
            k.dve("tensor_tensor", ["ident", "cw"], [f"diag4_{c}"], out=diag4[:, c, :, :], in0=bc(ident[:].unsqueeze(1), [128, 4, 128]),
                  in1=bc(cw[:, c, :].unsqueeze(2), [128, 4, 128]), op=ALU.mult)
        hT_r = Rot(k, "hT1_", 2, [128, 8, MT], BF16)
        xbcT_r = Rot(k, "xbcT_", 2, [128, 12, MT], BF16)
        dt4_r = Rot(k, "dt4_", 2, [128, 4, 16], F32)
        dA4_r = Rot(k, "dA4_", 2, [128, 4, 16], F32)
        sz_r = Rot(k, "sz_", 2, [128, 4, D], BF16)
        pre_rot = Rot(k, "pre", int(os.environ.get("PRE", "3")), [128, 515], BF16)
        hist = k.sb("hist", [128, 12, 3], BF16)
        dtp = k.sb("dtp", [128, 4, 16], F32)
        nA = k.sb("nA", [128, 16], F32)
        ex3_r = Rot(k, "ex3_", 2, [128, 48], F32)
        xdt_r = Rot(k, "xdt_", 2, [128, D], BF16)
        xD_r = Rot(k, "xD_", 1, [128, D], BF16)
        Btok_r = Rot(k, "Btok_", 2, [128, 256], BF16)
        E_r = Rot(k, "E_", 2, [128, 16, 128], BF16)
        CBm_r = Rot(k, "CBm_", 2, [128, 2, 128], BF16)
        SUhi = k.sb("SUhi", [128, 8, 128], BF16)
        SUlo = k.sb("SUlo", [128, 8, 128], BF16)
        su_bf = k.sb("su_bf", [128, 128], BF16)
        tri_bf = k.sb("tri_bf", [128, 128], BF16)
        dAh = k.sb("dAh", [128, 16], BF16)
        dAh32 = k.sb("dAh32", [128, 16], F32)
        dAl = k.sb("dAl", [128, 16], F32)
        k.dve("tensor_copy", ["cst"], ["su_bf"], out=su_bf[:], in_=cst[:, 1, :])
        k.dve("tensor_copy", ["cst"], ["tri_bf"], out=tri_bf[:], in_=cst[:, 0, :])
        xw_r = Rot(k, "xw_", 2, [128, D], BF16)
        y1 = k.sb("y1", [128, D], F32)
        yn = k.sb("yn", [128, D], BF16)
        ynT = k.sb("ynT", [128, 8, 128], BF16)
        Sst = k.sb("Sst", [128, D], F32)
        Sbf = k.sb("Sbf", [128, D], BF16)
        k.dve("memset", [], ["hist"], ap=hist[:], constant=0.0)
        k.dve("memset", [], ["Sst"], ap=Sst[:], constant=0.0)
        k.dve("memset", [], ["Sbf"], ap=Sbf[:], constant=0.0)
        k.act(["sm"], ["nA"], out=nA[:], in_=sm[:, 16:32], func=AF.Exp)
        k.dve("tensor_scalar", ["nA"], ["nA"], out=nA[:], in0=nA[:], scalar1=-1.0, scalar2=None, op0=ALU.mult)
        ATB = int(os.environ.get("ATB", "2"))
        YTB = int(os.environ.get("YTB", "0"))
        pT0 = pb(ATB).bitcast(BF16)
        pTy = pb(YTB).bitcast(BF16)
        G1 = gains[5]

        mt_state = {}

        def src_tile(g_mt, st):
            return (xp_t if g_mt < 4 else xo_t)[(g_mt % 4) * 4 + st]

        mt_h = {}

        def F1(mt, pair):
            if pair == 0:
                mt_h[mt] = hT_r.next()
            hT, hTk = mt_h[mt]
            items = []
            for st in (2 * pair, 2 * pair + 1):
                xt, xtk = xt_rot.next()
                k.dma("sp", xt[:], src_tile(mt, st), [], [xtk])
                ss, sskey = ss_rot.next()
                sq, sqkey = sq_rot.next()
                k.act([xtk], [sqkey, sskey], out=sq[:], in_=xt[:], func=AF.Square, accum_out=ss[:, 0:1])
                rstd_from_ss(ss, sskey, 1, 1.0 / D)
                items.append((st, xt, xtk, ss, sskey))
            hbs = []
            for (st, xt, xtk, ss, sskey) in items:
                hb, hk = h_rot.next()
                k.dve("scalar_tensor_tensor", [xtk, sskey, gains[0][1]], [hk], out=hb[:], in0=xt[:], scalar=ss[:, 0:1],
                      in1=gains[0][0], op0=ALU.mult, op1=ALU.mult)
                hbs.append((st, hb, hk))
            F1B = int(os.environ.get("F1B", "0"))
            for i, (st, hb, hk) in enumerate(hbs):
                pT = pb(F1B + i).bitcast(BF16)
                k.tr([(pT[:, kc * 128:(kc + 1) * 128], hb[:, kc * 128:(kc + 1) * 128]) for kc in range(8)], ident[:],
                     [hk, "ident"], [pbk(F1B + i)])
            for i, (st, hb, hk) in enumerate(hbs):
                pT = pb(F1B + i).bitcast(BF16)
                k.act([pbk(F1B + i)], [hTk], out=hT[:, :, st * 128:(st + 1) * 128], in_=pT.rearrange("p (a b) -> p a b", a=8), func=AF.Copy)

        def F2(mt, q):
            hT, hTk = mt_h[mt]
            if q == 0:
                xbcT, xbk = xbcT_r.next()
                dt4, dt4k = dt4_r.next()
                dA4, dA4k = dA4_r.next()
                szm, szk = sz_r.next()
                mt_state[mt] = (hT, hTk, xbcT, xbk, dt4, dt4k, dA4, dA4k, szm, szk)
            hT, hTk, xbcT, xbk, dt4, dt4k, dA4, dA4k, szm, szk = mt_state[mt]
            pres = {}

            def inproj(c):
                b1 = c % 2
                k.mm([(pb(b1), Wa[:, kc, 1024 + c * 128:1024 + (c + 1) * 128], hT[:, kc, :], kc == 0, kc == 7)
                      for kc in range(8)], [f"Wa_x{c // 3}", hTk], [pbk(b1)])
                pre, prek = pre_rot.next()
                pres[c] = (pre, prek)
                k.act([pbk(b1)], [prek], out=pre[:, 3:515], in_=pb(b1), func=AF.Copy)
                k.dve("tensor_copy", ["hist", prek], [prek], out=pre[:, 0:3], in_=hist[:, c, :])
                k.dve("tensor_copy", [prek], ["hist"], out=hist[:, c, :], in_=pre[:, 512:515])

            def conv(c):
                pre, prek = pres[c]
                b2 = 2 + c % 2
                k.mm([(pb(b2), diag4[:, c, kk, :], pre[:, kk:kk + 512], kk == 0, kk == 3) for kk in range(4)],
                     [f"diag4_{c}", prek], [pbk(b2)])
                k.act([pbk(b2), "cb"], [f"{xbk}_{c}"], out=xbcT[:, c, :], in_=pb(b2), func=AF.Silu, bias=cb[:, c:c + 1], scale=1.0)

            c0 = 3 * q
            skipC = (mt < 3 and q == 3)
            inproj(c0)
            if not skipC:
                inproj(c0 + 1)
            conv(c0)
            if not skipC:
                inproj(c0 + 2)
                conv(c0 + 1)
            if mt >= 4:
                st = q
                k.mm([(pb(hf2), hT[:, kc, st * 128:(st + 1) * 128], Wa[:, kc, hf2 * 512:(hf2 + 1) * 512], kc == 0, kc == 7)
                      for hf2 in range(2) for kc in range(8)], ["Wa_z", hTk], [pbk(0), pbk(1)])
            if not skipC:
                conv(c0 + 2)
            if mt >= 4:
                k.act([pbk(0), pbk(1)], [szk], out=szm[:, q, :].rearrange("p (a b) -> p a b", a=2), in_=PS[:, 0:2, :], func=AF.Silu)
            if q == 3:
                for st in range(4):
                    k.mm([(pb(1)[:, st * 16:(st + 1) * 16], hT[:, kc, st * 128:(st + 1) * 128], Wa[:, kc, 2560:2576], kc == 0, kc == 7)
                          for kc in range(8)], ["Wa_dt", hTk], [pbk(1)])
                k.dve("tensor_tensor", [pbk(1), "sm"], ["dtp"], out=dtp[:], in0=pb(1)[:, 0:64].rearrange("p (a b) -> p a b", a=4),
                      in1=bc(sm[:, 0:16].unsqueeze(1), [128, 4, 16]), op=ALU.add)
                k.act(["dtp"], ["dtp"], out=dtp[:], in_=dtp[:], func=AF.Exp)
                k.act(["dtp", "cst"], [dt4k], out=dt4[:], in_=dtp[:], func=AF.Ln, bias=cst[:, 2, 0:1], scale=1.0)
                k.dve("tensor_tensor", [dt4k, "nA"], [dA4k], out=dA4[:], in0=dt4[:], in1=bc(nA[:].unsqueeze(1), [128, 4, 16]), op=ALU.mult)

        ch_state = {}

        def A(g):
            mt, st = g // 4, g % 4
            full = mt >= 4
            hT, hTk, xbcT, xbk, dt4, dt4k, dA4, dA4k, szm, szk = mt_state[mt]
            cs = slice(st * 128, (st + 1) * 128)
            xdt, xdtk = xdt_r.next()
            xD, xDk = xD_r.next()
            Btok, Btk = Btok_r.next()
            ex3, ex3k = ex3_r.next()
            E, Ek = E_r.next()
            CBm, CBk = CBm_r.next()
            xw, xwk = xw_r.next()
            ch_state[g] = (xdt, xdtk, xD, xDk, Btok, Btk, ex3, ex3k, E, Ek, xw, xwk)
            k.tr([(pT0[:, c * 128:(c + 1) * 128], xbcT[:, c, cs]) for c in range(8)], ident[:],
                 [f"{xbk}_{c}" for c in range(8)] + ["ident"], [pbk(ATB)])
            k.dve("tensor_tensor", [pbk(ATB), dt4k], [xdtk], out=xdt[:].rearrange("p (h q) -> p h q", h=16),
                  in0=pT0.rearrange("p (h q) -> p h q", h=16), in1=bc(dt4[:, st, :].unsqueeze(2), [128, 16, 64]), op=ALU.mult)
            if full:
                k.dve("tensor_tensor", [pbk(ATB), "dbc"], [xDk], out=xD[:], in0=pT0, in1=dbc[:], op=ALU.mult)
            k.tr([(pT0[:, gg * 128:(gg + 1) * 128], xbcT[:, 8 + gg, cs]) for gg in range(2)], ident[:],
                 [f"{xbk}_8", f"{xbk}_9", "ident"], [pbk(ATB)])
            k.act([pbk(ATB)], [Btk], out=Btok[:], in_=pT0[:, 0:256], func=AF.Copy)
            dA = dA4[:, st, :]
            k.mm([(pb(1)[:, 0:16], tri, dA, True, True), (pb(1)[:, 16:32], su, dA, True, True),
                  (pb(1)[:, 32:48], onesf, dA, True, True)], ["cst", dA4k], [pbk(1)])
            k.act([pbk(1)], [ex3k], out=ex3[:], in_=pb(1)[:, 0:48], func=AF.Exp)
            k.pool("tensor_tensor", [xdtk, ex3k], [xwk], out=xw[:].rearrange("p (h q) -> p h q", h=16),
                   in0=xdt[:].rearrange("p (h q) -> p h q", h=16), in1=bc(ex3[:, 16:32].unsqueeze(2), [128, 16, 64]), op=ALU.mult)
            if full:
                k.dve("tensor_copy", [dA4k], ["dAh"], out=dAh[:], in_=dA)
                k.dve("tensor_copy", ["dAh"], ["dAh32"], out=dAh32[:], in_=dAh[:])
                k.dve("tensor_tensor", [dA4k, "dAh32"], ["dAl"], out=dAl[:], in0=dA, in1=dAh32[:], op=ALU.subtract)
                for r in range(2):
                    k.dve("tensor_tensor", ["su_bf", "dAh"], ["SUhi"], out=SUhi[:], in0=bc(su_bf[:].unsqueeze(1), [128, 8, 128]),
                          in1=bc(dAh[:, r * 8:(r + 1) * 8].unsqueeze(2), [128, 8, 128]), op=ALU.mult)
                    k.dve("tensor_tensor", ["su_bf", "dAl"], ["SUlo"], out=SUlo[:], in0=bc(su_bf[:].unsqueeze(1), [128, 8, 128]),
                          in1=bc(dAl[:, r * 8:(r + 1) * 8].unsqueeze(2), [128, 8, 128]), op=ALU.mult)
                    for hq in range(2):
                        bank = 2 + hq
                        lst = []
                        for hh in range(4):
                            h = hq * 4 + hh
                            lst.append((pb(bank)[:, hh * 128:(hh + 1) * 128], SUhi[:, h, :], tri_bf[:], True, False))
                            lst.append((pb(bank)[:, hh * 128:(hh + 1) * 128], SUlo[:, h, :], tri_bf[:], False, True))
                        k.mm(lst, ["SUhi", "SUlo", "tri_bf"], [pbk(bank)])
                        k.act([pbk(bank)], [Ek], out=E[:, r * 8 + hq * 4:r * 8 + hq * 4 + 4, :],
                              in_=pb(bank).rearrange("p (a b) -> p a b", a=4), func=AF.Exp)
                k.mm([(pb(1)[:, 64 + gg * 128:64 + (gg + 1) * 128], xbcT[:, 8 + gg, cs], xbcT[:, 10 + gg, cs], True, True) for gg in range(2)],
                     [f"{xbk}_8", f"{xbk}_9", f"{xbk}_10", f"{xbk}_11"], [pbk(1)])
                k.dve("tensor_tensor", [pbk(1), "cst"], [CBk], out=CBm[:], in0=pb(1)[:, 64:320].rearrange("p (a b) -> p a b", a=2),
                      in1=bc(mask01.unsqueeze(1), [128, 2, 128]), op=ALU.mult)
                k.dve("tensor_tensor", [Ek, CBk], [Ek], out=E[:].rearrange("p (g r) l -> p g r l", g=2),
                      in0=E[:].rearrange("p (g r) l -> p g r l", g=2), in1=bc(CBm[:].unsqueeze(2), [128, 2, 8, 128]), op=ALU.mult)

        bst = {}

        def B1(g):
            mt, st = g // 4, g % 4
            hT, hTk, xbcT, xbk, dt4, dt4k, dA4, dA4k, szm, szk = mt_state[mt]
            xdt, xdtk, xD, xDk, Btok, Btk, ex3, ex3k, E, Ek, xw, xwk = ch_state[g]
            cs = slice(st * 128, (st + 1) * 128)
            t = (mt % 4) * 4 + st
            lst = []
            for hf2 in range(2):
                lst.append((pb(4 + hf2), ident[:], xD[:, hf2 * 512:(hf2 + 1) * 512], True, False))
            for h in range(16):
                lst.append((pb(4 + h // 8)[:, (h % 8) * 64:(h % 8 + 1) * 64], E[:, h, :], xdt[:, h * 64:(h + 1) * 64], False, h % 8 == 7))
            k.mm(lst, ["ident", xDk, Ek, xdtk], [pbk(4), pbk(5)])
            k.mm([(pb(6 + gg), xbcT[:, 10 + gg, cs], Sbf[:, gg * 512:(gg + 1) * 512], True, True) for gg in range(2)],
                 [f"{xbk}_10", f"{xbk}_11", "Sbf"], [pbk(6), pbk(7)])
            k.dve("tensor_tensor", [pbk(6), pbk(7), ex3k], ["y1"], out=y1[:].rearrange("p (h q) -> p h q", h=16),
                  in0=PS[:, 6:8, :].rearrange("p a (h q) -> p (a h) q", q=64), in1=bc(ex3[:, 0:16].unsqueeze(2), [128, 16, 64]), op=ALU.mult)
            k.dve("tensor_tensor", [pbk(4), pbk(5), "y1"], ["y1"], out=y1[:].rearrange("p (a b) -> p a b", a=2),
                  in0=PS[:, 4:6, :], in1=y1[:].rearrange("p (a b) -> p a b", a=2), op=ALU.add)
            k.dve("tensor_tensor", ["y1", szk], ["y1"], out=y1[:], in0=y1[:], in1=szm[:, st, :], op=ALU.mult)
            k.dve("tensor_tensor", ["y1", G1[1]], ["yn"], out=yn[:], in0=y1[:], in1=G1[0], op=ALU.mult)
            ss, sskey = ss_rot.next()
            sq, sqkey = sq_rot.next()
            for gg in range(2):
                k.act(["y1"], [sqkey, sskey], out=sq[:, 0:512], in_=y1[:, gg * 512:(gg + 1) * 512], func=AF.Square, accum_out=ss[:, gg:gg + 1])
            rstd_from_ss(ss, sskey, 2, 1.0 / 512)
            bst[g] = (ss, sskey, t)

        def B2(g):
            k.tr([(pTy[:, kc * 128:(kc + 1) * 128], yn[:, kc * 128:(kc + 1) * 128]) for kc in range(8)], ident[:], ["yn", "ident"], [pbk(YTB)])
            pv = pTy.rearrange("p (a b) -> p a b", a=8)
            for gg in range(2):
                k.act([pbk(YTB)], [f"ynT{gg}"], out=ynT[:, 4 * gg:4 * gg + 4, :], in_=pv[:, 4 * gg:4 * gg + 4, :], func=AF.Copy)

        def B3(g):
            ss, sskey, t = bst[g]
            for gg in range(2):
                k.mm([(pb(4 + 2 * gg + hf2), ynT[:, 4 * gg + kc, :], Wot[:, 4 * gg + kc, hf2 * 512:(hf2 + 1) * 512], kc == 0, kc == 3)
                      for hf2 in range(2) for kc in range(4)], ["Wot", f"ynT{gg}"], [pbk(4 + 2 * gg), pbk(5 + 2 * gg)])
            xt, xtk = xb3_rot.next()
            k.dma("sp", xt[:], xo_t[t], [], [xtk])
            for gg in range(2):
                k.dve("scalar_tensor_tensor", [pbk(4 + 2 * gg), pbk(5 + 2 * gg), sskey, xtk], [xtk], out=xt[:].rearrange("p (a b) -> p a b", a=2),
                      in0=PS[:, 4 + 2 * gg:6 + 2 * gg, :], scalar=ss[:, gg:gg + 1], in1=xt[:].rearrange("p (a b) -> p a b", a=2),
                      op0=ALU.mult, op1=ALU.add)
            k.dma("sp", xr_t[t], xt[:], [xtk], [f"xr{t}"])

        def C(g):
            xdt, xdtk, xD, xDk, Btok, Btk, ex3, ex3k, E, Ek, xw, xwk = ch_state[g]
            k.mm([(pb(2 + gg), Btok[:, gg * 128:(gg + 1) * 128], xw[:, gg * 512:(gg + 1) * 512], True, True) for gg in range(2)],
                 [Btk, xwk], [pbk(2), pbk(3)])
            k.dve("tensor_tensor", ["Sst", ex3k], ["Sst"], out=Sst[:].rearrange("p (h q) -> p h q", h=16),
                  in0=Sst[:].rearrange("p (h q) -> p h q", h=16), in1=bc(ex3[:, 32:48].unsqueeze(2), [128, 16, 64]), op=ALU.mult)
            k.dve("tensor_tensor", ["Sst", pbk(2), pbk(3)], ["Sst"], out=Sst[:].rearrange("p (a b) -> p a b", a=2),
                  in0=Sst[:].rearrange("p (a b) -> p a b", a=2), in1=PS[:, 2:4, :], op=ALU.add)
            k.act(["Sst"], ["Sbf"], out=Sbf[:], in_=Sst[:], func=AF.Copy)

        NG = 32
        F1(0, 0)
        F1(0, 1)
        for q in range(4):
            F2(0, q)
        F1(1, 0)
        F1(1, 1)
        A(0)
        for g in range(NG):
            mt, st = g // 4, g % 4
            own = mt >= 4
            if own:
                B1(g)
            if g < NG - 1:
                C(g)
            if g == 15:
                k.dve("tensor_scalar", ["Sst", "flag"], ["Sst"], out=Sst[:], in0=Sst[:], scalar1=flag[:, 0:1], scalar2=None, op0=ALU.mult)
                k.act(["Sst"], ["Sbf"], out=Sbf[:], in_=Sst[:], func=AF.Copy)
            if mt + 1 < 8:
                F2(mt + 1, st)
            if own:
                B2(g)
            if g + 1 < NG:
                A(g + 1)
            if own:
                B3(g)
            if st in (0, 2) and mt + 2 < 8:
                F1(mt + 2, st // 2)
        k.pop()

    if do_conf:
        k.push()
        base_t = xr_t if do_ssd else xo_t
        load_gains([0])
        xt_rot = Rot(k, "xtb", int(os.environ.get("XTB", "3")), [128, D], F32)
        xb_rot = Rot(k, "xbb", int(os.environ.get("XBB", "3")), [128, D], F32)
        Wag = k.sb("Wag", [128, 8, 2048], BF16)
        Wob = k.sb("Wob", [128, 8, D], BF16)
        w_in_v2 = w_in.rearrange("(kc p) n -> p kc n", p=128)
        for cp in range(4):
            k.dma("pool", Wag[:, :, cp * 256:(cp + 1) * 256], w_in_v2[:, :, A0 + cp * 256:A0 + (cp + 1) * 256], [], [f"Wag_a{cp}"])
            k.dma("pool", Wag[:, :, D + cp * 256:D + (cp + 1) * 256], w_in_v2[:, :, G0 + cp * 256:G0 + (cp + 1) * 256], [], [f"Wag_g{cp}"])
        k.dma("pool", Wob[:], w_out[D:2 * D, :].rearrange("(kc p) n -> p kc n", p=128), [], ["Wob"])
        NDV = int(os.environ.get("NDV", "10"))
        NPE = 31 - NDV
        diag = k.sb("diag", [128, 8, NPE, 128], BF16)
        DGP = os.environ.get("DGP", "dve")
        for c in range(8):
            on_pool = (DGP == "pool") or (DGP == "alt" and c % 2 == 1)
            (k.pool if on_pool else k.dve)("tensor_tensor", ["ident", "fw"], [f"diag{c}"], out=diag[:, c, :, :], in0=bc(ident[:].unsqueeze(1), [128, NPE, 128]),
                  in1=bc(fw[:, c, NDV:31].unsqueeze(2), [128, NPE, 128]), op=ALU.mult)
        hT2_r = Rot(k, "hT2_", 2, [128, 8, MT], BF16)
        unT = k.sb("unT", [128, 8, MT], BF16)
        uT_r = Rot(k, "uT_", 2, [128, 8, 30 + MT], BF16)
        sgm_rot = Rot(k, "sgm", 2, [128, MT], F32)
        cv_r = Rot(k, "cv_", 2, [128, 8, MT], BF16)
        cvsq_rot = Rot(k, "cvsq", 2, [128, MT], BF16)
        mean_r = Rot(k, "mean_", 1, [128, MT], F32)
        rstdv_r = Rot(k, "rstdv_", 1, [128, MT], F32)
        t1_rot = Rot(k, "t1", 2, [128, MT], F32)
        accv_rot = Rot(k, "accv", 2, [128, MT], F32)
        gl_i = [0]

        def glu_chunks(hT2, hT2k, uT, uTk, ncols, col0):
            for c in range(8):
                ba = gl_i[0] % 2
                bg = 2 + gl_i[0] % 2
                gl_i[0] += 1
                k.mm([(pb(ba)[:, 0:ncols], Wag[:, kc, c * 128:(c + 1) * 128], hT2[:, kc, col0:col0 + ncols], kc == 0, kc == 7)
                      for kc in range(8)], [f"Wag_a{c // 2}", hT2k], [pbk(ba)])
                k.mm([(pb(bg)[:, 0:ncols], Wag[:, kc, D + c * 128:D + (c + 1) * 128], hT2[:, kc, col0:col0 + ncols], kc == 0, kc == 7)
                      for kc in range(8)], [f"Wag_g{c // 2}", hT2k], [pbk(bg)])
                sgm, sgk = sgm_rot.next()
                k.act([pbk(bg)], [sgk], out=sgm[:, 0:ncols], in_=pb(bg)[:, 0:ncols], func=AF.Sigmoid)
                k.dve("tensor_tensor", [pbk(ba), sgk], [f"{uTk}_{c}"], out=uT[:, c, 30 + col0:30 + col0 + ncols], in0=pb(ba)[:, 0:ncols],
                      in1=sgm[:, 0:ncols], op=ALU.mult)

        uTp, uTpk = uT_r.next()
        k.dve("memset", [], [f"{uTpk}_{c}" for c in range(8)] + [f"{uTpk}_h"], ap=uTp[:], constant=0.0)
        hTp, hTpk = hT2_r.next()
        xt, xtk = xt_rot.next()
        k.dma("sp", xt[:], xp_t[NT - 1], [], [xtk])
        hb, hk = h_rot.next()
        norm_tile(xt[:], xtk, 0, hb[:], hk)
        transpose_to(hb, hk, hTp, hTpk, 384, 7)
        glu_chunks(hTp, hTpk, uTp, uTpk, 128, 384)
        for m in range(NT // 4):
            hT2, hT2k = hT2_r.next()
            uT, uTk = uT_r.next()
            cv, cvk = cv_r.next()
            mean, meank = mean_r.next()
            rstdv, rstdk = rstdv_r.next()
            k.dve("tensor_copy", [f"{uTpk}_{c}" for c in range(8)], [f"{uTk}_h"], out=uT[:, :, 0:30], in_=uTp[:, :, MT:MT + 30])
            for st in range(4):
                t = m * 4 + st
                xt, xtk = xt_rot.next()
                k.dma("sp", xt[:], xo_t[t], [], [xtk])
                hb, hk = h_rot.next()
                norm_tile(xt[:], xtk, 0, hb[:], hk)
                transpose_to(hb, hk, hT2, hT2k, st * 128, 7)
            glu_chunks(hT2, hT2k, uT, uTk, MT, 0)
            for c in range(8):
                bcv = 4 + c % 2
                k.mm([(pb(bcv), diag[:, c, kk - NDV, :], uT[:, c, kk:kk + MT], kk == NDV, kk == 30) for kk in range(NDV, 31)],
                     [f"diag{c}", f"{uTk}_{c}", f"{uTk}_h"], [pbk(bcv)])
                accv, acck = accv_rot.next()
                k.dve("tensor_scalar", [f"{uTk}_{c}", f"{uTk}_h", "fw", "fv"], [acck], out=accv[:], in0=uT[:, c, 0:MT], scalar1=fw[:, c, 0:1],
                      scalar2=fv[:, c, 0:1], op0=ALU.mult, op1=ALU.add)
                for kk in range(1, NDV):
                    k.dve("scalar_tensor_tensor", [f"{uTk}_{c}", f"{uTk}_h", "fw", acck], [acck], out=accv[:], in0=uT[:, c, kk:kk + MT],
                          scalar=fw[:, c, kk:kk + 1], in1=accv[:], op0=ALU.mult, op1=ALU.add)
                k.dve("tensor_tensor", [pbk(bcv), acck], [f"{cvk}_{c}"], out=cv[:, c, :], in0=pb(bcv), in1=accv[:], op=ALU.add)
                cvsq, cqk = cvsq_rot.next()
                k.act([f"{cvk}_{c}"], [cqk], out=cvsq[:], in_=cv[:, c, :], func=AF.Square)
                k.mm([(pb(6), ones_bf[:], cv[:, c, :], c == 0, c == 7)], ["ones_bf", f"{cvk}_{c}"], [pbk(6)])
                k.mm([(pb(7), ones_bf[:], cvsq[:], c == 0, c == 7)], ["ones_bf", cqk], [pbk(7)])
            k.dve("tensor_scalar", [pbk(6)], [meank], out=mean[:], in0=pb(6), scalar1=1.0 / D, scalar2=None, op0=ALU.mult)
            k.dve("tensor_tensor", [meank], [rstdk], out=rstdv[:], in0=mean[:], in1=mean[:], op=ALU.mult)
            k.dve("scalar_tensor_tensor", [pbk(7), rstdk], [rstdk], out=rstdv[:], in0=pb(7), scalar=1.0 / D, in1=rstdv[:],
                  op0=ALU.mult, op1=ALU.subtract)
            k.act([rstdk, "epsT"], [rstdk], out=rstdv[:], in_=rstdv[:], func=AF.Ln, bias=epsT[:], scale=1.0)
            k.act([rstdk], [rstdk], out=rstdv[:], in_=rstdv[:], func=AF.Exp, scale=-0.5)
            for c in range(8):
                t1, t1k = t1_rot.next()
                lnp = os.environ.get("LNP", "0")
                e1 = k.pool if (lnp == "all" or (lnp == "half" and c % 2 == 1)) else k.dve
                e1("tensor_tensor", [f"{cvk}_{c}", meank], [t1k], out=t1[:], in0=cv[:, c, :], in1=mean[:], op=ALU.subtract)
                e1("tensor_tensor", [t1k, rstdk], [t1k], out=t1[:], in0=t1[:], in1=rstdv[:], op=ALU.mult)
                k.act([t1k, "fv"], [f"unT{c}"], out=unT[:, c, :], in_=t1[:], func=AF.Silu, scale=fv[:, c, 1:2], bias=fv[:, c, 2:3])
            for st in range(4):
                t = m * 4 + st
                ob = 2 * (st % 2)
                OPS = int(os.environ.get("OPS", "4"))
                per = 8 // OPS
                for part in range(OPS):
                    kcs = range(part * per, (part + 1) * per)
                    k.mm([(pb(ob + hf2), unT[:, kc, st * 128:(st + 1) * 128], Wob[:, kc, hf2 * 512:(hf2 + 1) * 512], kc == 0, kc == 7)
                          for hf2 in range(2) for kc in kcs], ["Wob"] + [f"unT{c}" for c in kcs], [pbk(ob), pbk(ob + 1)])
                xt, xtk = xb_rot.next()
                k.dma("sp", xt[:], base_t[t], [f"xr{t}"], [xtk])
                k.dve("tensor_tensor", [pbk(ob), pbk(ob + 1), xtk], [xtk], out=xt[:].rearrange("p (a b) -> p a b", a=2),
                      in0=xt[:].rearrange("p (a b) -> p a b", a=2), in1=PS[:, ob:ob + 2, :], op=ALU.add)
                k.dma("sp", xr_t[t], xt[:], [xtk], [f"xr{t}"])
            uTp, uTpk = uT, uTk
        k.pop()
    elif dbg == "ssd":
        pass

    k.push()
    x_res = k.sb("x_res", [128, NT, D], F32)
    KT = k.sb("KT", [128, 8, MEM], BF16)
    V = k.sb("V", [128, 2, D], BF16)
    with_kv = True
    if with_kv:
        k.push()
        load_gains([1])
        load_gains([2])
        WA3 = k.sb("WA3", [128, 16 * D], BF16)
        wq = WA3[:, 0:8 * D].rearrange("p (a b) -> p a b", a=8)
        wo = WA3[:, 8 * D:16 * D].rearrange("p (a b) -> p a b", a=8)
        for t in range(4):
            k.dma("sp", x_res[:, t, :], src_t[t], [f"xr{t}"], [f"xres{t}"])
        for qq in range(2):
            k.dma("pool", wq[:, :, qq * 512:(qq + 1) * 512], w_q.rearrange("(kc p) n -> p kc n", p=128)[:, :, qq * 512:(qq + 1) * 512], [], [f"WAq{qq}"])
        WA = k.sb("WA", [128, 8 * 2048], BF16)
        wkv = WA[:, 0:8 * 2048].rearrange("p (a b) -> p a b", a=8)
        w_kv_v = w_kv.rearrange("(kc p) n -> p kc n", p=128)
        for qq in range(4):
            k.dma("pool", wkv[:, :, qq * 512:(qq + 1) * 512], w_kv_v[:, :, qq * 512:(qq + 1) * 512], ["WAq1"], [f"WAkv{qq}"])
        for t in range(4, NT):
            k.dma("sp", x_res[:, t, :], src_t[t], [f"xr{t}", "WAkv1" if t < 8 else "WAkv3"], [f"xres{t}"])
        memT = k.sb("memT", [128, 8, MEM], BF16)
        mrot = Rot(k, "memx", 1, [128, D], F32)
        for mc in range(2):
            mx, mxk = mrot.next()
            k.dma("sp", mx[:], mem_d[mc * 128:(mc + 1) * 128, :], [], [mxk])
            hb, hk = h_rot.next()
            norm_tile(mx[:], mxk, 2, hb[:], hk)
            transpose_to(hb, hk, memT, "memT", mc * 128, 0)
        for c in range(8):
            bank = 1 + (c % 2)
            k.mm([(pb(bank)[:, 0:MEM], wkv[:, kc, c * 128:(c + 1) * 128], memT[:, kc, :], kc == 0, kc == 7)
                  for kc in range(8)], [f"WAkv{c // 4}", "memT"], [pbk(bank)])
            k.act([pbk(bank)], ["KT"], out=KT[:, c, :], in_=pb(bank)[:, 0:MEM], func=AF.Copy)
        for mc in range(2):
            for hf2 in range(2):
                bank = 3 + hf2
                k.mm([(pb(bank), memT[:, kc, mc * 128:(mc + 1) * 128],
                       wkv[:, kc, D + hf2 * 512:D + (hf2 + 1) * 512], kc == 0, kc == 7) for kc in range(8)],
                     [f"WAkv{2 + hf2}", "memT"], [pbk(bank)])
                k.dve("tensor_copy", [pbk(bank)], ["V"], out=V[:, mc, hf2 * 512:(hf2 + 1) * 512], in_=pb(bank))

    for qq in range(2):
        k.dma("pool", wo[:, :, qq * 512:(qq + 1) * 512], w_o.rearrange("(kc p) n -> p kc n", p=128)[:, :, qq * 512:(qq + 1) * 512], [f"WAkv{3}"], [f"WAo{qq}"])
    hxT_rot = Rot(k, "hxT", 2, [128, 8, MT], BF16)
    qT = k.sb("qT", [128, 8, MT], BF16)
    ET_rot = Rot(k, "ET", 2, [128, 2, MT], BF16)
    rden_rot = Rot(k, "rden", 2, [128, MT], F32)
    oT = k.sb("oT", [128, 8, MT], BF16)
    for m in range(NT // 4):
        hxT, hxk = hxT_rot.next()
        for st in range(4):
            t = m * 4 + st
            hb, hk = h_rot.next()
            norm_tile(x_res[:, t, :], f"xres{t}", 1, hb[:], hk)
            transpose_to(hb, hk, hxT, hxk, st * 128, 0)
        for c in range(8):
            bank = 1 + (c % 2)
            k.mm([(pb(bank), wq[:, kc, c * 128:(c + 1) * 128], hxT[:, kc, :], kc == 0, kc == 7) for kc in range(8)],
                 [f"WAq{c // 4}", hxk], [pbk(bank)])
            k.act([pbk(bank)], [f"qT{c}"], out=qT[:, c, :], in_=pb(bank), func=AF.Copy)
        for hd in range(4):
            ET, etk = ET_rot.next()
            for mc in range(2):
                bank = 3 + mc
                k.mm([(pb(bank), KT[:, 2 * hd + dc, mc * 128:(mc + 1) * 128], qT[:, 2 * hd + dc, :], dc == 0, dc == 1)
                      for dc in range(2)], ["KT", f"qT{2 * hd}", f"qT{2 * hd + 1}"], [pbk(bank)])
                k.act([pbk(bank)], [etk], out=ET[:, mc, :], in_=pb(bank), func=AF.Exp, scale=1.0 / 16.0)
            k.mm([(pb(5), ones_bf[:], ET[:, mc, :], mc == 0, mc == 1) for mc in range(2)], ["ones_bf", etk], [pbk(5)])
            rden, rdk = rden_rot.next()
            k.act([pbk(5)], [rdk], out=rden[:], in_=pb(5), func=AF.Ln)
            k.act([rdk], [rdk], out=rden[:], in_=rden[:], func=AF.Exp, scale=-1.0)
            for dc in range(2):
                bank = 6 + dc
                k.mm([(pb(bank), V[:, mc, hd * 256 + dc * 128:hd * 256 + (dc + 1) * 128], ET[:, mc, :], mc == 0, mc == 1)
                      for mc in range(2)], ["V", etk], [pbk(bank)])
                k.dve("tensor_tensor", [pbk(bank), rdk], [f"oT{2 * hd + dc}"], out=oT[:, 2 * hd + dc, :], in0=pb(bank),
                      in1=rden[:], op=ALU.mult)
        for st in range(4):
            t = m * 4 + st
            for hf2 in range(2):
                bank = 1 + hf2
                OP3 = int(os.environ.get("OP3", "1"))
                per3 = 8 // OP3
                for part in range(OP3):
                    kcs = range(part * per3, (part + 1) * per3)
                    k.mm([(pb(bank), oT[:, kc, st * 128:(st + 1) * 128], wo[:, kc, hf2 * 512:(hf2 + 1) * 512], kc == 0, kc == 7)
                          for kc in kcs], [f"WAo{hf2}"] + [f"oT{c}" for c in kcs], [pbk(bank)])
                k.dve("tensor_tensor", [pbk(bank), f"xres{t}"], [f"xres{t}"], out=x_res[:, t, hf2 * 512:(hf2 + 1) * 512],
                      in0=x_res[:, t, hf2 * 512:(hf2 + 1) * 512], in1=pb(bank), op=ALU.add)

    k.pop()
    k.push()
    load_gains([3])
    hfT = k.sb("hfT", [128, 8, HALF], BF16)
    groups = [(0, 4), (4, 4), (8, 4), (12, 4), (16, 3), (19, 3)]
    wg_rot = Rot(k, "wg", 2, [128, 8, 4 * 128], BF16)
    wu_rot = Rot(k, "wu", 2, [128, 8, 4 * 128], BF16)
    wd_rot = Rot(k, "wd", 2, [128, 4, D], BF16)
    sg_rot = Rot(k, "sg", 2, [128, MT], BF16)
    aT_rot = Rot(k, "aT", 2, [128, 4, MT], BF16)
    for gi, (c0, nch) in enumerate(groups):
        wg, wgk = wg_rot.next()
        wu, wuk = wu_rot.next()
        wd, wdk = wd_rot.next()
        k.dma("pool", wg[:, :, 0:nch * 128], w_gate.rearrange("(kc p) n -> p kc n", p=128)[:, :, c0 * 128:(c0 + nch) * 128], [], [wgk])
        k.dma("pool", wu[:, :, 0:nch * 128], w_up.rearrange("(kc p) n -> p kc n", p=128)[:, :, c0 * 128:(c0 + nch) * 128], [], [wuk])
        k.dma("pool", wd[:, 0:nch, :], w_down[c0 * 128:(c0 + nch) * 128, :].rearrange("(c p) n -> p c n", p=128), [], [wdk])
        for m in range(NT // 4):
            if gi == 0:
                for st in range(4):
                    t = m * 4 + st
                    hb, hk = h_rot.next()
                    norm_tile(x_res[:, t, :], f"xres{t}", 3, hb[:], hk)
                    transpose_to(hb, hk, hfT, f"hfT{m}", t * 128, 0)
            aT, aTk = aT_rot.next()
            for ci in range(nch):
                bg = 1 + (ci % 2)
                bu = 3 + (ci % 2)
                k.mm([(pb(bg), wg[:, kc, ci * 128:(ci + 1) * 128], hfT[:, kc, m * MT:(m + 1) * MT], kc == 0, kc == 7)
                      for kc in range(8)], [wgk, f"hfT{m}"], [pbk(bg)])
                k.mm([(pb(bu), wu[:, kc, ci * 128:(ci + 1) * 128], hfT[:, kc, m * MT:(m + 1) * MT], kc == 0, kc == 7)
                      for kc in range(8)], [wuk, f"hfT{m}"], [pbk(bu)])
                sg, sgk = sg_rot.next()
                k.act([pbk(bg)], [sgk], out=sg[:], in_=pb(bg), func=AF.Silu)
                k.dve("tensor_tensor", [pbk(bu), sgk], [aTk], out=aT[:, ci, :], in0=pb(bu), in1=sg[:], op=ALU.mult)
            for st in range(4):
                t = m * 4 + st
                for hf2 in range(2):
                    bank = 5 + hf2
                    k.mm([(pb(bank), aT[:, ci, st * 128:(st + 1) * 128], wd[:, ci, hf2 * 512:(hf2 + 1) * 512], ci == 0, ci == nch - 1)
                          for ci in range(nch)], [wdk, aTk], [pbk(bank)])
                    k.dve("tensor_tensor", [pbk(bank), f"xres{t}"], [f"xres{t}"],
                          out=x_res[:, t, hf2 * 512:(hf2 + 1) * 512], in0=x_res[:, t, hf2 * 512:(hf2 + 1) * 512],
                          in1=pb(bank), op=ALU.add)

    load_gains([4])
    o_rot = Rot(k, "ob", 2, [128, D], F32)
    for t in range(NT):
        ob, obk = o_rot.next()
        ss, sskey = ss_rot.next()
        sq, sqkey = sq_rot.next()
        k.act([f"xres{t}"], [sqkey, sskey], out=sq[:], in_=x_res[:, t, :], func=AF.Square, accum_out=ss[:, 0:1])
        rstd_from_ss(ss, sskey, 1, 1.0 / D)
        k.dve("scalar_tensor_tensor", [f"xres{t}", sskey, gains[4][1]], [obk], out=ob[:], in0=x_res[:, t, :], scalar=ss[:, 0:1],
              in1=gains[4][0], op0=ALU.mult, op1=ALU.mult)
        k.dma("sp", out_t[t], ob[:], [obk], [])
    S.emit()
    k.pop()
    k.pop()
    k.st.close()
    return nc


def make_inputs(inp, dbg=None):
    f = np.float32
    x = np.asarray(inp["x"], f)
    mem = np.asarray(inp["mem"], f)

    def row_bc(v):
        return np.broadcast_to(np.asarray(v, f).reshape(1, -1), (128, np.asarray(v).size))

    gbc = np.stack([row_bc(inp["norm_mix_g"][0]), row_bc(inp["norm_xattn_g"][0]), row_bc(inp["norm_mem_g"][0]),
                    row_bc(inp["norm_ffn_g"][0]), row_bc(inp["norm_final_g"]), row_bc(inp["ssd_norm_g"][0])], axis=1)
    dbc = row_bc(np.repeat(np.asarray(inp["ssd_D"][0], f), 64))
    small = np.zeros((128, 64), f)
    small[:, 0:16] = row_bc(inp["ssd_dt_bias"][0])
    small[:, 16:32] = row_bc(inp["ssd_A_log"][0])
    cw = np.asarray(inp["ssd_conv_w"][0], f).reshape(4, 12, 128).transpose(2, 1, 0)
    cb = np.asarray(inp["ssd_conv_b"][0], f).reshape(12, 128).T
    fw = np.asarray(inp["cf_conv_w"][0], f).reshape(31, 8, 128).transpose(2, 1, 0)
    fv = np.stack([np.asarray(inp[n][0], f).reshape(8, 128).T for n in ("cf_conv_b", "cf_ln_g", "cf_ln_b")], axis=2)
    ident = np.eye(128, dtype=f).astype(ml_dtypes.bfloat16)
    j = np.arange(128)
    tri = (j[:, None] <= j[None, :]).astype(f)
    su = (j[:, None] > j[None, :]).astype(f)
    cst = np.stack([tri, su, np.ones((128, 128), f), tri], axis=1)
    common = {
        "w_in": np.ascontiguousarray(inp["w_in"][0], f), "w_out": np.ascontiguousarray(inp["w_out"][0], f),
        "w_q": np.ascontiguousarray(inp["w_q"][0], f), "w_kv": np.ascontiguousarray(inp["w_kv"][0], f),
        "w_o": np.ascontiguousarray(inp["w_o"][0], f), "w_gate": np.ascontiguousarray(inp["w_gate"][0], f),
        "w_up": np.ascontiguousarray(inp["w_up"][0], f), "w_down": np.ascontiguousarray(inp["w_down"][0], f),
        "gbc": np.ascontiguousarray(gbc), "dbc": np.ascontiguousarray(dbc), "small": small,
        "cw": np.ascontiguousarray(cw), "cb": np.ascontiguousarray(cb), "fw": np.ascontiguousarray(fw),
        "fv": np.ascontiguousarray(fv), "ident": ident, "cst": np.ascontiguousarray(cst),
    }
    maps = []
    for c in range(8):
        b, hf = c // 2, c % 2
        d = dict(common)
        d["x_own"] = np.ascontiguousarray(x[b, hf * HALF:(hf + 1) * HALF])
        d["x_prev"] = np.ascontiguousarray(x[b, 0:HALF]) if hf == 1 else np.zeros((HALF, D), f)
        d["flag"] = np.full((128, 1), float(hf), f)
        d["mem"] = np.ascontiguousarray(mem[b])
        maps.append(d)
    return maps


_NC_CACHE = {}


def kernel(_dbg=None, **inputs):
    if _dbg not in _NC_CACHE:
        _NC_CACHE[_dbg] = build(_dbg)
    nc = _NC_CACHE[_dbg]
    maps = make_inputs(inputs, _dbg)
    res = run_bass_kernel_spmd(nc, maps, core_ids=list(range(8)))
    out = np.zeros((NB, SEQ, D), np.float32)
    for c in range(8):
        b, hf = c // 2, c % 2
        out[b, hf * HALF:(hf + 1) * HALF] = res.results[c]["out"]
    return out
```

```python
import contextlib
import os
import numpy as np
import ml_dtypes
import concourse.bass as bass
import concourse.mybir as mybir
from concourse.bass_utils import run_bass_kernel_spmd

F32 = mybir.dt.float32
BF16 = mybir.dt.bfloat16
ALU = mybir.AluOpType
AF = mybir.ActivationFunctionType

ENGS = ["pe", "dve", "act", "pool", "sp"]
SEG_CFG = {0: dict(lat=0.5, tsw=1.2), 1: dict(lat=0.2, tsw=1.0, seed=979871, jit=0.0),
           2: dict(lat=0.7, tsw=1.0, seed=396330, jit=0.05), 3: dict(lat=0.7, tsw=0.8, seed=362957, jit=0.0)}

D = 1024
SEQ = 4096
NB = 4
HALF = 2048
NT = 16
MT = 512
MEM = 256
INW = 4624
DFF = 2816
NFF = 22
EPS = 1e-6
XBC0 = 1024
DT0 = 2560
A0 = 2576
G0 = 3600


class Sched:
    LAT = 0.5
    DELTA_DEFAULT = 0.5

    def __init__(self, nc, n_dma_sems=4):
        self.nc = nc
        self.ops = []
        self.seg = 0
        self.n_dma = n_dma_sems
        self._stream = {}
        self._seg_counter = 0

    def barrier(self):
        self.seg += 1

    def add(self, eng, fn, reads=(), writes=(), dma=False, cost=1.0, tset=None):
        self.ops.append(dict(eng=eng, fn=fn, reads=tuple(reads), writes=tuple(writes), dma=int(dma), cost=float(cost),
                             seg=self.seg, idx=len(self.ops), tset=tset))

    def _schedule_segment(self, ops):
        n = len(ops)
        cfg = dict(lat=self.LAT, tsw=1.4, seed=0, jit=0.0)
        cfg.update(SEG_CFG.get(self._seg_counter, {}))
        env = os.environ.get("SCHED_CFG")
        if env:
            import json as _json
            cfg.update(_json.loads(env))
        self._seg_counter += 1
        LATV = cfg["lat"]
        rng = np.random.default_rng(cfg["seed"]) if cfg["jit"] > 0 else None
        DMAF = float(os.environ.get("DMAF", "0.75"))
        PEW = float(os.environ.get("PEW", "1.0"))
        preds = [set() for _ in range(n)]
        raw = [set() for _ in range(n)]
        last_writer, readers = {}, {}
        for i, op in enumerate(ops):
            for r in op["reads"]:
                lw = last_writer.get(r)
                if lw is not None:
                    preds[i].add(lw)
                    raw[i].add(lw)
            for w in op["writes"]:
                lw = last_writer.get(w)
                if lw is not None:
                    preds[i].add(lw)
                for rd in readers.get(w, ()):
                    if rd != i:
                        preds[i].add(rd)
            for r in op["reads"]:
                readers.setdefault(r, []).append(i)
            for w in op["writes"]:
                last_writer[w] = i
                readers[w] = []
        succs = [[] for _ in range(n)]
        for i in range(n):
            for p in preds[i]:
                succs[p].append(i)

        def dur(op):
            return max(0.06, op["cost"] * DMAF) if op["dma"] else op["cost"]

        def done_lat(op):
            return (2.0 + op["cost"]) if op["dma"] else op["cost"]

        prio = [0.0] * n
        for i in range(n - 1, -1, -1):
            m = 0.0
            for sidx in succs[i]:
                if prio[sidx] > m:
                    m = prio[sidx]
            prio[i] = m + done_lat(ops[i]) * (PEW if ops[i]["eng"] == "pe" else 1.0)
        if rng is not None:
            jit = rng.random(n)
            prio = [p * (1.0 + cfg["jit"] * (j - 0.5)) for p, j in zip(prio, jit)]
        npred = [len(preds[i]) for i in range(n)]
        ready_t = [0.0] * n
        finish = [0.0] * n
        free = {e: 0.0 for e in ENGS}
        ready = {e: [] for e in ENGS}
        for i in range(n):
            if npred[i] == 0:
                ready[ops[i]["eng"]].append(i)
        order = {e: [] for e in ENGS}
        left = n
        cur_set = [None]
        TSW = cfg["tsw"]
        DELTA = float(os.environ.get('SCHED_DELTA', self.DELTA_DEFAULT))
        while left:
            best = None
            for e in ENGS:
                if not ready[e]:
                    continue
                fe = free[e]
                cand = None
                stts = {}
                mn = None
                for i in ready[e]:
                    stt = ready_t[i] if ready_t[i] > fe else fe
                    if e == "act" and ops[i]["tset"] is not None and cur_set[0] is not None and ops[i]["tset"] != cur_set[0]:
                        stt = stt + TSW
                    stts[i] = stt
                    if mn is None or stt < mn:
                        mn = stt
                for i in ready[e]:
                    stt = stts[i]
                    if stt > mn + DELTA:
                        continue
                    key = (-prio[i], stt, i)
                    if cand is None or key < cand[0]:
                        cand = (key, i, stt)
                if best is None or cand[2] < best[2] - 1e-9 or (abs(cand[2] - best[2]) <= 1e-9 and cand[0] < best[0]):
                    best = cand
            _, i, stt = best
            op = ops[i]
            e = op["eng"]
            ready[e].remove(i)
            if e == "act" and op["tset"] is not None:
                cur_set[0] = op["tset"]
            free[e] = stt + dur(op)
            finish[i] = stt + done_lat(op)
            order[e].append(i)
            left -= 1
            for sidx in succs[i]:
                rt = finish[i] + (LATV if ops[sidx]["eng"] != e or op["dma"] else 0.0)
                if rt > ready_t[sidx]:
                    ready_t[sidx] = rt
                npred[sidx] -= 1
                if npred[sidx] == 0:
                    ready[ops[sidx]["eng"]].append(sidx)
        return order, preds, raw

    def emit(self, final_eng="sp"):
        nc = self.nc
        nseg = self.seg + 1
        segs = [[] for _ in range(nseg)]
        for op in self.ops:
            segs[op["seg"]].append(op)
        gorder = {e: [] for e in ENGS}
        dma_val = {}
        dma_rr = {e: 0 for e in ENGS}
        known_pos = {e: {e2: -1 for e2 in ENGS} for e in ENGS}
        known_dma = {e: {} for e in ENGS}
        pending_pos = {e: {} for e in ENGS}
        pending_dma = {e: {} for e in ENGS}
        for sops in segs:
            if not sops:
                continue
            order, preds, raw = self._schedule_segment(sops)
            rec = {}
            for e in ENGS:
                for i in order[e]:
                    op = sops[i]
                    r = dict(op=op, eng=e, pos=None, signal=False, waits_pos={}, waits_dma={}, dma_tok=None)
                    if op["dma"]:
                        j = dma_rr[e]
                        dma_rr[e] = (j + 1) % (self.n_dma if e == "sp" else 3)
                        key = ("dma", e, j)
                        prev = dma_val.get(key, 0)
                        val = prev + 16 * op["dma"]
                        dma_val[key] = val
                        r["dma_tok"] = (key, val)
                        r["dma_prev"] = (key, prev) if prev > 0 else None
                    else:
                        r["pos"] = len(gorder[e])
                        gorder[e].append(r)
                    rec[i] = r
            for e in ENGS:
                for i in order[e]:
                    r = rec[i]
                    op = sops[i]
                    wp, wd = {}, {}
                    for e2, p in pending_pos[e].items():
                        if known_pos[e][e2] < p:
                            wp[e2] = p
                    for kk, vv in pending_dma[e].items():
                        if known_dma[e].get(kk, 0) < vv:
                            wd[kk] = vv
                    pending_pos[e] = {}
                    pending_dma[e] = {}
                    for p in preds[i]:
                        pr = rec[p]
                        if pr["dma_tok"] is not None:
                            kk, vv = pr["dma_tok"]
                            if known_dma[e].get(kk, 0) < vv and wd.get(kk, 0) < vv:
                                wd[kk] = vv
                        elif pr["eng"] != e or e != "pe":
                            e2 = pr["eng"]
                            if known_pos[e][e2] < pr["pos"] and wp.get(e2, -1) < pr["pos"]:
                                wp[e2] = pr["pos"]
                    if op["dma"] and r["dma_prev"] is not None:
                        kk, vv = r["dma_prev"]
                        if known_dma[e].get(kk, 0) < vv and wd.get(kk, 0) < vv:
                            wd[kk] = vv
                    for e2, p in wp.items():
                        known_pos[e][e2] = p
                        gorder[e2][p]["signal"] = True
                    for kk, vv in wd.items():
                        known_dma[e][kk] = vv
                    r["waits_pos"] = wp
                    r["waits_dma"] = wd
                    r["stream_eng"] = e
            for e in ENGS:
                for e2 in ENGS:
                    if e2 != e and gorder[e2]:
                        pending_pos[e][e2] = len(gorder[e2]) - 1
                        gorder[e2][-1]["signal"] = True
                for kk, vv in dma_val.items():
                    pending_dma[e][kk] = vv
            self._last_rec = rec
            for e in ENGS:
                for i in order[e]:
                    self._stream.setdefault(e, []).append(rec[i])
        for e in ENGS:
            if gorder[e]:
                gorder[e][-1]["signal"] = True
        count = {e: 0 for e in ENGS}
        for e in ENGS:
            for r in gorder[e]:
                if r["signal"]:
                    count[e] += 1
                    r["val"] = count[e]
        keys = [("eng", e) for e in ENGS] + sorted(dma_val.keys())
        with contextlib.ExitStack() as st:
            sems = {}
            for kk in keys:
                sems[kk] = st.enter_context(nc.semaphore("s_" + "_".join(str(x) for x in kk)))
            fin_waits = []
            for e in ENGS:
                if gorder[e] and e != final_eng:
                    fin_waits.append((("eng", e), count[e]))
            for kk, vv in dma_val.items():
                fin_waits.append((kk, vv))
            block = st.enter_context(nc.Block())
            streams = self._stream

            def run(ename):
                def body(eng):
                    for r in streams.get(ename, []):
                        for e2, p in r["waits_pos"].items():
                            eng.wait_ge(sems[("eng", e2)], gorder[e2][p]["val"])
                        for kk, vv in r["waits_dma"].items():
                            eng.wait_ge(sems[kk], vv)
                        res = r["op"]["fn"](eng)
                        if r["op"]["dma"]:
                            if not isinstance(res, (list, tuple)):
                                res = [res]
                            assert len(res) == int(r["op"]["dma"])
                            for ins in res:
                                ins.then_inc(sems[r["dma_tok"][0]], 16)
                        elif r["signal"]:
                            if isinstance(res, (list, tuple)):
                                res = res[-1]
                            res.then_inc(sems[("eng", ename)], 1)
                    if ename == final_eng:
                        for kk, vv in fin_waits:
                            eng.wait_ge(sems[kk], vv)
                return body

            block.tensor(run("pe"))
            block.vector(run("dve"))
            block.scalar(run("act"))
            block.gpsimd(run("pool"))
            block.sync(run("sp"))


class Rot:
    def __init__(self, K, name, n, shape, dt):
        self.bufs = [K.sb(f"{name}{i}", shape, dt) for i in range(n)]
        self.keys = [f"{name}{i}" for i in range(n)]
        self.i = 0

    def next(self):
        j = self.i
        self.i = (j + 1) % len(self.bufs)
        return self.bufs[j], self.keys[j]


class K:
    def __init__(self, dbg=None):
        self.dbg = dbg
        self.nc = bass.Bass("TRN2", target_bir_lowering=False)
        self.st = contextlib.ExitStack()
        self.stacks = [self.st]
        self.S = Sched(self.nc)
        self.dram = {}

    def din(self, name, shape, dt=F32):
        ap = self.nc.dram_tensor(name, list(shape), dt, kind="ExternalInput").ap()
        self.dram[name] = ap
        return ap

    def sb(self, name, shape, dt):
        return self.stacks[-1].enter_context(self.nc.sbuf_tensor("sb_" + name, list(shape), dt))

    def push(self):
        self.stacks.append(contextlib.ExitStack())

    def pop(self):
        self.S.barrier()
        self.stacks.pop().close()

    @staticmethod
    def _fs(ap):
        try:
            return float(ap.free_size())
        except Exception:
            return 512.0

    TSETS = {AF.Silu: "silu", AF.Exp: "lnexp", AF.Ln: "lnexp", AF.Sigmoid: "sigm"}

    def act(self, r, w, **kw):
        c = 0.25 + self._fs(kw["out"]) / 1200.0
        self.S.add("act", lambda e: e.activation(**kw), r, w, cost=c, tset=self.TSETS.get(kw["func"]))

    def dve(self, opname, r, w, **kw):
        o = kw.get("out", kw.get("ap"))
        c = 0.08 + self._fs(o) / 960.0
        self.S.add("dve", lambda e: getattr(e, opname)(**kw), r, w, cost=c)

    def pool(self, opname, r, w, **kw):
        o = kw.get("out", kw.get("ap"))
        c = 0.15 + self._fs(o) / 500.0
        self.S.add("pool", lambda e: getattr(e, opname)(**kw), r, w, cost=c)

    def mm(self, lst, r, w):
        def f(e):
            ins = None
            for (o, l, rh, st, sp) in lst:
                ins = e.matmul(out=o, lhsT=l, rhs=rh, start=st, stop=sp)
            return ins
        c = 0.1
        for (o, l, rh, st, sp) in lst:
            n = self._fs(o)
            c += max(0.065, n / 2400.0) * (4.0 if l.dtype == F32 else 1.0)
        self.S.add("pe", f, r, w, cost=c)

    def tr(self, lst, ident, r, w):
        def f(e):
            ins = None
            for (o, i) in lst:
                ins = e.transpose(out=o, in_=i, identity=ident)
            return ins
        self.S.add("pe", f, r, w, cost=0.1 + 0.11 * len(lst))

    def dma(self, eng, out, in_, r, w):
        try:
            nbytes = float(out.nbytes())
        except Exception:
            nbytes = 5e5
        self.S.add(eng, lambda e: e.dma_start(out=out, in_=in_), r, w, dma=1, cost=nbytes / 1.5e5)


def bc(ap, shape):
    return ap.to_broadcast(list(shape))


import os
LVL = int(os.environ.get("SSD_LVL", "9"))


def build(dbg=None):
    k = K(dbg)
    nc = k.nc
    S = k.S
    x_own = k.din("x_own", [HALF, D])
    x_prev = k.din("x_prev", [HALF, D])
    flag_d = k.din("flag", [128, 1])
    mem_d = k.din("mem", [MEM, D])
    w_in = k.din("w_in", [D, INW])
    w_out = k.din("w_out", [2 * D, D])
    w_q = k.din("w_q", [D, D])
    w_kv = k.din("w_kv", [D, 2 * D])
    w_o = k.din("w_o", [D, D])
    w_gate = k.din("w_gate", [D, DFF])
    w_up = k.din("w_up", [D, DFF])
    w_down = k.din("w_down", [DFF, D])
    gbc_d = k.din("gbc", [128, 6, D])
    dbc_d = k.din("dbc", [128, D])
    sm_d = k.din("small", [128, 64])
    cw_d = k.din("cw", [128, 12, 4])
    cb_d = k.din("cb", [128, 12])
    fw_d = k.din("fw", [128, 8, 31])
    fv_d = k.din("fv", [128, 8, 3])
    ident_d = k.din("ident", [128, 128], BF16)
    cst_d = k.din("cst", [128, 4, 128])
    out_d = nc.dram_tensor("out", [HALF, D], F32, kind="ExternalOutput").ap()
    xr_d = nc.dram_tensor("xr", [HALF, D], F32, kind="Internal").ap()

    xo_t = x_own.rearrange("(t p) d -> t p d", p=128)
    xp_t = x_prev.rearrange("(t p) d -> t p d", p=128)
    xr_t = xr_d.rearrange("(t p) d -> t p d", p=128)
    out_t = out_d.rearrange("(t p) d -> t p d", p=128)

    gains = {}

    def load_gains(idxs):
        k.gcount = getattr(k, "gcount", 0) + 1
        gb = k.sb(f"gains{k.gcount}", [128, len(idxs), D], F32)
        for i, gi in enumerate(idxs):
            key = f"gain{gi}_{k.gcount}"
            k.dma("sp", gb[:, i, :], gbc_d[:, gi, :], [], [key])
            gains[gi] = (gb[:, i, :], key)
    sm = k.sb("sm", [128, 64], F32)
    cw = k.sb("cw", [128, 12, 4], F32)
    cb = k.sb("cb", [128, 12], F32)
    fw = k.sb("fw", [128, 8, 31], F32)
    fv = k.sb("fv", [128, 8, 3], F32)
    ident = k.sb("ident", [128, 128], BF16)
    cst = k.sb("cst", [128, 4, 128], F32)
    flag = k.sb("flag", [128, 1], F32)
    epsT = k.sb("epsT", [128, 1], F32)
    ones_bf = k.sb("ones_bf", [128, 128], BF16)
    for (dst, src, nm) in [ (sm, sm_d, "sm"), (cw, cw_d, "cw"),
                           (cb, cb_d, "cb"), (fw, fw_d, "fw"), (fv, fv_d, "fv"), (ident, ident_d, "ident"),
                           (cst, cst_d, "cst"), (flag, flag_d, "flag")]:
        k.dma("sp", dst[:], src, [], [nm])
    k.dve("memset", [], ["epsT"], ap=epsT[:], constant=EPS)
    k.dve("memset", [], ["ones_bf"], ap=ones_bf[:], constant=1.0)

    PS = k.st.enter_context(nc.psum_tensor("PS", [128, 8, 512], F32))

    def pb(i):
        return PS[:, i, :]

    def pbk(i):
        return f"ps{i}"

    ss_rot = Rot(k, "ss", int(os.environ.get("SSROT", "4")), [128, 4], F32)
    sq_rot = Rot(k, "sq", 1, [128, D], BF16)
    h_rot = Rot(k, "hb", int(os.environ.get("HROT", "2")), [128, D], BF16)

    def rstd_from_ss(ss, sskey, n, inv_n):
        k.act([sskey, "epsT"], [sskey], out=ss[:, 0:n], in_=ss[:, 0:n], func=AF.Ln, scale=inv_n, bias=epsT[:])
        k.act([sskey], [sskey], out=ss[:, 0:n], in_=ss[:, 0:n], func=AF.Exp, scale=-0.5)

    def norm_tile(x_ap, xkey, gidx, h_out, hkey):
        ss, sskey = ss_rot.next()
        sq, sqkey = sq_rot.next()
        k.act([xkey], [sqkey, sskey], out=sq[:], in_=x_ap, func=AF.Square, accum_out=ss[:, 0:1])
        rstd_from_ss(ss, sskey, 1, 1.0 / D)
        k.dve("scalar_tensor_tensor", [xkey, sskey, gains[gidx][1]], [hkey], out=h_out, in0=x_ap, scalar=ss[:, 0:1],
              in1=gains[gidx][0], op0=ALU.mult, op1=ALU.mult)

    def transpose_to(h_ap, hkey, dstT, dkey, col0, bank):
        pT = pb(bank).bitcast(BF16)
        k.tr([(pT[:, kc * 128:(kc + 1) * 128], h_ap[:, kc * 128:(kc + 1) * 128]) for kc in range(8)], ident[:],
             [hkey, "ident"], [pbk(bank)])
        k.act([pbk(bank)], [dkey], out=dstT[:, :, col0:col0 + 128], in_=pT.rearrange("p (a b) -> p a b", a=8),
              func=AF.Copy)

    x_res = None

    src_t = xo_t if dbg == "nomixer" else xr_t
    tri = cst[:, 0, :]
    su = cst[:, 1, :]
    onesf = cst[:, 2, :]
    mask01 = cst[:, 3, :]
    do_ssd = dbg in (None, "ssd")
    do_conf = dbg in (None, "conf")

    if do_ssd:
        k.push()
        load_gains([0, 5])
        dbc = k.sb("dbc", [128, D], F32)
        k.dma("sp", dbc[:], dbc_d, [], ["dbc"])
        xt_rot = Rot(k, "xta", int(os.environ.get("XTA", "3")), [128, D], F32)
        xb3_rot = Rot(k, "xtc", int(os.environ.get("XTC", "1")), [128, D], F32)
        Wa = k.sb("Wa", [128, 8, 2576], BF16)
        Wot = k.sb("Wot", [128, 8, D], BF16)
        w_in_v = w_in.rearrange("(kc p) n -> p kc n", p=128)
        for q in range(4):
            k.dma("pool", Wa[:, :, 1024 + 384 * q:1024 + 384 * (q + 1)], w_in_v[:, :, 1024 + 384 * q:1024 + 384 * (q + 1)], [], [f"Wa_x{q}"])
        k.dma("pool", Wa[:, :, 2560:2576], w_in_v[:, :, 2560:2576], [], ["Wa_dt"])
        k.dma("pool", Wa[:, :, 0:1024], w_in_v[:, :, 0:1024], [], ["Wa_z"])
        k.dma("pool", Wot[:], w_out[0:D, :].rearrange("(kc p) n -> p kc n", p=128), [], ["Wot"])
        diag4 = k.sb("diag4", [128, 12, 4, 128], BF16)
        for c in range(12):
            k.dve("tensor_tensor", ["ident", "cw"], [f"diag4_{c}"], out=diag4[:, c, :, :], in0=bc(ident[:].unsqueeze(1), [128, 4, 128]),
                  in1=bc(cw[:, c, :].unsqueeze(2), [128, 4, 128]), op=ALU.mult)
        hT_r = Rot(k, "hT1_", 2, [128, 8, MT], BF16)
        xbcT_r = Rot(k, "xbcT_", 2, [128, 12, MT], BF16)
        dt4_r = Rot(k, "dt4_", 2, [128, 4, 16], F32)
        dA4_r = Rot(k, "dA4_", 2, [128, 4, 16], F32)
        sz_r = Rot(k, "sz_", 2, [128, 4, D], BF16)
        pre_rot = Rot(k, "pre", int(os.environ.get("PRE", "3")), [128, 515], BF16)
        hist = k.sb("hist", [128, 12, 3], BF16)
        dtp = k.sb("dtp", [128, 4, 16], F32)
        nA = k.sb("nA", [128, 16], F32)
        ex3_r = Rot(k, "ex3_", 2, [128, 48], F32)
        xdt_r = Rot(k, "xdt_", 2, [128, D], BF16)
        xD_r = Rot(k, "xD_", 1, [128, D], BF16)
        Btok_r = Rot(k, "Btok_", 2, [128, 256], BF16)
        E_r = Rot(k, "E_", 2, [128, 16, 128], BF16)
        CBm_r = Rot(k, "CBm_", 2, [128, 2, 128], BF16)
        SUhi = k.sb("SUhi", [128, 8, 128], BF16)
        SUlo = k.sb("SUlo", [128, 8, 128], BF16)
        su_bf = k.sb("su_bf", [128, 128], BF16)
        tri_bf = k.sb("tri_bf", [128, 128], BF16)
        dAh = k.sb("dAh", [128, 16], BF16)
        dAh32 = k.sb("dAh32", [128, 16], F32)
        dAl = k.sb("dAl", [128, 16], F32)
        k.dve("tensor_copy", ["cst"], ["su_bf"], out=su_bf[:], in_=cst[:, 1, :])
        k.dve("tensor_copy", ["cst"], ["tri_bf"], out=tri_bf[:], in_=cst[:, 0, :])
        xw_r = Rot(k, "xw_", 2, [128, D], BF16)
        y1 = k.sb("y1", [128, D], F32)
        yn = k.sb("yn", [128, D], BF16)
        ynT = k.sb("ynT", [128, 8, 128], BF16)
        Sst = k.sb("Sst", [128, D], F32)
        Sbf = k.sb("Sbf", [128, D], BF16)
        k.dve("memset", [], ["hist"], ap=hist[:], constant=0.0)
        k.dve("memset", [], ["Sst"], ap=Sst[:], constant=0.0)
        k.dve("memset", [], ["Sbf"], ap=Sbf[:], constant=0.0)
        k.act(["sm"], ["nA"], out=nA[:], in_=sm[:, 16:32], func=AF.Exp)
        k.dve("tensor_scalar", ["nA"], ["nA"], out=nA[:], in0=nA[:], scalar1=-1.0, scalar2=None, op0=ALU.mult)
        ATB = int(os.environ.get("ATB", "2"))
        YTB = int(os.environ.get("YTB", "0"))
        pT0 = pb(ATB).bitcast(BF16)
        pTy = pb(YTB).bitcast(BF16)
        G1 = gains[5]

        mt_state = {}

        def src_tile(g_mt, st):
            return (xp_t if g_mt < 4 else xo_t)[(g_mt % 4) * 4 + st]

        mt_h = {}

        def F1(mt, pair):
            if pair == 0:
                mt_h[mt] = hT_r.next()
            hT, hTk = mt_h[mt]
            items = []
            for st in (2 * pair, 2 * pair + 1):
                xt, xtk = xt_rot.next()
                k.dma("sp", xt[:], src_tile(mt, st), [], [xtk])
                ss, sskey = ss_rot.next()
                sq, sqkey = sq_rot.next()
                k.act([xtk], [sqkey, sskey], out=sq[:], in_=xt[:], func=AF.Square, accum_out=ss[:, 0:1])
                rstd_from_ss(ss, sskey, 1, 1.0 / D)
                items.append((st, xt, xtk, ss, sskey))
            hbs = []
            for (st, xt, xtk, ss, sskey) in items:
                hb, hk = h_rot.next()
                k.dve("scalar_tensor_tensor", [xtk, sskey, gains[0][1]], [hk], out=hb[:], in0=xt[:], scalar=ss[:, 0:1],
                      in1=gains[0][0], op0=ALU.mult, op1=ALU.mult)
                hbs.append((st, hb, hk))
            F1B = int(os.environ.get("F1B", "0"))
            for i, (st, hb, hk) in enumerate(hbs):
                pT = pb(F1B + i).bitcast(BF16)
                k.tr([(pT[:, kc * 128:(kc + 1) * 128], hb[:, kc * 128:(kc + 1) * 128]) for kc in range(8)], ident[:],
                     [hk, "ident"], [pbk(F1B + i)])
            for i, (st, hb, hk) in enumerate(hbs):
                pT = pb(F1B + i).bitcast(BF16)
                k.act([pbk(F1B + i)], [hTk], out=hT[:, :, st * 128:(st + 1) * 128], in_=pT.rearrange("p (a b) -> p a b", a=8), func=AF.Copy)

        def F2(mt, q):
            hT, hTk = mt_h[mt]
            if q == 0:
                xbcT, xbk = xbcT_r.next()
                dt4, dt4k = dt4_r.next()
                dA4, dA4k = dA4_r.next()
                szm, szk = sz_r.next()
                mt_state[mt] = (hT, hTk, xbcT, xbk, dt4, dt4k, dA4, dA4k, szm, szk)
            hT, hTk, xbcT, xbk, dt4, dt4k, dA4, dA4k, szm, szk = mt_state[mt]
            pres = {}

            def inproj(c):
                b1 = c % 2
                k.mm([(pb(b1), Wa[:, kc, 1024 + c * 128:1024 + (c + 1) * 128], hT[:, kc, :], kc == 0, kc == 7)
                      for kc in range(8)], [f"Wa_x{c // 3}", hTk], [pbk(b1)])
                pre, prek = pre_rot.next()
                pres[c] = (pre, prek)
                k.act([pbk(b1)], [prek], out=pre[:, 3:515], in_=pb(b1), func=AF.Copy)
                k.dve("tensor_copy", ["hist", prek], [prek], out=pre[:, 0:3], in_=hist[:, c, :])
                k.dve("tensor_copy", [prek], ["hist"], out=hist[:, c, :], in_=pre[:, 512:515])

            def conv(c):
                pre, prek = pres[c]
                b2 = 2 + c % 2
                k.mm([(pb(b2), diag4[:, c, kk, :], pre[:, kk:kk + 512], kk == 0, kk == 3) for kk in range(4)],
                     [f"diag4_{c}", prek], [pbk(b2)])
                k.act([pbk(b2), "cb"], [f"{xbk}_{c}"], out=xbcT[:, c, :], in_=pb(b2), func=AF.Silu, bias=cb[:, c:c + 1], scale=1.0)

            c0 = 3 * q
            skipC = (mt < 3 and q == 3)
            inproj(c0)
            if not skipC:
                inproj(c0 + 1)
            conv(c0)
            if not skipC:
                inproj(c0 + 2)
                conv(c0 + 1)
            if mt >= 4:
                st = q
                k.mm([(pb(hf2), hT[:, kc, st * 128:(st + 1) * 128], Wa[:, kc, hf2 * 512:(hf2 + 1) * 512], kc == 0, kc == 7)
                      for hf2 in range(2) for kc in range(8)], ["Wa_z", hTk], [pbk(0), pbk(1)])
            if not skipC:
                conv(c0 + 2)
            if mt >= 4:
                k.act([pbk(0), pbk(1)], [szk], out=szm[:, q, :].rearrange("p (a b) -> p a b", a=2), in_=PS[:, 0:2, :], func=AF.Silu)
            if q == 3:
                for st in range(4):
                    k.mm([(pb(1)[:, st * 16:(st + 1) * 16], hT[:, kc, st * 128:(st + 1) * 128], Wa[:, kc, 2560:2576], kc == 0, kc == 7)
                          for kc in range(8)], ["Wa_dt", hTk], [pbk(1)])
                k.dve("tensor_tensor", [pbk(1), "sm"], ["dtp"], out=dtp[:], in0=pb(1)[:, 0:64].rearrange("p (a b) -> p a b", a=4),
                      in1=bc(sm[:, 0:16].unsqueeze(1), [128, 4, 16]), op=ALU.add)
                k.act(["dtp"], ["dtp"], out=dtp[:], in_=dtp[:], func=AF.Exp)
                k.act(["dtp", "cst"], [dt4k], out=dt4[:], in_=dtp[:], func=AF.Ln, bias=cst[:, 2, 0:1], scale=1.0)
                k.dve("tensor_tensor", [dt4k, "nA"], [dA4k], out=dA4[:], in0=dt4[:], in1=bc(nA[:].unsqueeze(1), [128, 4, 16]), op=ALU.mult)

        ch_state = {}

        def A(g):
            mt, st = g // 4, g % 4
            full = mt >= 4
            hT, hTk, xbcT, xbk, dt4, dt4k, dA4, dA4k, szm, szk = mt_state[mt]
            cs = slice(st * 128, (st + 1) * 128)
            xdt, xdtk = xdt_r.next()
            xD, xDk = xD_r.next()
            Btok, Btk = Btok_r.next()
            ex3, ex3k = ex3_r.next()
            E, Ek = E_r.next()
            CBm, CBk = CBm_r.next()
            xw, xwk = xw_r.next()
            ch_state[g] = (xdt, xdtk, xD, xDk, Btok, Btk, ex3, ex3k, E, Ek, xw, xwk)
            k.tr([(pT0[:, c * 128:(c + 1) * 128], xbcT[:, c, cs]) for c in range(8)], ident[:],
                 [f"{xbk}_{c}" for c in range(8)] + ["ident"], [pbk(ATB)])
            k.dve("tensor_tensor", [pbk(ATB), dt4k], [xdtk], out=xdt[:].rearrange("p (h q) -> p h q", h=16),
                  in0=pT0.rearrange("p (h q) -> p h q", h=16), in1=bc(dt4[:, st, :].unsqueeze(2), [128, 16, 64]), op=ALU.mult)
            if full:
                k.dve("tensor_tensor", [pbk(ATB), "dbc"], [xDk], out=xD[:], in0=pT0, in1=dbc[:], op=ALU.mult)
            k.tr([(pT0[:, gg * 128:(gg + 1) * 128], xbcT[:, 8 + gg, cs]) for gg in range(2)], ident[:],
                 [f"{xbk}_8", f"{xbk}_9", "ident"], [pbk(ATB)])
            k.act([pbk(ATB)], [Btk], out=Btok[:], in_=pT0[:, 0:256], func=AF.Copy)
            dA = dA4[:, st, :]
            k.mm([(pb(1)[:, 0:16], tri, dA, True, True), (pb(1)[:, 16:32], su, dA, True, True),
                  (pb(1)[:, 32:48], onesf, dA, True, True)], ["cst", dA4k], [pbk(1)])
            k.act([pbk(1)], [ex3k], out=ex3[:], in_=pb(1)[:, 0:48], func=AF.Exp)
            k.pool("tensor_tensor", [xdtk, ex3k], [xwk], out=xw[:].rearrange("p (h q) -> p h q", h=16),
                   in0=xdt[:].rearrange("p (h q) -> p h q", h=16), in1=bc(ex3[:, 16:32].unsqueeze(2), [128, 16, 64]), op=ALU.mult)
            if full:
                k.dve("tensor_copy", [dA4k], ["dAh"], out=dAh[:], in_=dA)
                k.dve("tensor_copy", ["dAh"], ["dAh32"], out=dAh32[:], in_=dAh[:])
                k.dve("tensor_tensor", [dA4k, "dAh32"], ["dAl"], out=dAl[:], in0=dA, in1=dAh32[:], op=ALU.subtract)
                for r in range(2):
                    k.dve("tensor_tensor", ["su_bf", "dAh"], ["SUhi"], out=SUhi[:], in0=bc(su_bf[:].unsqueeze(1), [128, 8, 128]),
                          in1=bc(dAh[:, r * 8:(r + 1) * 8].unsqueeze(2), [128, 8, 128]), op=ALU.mult)
                    k.dve("tensor_tensor", ["su_bf", "dAl"], ["SUlo"], out=SUlo[:], in0=bc(su_bf[:].unsqueeze(1), [128, 8, 128]),
                          in1=bc(dAl[:, r * 8:(r + 1) * 8].unsqueeze(2), [128, 8, 128]), op=ALU.mult)
                    for hq in range(2):
                        bank = 2 + hq
                        lst = []
                        for hh in range(4):
                            h = hq * 4 + hh
                            lst.append((pb(bank)[:, hh * 128:(hh + 1) * 128], SUhi[:, h, :], tri_bf[:], True, False))
                            lst.append((pb(bank)[:, hh * 128:(hh + 1) * 128], SUlo[:, h, :], tri_bf[:], False, True))
                        k.mm(lst, ["SUhi", "SUlo", "tri_bf"], [pbk(bank)])
                        k.act([pbk(bank)], [Ek], out=E[:, r * 8 + hq * 4:r * 8 + hq * 4 + 4, :],
                              in_=pb(bank).rearrange("p (a b) -> p a b", a=4), func=AF.Exp)
                k.mm([(pb(1)[:, 64 + gg * 128:64 + (gg + 1) * 128], xbcT[:, 8 + gg, cs], xbcT[:, 10 + gg, cs], True, True) for gg in range(2)],
                     [f"{xbk}_8", f"{xbk}_9", f"{xbk}_10", f"{xbk}_11"], [pbk(1)])
                k.dve("tensor_tensor", [pbk(1), "cst"], [CBk], out=CBm[:], in0=pb(1)[:, 64:320].rearrange("p (a b) -> p a b", a=2),
                      in1=bc(mask01.unsqueeze(1), [128, 2, 128]), op=ALU.mult)
                k.dve("tensor_tensor", [Ek, CBk], [Ek], out=E[:].rearrange("p (g r) l -> p g r l", g=2),
                      in0=E[:].rearrange("p (g r) l -> p g r l", g=2), in1=bc(CBm[:].unsqueeze(2), [128, 2, 8, 128]), op=ALU.mult)

        bst = {}

        def B1(g):
            mt, st = g // 4, g % 4
            hT, hTk, xbcT, xbk, dt4, dt4k, dA4, dA4k, szm, szk = mt_state[mt]
            xdt, xdtk, xD, xDk, Btok, Btk, ex3, ex3k, E, Ek, xw, xwk = ch_state[g]
            cs = slice(st * 128, (st + 1) * 128)
            t = (mt % 4) * 4 + st
            lst = []
            for hf2 in range(2):
                lst.append((pb(4 + hf2), ident[:], xD[:, hf2 * 512:(hf2 + 1) * 512], True, False))
            for h in range(16):
                lst.append((pb(4 + h // 8)[:, (h % 8) * 64:(h % 8 + 1) * 64], E[:, h, :], xdt[:, h * 64:(h + 1) * 64], False, h % 8 == 7))
            k.mm(lst, ["ident", xDk, Ek, xdtk], [pbk(4), pbk(5)])
            k.mm([(pb(6 + gg), xbcT[:, 10 + gg, cs], Sbf[:, gg * 512:(gg + 1) * 512], True, True) for gg in range(2)],
                 [f"{xbk}_10", f"{xbk}_11", "Sbf"], [pbk(6), pbk(7)])
            k.dve("tensor_tensor", [pbk(6), pbk(7), ex3k], ["y1"], out=y1[:].rearrange("p (h q) -> p h q", h=16),
                  in0=PS[:, 6:8, :].rearrange("p a (h q) -> p (a h) q", q=64), in1=bc(ex3[:, 0:16].unsqueeze(2), [128, 16, 64]), op=ALU.mult)
            k.dve("tensor_tensor", [pbk(4), pbk(5), "y1"], ["y1"], out=y1[:].rearrange("p (a b) -> p a b", a=2),
                  in0=PS[:, 4:6, :], in1=y1[:].rearrange("p (a b) -> p a b", a=2), op=ALU.add)
            k.dve("tensor_tensor", ["y1", szk], ["y1"], out=y1[:], in0=y1[:], in1=szm[:, st, :], op=ALU.mult)
            k.dve("tensor_tensor", ["y1", G1[1]], ["yn"], out=yn[:], in0=y1[:], in1=G1[0], op=ALU.mult)
            ss, sskey = ss_rot.next()
            sq, sqkey = sq_rot.next()
            for gg in range(2):
                k.act(["y1"], [sqkey, sskey], out=sq[:, 0:512], in_=y1[:, gg * 512:(gg + 1) * 512], func=AF.Square, accum_out=ss[:, gg:gg + 1])
            rstd_from_ss(ss, sskey, 2, 1.0 / 512)
            bst[g] = (ss, sskey, t)

        def B2(g):
            k.tr([(pTy[:, kc * 128:(kc + 1) * 128], yn[:, kc * 128:(kc + 1) * 128]) for kc in range(8)], ident[:], ["yn", "ident"], [pbk(YTB)])
            pv = pTy.rearrange("p (a b) -> p a b", a=8)
            for gg in range(2):
                k.act([pbk(YTB)], [f"ynT{gg}"], out=ynT[:, 4 * gg:4 * gg + 4, :], in_=pv[:, 4 * gg:4 * gg + 4, :], func=AF.Copy)

        def B3(g):
            ss, sskey, t = bst[g]
            for gg in range(2):
                k.mm([(pb(4 + 2 * gg + hf2), ynT[:, 4 * gg + kc, :], Wot[:, 4 * gg + kc, hf2 * 512:(hf2 + 1) * 512], kc == 0, kc == 3)
                      for hf2 in range(2) for kc in range(4)], ["Wot", f"ynT{gg}"], [pbk(4 + 2 * gg), pbk(5 + 2 * gg)])
            xt, xtk = xb3_rot.next()
            k.dma("sp", xt[:], xo_t[t], [], [xtk])
            for gg in range(2):
                k.dve("scalar_tensor_tensor", [pbk(4 + 2 * gg), pbk(5 + 2 * gg), sskey, xtk], [xtk], out=xt[:].rearrange("p (a b) -> p a b", a=2),
                      in0=PS[:, 4 + 2 * gg:6 + 2 * gg, :], scalar=ss[:, gg:gg + 1], in1=xt[:].rearrange("p (a b) -> p a b", a=2),
                      op0=ALU.mult, op1=ALU.add)
            k.dma("sp", xr_t[t], xt[:], [xtk], [f"xr{t}"])

        def C(g):
            xdt, xdtk, xD, xDk, Btok, Btk, ex3, ex3k, E, Ek, xw, xwk = ch_state[g]
            k.mm([(pb(2 + gg), Btok[:, gg * 128:(gg + 1) * 128], xw[:, gg * 512:(gg + 1) * 512], True, True) for gg in range(2)],
                 [Btk, xwk], [pbk(2), pbk(3)])
            k.dve("tensor_tensor", ["Sst", ex3k], ["Sst"], out=Sst[:].rearrange("p (h q) -> p h q", h=16),
                  in0=Sst[:].rearrange("p (h q) -> p h q", h=16), in1=bc(ex3[:, 32:48].unsqueeze(2), [128, 16, 64]), op=ALU.mult)
            k.dve("tensor_tensor", ["Sst", pbk(2), pbk(3)], ["Sst"], out=Sst[:].rearrange("p (a b) -> p a b", a=2),
                  in0=Sst[:].rearrange("p (a b) -> p a b", a=2), in1=PS[:, 2:4, :], op=ALU.add)
            k.act(["Sst"], ["Sbf"], out=Sbf[:], in_=Sst[:], func=AF.Copy)

        NG = 32
        F1(0, 0)
        F1(0, 1)
        for q in range(4):
            F2(0, q)
        F1(1, 0)
        F1(1, 1)
        A(0)
        for g in range(NG):
            mt, st = g // 4, g % 4
            own = mt >= 4
            if own:
                B1(g)
            if g < NG - 1:
                C(g)
            if g == 15:
                k.dve("tensor_scalar", ["Sst", "flag"], ["Sst"], out=Sst[:], in0=Sst[:], scalar1=flag[:, 0:1], scalar2=None, op0=ALU.mult)
                k.act(["Sst"], ["Sbf"], out=Sbf[:], in_=Sst[:], func=AF.Copy)
            if mt + 1 < 8:
                F2(mt + 1, st)
            if own:
                B2(g)
            if g + 1 < NG:
                A(g + 1)
            if own:
                B3(g)
            if st in (0, 2) and mt + 2 < 8:
                F1(mt + 2, st // 2)
        k.pop()

    if do_conf:
        k.push()
        base_t = xr_t if do_ssd else xo_t
        load_gains([0])
        xt_rot = Rot(k, "xtb", int(os.environ.get("XTB", "3")), [128, D], F32)
        xb_rot = Rot(k, "xbb", int(os.environ.get("XBB", "3")), [128, D], F32)
        Wag = k.sb("Wag", [128, 8, 2048], BF16)
        Wob = k.sb("Wob", [128, 8, D], BF16)
        w_in_v2 = w_in.rearrange("(kc p) n -> p kc n", p=128)
        for cp in range(4):
            k.dma("pool", Wag[:, :, cp * 256:(cp + 1) * 256], w_in_v2[:, :, A0 + cp * 256:A0 + (cp + 1) * 256], [], [f"Wag_a{cp}"])
            k.dma("pool", Wag[:, :, D + cp * 256:D + (cp + 1) * 256], w_in_v2[:, :, G0 + cp * 256:G0 + (cp + 1) * 256], [], [f"Wag_g{cp}"])
        k.dma("pool", Wob[:], w_out[D:2 * D, :].rearrange("(kc p) n -> p kc n", p=128), [], ["Wob"])
        NDV = int(os.environ.get("NDV", "10"))
        NPE = 31 - NDV
        diag = k.sb("diag", [128, 8, NPE, 128], BF16)
        DGP = os.environ.get("DGP", "dve")
        for c in range(8):
            on_pool = (DGP == "pool") or (DGP == "alt" and c % 2 == 1)
            (k.pool if on_pool else k.dve)("tensor_tensor", ["ident", "fw"], [f"diag{c}"], out=diag[:, c, :, :], in0=bc(ident[:].unsqueeze(1), [128, NPE, 128]),
                  in1=bc(fw[:, c, NDV:31].unsqueeze(2), [128, NPE, 128]), op=ALU.mult)
        hT2_r = Rot(k, "hT2_", 2, [128, 8, MT], BF16)
        unT = k.sb("unT", [128, 8, MT], BF16)
        uT_r = Rot(k, "uT_", 2, [128, 8, 30 + MT], BF16)
        sgm_rot = Rot(k, "sgm", 2, [128, MT], F32)
        cv_r = Rot(k, "cv_", 2, [128, 8, MT], BF16)
        cvsq_rot = Rot(k, "cvsq", 2, [128, MT], BF16)
        mean_r = Rot(k, "mean_", 1, [128, MT], F32)
        rstdv_r = Rot(k, "rstdv_", 1, [128, MT], F32)
        t1_rot = Rot(k, "t1", 2, [128, MT], F32)
        accv_rot = Rot(k, "accv", 2, [128, MT], F32)
        gl_i = [0]

        def glu_chunks(hT2, hT2k, uT, uTk, ncols, col0):
            for c in range(8):
                ba = gl_i[0] % 2
                bg = 2 + gl_i[0] % 2
                gl_i[0] += 1
                k.mm([(pb(ba)[:, 0:ncols], Wag[:, kc, c * 128:(c + 1) * 128], hT2[:, kc, col0:col0 + ncols], kc == 0, kc == 7)
                      for kc in range(8)], [f"Wag_a{c // 2}", hT2k], [pbk(ba)])
                k.mm([(pb(bg)[:, 0:ncols], Wag[:, kc, D + c * 128:D + (c + 1) * 128], hT2[:, kc, col0:col0 + ncols], kc == 0, kc == 7)
                      for kc in range(8)], [f"Wag_g{c // 2}", hT2k], [pbk(bg)])
                sgm, sgk = sgm_rot.next()
                k.act([pbk(bg)], [sgk], out=sgm[:, 0:ncols], in_=pb(bg)[:, 0:ncols], func=AF.Sigmoid)
                k.dve("tensor_tensor", [pbk(ba), sgk], [f"{uTk}_{c}"], out=uT[:, c, 30 + col0:30 + col0 + ncols], in0=pb(ba)[:, 0:ncols],
                      in1=sgm[:, 0:ncols], op=ALU.mult)

        uTp, uTpk = uT_r.next()
        k.dve("memset", [], [f"{uTpk}_{c}" for c in range(8)] + [f"{uTpk}_h"], ap=uTp[:], constant=0.0)
        hTp, hTpk = hT2_r.next()
        xt, xtk = xt_rot.next()
        k.dma("sp", xt[:], xp_t[NT - 1], [], [xtk])
        hb, hk = h_rot.next()
        norm_tile(xt[:], xtk, 0, hb[:], hk)
        transpose_to(hb, hk, hTp, hTpk, 384, 7)
        glu_chunks(hTp, hTpk, uTp, uTpk, 128, 384)
        for m in range(NT // 4):
            hT2, hT2k = hT2_r.next()
            uT, uTk = uT_r.next()
            cv, cvk = cv_r.next()
            mean, meank = mean_r.next()
            rstdv, rstdk = rstdv_r.next()
            k.dve("tensor_copy", [f"{uTpk}_{c}" for c in range(8)], [f"{uTk}_h"], out=uT[:, :, 0:30], in_=uTp[:, :, MT:MT + 30])
            for st in range(4):
                t = m * 4 + st
                xt, xtk = xt_rot.next()
                k.dma("sp", xt[:], xo_t[t], [], [xtk])
                hb, hk = h_rot.next()
                norm_tile(xt[:], xtk, 0, hb[:], hk)
                transpose_to(hb, hk, hT2, hT2k, st * 128, 7)
            glu_chunks(hT2, hT2k, uT, uTk, MT, 0)
            for c in range(8):
                bcv = 4 + c % 2
                k.mm([(pb(bcv), diag[:, c, kk - NDV, :], uT[:, c, kk:kk + MT], kk == NDV, kk == 30) for kk in range(NDV, 31)],
                     [f"diag{c}", f"{uTk}_{c}", f"{uTk}_h"], [pbk(bcv)])
                accv, acck = accv_rot.next()
                k.dve("tensor_scalar", [f"{uTk}_{c}", f"{uTk}_h", "fw", "fv"], [acck], out=accv[:], in0=uT[:, c, 0:MT], scalar1=fw[:, c, 0:1],
                      scalar2=fv[:, c, 0:1], op0=ALU.mult, op1=ALU.add)
                for kk in range(1, NDV):
                    k.dve("scalar_tensor_tensor", [f"{uTk}_{c}", f"{uTk}_h", "fw", acck], [acck], out=accv[:], in0=uT[:, c, kk:kk + MT],
                          scalar=fw[:, c, kk:kk + 1], in1=accv[:], op0=ALU.mult, op1=ALU.add)
                k.dve("tensor_tensor", [pbk(bcv), acck], [f"{cvk}_{c}"], out=cv[:, c, :], in0=pb(bcv), in1=accv[:], op=ALU.add)
                cvsq, cqk = cvsq_rot.next()
                k.act([f"{cvk}_{c}"], [cqk], out=cvsq[:], in_=cv[:, c, :], func=AF.Square)
                k.mm([(pb(6), ones_bf[:], cv[:, c, :], c == 0, c == 7)], ["ones_bf", f"{cvk}_{c}"], [pbk(6)])
                k.mm([(pb(7), ones_bf[:], cvsq[:], c == 0, c == 7)], ["ones_bf", cqk], [pbk(7)])
            k.dve("tensor_scalar", [pbk(6)], [meank], out=mean[:], in0=pb(6), scalar1=1.0 / D, scalar2=None, op0=ALU.mult)
            k.dve("tensor_tensor", [meank], [rstdk], out=rstdv[:], in0=mean[:], in1=mean[:], op=ALU.mult)
            k.dve("scalar_tensor_tensor", [pbk(7), rstdk], [rstdk], out=rstdv[:], in0=pb(7), scalar=1.0 / D, in1=rstdv[:],
                  op0=ALU.mult, op1=ALU.subtract)
            k.act([rstdk, "epsT"], [rstdk], out=rstdv[:], in_=rstdv[:], func=AF.Ln, bias=epsT[:], scale=1.0)
            k.act([rstdk], [rstdk], out=rstdv[:], in_=rstdv[:], func=AF.Exp, scale=-0.5)
            for c in range(8):
                t1, t1k = t1_rot.next()
                lnp = os.environ.get("LNP", "0")
                e1 = k.pool if (lnp == "all" or (lnp == "half" and c % 2 == 1)) else k.dve
                e1("tensor_tensor", [f"{cvk}_{c}", meank], [t1k], out=t1[:], in0=cv[:, c, :], in1=mean[:], op=ALU.subtract)
                e1("tensor_tensor", [t1k, rstdk], [t1k], out=t1[:], in0=t1[:], in1=rstdv[:], op=ALU.mult)
                k.act([t1k, "fv"], [f"unT{c}"], out=unT[:, c, :], in_=t1[:], func=AF.Silu, scale=fv[:, c, 1:2], bias=fv[:, c, 2:3])
            for st in range(4):
                t = m * 4 + st
                ob = 2 * (st % 2)
                OPS = int(os.environ.get("OPS", "4"))
                per = 8 // OPS
                for part in range(OPS):
                    kcs = range(part * per, (part + 1) * per)
                    k.mm([(pb(ob + hf2), unT[:, kc, st * 128:(st + 1) * 128], Wob[:, kc, hf2 * 512:(hf2 + 1) * 512], kc == 0, kc == 7)
                          for hf2 in range(2) for kc in kcs], ["Wob"] + [f"unT{c}" for c in kcs], [pbk(ob), pbk(ob + 1)])
                xt, xtk = xb_rot.next()
                k.dma("sp", xt[:], base_t[t], [f"xr{t}"], [xtk])
                k.dve("tensor_tensor", [pbk(ob), pbk(ob + 1), xtk], [xtk], out=xt[:].rearrange("p (a b) -> p a b", a=2),
                      in0=xt[:].rearrange("p (a b) -> p a b", a=2), in1=PS[:, ob:ob + 2, :], op=ALU.add)
                k.dma("sp", xr_t[t], xt[:], [xtk], [f"xr{t}"])
            uTp, uTpk = uT, uTk
        k.pop()
    elif dbg == "ssd":
        pass

    k.push()
    x_res = k.sb("x_res", [128, NT, D], F32)
    KT = k.sb("KT", [128, 8, MEM], BF16)
    V = k.sb("V", [128, 2, D], BF16)
    with_kv = True
    if with_kv:
        k.push()
        load_gains([1])
        load_gains([2])
        WA3 = k.sb("WA3", [128, 16 * D], BF16)
        wq = WA3[:, 0:8 * D].rearrange("p (a b) -> p a b", a=8)
        wo = WA3[:, 8 * D:16 * D].rearrange("p (a b) -> p a b", a=8)
        for t in range(4):
            k.dma("sp", x_res[:, t, :], src_t[t], [f"xr{t}"], [f"xres{t}"])
        for qq in range(2):
            k.dma("pool", wq[:, :, qq * 512:(qq + 1) * 512], w_q.rearrange("(kc p) n -> p kc n", p=128)[:, :, qq * 512:(qq + 1) * 512], [], [f"WAq{qq}"])
        WA = k.sb("WA", [128, 8 * 2048], BF16)
        wkv = WA[:, 0:8 * 2048].rearrange("p (a b) -> p a b", a=8)
        w_kv_v = w_kv.rearrange("(kc p) n -> p kc n", p=128)
        for qq in range(4):
            k.dma("pool", wkv[:, :, qq * 512:(qq + 1) * 512], w_kv_v[:, :, qq * 512:(qq + 1) * 512], ["WAq1"], [f"WAkv{qq}"])
        for t in range(4, NT):
            k.dma("sp", x_res[:, t, :], src_t[t], [f"xr{t}", "WAkv1" if t < 8 else "WAkv3"], [f"xres{t}"])
        memT = k.sb("memT", [128, 8, MEM], BF16)
        mrot = Rot(k, "memx", 1, [128, D], F32)
        for mc in range(2):
            mx, mxk = mrot.next()
            k.dma("sp", mx[:], mem_d[mc * 128:(mc + 1) * 128, :], [], [mxk])
            hb, hk = h_rot.next()
            norm_tile(mx[:], mxk, 2, hb[:], hk)
            transpose_to(hb, hk, memT, "memT", mc * 128, 0)
        for c in range(8):
            bank = 1 + (c % 2)
            k.mm([(pb(bank)[:, 0:MEM], wkv[:, kc, c * 128:(c + 1) * 128], memT[:, kc, :], kc == 0, kc == 7)
                  for kc in range(8)], [f"WAkv{c // 4}", "memT"], [pbk(bank)])
            k.act([pbk(bank)], ["KT"], out=KT[:, c, :], in_=pb(bank)[:, 0:MEM], func=AF.Copy)
        for mc in range(2):
            for hf2 in range(2):
                bank = 3 + hf2
                k.mm([(pb(bank), memT[:, kc, mc * 128:(mc + 1) * 128],
                       wkv[:, kc, D + hf2 * 512:D + (hf2 + 1) * 512], kc == 0, kc == 7) for kc in range(8)],
                     [f"WAkv{2 + hf2}", "memT"], [pbk(bank)])
                k.dve("tensor_copy", [pbk(bank)], ["V"], out=V[:, mc, hf2 * 512:(hf2 + 1) * 512], in_=pb(bank))

    for qq in range(2):
        k.dma("pool", wo[:, :, qq * 512:(qq + 1) * 512], w_o.rearrange("(kc p) n -> p kc n", p=128)[:, :, qq * 512:(qq + 1) * 512], [f"WAkv{3}"], [f"WAo{qq}"])
    hxT_rot = Rot(k, "hxT", 2, [128, 8, MT], BF16)
    qT = k.sb("qT", [128, 8, MT], BF16)
    ET_rot = Rot(k, "ET", 2, [128, 2, MT], BF16)
    rden_rot = Rot(k, "rden", 2, [128, MT], F32)
    oT = k.sb("oT", [128, 8, MT], BF16)
    for m in range(NT // 4):
        hxT, hxk = hxT_rot.next()
        for st in range(4):
            t = m * 4 + st
            hb, hk = h_rot.next()
            norm_tile(x_res[:, t, :], f"xres{t}", 1, hb[:], hk)
            transpose_to(hb, hk, hxT, hxk, st * 128, 0)
        for c in range(8):
            bank = 1 + (c % 2)
            k.mm([(pb(bank), wq[:, kc, c * 128:(c + 1) * 128], hxT[:, kc, :], kc == 0, kc == 7) for kc in range(8)],
                 [f"WAq{c // 4}", hxk], [pbk(bank)])
            k.act([pbk(bank)], [f"qT{c}"], out=qT[:, c, :], in_=pb(bank), func=AF.Copy)
        for hd in range(4):
            ET, etk = ET_rot.next()
            for mc in range(2):
                bank = 3 + mc
                k.mm([(pb(bank), KT[:, 2 * hd + dc, mc * 128:(mc + 1) * 128], qT[:, 2 * hd + dc, :], dc == 0, dc == 1)
                      for dc in range(2)], ["KT", f"qT{2 * hd}", f"qT{2 * hd + 1}"], [pbk(bank)])
                k.act([pbk(bank)], [etk], out=ET[:, mc, :], in_=pb(bank), func=AF.Exp, scale=1.0 / 16.0)
            k.mm([(pb(5), ones_bf[:], ET[:, mc, :], mc == 0, mc == 1) for mc in range(2)], ["ones_bf", etk], [pbk(5)])
            rden, rdk = rden_rot.next()
            k.act([pbk(5)], [rdk], out=rden[:], in_=pb(5), func=AF.Ln)
            k.act([rdk], [rdk], out=rden[:], in_=rden[:], func=AF.Exp, scale=-1.0)
            for dc in range(2):
                bank = 6 + dc
                k.mm([(pb(bank), V[:, mc, hd * 256 + dc * 128:hd * 256 + (dc + 1) * 128], ET[:, mc, :], mc == 0, mc == 1)
                      for mc in range(2)], ["V", etk], [pbk(bank)])
                k.dve("tensor_tensor", [pbk(bank), rdk], [f"oT{2 * hd + dc}"], out=oT[:, 2 * hd + dc, :], in0=pb(bank),
                      in1=rden[:], op=ALU.mult)
        for st in range(4):
            t = m * 4 + st
            for hf2 in range(2):
                bank = 1 + hf2
                OP3 = int(os.environ.get("OP3", "1"))
                per3 = 8 // OP3
                for part in range(OP3):
                    kcs = range(part * per3, (part + 1) * per3)
                    k.mm([(pb(bank), oT[:, kc, st * 128:(st + 1) * 128], wo[:, kc, hf2 * 512:(hf2 + 1) * 512], kc == 0, kc == 7)
                          for kc in kcs], [f"WAo{hf2}"] + [f"oT{c}" for c in kcs], [pbk(bank)])
                k.dve("tensor_tensor", [pbk(bank), f"xres{t}"], [f"xres{t}"], out=x_res[:, t, hf2 * 512:(hf2 + 1) * 512],
                      in0=x_res[:, t, hf2 * 512:(hf2 + 1) * 512], in1=pb(bank), op=ALU.add)

    k.pop()
    k.push()
    load_gains([3])
    hfT = k.sb("hfT", [128, 8, HALF], BF16)
    groups = [(0, 4), (4, 4), (8, 4), (12, 4), (16, 3), (19, 3)]
    wg_rot = Rot(k, "wg", 2, [128, 8, 4 * 128], BF16)
    wu_rot = Rot(k, "wu", 2, [128, 8, 4 * 128], BF16)
    wd_rot = Rot(k, "wd", 2, [128, 4, D], BF16)
    sg_rot = Rot(k, "sg", 2, [128, MT], BF16)
    aT_rot = Rot(k, "aT", 2, [128, 4, MT], BF16)
    for gi, (c0, nch) in enumerate(groups):
        wg, wgk = wg_rot.next()
        wu, wuk = wu_rot.next()
        wd, wdk = wd_rot.next()
        k.dma("pool", wg[:, :, 0:nch * 128], w_gate.rearrange("(kc p) n -> p kc n", p=128)[:, :, c0 * 128:(c0 + nch) * 128], [], [wgk])
        k.dma("pool", wu[:, :, 0:nch * 128], w_up.rearrange("(kc p) n -> p kc n", p=128)[:, :, c0 * 128:(c0 + nch) * 128], [], [wuk])
        k.dma("pool", wd[:, 0:nch, :], w_down[c0 * 128:(c0 + nch) * 128, :].rearrange("(c p) n -> p c n", p=128), [], [wdk])
        for m in range(NT // 4):
            if gi == 0:
                for st in range(4):
                    t = m * 4 + st
                    hb, hk = h_rot.next()
                    norm_tile(x_res[:, t, :], f"xres{t}", 3, hb[:], hk)
                    transpose_to(hb, hk, hfT, f"hfT{m}", t * 128, 0)
            aT, aTk = aT_rot.next()
            for ci in range(nch):
                bg = 1 + (ci % 2)
                bu = 3 + (ci % 2)
                k.mm([(pb(bg), wg[:, kc, ci * 128:(ci + 1) * 128], hfT[:, kc, m * MT:(m + 1) * MT], kc == 0, kc == 7)
                      for kc in range(8)], [wgk, f"hfT{m}"], [pbk(bg)])
                k.mm([(pb(bu), wu[:, kc, ci * 128:(ci + 1) * 128], hfT[:, kc, m * MT:(m + 1) * MT], kc == 0, kc == 7)
                      for kc in range(8)], [wuk, f"hfT{m}"], [pbk(bu)])
                sg, sgk = sg_rot.next()
                k.act([pbk(bg)], [sgk], out=sg[:], in_=pb(bg), func=AF.Silu)
                k.dve("tensor_tensor", [pbk(bu), sgk], [aTk], out=aT[:, ci, :], in0=pb(bu), in1=sg[:], op=ALU.mult)
            for st in range(4):
                t = m * 4 + st
                for hf2 in range(2):
                    bank = 5 + hf2
                    k.mm([(pb(bank), aT[:, ci, st * 128:(st + 1) * 128], wd[:, ci, hf2 * 512:(hf2 + 1) * 512], ci == 0, ci == nch - 1)
                          for ci in range(nch)], [wdk, aTk], [pbk(bank)])
                    k.dve("tensor_tensor", [pbk(bank), f"xres{t}"], [f"xres{t}"],
                          out=x_res[:, t, hf2 * 512:(hf2 + 1) * 512], in0=x_res[:, t, hf2 * 512:(hf2 + 1) * 512],
                          in1=pb(bank), op=ALU.add)

    load_gains([4])
    o_rot = Rot(k, "ob", 2, [128, D], F32)
    for t in range(NT):
        ob, obk = o_rot.next()
        ss, sskey = ss_rot.next()
        sq, sqkey = sq_rot.next()
        k.act([f"xres{t}"], [sqkey, sskey], out=sq[:], in_=x_res[:, t, :], func=AF.Square, accum_out=ss[:, 0:1])
        rstd_from_ss(ss, sskey, 1, 1.0 / D)
        k.dve("scalar_tensor_tensor", [f"xres{t}", sskey, gains[4][1]], [obk], out=ob[:], in0=x_res[:, t, :], scalar=ss[:, 0:1],
              in1=gains[4][0], op0=ALU.mult, op1=ALU.mult)
        k.dma("sp", out_t[t], ob[:], [obk], [])
    S.emit()
    k.pop()
    k.pop()
    k.st.close()
    return nc


def make_inputs(inp, dbg=None):
    f = np.float32
    x = np.asarray(inp["x"], f)
    mem = np.asarray(inp["mem"], f)

    def row_bc(v):
        return np.broadcast_to(np.asarray(v, f).reshape(1, -1), (128, np.asarray(v).size))

    gbc = np.stack([row_bc(inp["norm_mix_g"][0]), row_bc(inp["norm_xattn_g"][0]), row_bc(inp["norm_mem_g"][0]),
                    row_bc(inp["norm_ffn_g"][0]), row_bc(inp["norm_final_g"]), row_bc(inp["ssd_norm_g"][0])], axis=1)
    dbc = row_bc(np.repeat(np.asarray(inp["ssd_D"][0], f), 64))
    small = np.zeros((128, 64), f)
    small[:, 0:16] = row_bc(inp["ssd_dt_bias"][0])
    small[:, 16:32] = row_bc(inp["ssd_A_log"][0])
    cw = np.asarray(inp["ssd_conv_w"][0], f).reshape(4, 12, 128).transpose(2, 1, 0)
    cb = np.asarray(inp["ssd_conv_b"][0], f).reshape(12, 128).T
    fw = np.asarray(inp["cf_conv_w"][0], f).reshape(31, 8, 128).transpose(2, 1, 0)
    fv = np.stack([np.asarray(inp[n][0], f).reshape(8, 128).T for n in ("cf_conv_b", "cf_ln_g", "cf_ln_b")], axis=2)
    ident = np.eye(128, dtype=f).astype(ml_dtypes.bfloat16)
    j = np.arange(128)
    tri = (j[:, None] <= j[None, :]).astype(f)
    su = (j[:, None] > j[None, :]).astype(f)
    cst = np.stack([tri, su, np.ones((128, 128), f), tri], axis=1)
    common = {
        "w_in": np.ascontiguousarray(inp["w_in"][0], f), "w_out": np.ascontiguousarray(inp["w_out"][0], f),
        "w_q": np.ascontiguousarray(inp["w_q"][0], f), "w_kv": np.ascontiguousarray(inp["w_kv"][0], f),
        "w_o": np.ascontiguousarray(inp["w_o"][0], f), "w_gate": np.ascontiguousarray(inp["w_gate"][0], f),
        "w_up": np.ascontiguousarray(inp["w_up"][0], f), "w_down": np.ascontiguousarray(inp["w_down"][0], f),
        "gbc": np.ascontiguousarray(gbc), "dbc": np.ascontiguousarray(dbc), "small": small,
        "cw": np.ascontiguousarray(cw), "cb": np.ascontiguousarray(cb), "fw": np.ascontiguousarray(fw),
        "fv": np.ascontiguousarray(fv), "ident": ident, "cst": np.ascontiguousarray(cst),
    }
    maps = []
    for c in range(8):
        b, hf = c // 2, c % 2
        d = dict(common)
        d["x_own"] = np.ascontiguousarray(x[b, hf * HALF:(hf + 1) * HALF])
        d["x_prev"] = np.ascontiguousarray(x[b, 0:HALF]) if hf == 1 else np.zeros((HALF, D), f)
        d["flag"] = np.full((128, 1), float(hf), f)
        d["mem"] = np.ascontiguousarray(mem[b])
        maps.append(d)
    return maps


_NC_CACHE = {}


def kernel(_dbg=None, **inputs):
    if _dbg not in _NC_CACHE:
        _NC_CACHE[_dbg] = build(_dbg)
    nc = _NC_CACHE[_dbg]
    maps = make_inputs(inputs, _dbg)
    res = run_bass_kernel_spmd(nc, maps, core_ids=list(range(8)))
    out = np.zeros((NB, SEQ, D), np.float32)
    for c in range(8):
        b, hf = c // 2, c % 2
        out[b, hf * HALF:(hf + 1) * HALF] = res.results[c]["out"]
    return out
```
